# Optimizing a Trainium2 kernel written in Bass

```python
import math
import jax, jax.numpy as jnp
from jax import lax
import numpy as np

D_MODEL = 1024
BATCH = 2
SEQ = 8192
DEPTH = 1

MEM_LEN = 256
EPS = 1e-6
CHUNK = 64

SSD_D_INNER = D_MODEL
SSD_HEAD_DIM = 64
SSD_HEADS = SSD_D_INNER // SSD_HEAD_DIM
SSD_GROUPS = 2
SSD_STATE = 128
SSD_CONV = 4
SSD_CONV_CH = SSD_D_INNER + 2 * SSD_GROUPS * SSD_STATE
DT_MIN = 1e-3
DT_MAX = 1e-1

GLA_HEADS = 4
GLA_DK = D_MODEL // 2
GLA_DV = D_MODEL
GLA_HEAD_K = GLA_DK // GLA_HEADS
GLA_HEAD_V = GLA_DV // GLA_HEADS
GLA_GATE_RANK = 16
GLA_TAU = 16.0

XATTN_HEADS = 4
XATTN_HEAD_DIM = D_MODEL // XATTN_HEADS

D_FF = -((-8 * D_MODEL) // (3 * 256)) * 256

IN_SIZES = (SSD_D_INNER, SSD_CONV_CH, SSD_HEADS,
            GLA_DK, GLA_DK, GLA_DV, GLA_DV, GLA_GATE_RANK,
            D_MODEL, D_MODEL)
IN_WIDTH = sum(IN_SIZES)

kernel_name = "hybrid_ssd_gla_gated_merge_block"


def rmsnorm(x, g):
    xf = x.astype(jnp.float32)
    y = xf * lax.rsqrt(jnp.mean(xf * xf, axis=-1, keepdims=True) + EPS)
    return (y * g.astype(jnp.float32)).astype(x.dtype)


def causal_dwconv(x, w, b):
    y = lax.conv_general_dilated(x, w, window_strides=(1,), padding=[(SSD_CONV - 1, 0)],
                                 dimension_numbers=('NWC', 'WIO', 'NWC'),
                                 feature_group_count=x.shape[-1])
    return y + b


def segsum_exp(a):
    L = a.shape[-1]
    cs = jnp.cumsum(a, axis=-1)
    mask = jnp.tril(jnp.ones((L, L), dtype=bool))
    diff = cs[..., :, None] - cs[..., None, :]
    return jnp.where(mask, jnp.exp(jnp.where(mask, diff, 0.0)), 0.0)


def ssd_mixer(z, xBC, dt_raw, conv_w, conv_b, dt_bias, A_log, D_skip, norm_g):
    Bsz, S, _ = z.shape
    nc = S // CHUNK
    G, R, P, N = SSD_GROUPS, SSD_HEADS // SSD_GROUPS, SSD_HEAD_DIM, SSD_STATE
    xBC = jax.nn.silu(causal_dwconv(xBC, conv_w, conv_b)).astype(jnp.float32)
    xs, Bm, Cm = jnp.split(xBC, [SSD_D_INNER, SSD_D_INNER + G * N], axis=-1)
    dt = jax.nn.softplus(dt_raw.astype(jnp.float32) + dt_bias.astype(jnp.float32))
    A = -jnp.exp(A_log.astype(jnp.float32))
    x_heads = xs.reshape(Bsz, S, SSD_HEADS, P)
    X = (x_heads * dt[..., None]).reshape(Bsz, nc, CHUNK, G, R, P)
    a = (dt * A).reshape(Bsz, nc, CHUNK, G, R).transpose(0, 3, 4, 1, 2)
    Bc = Bm.reshape(Bsz, nc, CHUNK, G, N)
    Cc = Cm.reshape(Bsz, nc, CHUNK, G, N)
    a_cs = jnp.cumsum(a, axis=-1)
    Lmat = segsum_exp(a)
    CB = jnp.einsum('bclgn,bcsgn->bcgls', Cc, Bc)
    y_diag = jnp.einsum('bcgls,bgrcls,bcsgrp->bclgrp', CB, Lmat, X)
    decay_states = jnp.exp(a_cs[..., -1:] - a_cs)
    states = jnp.einsum('bclgn,bgrcl,bclgrp->bcgrpn', Bc, decay_states, X)
    chunk_decay = jnp.exp(a_cs[..., -1])

    def chunk_step(state, inp):
        st_c, dec_c = inp
        return state * dec_c[..., None, None] + st_c, state

    init = jnp.zeros((Bsz, G, R, P, N), jnp.float32)
    _, prev = lax.scan(chunk_step, init, (jnp.moveaxis(states, 1, 0), jnp.moveaxis(chunk_decay, 3, 0)))
    prev = jnp.moveaxis(prev, 0, 1)
    y_off = jnp.einsum('bclgn,bcgrpn,bgrcl->bclgrp', Cc, prev, jnp.exp(a_cs))
    y = (y_diag + y_off).reshape(Bsz, S, SSD_HEADS, P) + D_skip.astype(jnp.float32)[:, None] * x_heads
    y = y.reshape(Bsz, S, SSD_D_INNER) * jax.nn.silu(z.astype(jnp.float32))
    y = rmsnorm(y.reshape(Bsz, S, G, SSD_D_INNER // G), norm_g.reshape(G, SSD_D_INNER // G))
    return y.reshape(Bsz, S, SSD_D_INNER).astype(z.dtype)


def gla_mixer(q, k, v, r, a1, w_a2, b_a, norm_g):
    Bsz, S, _ = q.shape
    nc = S // CHUNK
    H, dk, dv = GLA_HEADS, GLA_HEAD_K, GLA_HEAD_V
    log_alpha = jax.nn.log_sigmoid((a1 @ w_a2 + b_a).astype(jnp.float32)) / GLA_TAU
    qc = q.astype(jnp.float32).reshape(Bsz, nc, CHUNK, H, dk) * (dk ** -0.5)
    kc = k.astype(jnp.float32).reshape(Bsz, nc, CHUNK, H, dk)
    vc = v.astype(jnp.float32).reshape(Bsz, nc, CHUNK, H, dv)
    bcum = jnp.cumsum(log_alpha.reshape(Bsz, nc, CHUNK, H, dk), axis=2)
    b_last = bcum[:, :, -1:]
    q_t = qc * jnp.exp(bcum)
    k_t = kc * jnp.exp(-bcum)
    k_h = kc * jnp.exp(b_last - bcum)
    mask = jnp.tril(jnp.ones((CHUNK, CHUNK), dtype=bool))
    att = jnp.where(mask, jnp.einsum('bclhd,bcshd->bchls', q_t, k_t), 0.0)
    o_intra = jnp.einsum('bchls,bcshv->bclhv', att, vc)
    U = jnp.einsum('bclhd,bclhv->bchdv', k_h, vc)
    dec = jnp.exp(b_last[:, :, 0])

    def chunk_step(state, inp):
        u_c, d_c = inp
        return state * d_c[..., None] + u_c, state

    init = jnp.zeros((Bsz, H, dk, dv), jnp.float32)
    _, prev = lax.scan(chunk_step, init, (jnp.moveaxis(U, 1, 0), jnp.moveaxis(dec, 1, 0)))
    prev = jnp.moveaxis(prev, 0, 1)
    o_inter = jnp.einsum('bclhd,bchdv->bclhv', q_t, prev)
    o = (o_intra + o_inter).reshape(Bsz, S, H, dv)
    o = rmsnorm(o, norm_g).reshape(Bsz, S, GLA_DV)
    return (o * jax.nn.silu(r.astype(jnp.float32))).astype(q.dtype)


def hybrid_mixer(n, w_in, conv_w, conv_b, dt_bias, A_log, D_skip, ssd_norm,
                 gla_w_a2, gla_b_a, gla_norm, w_up_ssd, w_up_gla, w_o):
    proj = n @ w_in
    idx = np.cumsum(IN_SIZES)[:-1].tolist()
    z, xBC, dt_raw, q, k, v, r, a1, g_ssd, g_gla = jnp.split(proj, idx, axis=-1)
    y_ssd = ssd_mixer(z, xBC, dt_raw, conv_w, conv_b, dt_bias, A_log, D_skip, ssd_norm)
    y_gla = gla_mixer(q, k, v, r, a1, gla_w_a2, gla_b_a, gla_norm)
    merged = jax.nn.sigmoid(g_ssd) * (y_ssd @ w_up_ssd) + jax.nn.sigmoid(g_gla) * (y_gla @ w_up_gla)
    return merged @ w_o


def mem_cross_attention(n, m, w_xq, w_xkv, w_xo):
    Bsz, S, _ = n.shape
    q = (n @ w_xq).reshape(Bsz, S, XATTN_HEADS, XATTN_HEAD_DIM)
    k, v = jnp.split(m @ w_xkv, 2, axis=-1)
    k = k.reshape(Bsz, -1, XATTN_HEADS, XATTN_HEAD_DIM)
    v = v.reshape(Bsz, -1, XATTN_HEADS, XATTN_HEAD_DIM)
    s = jnp.einsum('bshd,bmhd->bhsm', q, k).astype(jnp.float32) * (XATTN_HEAD_DIM ** -0.5)
    p = jax.nn.softmax(s, axis=-1).astype(v.dtype)
    o = jnp.einsum('bhsm,bmhd->bshd', p, v).reshape(Bsz, S, D_MODEL)
    return o @ w_xo


def swiglu(n, w_ffn_in, w_ffn_out):
    g, u = jnp.split(n @ w_ffn_in, 2, axis=-1)
    return (jax.nn.silu(g) * u) @ w_ffn_out


def setup_inputs(seed: int = 0) -> dict:
    key = jax.random.key(seed)
    ks = jax.random.split(key, 26)
    f32 = jnp.float32
    L = DEPTH

    def nrm(k, shape, scale):
        return jax.random.normal(k, shape, f32) * scale

    def gain(k, shape):
        return 1.0 + 0.02 * jax.random.normal(k, shape, f32)

    dt = jnp.exp(jax.random.uniform(ks[6], (L, SSD_HEADS), f32, math.log(DT_MIN), math.log(DT_MAX)))
    dt_bias = dt + jnp.log(-jnp.expm1(-dt))
    A_log = jnp.log(jax.random.uniform(ks[7], (L, SSD_HEADS), f32, 1.0, 16.0))
    return {
        "x": jax.random.normal(ks[0], (BATCH, SEQ, D_MODEL), f32),
        "mem": jax.random.normal(ks[1], (BATCH, MEM_LEN, D_MODEL), f32),
        "norm_mix": gain(ks[2], (L, D_MODEL)),
        "w_in": nrm(ks[3], (L, D_MODEL, IN_WIDTH), D_MODEL ** -0.5),
        "ssd_conv_w": nrm(ks[4], (L, SSD_CONV, 1, SSD_CONV_CH), SSD_CONV ** -0.5),
        "ssd_conv_b": nrm(ks[5], (L, SSD_CONV_CH), 0.02),
        "ssd_dt_bias": dt_bias,
        "ssd_A_log": A_log,
        "ssd_D": 1.0 + 0.1 * jax.random.normal(ks[8], (L, SSD_HEADS), f32),
        "ssd_norm": gain(ks[9], (L, SSD_D_INNER)),
        "gla_w_a2": nrm(ks[10], (L, GLA_GATE_RANK, GLA_DK), GLA_GATE_RANK ** -0.5),
        "gla_b_a": nrm(ks[11], (L, GLA_DK), 0.02),
        "gla_norm": gain(ks[12], (L, GLA_HEAD_V)),
        "w_up_ssd": nrm(ks[13], (L, SSD_D_INNER, D_MODEL), SSD_D_INNER ** -0.5),
        "w_up_gla": nrm(ks[14], (L, GLA_DV, D_MODEL), GLA_DV ** -0.5),
        "w_o": nrm(ks[15], (L, D_MODEL, D_MODEL), D_MODEL ** -0.5),
        "norm_xattn": gain(ks[16], (L, D_MODEL)),
        "norm_mem": gain(ks[17], (L, D_MODEL)),
        "w_xq": nrm(ks[18], (L, D_MODEL, D_MODEL), D_MODEL ** -0.5),
        "w_xkv": nrm(ks[19], (L, D_MODEL, 2 * D_MODEL), D_MODEL ** -0.5),
        "w_xo": nrm(ks[20], (L, D_MODEL, D_MODEL), D_MODEL ** -0.5),
        "norm_ffn": gain(ks[21], (L, D_MODEL)),
        "w_ffn_in": nrm(ks[22], (L, D_MODEL, 2 * D_FF), D_MODEL ** -0.5),
        "w_ffn_out": nrm(ks[23], (L, D_FF, D_MODEL), D_FF ** -0.5),
        "norm_final": gain(ks[24], (D_MODEL,)),
    }


def reference(x, mem, norm_mix, w_in, ssd_conv_w, ssd_conv_b, ssd_dt_bias, ssd_A_log, ssd_D, ssd_norm,
              gla_w_a2, gla_b_a, gla_norm, w_up_ssd, w_up_gla, w_o, norm_xattn, norm_mem, w_xq, w_xkv,
              w_xo, norm_ffn, w_ffn_in, w_ffn_out, norm_final):
    h = x
    for i in range(DEPTH):
        h = h + hybrid_mixer(rmsnorm(h, norm_mix[i]), w_in[i], ssd_conv_w[i], ssd_conv_b[i], ssd_dt_bias[i],
                             ssd_A_log[i], ssd_D[i], ssd_norm[i], gla_w_a2[i], gla_b_a[i], gla_norm[i],
                             w_up_ssd[i], w_up_gla[i], w_o[i])
        h = h + mem_cross_attention(rmsnorm(h, norm_xattn[i]), rmsnorm(mem, norm_mem[i]),
                                    w_xq[i], w_xkv[i], w_xo[i])
        h = h + swiglu(rmsnorm(h, norm_ffn[i]), w_ffn_in[i], w_ffn_out[i])
    return rmsnorm(h, norm_final)
```

```python
import numpy as np
from contextlib import ExitStack
import concourse.bass as bass
import concourse.mybir as mybir
from concourse.bass_utils import run_bass_kernel_spmd

F32 = mybir.dt.float32
BF16 = mybir.dt.bfloat16
AF = mybir.ActivationFunctionType
ALU = mybir.AluOpType

NCORES = 8
D = 1024
SEG = 2048
NT = 512
HALO = 128
EPS = 1e-6
EPOCH = 8000
import os
DBG = bool(os.environ.get('KDBG'))

O_Z, O_XBC, O_DT, O_Q, O_K, O_V, O_R, O_A1, O_GS, O_GG = 0, 1024, 2560, 2576, 3088, 3600, 4624, 5648, 5664, 6688
DFF = 2816

C_GMIX, C_GXA, C_GFFN, C_GFIN = 0, 8, 16, 24
C_CONVW = 32
C_CONVB = 80
C_DTB, C_ALOG, C_DSK = 92, 108, 124
C_SSDN = 140
C_GLAN = 1164
C_MEMN = 1420
C_CMASK = 2444
NCST = 2452


class Op:
    __slots__ = ("eng", "fn", "deps", "sig", "signal", "is_dma", "dkey", "ndma", "name", "inc")


class Sched:
    ENGS = ("pe", "act", "dve", "pool", "sp")

    def __init__(self):
        self.ops = []
        self.recs = {}

    @staticmethod
    def _k(k):
        return (k, 0, 1) if isinstance(k, str) else k

    def add(self, eng, fn, reads=(), writes=(), dma=False, dkey=None, ndma=1, name="", inc=16):
        op = Op()
        op.eng, op.fn, op.is_dma, op.dkey, op.ndma, op.inc, op.name = eng, fn, dma, dkey, ndma, inc, name
        op.sig = False
        op.signal = None
        deps = []
        seen = set()

        def push(d, raw):
            if d is None or d is op or id(d) in seen:
                return
            if not raw and not (d.is_dma or dma or d.eng != eng):
                return
            seen.add(id(d))
            deps.append(d)

        for k in reads:
            a, lo, hi = self._k(k)
            for r in self.recs.get(a, ()):
                if r[0] < hi and lo < r[1]:
                    push(r[2], True)
                    r[3].append(op)
        for k in writes:
            a, lo, hi = self._k(k)
            lst = self.recs.setdefault(a, [])
            new = []
            for r in lst:
                if r[0] < hi and lo < r[1]:
                    push(r[2], False)
                    for rd in r[3]:
                        push(rd, False)
                    if r[0] < lo:
                        new.append([r[0], lo, r[2], list(r[3])])
                    if hi < r[1]:
                        new.append([hi, r[1], r[2], list(r[3])])
                else:
                    new.append(r)
            new.append([lo, hi, op, []])
            self.recs[a] = new
        if eng == "pe" and not dma:
            deps = [d for d in deps if d.is_dma or d.eng != "pe"]
        op.deps = deps
        for d in deps:
            d.sig = True
        self.ops.append(op)
        return op

    def emit(self, nc, stack):
        eng_count = {e: 0 for e in self.ENGS}
        eng_sems = {e: [] for e in self.ENGS}
        dma_sems = {}
        dma_vals = {}
        for op in self.ops:
            if op.is_dma:
                if op.dkey not in dma_sems:
                    dma_sems[op.dkey] = stack.enter_context(nc.semaphore("d%d" % len(dma_sems)))
                    dma_vals[op.dkey] = 0
                dma_vals[op.dkey] += op.inc * op.ndma
                op.signal = (dma_sems[op.dkey], dma_vals[op.dkey])
            elif op.sig:
                c = eng_count[op.eng]
                ep = c // EPOCH
                if ep >= len(eng_sems[op.eng]):
                    eng_sems[op.eng].append(stack.enter_context(nc.semaphore("e%s%d" % (op.eng, ep))))
                op.signal = (eng_sems[op.eng][ep], c % EPOCH + 1)
                eng_count[op.eng] = c + 1
        by_eng = {e: [o for o in self.ops if o.eng == e] for e in self.ENGS}
        finals = {}
        for op in self.ops:
            if op.is_dma and op.name.startswith("OUT"):
                sem, val = op.signal
                if finals.get(id(sem), (None, 0))[1] < val:
                    finals[id(sem)] = (sem, val)

        def run(engh, ename):
            waited = {}
            for op in by_eng[ename]:
                for d in op.deps:
                    sem, val = d.signal
                    if waited.get(id(sem), 0) < val:
                        engh.wait_ge(sem, val)
                        waited[id(sem)] = val
                r = op.fn(engh)
                if op.is_dma:
                    assert len(r) == op.ndma, (op.name, len(r), op.ndma)
                    for ins in r:
                        ins.then_inc(op.signal[0], op.inc)
                elif op.sig:
                    r.then_inc(op.signal[0], 1)
            if ename == "sp":
                for sem, val in finals.values():
                    engh.wait_ge(sem, val)

        with nc.Block() as block:
            @block.tensor
            def _(e):
                run(e, "pe")

            @block.scalar
            def _(e):
                run(e, "act")

            @block.vector
            def _(e):
                run(e, "dve")

            @block.gpsimd
            def _(e):
                run(e, "pool")

            @block.sync
            def _(e):
                run(e, "sp")


class Arena:
    def __init__(self, name, tensor, n):
        self.name, self.t, self.n = name, tensor, n
        self.used = []
        self.peak = 0

    def alloc(self, n):
        n = (n + 15) // 16 * 16
        self.used.sort()
        pos = 0
        for off, sz in self.used:
            if off - pos >= n:
                break
            pos = off + sz
        if pos + n > self.n:
            raise RuntimeError("arena %s full: need %d at %d of %d" % (self.name, n, pos, self.n))
        self.used.append((pos, n))
        self.peak = max(self.peak, pos + n)
        return pos

    def free(self, off):
        self.used = [u for u in self.used if u[0] != off]


class Buf:
    def __init__(self, arena, shape):
        self.arena = arena
        self.shape = list(shape)
        self.n = int(np.prod(shape))
        self.off = arena.alloc(self.n)
        self.inner = self.n // self.shape[0] if len(shape) == 2 else self.n

    def free(self):
        self.arena.free(self.off)

    @property
    def ap(self):
        a = self.arena.t[:, self.off:self.off + self.n]
        if len(self.shape) == 2:
            return a.rearrange("p (a b) -> p a b", a=self.shape[0])
        return a

    def k(self, i=None, j=None):
        if i is None:
            return (self.arena.name, self.off, self.off + self.n)
        if j is None:
            j = i + 1
        return (self.arena.name, self.off + i * self.inner, self.off + j * self.inner)

    def kr(self, lo, hi):
        return (self.arena.name, self.off + lo, self.off + hi)


def build_program():
    nc = bass.Bass("TRN2", target_bir_lowering=False)
    TPS = NT // 128
    NSTEP = SEG // NT

    def din(name, shape):
        return nc.dram_tensor(name, list(shape), F32, kind="ExternalInput").ap()

    NPRE = 3 * SEG
    x_d = din("x_ext", [NPRE + SEG, D])
    mem_d = din("mem_b", [256, D])
    cst_d = din("cst", [128, NCST])
    w2_d = din("w2aug", [17, 512])
    w_in = din("w_in", [D, 7712])
    w_ups = din("w_up_ssd", [D, D])
    w_upg = din("w_up_gla", [D, D])
    w_o = din("w_o", [D, D])
    w_xq = din("w_xq", [D, D])
    w_xkv = din("w_xkv", [D, 2 * D])
    w_xo = din("w_xo", [D, D])
    w_fi = din("w_ffn_in", [D, 2 * DFF])
    w_fo = din("w_ffn_out", [DFF, D])
    out_d = nc.dram_tensor("out", [SEG, D], F32, kind="ExternalOutput").ap()
    dbg_outs = {}

    S = Sched()

    def chain(eng, fns, reads, writes):
        for f in fns:
            S.add(eng, f, reads=list(reads) + list(writes), writes=writes)
    st = ExitStack()
    with st:
        def sb(name, shape, dt):
            return st.enter_context(nc.sbuf_tensor(name, list(shape), dt))

        def pst(name, shape, dt):
            return st.enter_context(nc.psum_tensor(name, list(shape), dt))

        N16 = 70000
        N32 = 18000
        a16 = Arena("A16", sb("A16", [128, N16], BF16), N16)
        a32 = Arena("A32", sb("A32", [128, N32], F32), N32)

        def B16(*shape):
            return Buf(a16, shape)

        def B32(*shape):
            return Buf(a32, shape)

        pa = [pst("pa0", [128, 512], F32), pst("pa1", [128, 512], F32)]
        ptr = pst("ptr", [128, 1024], BF16)
        psm = pst("psm", [128, 512], F32)
        pwa = pst("pwa", [128, 1024], F32)
        pwb = pst("pwb", [128, 1024], F32)
        PWA = ("pwa", 0, 2)
        PWB = ("pwb", 0, 2)
        pa_i = [0]
        pa6_i = [0]
        banks6 = [(pa[0], "pa0"), (pa[1], "pa1"), (pwa[:, 0:512], ("pwa", 0, 1)), (pwa[:, 512:1024], ("pwa", 1, 2)),
                  (pwb[:, 0:512], ("pwb", 0, 1)), (pwb[:, 512:1024], ("pwb", 1, 2))]

        def next_pa():
            i = pa_i[0]
            pa_i[0] ^= 1
            return pa[i], "pa%d" % i

        def next_pa6():
            i = pa6_i[0]
            pa6_i[0] = (i + 1) % 6
            return banks6[i]

        cst = B32(NCST)
        w2a = B32(512)
        idf = B32(128)
        tri = B32(128)
        stri = B32(128)
        trig = B32(128)
        strig = B32(128)
        strif = B32(128)
        strigf = B32(128)
        onesf = B32(128)
        negc = B32(2)
        onesc = B32(2, 128)
        cm = B32(2)
        idb = B16(128)
        ones_mean = B16(128)
        ones1 = B16(128)
        aneg = B32(16)

        S.add("sp", lambda e: [e.dma_start(out=cst.ap, in_=cst_d)], writes=[cst.k()], dma=True, dkey="cst")
        S.add("dve", lambda e: e.memset(w2a.ap, 0.0), writes=[w2a.k()])
        S.add("sp", lambda e: [e.dma_start(out=w2a.ap[0:17, :], in_=w2_d)], writes=[w2a.k()], dma=True, dkey="w2a")

        def mk_c1(e):
            e.memset(idf.ap, 0.0)
            e.memset(tri.ap, 1.0)
            e.memset(stri.ap, 1.0)
            e.memset(onesc.ap, 0.0)
            e.memset(cm.ap, 0.0)
            e.memset(ones_mean.ap, 1.0 / 1024.0)
            e.memset(strif.ap, 1.0)
            e.memset(onesf.ap, 1.0)
            e.memset(negc.ap, -1.0 / 16.0)
            return e.memset(ones1.ap, 1.0)

        def mk_c2(e):
            e.affine_select(out=strif.ap, in_=strif.ap, pattern=[[-1, 128]], compare_op=ALU.is_gt,
                            fill=0.0, base=0, channel_multiplier=1)
            e.affine_select(out=idf.ap, in_=idf.ap, pattern=[[-1, 128]], compare_op=ALU.not_equal,
                            fill=1.0, base=0, channel_multiplier=1)
            e.affine_select(out=tri.ap, in_=tri.ap, pattern=[[1, 128]], compare_op=ALU.is_ge,
                            fill=0.0, base=0, channel_multiplier=-1)
            e.affine_select(out=stri.ap, in_=stri.ap, pattern=[[-1, 128]], compare_op=ALU.is_gt,
                            fill=0.0, base=0, channel_multiplier=1)
            e.memset(onesc.ap[0:64, 0, :], 1.0)
            e.memset(onesc.ap[64:128, 1, :], 1.0)
            e.memset(cm.ap[0:64, 0:1], 1.0)
            return e.memset(cm.ap[64:128, 1:2], 1.0)

        def mk_c3(e):
            e.memset(tri.ap[0:64, 64:128], 0.0)
            return e.memset(stri.ap[64:128, 0:64], 0.0)
        chain("pool", [mk_c1, mk_c2, mk_c3], [], [idf.k(), tri.k(), stri.k(), onesc.k(), cm.k(), ones_mean.k(), ones1.k(),
                                                  strif.k(), onesf.k(), negc.k()])

        def mk_consts2(e):
            e.tensor_copy(out=idb.ap, in_=idf.ap)
            e.tensor_scalar(out=trig.ap, in0=tri.ap, scalar1=-1.0 / 16.0, scalar2=None, op0=ALU.mult)
            e.tensor_scalar(out=strigf.ap, in0=strif.ap, scalar1=-1.0 / 16.0, scalar2=None, op0=ALU.mult)
            return e.tensor_scalar(out=strig.ap, in0=stri.ap, scalar1=-1.0 / 16.0, scalar2=None, op0=ALU.mult)
        S.add("dve", mk_consts2, reads=[idf.k(), tri.k(), stri.k(), strif.k()], writes=[idb.k(), trig.k(), strig.k(), strigf.k()])

        def mk_aneg(e):
            return e.activation(out=aneg.ap, in_=cst.ap[:, C_ALOG:C_ALOG + 16], func=AF.Exp)
        S.add("act", mk_aneg, reads=[cst.k()], writes=[aneg.k()])
        S.add("dve", lambda e: e.tensor_scalar(out=aneg.ap, in0=aneg.ap, scalar1=-1.0, scalar2=None, op0=ALU.mult),
              reads=[aneg.k()], writes=[aneg.k()])

        def cvec(off, n):
            return cst.ap[:, off:off + n]

        NSLOT = 3
        WSZ = 4096
        wslots = [B16(WSZ) for _ in range(NSLOT)]
        wctr = [0]
        SSZ = 1024
        sslots = [B16(SSZ) for _ in range(2)]
        sctr = [0]

        def wload(wd, kc, c0, cw):
            if kc * cw <= SSZ:
                i = NSLOT + sctr[0] % 2
                sctr[0] += 1
                slot = sslots[i - NSLOT]
            else:
                i = wctr[0] % NSLOT
                wctr[0] += 1
                slot = wslots[i]
            assert kc * cw <= WSZ
            dst = slot.ap[:, 0:kc * cw].rearrange("p (k n) -> p k n", k=kc)
            src = wd.rearrange("(k p) n -> p k n", p=128)[:, :, c0:c0 + cw]
            if kc > 8:
                h = kc // 2
                S.add("pool", lambda e: [e.dma_start(out=dst[:, 0:h, :], in_=src[:, 0:h, :]),
                                         e.dma_start(out=dst[:, h:kc, :], in_=src[:, h:kc, :])],
                      writes=[slot.k()], dma=True, dkey="w%d" % i, ndma=2)
            else:
                S.add("pool", lambda e: [e.dma_start(out=dst, in_=src)], writes=[slot.k()], dma=True, dkey="w%d" % i)
            return dst, slot.k()

        def proj_fm(wd, c0, ncols, src, nt, epi, kc=8, blk=512):
            col = 0
            while col < ncols:
                cw = min(blk, ncols - col)
                wap, wkey = wload(wd, kc, c0 + col, cw)
                nch_all = (cw + 127) // 128
                gsz = max(1, 512 // nt)
                for cg in range(0, nch_all, gsz):
                    nch = min(gsz, nch_all - cg)
                    p, pk = next_pa6()
                    pv = p[:, 0:nch * nt].rearrange("p (c n) -> p c n", c=nch)

                    def mm(e, wap=wap, pv=pv, nch=nch, cw=cw, cg=cg):
                        r = None
                        for c in range(nch):
                            cc = cg + c
                            m = min(128, cw - cc * 128)
                            for k in range(kc):
                                r = e.matmul(out=pv[0:m, c, :], lhsT=wap[:, k, cc * 128:cc * 128 + m], rhs=src.ap[:, k, 0:nt],
                                             start=(k == 0), stop=(k == kc - 1))
                        return r
                    S.add("pe", mm, reads=[wkey, src.k()], writes=[pk])
                    epi(col // 128 + cg, nch, pv, pk)
                col += cw

        def proj_tm(wd, c0, ncols, src, ntiles, epi, kc=8):
            col = 0
            while col < ncols:
                cw = min(512, ncols - col)
                wap, wkey = wload(wd, kc, c0 + col, cw)
                for t in range(ntiles):
                    p, pk = next_pa6()

                    def mm(e, wap=wap, p=p, t=t, cw=cw):
                        r = None
                        for k in range(kc):
                            r = e.matmul(out=p[:, 0:cw], lhsT=src.ap[:, k, t * 128:(t + 1) * 128], rhs=wap[:, k, :],
                                         start=(k == 0), stop=(k == kc - 1))
                        return r
                    S.add("pe", mm, reads=[wkey, src.k()], writes=[pk])
                    epi(t, col, cw, p[:, 0:cw], pk)
                col += cw

        hT = B32(8, NT)
        nT = B16(8, NT)
        rstd = B32(NT)
        Sst = B32(1024)
        Sg = B32(1024)
        dtot = B32(20)
        halo3 = B16(12, 3)
        ctpad = [B16(2, 128), B16(2, 128)]
        qtpad = [B16(4, 128), B16(4, 128)]
        a1T = B32(NT)
        KT = B16(8, 256)
        Vx = B16(2, 1024)

        def init_state(e):
            e.memset(Sst.ap, 0.0)
            e.memset(Sg.ap, 0.0)
            e.memset(dtot.ap, 1.0)
            e.memset(ctpad[0].ap, 0.0)
            e.memset(ctpad[1].ap, 0.0)
            e.memset(qtpad[0].ap, 0.0)
            e.memset(qtpad[1].ap, 0.0)
            e.memset(halo3.ap, 0.0)
            return e.memset(a1T.ap, 0.0)
        S.add("pool", init_state, writes=[Sst.k(), Sg.k(), dtot.k(), ctpad[0].k(), ctpad[1].k(), qtpad[0].k(),
                                          qtpad[1].k(), a1T.k(), halo3.k()])
        S.add("dve", lambda e: e.memset(a1T.ap[0:32, :], 1.0), reads=[a1T.k()], writes=[a1T.k()])

        xs_i = [0]

        def load_xT(row0, nt, xreads=()):
            xsb = [B32(1024), B32(1024)]
            for t in range(nt // 128):
                j = xs_i[0]
                xs_i[0] ^= 1
                xs = xsb[j]
                pw, pwk = (pwa, PWA) if j == 0 else (pwb, PWB)
                r0 = row0 + t * 128
                S.add("sp", lambda e, xs=xs, r0=r0: [e.dma_start(out=xs.ap, in_=x_d[r0:r0 + 128, :])],
                      writes=[xs.k()], reads=list(xreads), dma=True, dkey="xs%d" % xs.off)

                def tr(e, xs=xs, pw=pw):
                    r = None
                    for c in range(8):
                        r = e.transpose(out=pw[:, c * 128:(c + 1) * 128], in_=xs.ap[:, c * 128:(c + 1) * 128], identity=idf.ap)
                    return r
                S.add("pe", tr, reads=[xs.k(), idf.k()], writes=[pwk])
                S.add("act", lambda e, t=t, pw=pw: e.activation(out=hT.ap[:, :, t * 128:(t + 1) * 128],
                                                               in_=pw[:, :].rearrange("p (c n) -> p c n", c=8), func=AF.Copy),
                      reads=[pwk], writes=[hT.k()])
            xsb[0].free(); xsb[1].free()

        def norm_fm(goff, nt, stats_only=False):
            sqb = B16(8, NT)
            S.add("act", lambda e: e.activation(out=sqb.ap[:, :, 0:nt], in_=hT.ap[:, :, 0:nt], func=AF.Square),
                  reads=[hT.k()], writes=[sqb.k()])

            def mm(e):
                r = None
                for k in range(8):
                    r = e.matmul(out=psm[:, 0:nt], lhsT=ones_mean.ap, rhs=sqb.ap[:, k, 0:nt], start=(k == 0), stop=(k == 7))
                return r
            S.add("pe", mm, reads=[sqb.k(), ones_mean.k()], writes=["psm"])
            sqb.free()
            S.add("act", lambda e: e.activation(out=rstd.ap[:, 0:nt], in_=psm[:, 0:nt], func=AF.Ln, bias=EPS), reads=["psm"], writes=[rstd.k()])
            S.add("act", lambda e: e.activation(out=rstd.ap[:, 0:nt], in_=rstd.ap[:, 0:nt], func=AF.Exp, scale=-0.5), reads=[rstd.k()], writes=[rstd.k()])
            if stats_only:
                return
            tgt = nT

            def nrm(e):
                r = None
                for k in range(8):
                    r = e.scalar_tensor_tensor(out=tgt.ap[:, k, 0:nt], in0=hT.ap[:, k, 0:nt], scalar=cst.ap[:, goff + k:goff + k + 1],
                                               in1=rstd.ap[:, 0:nt], op0=ALU.mult, op1=ALU.mult)
                return r
            S.add("dve", nrm, reads=[hT.k(), rstd.k(), cst.k()], writes=[tgt.k()])

        def resid_epi(c0, nch, pv, pk):
            S.add("dve", lambda e: e.tensor_tensor(out=hT.ap[:, c0:c0 + nch, :], in0=hT.ap[:, c0:c0 + nch, :], in1=pv, op=ALU.add),
                  reads=[pk, hT.k(c0, c0 + nch)], writes=[hT.k(c0, c0 + nch)])

        def dbg_dump(name, buf, n, dt=F32):
            if not DBG or name not in os.environ.get('KDBG', '').split(','):
                return
            d = nc.dram_tensor("dbg_" + name, [128, n], dt, kind="ExternalOutput").ap()
            dbg_outs[name] = d
            flat = buf.arena.t[:, buf.off:buf.off + n]
            S.add("sp", lambda e: [e.dma_start(out=d, in_=flat)], reads=[buf.k()], dma=True, dkey="dbg_" + name, name="OUTdbg")

        def interleave(gens):
            gens = list(gens)
            while gens:
                for g in list(gens):
                    try:
                        next(g)
                    except StopIteration:
                        gens.remove(g)

        def step(row0, nt, mode, orow0=None, first_dbg=False, xreads=()):
            full = mode == "p2"
            tps = nt // 128
            load_xT(row0, nt, xreads)
            norm_fm(C_GMIX, nt)
            if first_dbg:
                dbg_dump("nT", nT, 8 * NT, BF16)
            xbr = B16(12, nt + 3)

            def xbc_epi(c0, nch, pv, pk):
                S.add("act", lambda e: e.activation(out=xbr.ap[:, c0:c0 + nch, 3:3 + nt], in_=pv, func=AF.Copy),
                      reads=[pk], writes=[xbr.k(c0, c0 + nch)])
            proj_fm(w_in, O_XBC, 1536, nT, nt, xbc_epi)
            if mode == "halo":
                S.add("dve", lambda e: e.tensor_copy(out=halo3.ap, in_=xbr.ap[:, :, nt:nt + 3]), reads=[xbr.k()], writes=[halo3.k()])
                xbr.free()
                return
            S.add("dve", lambda e: e.tensor_copy(out=xbr.ap[:, :, 0:3], in_=halo3.ap), reads=[halo3.k()], writes=[xbr.k()])
            xsT = B16(8, nt)
            BT = B16(2, nt)
            CT = B16(2, nt)
            CW = min(256, nt)
            NG = nt // CW
            caccs = {}

            def conv_start(hh):
                cacc = B32(12, CW)
                caccs[hh] = cacc
                o0 = hh * CW

                def conv0(e):
                    r = None
                    for c in range(12):
                        r = e.activation(out=cacc.ap[:, c, :], in_=xbr.ap[:, c, o0:o0 + CW], func=AF.Copy,
                                         scale=cst.ap[:, C_CONVW + 4 * c:C_CONVW + 4 * c + 1])
                    return r
                S.add("act", conv0, reads=[xbr.k(), cst.k()], writes=[cacc.k()])

                def convk(k):
                    def f(e):
                        r = None
                        for c in range(12):
                            r = e.scalar_tensor_tensor(out=cacc.ap[:, c, :], in0=xbr.ap[:, c, o0 + k:o0 + k + CW],
                                                       scalar=cst.ap[:, C_CONVW + 4 * c + k:C_CONVW + 4 * c + k + 1],
                                                       in1=cacc.ap[:, c, :], op0=ALU.mult, op1=ALU.add)
                        return r
                    return f
                chain("dve", [convk(1), convk(2), convk(3)], [xbr.k(), cst.k()], [cacc.k()])
                if hh == NG - 1:
                    S.add("dve", lambda e: e.tensor_copy(out=halo3.ap, in_=xbr.ap[:, :, nt:nt + 3]), reads=[xbr.k()], writes=[halo3.k()])
                    xbr.free()

            def conv_finish(hh):
                cacc = caccs[hh]
                o0 = hh * CW

                def csilu(e):
                    r = None
                    for c in range(12):
                        dst = xsT.ap[:, c, o0:o0 + CW] if c < 8 else (BT.ap[:, c - 8, o0:o0 + CW] if c < 10 else CT.ap[:, c - 10, o0:o0 + CW])
                        r = e.activation(out=dst, in_=cacc.ap[:, c, :], func=AF.Silu, bias=cst.ap[:, C_CONVB + c:C_CONVB + c + 1])
                    return r
                S.add("act", csilu, reads=[cacc.k(), cst.k()], writes=[xsT.k(), BT.k(), CT.k()])
                cacc.free()
            for hh_ in range(NG):
                conv_start(hh_)
            dtb = B32(tps, 16)
            ab = B32(tps, 16)

            def dt_epi(t, col, cw, p, pk):
                tmp = B32(16)
                S.add("dve", lambda e: e.tensor_tensor(out=tmp.ap, in0=p, in1=cvec(C_DTB, 16), op=ALU.add),
                      reads=[pk, cst.k()], writes=[tmp.k()])

                S.add("act", lambda e: e.activation(out=tmp.ap, in_=tmp.ap, func=AF.Exp), reads=[tmp.k()], writes=[tmp.k()])
                S.add("act", lambda e: e.activation(out=dtb.ap[:, t, :], in_=tmp.ap, func=AF.Ln, bias=1.0), reads=[tmp.k()], writes=[dtb.k(t)])
                S.add("dve", lambda e: e.tensor_tensor(out=ab.ap[:, t, :], in0=dtb.ap[:, t, :], in1=aneg.ap, op=ALU.mult),
                      reads=[dtb.k(t), aneg.k()], writes=[ab.k(t)])
                tmp.free()
            proj_tm(w_in, O_DT, 16, nT, tps, dt_epi)
            zs = None
            if full:
                zs = B16(tps, 1024)

                def z_epi(t, col, cw, p, pk):
                    S.add("act", lambda e: e.activation(out=zs.ap[:, t, col:col + cw], in_=p, func=AF.Silu),
                          reads=[pk], writes=[zs.kr(t * 1024 + col, t * 1024 + col + cw)])
                proj_tm(w_in, O_Z, 1024, nT, tps, z_epi)
            qT = B16(4, nt) if full else None
            kT = B16(4, nt) if full else None
            ktm = B16(tps, 512)
            vtm = B16(tps, 1024)
            rs = B16(tps, 1024) if full else None
            if full:
                def q_epi(c0, nch, pv, pk):
                    S.add("act", lambda e: e.activation(out=qT.ap[:, c0:c0 + nch, :], in_=pv, func=AF.Copy, scale=float(128 ** -0.5)),
                          reads=[pk], writes=[qT.k(c0, c0 + nch)])
                proj_fm(w_in, O_Q, 512, nT, nt, q_epi)
            wap, wkey = wload(w_in, 8, O_K, 512)
            if full:
                gsz = max(1, 512 // nt)
                for cg in range(0, 4, gsz):
                    p, pk = next_pa6()
                    pv = p[:, 0:gsz * nt].rearrange("p (c n) -> p c n", c=gsz)

                    def mmk(e, wap=wap, pv=pv, cg=cg, gsz=gsz):
                        r = None
                        for c in range(gsz):
                            for k in range(8):
                                r = e.matmul(out=pv[:, c, :], lhsT=wap[:, k, (cg + c) * 128:(cg + c + 1) * 128], rhs=nT.ap[:, k, 0:nt],
                                             start=(k == 0), stop=(k == 7))
                        return r
                    S.add("pe", mmk, reads=[wkey, nT.k()], writes=[pk])
                    S.add("act", lambda e, pv=pv, cg=cg, gsz=gsz: e.activation(out=kT.ap[:, cg:cg + gsz, :], in_=pv, func=AF.Copy),
                          reads=[pk], writes=[kT.k(cg, cg + gsz)])
            for t in range(tps):
                p, pk = next_pa6()

                def mmk2(e, wap=wap, p=p, t=t):
                    r = None
                    for k in range(8):
                        r = e.matmul(out=p[:, 0:512], lhsT=nT.ap[:, k, t * 128:(t + 1) * 128], rhs=wap[:, k, :], start=(k == 0), stop=(k == 7))
                    return r
                S.add("pe", mmk2, reads=[wkey, nT.k()], writes=[pk])
                S.add("act", lambda e, p=p, t=t: e.activation(out=ktm.ap[:, t, :], in_=p[:, 0:512], func=AF.Copy), reads=[pk], writes=[ktm.k(t)])

            def a1_epi(c0, nch, pv, pk):
                S.add("act", lambda e: e.activation(out=a1T.ap[0:16, 0:nt], in_=pv[0:16, 0, :], func=AF.Copy), reads=[pk], writes=[a1T.k()])
            proj_fm(w_in, O_A1, 128, nT, nt, a1_epi)

            def v_epi(t, col, cw, p, pk):
                S.add("act", lambda e: e.activation(out=vtm.ap[:, t, col:col + cw], in_=p, func=AF.Copy),
                      reads=[pk], writes=[vtm.kr(t * 1024 + col, t * 1024 + col + cw)])
            proj_tm(w_in, O_V, 1024, nT, tps, v_epi)
            if full:
                def r_epi(t, col, cw, p, pk):
                    S.add("act", lambda e: e.activation(out=rs.ap[:, t, col:col + cw], in_=p, func=AF.Silu),
                          reads=[pk], writes=[rs.kr(t * 1024 + col, t * 1024 + col + cw)])
                proj_tm(w_in, O_R, 1024, nT, tps, r_epi)
            for hh_ in range(NG):
                conv_finish(hh_)
            ynT = B16(8, nt) if full else None
            onT = B16(8, nt) if full else None
            for t in range(tps):
                interleave([ssd_tile(t, xsT, BT, CT, dtb, ab, zs, ynT, full, False),
                            gla_tile(t, qT, kT, ktm, vtm, rs, onT, full, False)])
            xsT.free(); BT.free(); CT.free(); dtb.free(); ab.free()
            if zs is not None:
                zs.free()
            ktm.free(); vtm.free()
            if not full:
                return
            qT.free(); kT.free(); rs.free()
            if first_dbg:
                dbg_dump("onT", onT, 8 * NT, BF16)
            gsT = B16(8, nt)
            ggT = B16(8, nt)
            mT = B16(8, nt)

            def gs_epi(c0, nch, pv, pk):
                S.add("act", lambda e: e.activation(out=gsT.ap[:, c0:c0 + nch, :], in_=pv, func=AF.Sigmoid), reads=[pk], writes=[gsT.k(c0, c0 + nch)])

            def gg_epi(c0, nch, pv, pk):
                S.add("act", lambda e: e.activation(out=ggT.ap[:, c0:c0 + nch, :], in_=pv, func=AF.Sigmoid), reads=[pk], writes=[ggT.k(c0, c0 + nch)])
            proj_fm(w_in, O_GS, 1024, nT, nt, gs_epi)
            proj_fm(w_in, O_GG, 1024, nT, nt, gg_epi)

            def ups_epi(c0, nch, pv, pk):
                S.add("dve", lambda e: e.tensor_tensor(out=gsT.ap[:, c0:c0 + nch, :], in0=gsT.ap[:, c0:c0 + nch, :], in1=pv, op=ALU.mult),
                      reads=[pk, gsT.k(c0, c0 + nch)], writes=[gsT.k(c0, c0 + nch)])
            proj_fm(w_ups, 0, 1024, ynT, nt, ups_epi)

            def upg_epi(c0, nch, pv, pk):
                S.add("dve", lambda e: e.tensor_tensor(out=ggT.ap[:, c0:c0 + nch, :], in0=ggT.ap[:, c0:c0 + nch, :], in1=pv, op=ALU.mult),
                      reads=[pk, ggT.k(c0, c0 + nch)], writes=[ggT.k(c0, c0 + nch)])
                S.add("dve", lambda e: e.tensor_tensor(out=mT.ap[:, c0:c0 + nch, :], in0=ggT.ap[:, c0:c0 + nch, :], in1=gsT.ap[:, c0:c0 + nch, :], op=ALU.add),
                      reads=[ggT.k(c0, c0 + nch), gsT.k(c0, c0 + nch)], writes=[mT.k(c0, c0 + nch)])
            proj_fm(w_upg, 0, 1024, onT, nt, upg_epi)
            ynT.free(); onT.free(); gsT.free(); ggT.free()
            proj_fm(w_o, 0, 1024, mT, nt, resid_epi)
            mT.free()
            if first_dbg:
                dbg_dump("h1", hT, 8 * NT)
            norm_fm(C_GXA, nt)
            qx = B16(8, nt)
            ox = B16(8, nt)

            def qx_epi(c0, nch, pv, pk):
                S.add("act", lambda e: e.activation(out=qx.ap[:, c0:c0 + nch, :], in_=pv, func=AF.Copy, scale=1.0 / 16.0),
                      reads=[pk], writes=[qx.k(c0, c0 + nch)])
            proj_fm(w_xq, 0, 1024, nT, nt, qx_epi)
            for hd in range(4):
                ET = B16(2, nt)
                for mc in range(2):
                    p, pk = next_pa()

                    def sc(e, p=p, hd=hd, mc=mc):
                        r = None
                        for dc in range(2):
                            r = e.matmul(out=p[:, 0:nt], lhsT=KT.ap[:, hd * 2 + dc, mc * 128:(mc + 1) * 128], rhs=qx.ap[:, hd * 2 + dc, :],
                                         start=(dc == 0), stop=(dc == 1))
                        return r
                    S.add("pe", sc, reads=[KT.k(), qx.k(hd * 2, hd * 2 + 2)], writes=[pk])
                    S.add("act", lambda e, p=p, mc=mc, ET=ET: e.activation(out=ET.ap[:, mc, :], in_=p[:, 0:nt], func=AF.Exp),
                          reads=[pk], writes=[ET.k(mc)])

                def den(e, ET=ET):
                    e.matmul(out=psm[:, 0:nt], lhsT=ones1.ap, rhs=ET.ap[:, 0, :], start=True, stop=False)
                    return e.matmul(out=psm[:, 0:nt], lhsT=ones1.ap, rhs=ET.ap[:, 1, :], start=False, stop=True)
                S.add("pe", den, reads=[ET.k(), ones1.k()], writes=["psm"])
                rden = B32(nt)
                S.add("dve", lambda e, rden=rden: e.reciprocal(out=rden.ap, in_=psm[:, 0:nt]), reads=["psm"], writes=[rden.k()])
                for dc in range(2):
                    p, pk = next_pa()

                    def pvm(e, p=p, hd=hd, dc=dc, ET=ET):
                        r = None
                        for mc in range(2):
                            r = e.matmul(out=p[:, 0:nt], lhsT=Vx.ap[:, mc, hd * 256 + dc * 128:hd * 256 + dc * 128 + 128], rhs=ET.ap[:, mc, :],
                                         start=(mc == 0), stop=(mc == 1))
                        return r
                    S.add("pe", pvm, reads=[Vx.k(), ET.k()], writes=[pk])
                    S.add("dve", lambda e, p=p, hd=hd, dc=dc, rden=rden: e.tensor_tensor(out=ox.ap[:, hd * 2 + dc, :], in0=p[:, 0:nt], in1=rden.ap, op=ALU.mult),
                          reads=[pk, rden.k()], writes=[ox.k(hd * 2 + dc)])
                ET.free(); rden.free()
            qx.free()
            proj_fm(w_xo, 0, 1024, ox, nt, resid_epi)
            ox.free()
            if first_dbg:
                dbg_dump("h2", hT, 8 * NT)
            norm_fm(C_GFFN, nt)
            aT = B16(22, nt)
            col = 0
            while col < DFF:
                cw = min(512, DFF - col)

                def g_epi(c0, nch, pv, pk, col=col):
                    cc = col // 128 + c0
                    S.add("act", lambda e: e.activation(out=aT.ap[:, cc:cc + nch, :], in_=pv, func=AF.Silu), reads=[pk], writes=[aT.k(cc, cc + nch)])

                def u_epi(c0, nch, pv, pk, col=col):
                    cc = col // 128 + c0
                    S.add("dve", lambda e: e.tensor_tensor(out=aT.ap[:, cc:cc + nch, :], in0=aT.ap[:, cc:cc + nch, :], in1=pv, op=ALU.mult),
                          reads=[pk, aT.k(cc, cc + nch)], writes=[aT.k(cc, cc + nch)])
                proj_fm(w_fi, col, cw, nT, nt, g_epi)
                proj_fm(w_fi, DFF + col, cw, nT, nt, u_epi)
                col += cw
            proj_fm(w_fo, 0, 1024, aT, nt, resid_epi, kc=22, blk=128)
            aT.free()
            if first_dbg:
                dbg_dump("h3", hT, 8 * NT)
            norm_fm(C_GFIN, nt, stats_only=True)
            for t in range(tps):
                of = B32(8, 128)

                def nrm_t(e, t=t, of=of):
                    r = None
                    for k in range(8):
                        r = e.scalar_tensor_tensor(out=of.ap[:, k, :], in0=hT.ap[:, k, t * 128:(t + 1) * 128], scalar=cst.ap[:, C_GFIN + k:C_GFIN + k + 1],
                                                   in1=rstd.ap[:, t * 128:(t + 1) * 128], op0=ALU.mult, op1=ALU.mult)
                    return r
                S.add("dve", nrm_t, reads=[hT.k(), rstd.k(), cst.k()], writes=[of.k()])

                def tr(e, t=t, of=of):
                    r = None
                    for c in range(8):
                        r = e.transpose(out=pwa[:, c * 128:(c + 1) * 128], in_=of.ap[:, c, :], identity=idf.ap)
                    return r
                S.add("pe", tr, reads=[of.k(), idf.k()], writes=[PWA])
                of.free()
                og = B32(1024)
                S.add("act", lambda e, og=og: e.activation(out=og.ap, in_=pwa[:, :], func=AF.Copy), reads=[PWA], writes=[og.k()])
                r0 = orow0 + t * 128
                S.add("sp", lambda e, og=og, r0=r0: [e.dma_start(out=out_d[r0:r0 + 128, :], in_=og.ap)], reads=[og.k()],
                      dma=True, dkey="og%d" % og.off, name="OUT")
                og.free()

        def ssd_tile_p1(t, xsT, BT, dtb, ab):
            cols = slice(t * 128, (t + 1) * 128)
            a_t = ab.ap[:, t, :]
            dt_t = dtb.ap[:, t, :]
            Btm = B16(2, 128)
            Xd = B16(1024)
            ex = B32(32)
            dd = B32(16)

            def trx(e):
                r = None
                for c in range(8):
                    r = e.transpose(out=ptr[:, c * 128:(c + 1) * 128], in_=xsT.ap[:, c, cols], identity=idb.ap)
                return r

            def smalls(e):
                e.matmul(out=psm[:, 0:16], lhsT=strif.ap, rhs=a_t, start=True, stop=True)
                return e.matmul(out=psm[:, 16:32], lhsT=onesf.ap, rhs=a_t, start=True, stop=True)
            S.add("pe", smalls, reads=[ab.k(t), strif.k(), onesf.k()], writes=["psm"])
            S.add("act", lambda e: e.activation(out=ex.ap, in_=psm[:, 0:32], func=AF.Exp), reads=["psm"], writes=[ex.k()])
            yield
            S.add("dve", lambda e: e.tensor_tensor(out=dd.ap, in0=ex.ap[:, 0:16], in1=dt_t, op=ALU.mult), reads=[ex.k(), dtb.k(t)], writes=[dd.k()])
            S.add("pe", trx, reads=[xsT.k(), idb.k()], writes=["ptr"])
            yield
            S.add("dve", lambda e: e.tensor_tensor(out=Xd.ap.rearrange("p (h q) -> p h q", h=16),
                                                   in0=ptr[:, :].rearrange("p (h q) -> p h q", h=16),
                                                   in1=dd.ap.unsqueeze(2).to_broadcast([128, 16, 64]), op=ALU.mult),
                  reads=["ptr", dd.k()], writes=[Xd.k()])
            yield

            def trb(e):
                e.transpose(out=ptr[:, 0:128], in_=BT.ap[:, 0, cols], identity=idb.ap)
                return e.transpose(out=ptr[:, 128:256], in_=BT.ap[:, 1, cols], identity=idb.ap)
            S.add("pe", trb, reads=[BT.k(), idb.k()], writes=["ptr"])
            S.add("act", lambda e: e.activation(out=Btm.ap, in_=ptr[:, 0:256].rearrange("p (g n) -> p g n", g=2), func=AF.Copy),
                  reads=["ptr"], writes=[Btm.k()])
            yield

            def sloc(e):
                e.matmul(out=pwb[:, 0:512], lhsT=Btm.ap[:, 0, :], rhs=Xd.ap[:, 0:512], start=True, stop=True)
                return e.matmul(out=pwb[:, 512:1024], lhsT=Btm.ap[:, 1, :], rhs=Xd.ap[:, 512:1024], start=True, stop=True)
            S.add("pe", sloc, reads=[Btm.k(), Xd.k()], writes=[PWB])
            S.add("dve", lambda e: e.tensor_tensor(out=Sst.ap.rearrange("p (h q) -> p h q", h=16), in0=Sst.ap.rearrange("p (h q) -> p h q", h=16),
                                                   in1=ex.ap[:, 16:32].unsqueeze(2).to_broadcast([128, 16, 64]), op=ALU.mult),
                  reads=[ex.k(), Sst.k()], writes=[Sst.k()])
            yield
            S.add("dve", lambda e: e.tensor_tensor(out=Sst.ap, in0=Sst.ap, in1=pwb[:, :], op=ALU.add), reads=[PWB, Sst.k()], writes=[Sst.k()])
            yield
            Btm.free(); Xd.free(); ex.free(); dd.free()

        def ssd_tile(t, xsT, BT, CT, dtb, ab, zs, ynT, full, dbg):
            cols = slice(t * 128, (t + 1) * 128)
            a_t = ab.ap[:, t, :]
            dt_t = dtb.ap[:, t, :]
            if not full:
                yield from ssd_tile_p1(t, xsT, BT, dtb, ab)
                return
            Xtm = B16(1024)
            xstm = B16(1024) if full else None
            Btm = B16(2, 128)

            def trx(e):
                r = None
                for c in range(8):
                    r = e.transpose(out=ptr[:, c * 128:(c + 1) * 128], in_=xsT.ap[:, c, cols], identity=idb.ap)
                return r
            S.add("pe", trx, reads=[xsT.k(), idb.k()], writes=["ptr"])
            yield
            S.add("dve", lambda e: e.tensor_tensor(out=Xtm.ap.rearrange("p (h q) -> p h q", h=16),
                                                   in0=ptr[:, :].rearrange("p (h q) -> p h q", h=16),
                                                   in1=dt_t.unsqueeze(2).to_broadcast([128, 16, 64]), op=ALU.mult),
                  reads=["ptr", dtb.k(t)], writes=[Xtm.k()])
            yield
            if full:
                S.add("dve", lambda e: e.tensor_tensor(out=xstm.ap.rearrange("p (h q) -> p h q", h=16),
                                                       in0=ptr[:, :].rearrange("p (h q) -> p h q", h=16),
                                                       in1=cvec(C_DSK, 16).unsqueeze(2).to_broadcast([128, 16, 64]), op=ALU.mult),
                      reads=["ptr", cst.k()], writes=[xstm.k()])
                yield

            def trb(e):
                e.transpose(out=ptr[:, 0:128], in_=BT.ap[:, 0, cols], identity=idb.ap)
                return e.transpose(out=ptr[:, 128:256], in_=BT.ap[:, 1, cols], identity=idb.ap)
            S.add("pe", trb, reads=[BT.k(), idb.k()], writes=["ptr"])
            yield
            S.add("act", lambda e: e.activation(out=Btm.ap, in_=ptr[:, 0:256].rearrange("p (g n) -> p g n", g=2), func=AF.Copy),
                  reads=["ptr"], writes=[Btm.k()])
            yield
            def smalls(e):
                e.matmul(out=psm[:, 0:16], lhsT=tri.ap, rhs=a_t, start=True, stop=True)
                e.matmul(out=psm[:, 16:32], lhsT=stri.ap, rhs=a_t, start=True, stop=True)
                e.matmul(out=psm[:, 32:48], lhsT=onesc.ap[:, 0, :], rhs=a_t, start=True, stop=True)
                return e.matmul(out=psm[:, 48:64], lhsT=onesc.ap[:, 1, :], rhs=a_t, start=True, stop=True)
            S.add("pe", smalls, reads=[ab.k(t), tri.k(), stri.k(), onesc.k()], writes=["psm"])
            yield
            ex = B32(64)
            S.add("act", lambda e: e.activation(out=ex.ap, in_=psm[:, 0:64], func=AF.Exp), reads=["psm"], writes=[ex.k()])
            yield
            eacs = ex.ap[:, 0:16]
            if dbg:
                dbg_dump("Xtm", Xtm, 1024, BF16)
                dbg_dump("ex", ex, 64)
                dbg_dump("dtb", dtb, 16)
            ds = B32(2, 16)

            def mkds(e):
                e.tensor_scalar(out=ds.ap[:, 0, :], in0=ex.ap[:, 16:32], scalar1=cm.ap[:, 0:1], scalar2=None, op0=ALU.mult)
                return e.tensor_scalar(out=ds.ap[:, 1, :], in0=ex.ap[:, 16:32], scalar1=cm.ap[:, 1:2], scalar2=None, op0=ALU.mult)
            S.add("dve", mkds, reads=[ex.k(), cm.k()], writes=[ds.k()])
            yield
            Xd = [B16(1024), B16(1024)]
            for c in range(2):
                S.add("dve", lambda e, c=c: e.tensor_tensor(out=Xd[c].ap.rearrange("p (h q) -> p h q", h=16),
                                                            in0=Xtm.ap.rearrange("p (h q) -> p h q", h=16),
                                                            in1=ds.ap[:, c, :].unsqueeze(2).to_broadcast([128, 16, 64]), op=ALU.mult),
                      reads=[Xtm.k(), ds.k()], writes=[Xd[c].k()])
                yield
            MT = None
            if full:
                MT = B16(16, 128)
                for g in range(2):
                    rhs_all = B32(8, 128)
                    S.add("dve", lambda e, g=g, rhs_all=rhs_all: e.tensor_tensor(
                        out=rhs_all.ap, in0=tri.ap.unsqueeze(1).to_broadcast([128, 8, 128]),
                        in1=a_t[:, g * 8:(g + 1) * 8].unsqueeze(2).to_broadcast([128, 8, 128]), op=ALU.mult),
                        reads=[tri.k(), ab.k(t)], writes=[rhs_all.k()])
                    yield

                    def dmm(e, rhs_all=rhs_all):
                        e.matmul(out=pwa[:, 0:512], lhsT=stri.ap, rhs=rhs_all.ap[:, 0:4, :], start=True, stop=True)
                        return e.matmul(out=pwa[:, 512:1024], lhsT=stri.ap, rhs=rhs_all.ap[:, 4:8, :], start=True, stop=True)
                    S.add("pe", dmm, reads=[rhs_all.k(), stri.k()], writes=[PWA])
                    yield
                    E = B16(8, 128)

                    def eexp(e, E=E):
                        e.activation(out=E.ap[:, 0:4, :], in_=pwa[:, 0:512].rearrange("p (h l) -> p h l", h=4), func=AF.Exp)
                        return e.activation(out=E.ap[:, 4:8, :], in_=pwa[:, 512:1024].rearrange("p (h l) -> p h l", h=4), func=AF.Exp)
                    S.add("act", eexp, reads=[PWA], writes=[E.k()])
                    yield
                    S.add("pe", lambda e, g=g: e.matmul(out=psm[:, 128 + g * 128:256 + g * 128], lhsT=BT.ap[:, g, cols], rhs=CT.ap[:, g, cols],
                                                        start=True, stop=True), reads=[BT.k(), CT.k()], writes=["psm"])
                    yield
                    cbm = B32(128)
                    S.add("dve", lambda e, g=g, cbm=cbm: e.tensor_tensor(out=cbm.ap, in0=psm[:, 128 + g * 128:256 + g * 128], in1=tri.ap, op=ALU.mult),
                          reads=["psm", tri.k()], writes=[cbm.k()])
                    yield
                    S.add("dve", lambda e, g=g, cbm=cbm, E=E: e.tensor_tensor(out=MT.ap[:, g * 8:(g + 1) * 8, :], in0=E.ap,
                                                                               in1=cbm.ap.unsqueeze(1).to_broadcast([128, 8, 128]), op=ALU.mult),
                          reads=[E.k(), cbm.k()], writes=[MT.k(g * 8, (g + 1) * 8)])
                    yield
                    rhs_all.free(); E.free(); cbm.free()
                for c in range(2):
                    S.add("act", lambda e, c=c: e.activation(out=ctpad[c].ap[:, :, c * 64:(c + 1) * 64],
                                                             in_=CT.ap[:, :, t * 128 + c * 64:t * 128 + (c + 1) * 64], func=AF.Copy),
                          reads=[CT.k()], writes=[ctpad[c].k()])
                    yield
            Sbf = [B16(1024), B16(1024)] if full else None
            for c in range(2):
                def sloc(e, c=c):
                    e.matmul(out=pwb[:, 0:512], lhsT=Btm.ap[:, 0, :], rhs=Xd[c].ap[:, 0:512], start=True, stop=True)
                    return e.matmul(out=pwb[:, 512:1024], lhsT=Btm.ap[:, 1, :], rhs=Xd[c].ap[:, 512:1024], start=True, stop=True)
                S.add("pe", sloc, reads=[Btm.k(), Xd[c].k()], writes=[PWB])
                yield
                if full:
                    S.add("act", lambda e, c=c: e.activation(out=Sbf[c].ap, in_=Sst.ap, func=AF.Copy), reads=[Sst.k()], writes=[Sbf[c].k()])
                    yield

                def supd(e, c=c):
                    e.tensor_tensor(out=dtot.ap[:, 0:16], in0=dtot.ap[:, 0:16], in1=ex.ap[:, 32 + 16 * c:48 + 16 * c], op=ALU.mult)
                    return e.tensor_tensor(out=Sst.ap.rearrange("p (h q) -> p h q", h=16), in0=Sst.ap.rearrange("p (h q) -> p h q", h=16),
                                           in1=ex.ap[:, 32 + 16 * c:48 + 16 * c].unsqueeze(2).to_broadcast([128, 16, 64]), op=ALU.mult)
                S.add("dve", supd, reads=[ex.k(), Sst.k(), dtot.kr(0, 16)], writes=[Sst.k(), dtot.kr(0, 16)])
                yield
                S.add("dve", lambda e: e.tensor_tensor(out=Sst.ap, in0=Sst.ap, in1=pwb[:, :], op=ALU.add), reads=[PWB, Sst.k()], writes=[Sst.k()])
                yield
            Xd[0].free(); Xd[1].free(); Btm.free(); ds.free()
            if not full:
                Xtm.free(); ex.free()
                return
            def ymm(e):
                r = None
                for b in range(2):
                    e.matmul(out=pwa[:, b * 512:(b + 1) * 512], lhsT=idb.ap, rhs=xstm.ap[:, b * 512:(b + 1) * 512], start=True, stop=False)
                    for hh in range(8):
                        h = b * 8 + hh
                        r = e.matmul(out=pwa[:, h * 64:(h + 1) * 64], lhsT=MT.ap[:, h, :], rhs=Xtm.ap[:, h * 64:(h + 1) * 64],
                                     start=False, stop=(hh == 7))
                return r
            S.add("pe", ymm, reads=[idb.k(), xstm.k(), MT.k(), Xtm.k()], writes=[PWA])
            yield

            def yoff(e):
                r = None
                for g in range(2):
                    for c in range(2):
                        r = e.matmul(out=pwb[:, g * 512:(g + 1) * 512], lhsT=ctpad[c].ap[:, g, :], rhs=Sbf[c].ap[:, g * 512:(g + 1) * 512],
                                     start=(c == 0), stop=(c == 1))
                return r
            S.add("pe", yoff, reads=[ctpad[0].k(), ctpad[1].k(), Sbf[0].k(), Sbf[1].k()], writes=[PWB])
            yield
            yt = B32(1024)

            S.add("dve", lambda e: e.tensor_tensor(out=yt.ap.rearrange("p (h q) -> p h q", h=16), in0=pwb[:, :].rearrange("p (h q) -> p h q", h=16),
                                                   in1=eacs.unsqueeze(2).to_broadcast([128, 16, 64]), op=ALU.mult),
                  reads=[PWB, ex.k()], writes=[yt.k()])
            yield
            S.add("dve", lambda e: e.tensor_tensor(out=yt.ap, in0=yt.ap, in1=pwa[:, :], op=ALU.add), reads=[PWA, yt.k()], writes=[yt.k()])
            yield
            S.add("dve", lambda e: e.tensor_tensor(out=yt.ap, in0=yt.ap, in1=zs.ap[:, t, :], op=ALU.mult), reads=[yt.k(), zs.k(t)], writes=[yt.k()])
            yield
            if dbg:
                dbg_dump("yt", yt, 1024)
                dbg_dump("MT", MT, 2048, BF16)
                dbg_dump("Sbf1", Sbf[1], 1024, BF16)
            Xtm.free(); xstm.free(); MT.free(); Sbf[0].free(); Sbf[1].free(); ex.free()
            junk = B16(1024)
            ss = B32(2)

            def ysq(e):
                e.activation(out=junk.ap[:, 0:512], in_=yt.ap[:, 0:512], func=AF.Square, accum_out=ss.ap[:, 0:1])
                return e.activation(out=junk.ap[:, 512:1024], in_=yt.ap[:, 512:1024], func=AF.Square, accum_out=ss.ap[:, 1:2])
            S.add("act", ysq, reads=[yt.k()], writes=[junk.k(), ss.k()])
            yield
            chain("act", [lambda e: e.activation(out=ss.ap, in_=ss.ap, func=AF.Ln, bias=EPS, scale=1.0 / 512.0),
                          lambda e: e.activation(out=ss.ap, in_=ss.ap, func=AF.Exp, scale=-0.5)], [], [ss.k()])
            yield
            yn = B16(1024)

            def ynorm(e):
                e.scalar_tensor_tensor(out=yn.ap[:, 0:512], in0=yt.ap[:, 0:512], scalar=ss.ap[:, 0:1], in1=cvec(C_SSDN, 512), op0=ALU.mult, op1=ALU.mult)
                return e.scalar_tensor_tensor(out=yn.ap[:, 512:1024], in0=yt.ap[:, 512:1024], scalar=ss.ap[:, 1:2], in1=cvec(C_SSDN + 512, 512),
                                              op0=ALU.mult, op1=ALU.mult)
            S.add("dve", ynorm, reads=[yt.k(), ss.k(), cst.k()], writes=[yn.k()])
            yield

            def try_(e):
                r = None
                for c in range(8):
                    r = e.transpose(out=ptr[:, c * 128:(c + 1) * 128], in_=yn.ap[:, c * 128:(c + 1) * 128], identity=idb.ap)
                return r
            S.add("pe", try_, reads=[yn.k(), idb.k()], writes=["ptr"])
            yield
            S.add("act", lambda e: e.activation(out=ynT.ap[:, :, cols], in_=ptr[:, :].rearrange("p (c n) -> p c n", c=8), func=AF.Copy),
                  reads=["ptr"], writes=[ynT.k()])
            yield
            yt.free(); junk.free(); ss.free(); yn.free()

        def gla_tile(t, qT, kT, ktm, vtm, rs, onT, full, dbg=False):
            cols = slice(t * 128, (t + 1) * 128)
            p0, pk0 = next_pa()
            S.add("pe", lambda e: e.matmul(out=p0[:, 0:512], lhsT=a1T.ap[:, cols], rhs=w2a.ap, start=True, stop=True),
                  reads=[a1T.k(), w2a.k()], writes=[pk0])
            yield
            la = B32(512)

            S.add("act", lambda e: e.activation(out=la.ap, in_=p0[:, 0:512], func=AF.Exp, scale=-1.0), reads=[pk0], writes=[la.k()])
            yield
            S.add("act", lambda e: e.activation(out=la.ap, in_=la.ap, func=AF.Ln, bias=1.0), reads=[la.k()], writes=[la.k()])
            yield
            if dbg:
                dbg_dump("la", la, 512)
            if not full:
                pd, pkd = next_pa()

                def dmm_(e):
                    r = None
                    for hd in range(4):
                        r = e.matmul(out=pd[:, hd * 2:hd * 2 + 2], lhsT=la.ap[:, hd * 128:(hd + 1) * 128], rhs=negc.ap, start=True, stop=True)
                    return r
                S.add("pe", dmm_, reads=[la.k(), negc.k()], writes=[pkd])
                dec8 = B32(8)
                S.add("act", lambda e: e.activation(out=dec8.ap, in_=pd[:, 0:8], func=AF.Exp), reads=[pkd], writes=[dec8.k()])
                yield
                pe_, pke = next_pa()
                S.add("pe", lambda e: e.matmul(out=pe_[:, 0:512], lhsT=strigf.ap, rhs=la.ap, start=True, stop=True), reads=[strigf.k(), la.k()], writes=[pke])
                khf = B16(512)
                Ekf = B32(512)
                S.add("act", lambda e: e.activation(out=Ekf.ap, in_=pe_[:, 0:512], func=AF.Exp), reads=[pke], writes=[Ekf.k()])
                yield
                S.add("dve", lambda e: e.tensor_tensor(out=khf.ap, in0=Ekf.ap, in1=ktm.ap[:, t, :], op=ALU.mult), reads=[Ekf.k(), ktm.k(t)], writes=[khf.k()])
                yield
                for hp in range(2):
                    pu, pku = next_pa()

                    def umm_(e, hp=hp, pu=pu):
                        r = None
                        for h2 in range(2):
                            hd = hp * 2 + h2
                            r = e.matmul(out=pu[:, h2 * 256:(h2 + 1) * 256], lhsT=khf.ap[:, hd * 128:(hd + 1) * 128],
                                         rhs=vtm.ap[:, t, hd * 256:(hd + 1) * 256], start=True, stop=True)
                        return r
                    S.add("pe", umm_, reads=[khf.k(), vtm.k(t)], writes=[pku])

                    def gupd_(e, hp=hp, pu=pu):
                        r = None
                        for h2 in range(2):
                            hd = hp * 2 + h2
                            r = e.scalar_tensor_tensor(out=Sg.ap[:, hd * 256:(hd + 1) * 256], in0=Sg.ap[:, hd * 256:(hd + 1) * 256],
                                                       scalar=dec8.ap[:, hd * 2:hd * 2 + 1], in1=pu[:, h2 * 256:(h2 + 1) * 256],
                                                       op0=ALU.mult, op1=ALU.add)
                        return r
                    S.add("dve", gupd_, reads=[pku, dec8.k(), Sg.kr(hp * 512, hp * 512 + 512)], writes=[Sg.kr(hp * 512, hp * 512 + 512)])
                    yield
                la.free(); dec8.free(); khf.free(); Ekf.free()
                return
            p1, pk1 = next_pa()

            def bc(e):
                r = None
                for hd in range(4):
                    r = e.matmul(out=p1[:, hd * 128:(hd + 1) * 128], lhsT=la.ap[:, hd * 128:(hd + 1) * 128], rhs=trig.ap, start=True, stop=True)
                return r
            S.add("pe", bc, reads=[la.k(), trig.k()], writes=[pk1])
            yield
            EqT = B32(4, 128)
            S.add("act", lambda e: e.activation(out=EqT.ap, in_=p1[:, 0:512].rearrange("p (h l) -> p h l", h=4), func=AF.Exp),
                  reads=[pk1], writes=[EqT.k()])
            yield
            ktT = None
            if full:
                EkT = B32(4, 128)
                S.add("act", lambda e: e.activation(out=EkT.ap, in_=p1[:, 0:512].rearrange("p (h l) -> p h l", h=4), func=AF.Exp, scale=-1.0),
                      reads=[pk1], writes=[EkT.k()])
                yield
                ktT = B16(4, 128)
                S.add("dve", lambda e: e.tensor_tensor(out=ktT.ap, in0=kT.ap[:, :, cols], in1=EkT.ap, op=ALU.mult),
                      reads=[kT.k(), EkT.k()], writes=[ktT.k()])
                yield
                for c in range(2):
                    S.add("dve", lambda e, c=c: e.tensor_tensor(out=qtpad[c].ap[:, :, c * 64:(c + 1) * 64],
                                                                in0=qT.ap[:, :, t * 128 + c * 64:t * 128 + (c + 1) * 64],
                                                                in1=EqT.ap[:, :, c * 64:(c + 1) * 64], op=ALU.mult),
                          reads=[qT.k(), EqT.k()], writes=[qtpad[c].k()])
                    yield
                EkT.free()
            p2, pk2 = next_pa()
            S.add("pe", lambda e: e.matmul(out=p2[:, 0:512], lhsT=strig.ap, rhs=la.ap, start=True, stop=True), reads=[strig.k(), la.k()], writes=[pk2])
            yield
            Ekh = B32(512)
            S.add("act", lambda e: e.activation(out=Ekh.ap, in_=p2[:, 0:512], func=AF.Exp), reads=[pk2], writes=[Ekh.k()])
            yield
            kh = [B16(512), B16(512)]
            for c in range(2):
                S.add("dve", lambda e, c=c: e.scalar_tensor_tensor(out=kh[c].ap, in0=Ekh.ap, scalar=cm.ap[:, c:c + 1], in1=ktm.ap[:, t, :],
                                                                   op0=ALU.mult, op1=ALU.mult),
                      reads=[Ekh.k(), cm.k(), ktm.k(t)], writes=[kh[c].k()])
                yield
            la.free(); Ekh.free()
            attT = None
            if full:
                p3, pk3 = next_pa()

                def att(e):
                    r = None
                    for hd in range(4):
                        e.matmul(out=p3[:, hd * 128:(hd + 1) * 128], lhsT=ktT.ap[:, hd, :], rhs=qtpad[0].ap[:, hd, :], start=True, stop=False)
                        r = e.matmul(out=p3[:, hd * 128:(hd + 1) * 128], lhsT=ktT.ap[:, hd, :], rhs=qtpad[1].ap[:, hd, :], start=False, stop=True)
                    return r
                S.add("pe", att, reads=[ktT.k(), qtpad[0].k(), qtpad[1].k()], writes=[pk3])
                yield
                attT = B16(4, 128)
                S.add("dve", lambda e: e.tensor_tensor(out=attT.ap, in0=p3[:, 0:512].rearrange("p (h l) -> p h l", h=4),
                                                       in1=tri.ap.unsqueeze(1).to_broadcast([128, 4, 128]), op=ALU.mult),
                      reads=[pk3, tri.k()], writes=[attT.k()])
                yield
                ktT.free()
            Sgb = [B16(1024), B16(1024)] if full else None
            for c in range(2):
                if full:
                    S.add("act", lambda e, c=c: e.activation(out=Sgb[c].ap, in_=Sg.ap, func=AF.Copy), reads=[Sg.k()], writes=[Sgb[c].k()])
                    yield
                for hp in range(2):
                    pu, pku = next_pa()

                    def umm(e, c=c, hp=hp, pu=pu):
                        r = None
                        for h2 in range(2):
                            hd = hp * 2 + h2
                            r = e.matmul(out=pu[:, h2 * 256:(h2 + 1) * 256], lhsT=kh[c].ap[:, hd * 128:(hd + 1) * 128],
                                         rhs=vtm.ap[:, t, hd * 256:(hd + 1) * 256], start=True, stop=True)
                        return r
                    S.add("pe", umm, reads=[kh[c].k(), vtm.k(t)], writes=[pku])
                    yield

                    def gupd(e, c=c, hp=hp, pu=pu):
                        r = None
                        for h2 in range(2):
                            hd = hp * 2 + h2
                            r = e.scalar_tensor_tensor(out=Sg.ap[:, hd * 256:(hd + 1) * 256], in0=Sg.ap[:, hd * 256:(hd + 1) * 256],
                                                       scalar=EqT.ap[:, hd, c * 64 + 63:c * 64 + 64], in1=pu[:, h2 * 256:(h2 + 1) * 256],
                                                       op0=ALU.mult, op1=ALU.add)
                        return r
                    S.add("dve", gupd, reads=[pku, EqT.k(), Sg.kr(hp * 512, hp * 512 + 512)], writes=[Sg.kr(hp * 512, hp * 512 + 512)])
                    yield
                    yield
            kh[0].free(); kh[1].free(); EqT.free()
            if not full:
                return

            junk = B16(1024)
            ss = B32(4)
            pos = []
            for hp in range(2):
                po, pko = next_pa()
                pos.append((po, pko))

                def omm(e, hp=hp, po=po):
                    r = None
                    for h2 in range(2):
                        hd = hp * 2 + h2
                        o = po[:, h2 * 256:(h2 + 1) * 256]
                        e.matmul(out=o, lhsT=attT.ap[:, hd, :], rhs=vtm.ap[:, t, hd * 256:(hd + 1) * 256], start=True, stop=False)
                        e.matmul(out=o, lhsT=qtpad[0].ap[:, hd, :], rhs=Sgb[0].ap[:, hd * 256:(hd + 1) * 256], start=False, stop=False)
                        r = e.matmul(out=o, lhsT=qtpad[1].ap[:, hd, :], rhs=Sgb[1].ap[:, hd * 256:(hd + 1) * 256], start=False, stop=True)
                    return r
                S.add("pe", omm, reads=[attT.k(), vtm.k(t), qtpad[0].k(), qtpad[1].k(), Sgb[0].k(), Sgb[1].k()], writes=[pko])
                yield

                def osq(e, hp=hp, po=po):
                    r = None
                    for h2 in range(2):
                        hd = hp * 2 + h2
                        r = e.activation(out=junk.ap[:, hd * 256:(hd + 1) * 256], in_=po[:, h2 * 256:(h2 + 1) * 256], func=AF.Square,
                                         accum_out=ss.ap[:, hd:hd + 1])
                    return r
                S.add("act", osq, reads=[pko], writes=[junk.kr(hp * 512, hp * 512 + 512), ss.k()])
                yield
                yield
            attT.free(); Sgb[0].free(); Sgb[1].free()
            chain("act", [lambda e: e.activation(out=ss.ap, in_=ss.ap, func=AF.Ln, bias=EPS, scale=1.0 / 256.0),
                          lambda e: e.activation(out=ss.ap, in_=ss.ap, func=AF.Exp, scale=-0.5)], [], [ss.k()])
            yield
            yield
            on = B32(1024)
            onb = B16(1024)
            for hp in range(2):
                po, pko = pos[hp]

                def onorm(e, hp=hp, po=po):
                    r = None
                    for h2 in range(2):
                        hd = hp * 2 + h2
                        r = e.scalar_tensor_tensor(out=on.ap[:, hd * 256:(hd + 1) * 256], in0=po[:, h2 * 256:(h2 + 1) * 256], scalar=ss.ap[:, hd:hd + 1],
                                                   in1=cvec(C_GLAN, 256), op0=ALU.mult, op1=ALU.mult)
                    return r
                S.add("dve", onorm, reads=[pko, ss.k(), cst.k()], writes=[on.kr(hp * 512, hp * 512 + 512)])
                yield
            S.add("dve", lambda e: e.tensor_tensor(out=onb.ap, in0=on.ap, in1=rs.ap[:, t, :], op=ALU.mult), reads=[on.k(), rs.k(t)], writes=[onb.k()])
            yield
            yield

            def tro(e):
                r = None
                for c in range(8):
                    r = e.transpose(out=ptr[:, c * 128:(c + 1) * 128], in_=onb.ap[:, c * 128:(c + 1) * 128], identity=idb.ap)
                return r
            S.add("pe", tro, reads=[onb.k(), idb.k()], writes=["ptr"])
            yield
            S.add("act", lambda e: e.activation(out=onT.ap[:, :, cols], in_=ptr[:, :].rearrange("p (c n) -> p c n", c=8), func=AF.Copy),
                  reads=["ptr"], writes=[onT.k()])
            yield
            junk.free(); ss.free(); on.free(); onb.free()

        def prologue_mem():
            mn = B16(2, 1024)
            for mc in range(2):
                ms = B32(1024)
                S.add("sp", lambda e, ms=ms, mc=mc: [e.dma_start(out=ms.ap, in_=mem_d[mc * 128:(mc + 1) * 128, :])], writes=[ms.k()], dma=True, dkey="ms%d" % ms.off)
                junk = B16(1024)
                ss = B32(1)
                S.add("act", lambda e, ms=ms, junk=junk, ss=ss: e.activation(out=junk.ap, in_=ms.ap, func=AF.Square, accum_out=ss.ap),
                      reads=[ms.k()], writes=[junk.k(), ss.k()])
                chain("act", [lambda e, ss=ss: e.activation(out=ss.ap, in_=ss.ap, func=AF.Ln, bias=EPS, scale=1.0 / 1024.0),
                              lambda e, ss=ss: e.activation(out=ss.ap, in_=ss.ap, func=AF.Exp, scale=-0.5)], [], [ss.k()])
                S.add("dve", lambda e, ms=ms, ss=ss, mc=mc: e.scalar_tensor_tensor(out=mn.ap[:, mc, :], in0=ms.ap, scalar=ss.ap[:, 0:1], in1=cvec(C_MEMN, 1024),
                                                                                   op0=ALU.mult, op1=ALU.mult),
                      reads=[ms.k(), ss.k(), cst.k()], writes=[mn.k(mc)])
                ms.free(); junk.free(); ss.free()
            mnT = B16(8, 256)
            for mc in range(2):
                def trm(e, mc=mc):
                    r = None
                    for c in range(8):
                        r = e.transpose(out=ptr[:, c * 128:(c + 1) * 128], in_=mn.ap[:, mc, c * 128:(c + 1) * 128], identity=idb.ap)
                    return r
                S.add("pe", trm, reads=[mn.k(mc), idb.k()], writes=["ptr"])
                S.add("act", lambda e, mc=mc: e.activation(out=mnT.ap[:, :, mc * 128:(mc + 1) * 128], in_=ptr[:, :].rearrange("p (c n) -> p c n", c=8), func=AF.Copy),
                      reads=["ptr"], writes=[mnT.k()])
            mn.free()

            def k_epi(c0, nch, pv, pk):
                S.add("act", lambda e: e.activation(out=KT.ap[:, c0:c0 + nch, :], in_=pv, func=AF.Copy), reads=[pk], writes=[KT.k(c0, c0 + nch)])
            proj_fm(w_xkv, 0, 1024, mnT, 256, k_epi, blk=256)

            def v_epi(t, col, cw, p, pk):
                S.add("act", lambda e: e.activation(out=Vx.ap[:, t, col:col + cw], in_=p, func=AF.Copy), reads=[pk],
                      writes=[Vx.kr(t * 1024 + col, t * 1024 + col + cw)])
            proj_tm(w_xkv, 1024, 1024, mnT, 2, v_epi)
            mnT.free()

        prologue_mem()
        for s_ in range(NPRE // NT):
            step(s_ * NT, NT, "p1")
            if (s_ + 1) % NSTEP == 0:
                zi = cst.ap[:, C_CMASK + (s_ + 1) // NSTEP - 1:C_CMASK + (s_ + 1) // NSTEP]

                def zs_(e, zi=zi):
                    e.tensor_scalar(out=Sst.ap, in0=Sst.ap, scalar1=zi, scalar2=None, op0=ALU.mult)
                    return e.tensor_scalar(out=Sg.ap, in0=Sg.ap, scalar1=zi, scalar2=None, op0=ALU.mult)
                S.add("dve", zs_, reads=[Sst.k(), Sg.k(), cst.k()], writes=[Sst.k(), Sg.k()])
        for s_ in range(NSTEP):
            step(NPRE + s_ * NT, NT, "p2", orow0=s_ * NT, first_dbg=(s_ == 0))
        print("arena peaks: A16 %d / %d, A32 %d / %d; ops %d" % (a16.peak, N16, a32.peak, N32, len(S.ops)))
        S.emit(nc, st)
    return nc, dbg_outs


_CACHE = {}


def host_inputs(x, mem, norm_mix, w_in, ssd_conv_w, ssd_conv_b, ssd_dt_bias, ssd_A_log, ssd_D, ssd_norm,
                gla_w_a2, gla_b_a, gla_norm, w_up_ssd, w_up_gla, w_o, norm_xattn, norm_mem, w_xq, w_xkv,
                w_xo, norm_ffn, w_ffn_in, w_ffn_out, norm_final):
    f = lambda a: np.ascontiguousarray(np.asarray(a, dtype=np.float32))
    x = f(x); mem = f(mem)

    def fm(g):
        return f(g).reshape(8, 128).T

    def rep(v):
        v = f(v).reshape(1, -1)
        return np.broadcast_to(v, (128, v.shape[1]))
    cst = np.zeros((128, NCST), np.float32)
    cst[:, C_GMIX:C_GMIX + 8] = fm(norm_mix[0])
    cst[:, C_GXA:C_GXA + 8] = fm(norm_xattn[0])
    cst[:, C_GFFN:C_GFFN + 8] = fm(norm_ffn[0])
    cst[:, C_GFIN:C_GFIN + 8] = fm(norm_final)
    cw = f(ssd_conv_w[0])[:, 0, :]
    cst[:, C_CONVW:C_CONVW + 48] = cw.reshape(4, 12, 128).transpose(2, 1, 0).reshape(128, 48)
    cst[:, C_CONVB:C_CONVB + 12] = f(ssd_conv_b[0]).reshape(12, 128).T
    cst[:, C_DTB:C_DTB + 16] = rep(ssd_dt_bias[0])
    cst[:, C_ALOG:C_ALOG + 16] = rep(ssd_A_log[0])
    cst[:, C_DSK:C_DSK + 16] = rep(ssd_D[0])
    cst[:, C_SSDN:C_SSDN + 1024] = rep(ssd_norm[0])
    cst[:, C_GLAN:C_GLAN + 256] = rep(gla_norm[0])
    cst[:, C_MEMN:C_MEMN + 1024] = rep(norm_mem[0])
    w2aug = np.concatenate([f(gla_w_a2[0]), f(gla_b_a[0]).reshape(1, 512)], 0)
    shared = {"w2aug": f(w2aug), "w_in": f(w_in[0]), "w_up_ssd": f(w_up_ssd[0]), "w_up_gla": f(w_up_gla[0]), "w_o": f(w_o[0]),
              "w_xq": f(w_xq[0]), "w_xkv": f(w_xkv[0]), "w_xo": f(w_xo[0]), "w_ffn_in": f(w_ffn_in[0]), "w_ffn_out": f(w_ffn_out[0])}
    in_maps = []
    for c in range(NCORES):
        b, j = divmod(c, 4)
        xe = np.zeros((4 * SEG, D), np.float32)
        xe[(3 - j) * SEG:3 * SEG] = x[b, 0:j * SEG]
        xe[3 * SEG:] = x[b, j * SEG:(j + 1) * SEG]
        cc = cst.copy()
        for i in range(3):
            cc[:, C_CMASK + i] = 1.0 if i >= 3 - j else 0.0
        m = {"x_ext": xe, "mem_b": mem[b], "cst": cc}
        m.update(shared)
        in_maps.append(m)
    return in_maps


def kernel(**inputs):
    if "nc" not in _CACHE:
        _CACHE["nc"] = build_program()
    nc, dbg = _CACHE["nc"]
    in_maps = host_inputs(**inputs)
    res = run_bass_kernel_spmd(nc, in_maps, core_ids=list(range(NCORES)))
    _CACHE["last"] = res
    out = np.zeros((2, 4 * SEG, D), np.float32)
    for c in range(NCORES):
        b, j = divmod(c, 4)
        out[b, j * SEG:(j + 1) * SEG] = res.results[c]["out"]
    return out
```

```python
import numpy as np
from contextlib import ExitStack
import concourse.bass as bass
import concourse.mybir as mybir
from concourse.bass_utils import run_bass_kernel_spmd

F32 = mybir.dt.float32
BF16 = mybir.dt.bfloat16
AF = mybir.ActivationFunctionType
ALU = mybir.AluOpType

NCORES = 8
D = 1024
SEG = 2048
NT = 512
HALO = 128
EPS = 1e-6
EPOCH = 8000
import os
DBG = bool(os.environ.get('KDBG'))

O_Z, O_XBC, O_DT, O_Q, O_K, O_V, O_R, O_A1, O_GS, O_GG = 0, 1024, 2560, 2576, 3088, 3600, 4624, 5648, 5664, 6688
DFF = 2816

C_GMIX, C_GXA, C_GFFN, C_GFIN = 0, 8, 16, 24
C_CONVW = 32
C_CONVB = 80
C_DTB, C_ALOG, C_DSK = 92, 108, 124
C_SSDN = 140
C_GLAN = 1164
C_MEMN = 1420
C_CMASK = 2444
NCST = 2452


class Op:
    __slots__ = ("eng", "fn", "deps", "sig", "signal", "is_dma", "dkey", "ndma", "name", "inc")


class Sched:
    ENGS = ("pe", "act", "dve", "pool", "sp")

    def __init__(self):
        self.ops = []
        self.recs = {}

    @staticmethod
    def _k(k):
        return (k, 0, 1) if isinstance(k, str) else k

    def add(self, eng, fn, reads=(), writes=(), dma=False, dkey=None, ndma=1, name="", inc=16):
        op = Op()
        op.eng, op.fn, op.is_dma, op.dkey, op.ndma, op.inc, op.name = eng, fn, dma, dkey, ndma, inc, name
        op.sig = False
        op.signal = None
        deps = []
        seen = set()

        def push(d, raw):
            if d is None or d is op or id(d) in seen:
                return
            if not raw and not (d.is_dma or dma or d.eng != eng):
                return
            seen.add(id(d))
            deps.append(d)

        for k in reads:
            a, lo, hi = self._k(k)
            for r in self.recs.get(a, ()):
                if r[0] < hi and lo < r[1]:
                    push(r[2], True)
                    r[3].append(op)
        for k in writes:
            a, lo, hi = self._k(k)
            lst = self.recs.setdefault(a, [])
            new = []
            for r in lst:
                if r[0] < hi and lo < r[1]:
                    push(r[2], False)
                    for rd in r[3]:
                        push(rd, False)
                    if r[0] < lo:
                        new.append([r[0], lo, r[2], list(r[3])])
                    if hi < r[1]:
                        new.append([hi, r[1], r[2], list(r[3])])
                else:
                    new.append(r)
            new.append([lo, hi, op, []])
            self.recs[a] = new
        if eng == "pe" and not dma:
            deps = [d for d in deps if d.is_dma or d.eng != "pe"]
        op.deps = deps
        for d in deps:
            d.sig = True
        self.ops.append(op)
        return op

    def emit(self, nc, stack):
        eng_count = {e: 0 for e in self.ENGS}
        eng_sems = {e: [] for e in self.ENGS}
        dma_sems = {}
        dma_vals = {}
        for op in self.ops:
            if op.is_dma:
                if op.dkey not in dma_sems:
                    dma_sems[op.dkey] = stack.enter_context(nc.semaphore("d%d" % len(dma_sems)))
                    dma_vals[op.dkey] = 0
                dma_vals[op.dkey] += op.inc * op.ndma
                op.signal = (dma_sems[op.dkey], dma_vals[op.dkey])
            elif op.sig:
                c = eng_count[op.eng]
                ep = c // EPOCH
                if ep >= len(eng_sems[op.eng]):
                    eng_sems[op.eng].append(stack.enter_context(nc.semaphore("e%s%d" % (op.eng, ep))))
                op.signal = (eng_sems[op.eng][ep], c % EPOCH + 1)
                eng_count[op.eng] = c + 1
        by_eng = {e: [o for o in self.ops if o.eng == e] for e in self.ENGS}
        finals = {}
        for op in self.ops:
            if op.is_dma and op.name.startswith("OUT"):
                sem, val = op.signal
                if finals.get(id(sem), (None, 0))[1] < val:
                    finals[id(sem)] = (sem, val)

        def run(engh, ename):
            waited = {}
            for op in by_eng[ename]:
                for d in op.deps:
                    sem, val = d.signal
                    if waited.get(id(sem), 0) < val:
                        engh.wait_ge(sem, val)
                        waited[id(sem)] = val
                r = op.fn(engh)
                if op.is_dma:
                    assert len(r) == op.ndma, (op.name, len(r), op.ndma)
                    for ins in r:
                        ins.then_inc(op.signal[0], op.inc)
                elif op.sig:
                    r.then_inc(op.signal[0], 1)
            if ename == "sp":
                for sem, val in finals.values():
                    engh.wait_ge(sem, val)

        with nc.Block() as block:
            @block.tensor
            def _(e):
                run(e, "pe")

            @block.scalar
            def _(e):
                run(e, "act")

            @block.vector
            def _(e):
                run(e, "dve")

            @block.gpsimd
            def _(e):
                run(e, "pool")

            @block.sync
            def _(e):
                run(e, "sp")


class Arena:
    def __init__(self, name, tensor, n):
        self.name, self.t, self.n = name, tensor, n
        self.used = []
        self.peak = 0

    def alloc(self, n):
        n = (n + 15) // 16 * 16
        self.used.sort()
        pos = 0
        for off, sz in self.used:
            if off - pos >= n:
                break
            pos = off + sz
        if pos + n > self.n:
            raise RuntimeError("arena %s full: need %d at %d of %d" % (self.name, n, pos, self.n))
        self.used.append((pos, n))
        self.peak = max(self.peak, pos + n)
        return pos

    def free(self, off):
        self.used = [u for u in self.used if u[0] != off]


class Buf:
    def __init__(self, arena, shape):
        self.arena = arena
        self.shape = list(shape)
        self.n = int(np.prod(shape))
        self.off = arena.alloc(self.n)
        self.inner = self.n // self.shape[0] if len(shape) == 2 else self.n

    def free(self):
        self.arena.free(self.off)

    @property
    def ap(self):
        a = self.arena.t[:, self.off:self.off + self.n]
        if len(self.shape) == 2:
            return a.rearrange("p (a b) -> p a b", a=self.shape[0])
        return a

    def k(self, i=None, j=None):
        if i is None:
            return (self.arena.name, self.off, self.off + self.n)
        if j is None:
            j = i + 1
        return (self.arena.name, self.off + i * self.inner, self.off + j * self.inner)

    def kr(self, lo, hi):
        return (self.arena.name, self.off + lo, self.off + hi)


def build_program():
    nc = bass.Bass("TRN2", target_bir_lowering=False)
    TPS = NT // 128
    NSTEP = SEG // NT

    def din(name, shape):
        return nc.dram_tensor(name, list(shape), F32, kind="ExternalInput").ap()

    NPRE = 3 * SEG
    x_d = din("x_ext", [NPRE + SEG, D])
    mem_d = din("mem_b", [256, D])
    cst_d = din("cst", [128, NCST])
    w2_d = din("w2aug", [17, 512])
    w_in = din("w_in", [D, 7712])
    w_ups = din("w_up_ssd", [D, D])
    w_upg = din("w_up_gla", [D, D])
    w_o = din("w_o", [D, D])
    w_xq = din("w_xq", [D, D])
    w_xkv = din("w_xkv", [D, 2 * D])
    w_xo = din("w_xo", [D, D])
    w_fi = din("w_ffn_in", [D, 2 * DFF])
    w_fo = din("w_ffn_out", [DFF, D])
    out_d = nc.dram_tensor("out", [SEG, D], F32, kind="ExternalOutput").ap()
    dbg_outs = {}

    S = Sched()

    def chain(eng, fns, reads, writes):
        for f in fns:
            S.add(eng, f, reads=list(reads) + list(writes), writes=writes)
    st = ExitStack()
    with st:
        def sb(name, shape, dt):
            return st.enter_context(nc.sbuf_tensor(name, list(shape), dt))

        def pst(name, shape, dt):
            return st.enter_context(nc.psum_tensor(name, list(shape), dt))

        N16 = 70000
        N32 = 18000
        a16 = Arena("A16", sb("A16", [128, N16], BF16), N16)
        a32 = Arena("A32", sb("A32", [128, N32], F32), N32)

        def B16(*shape):
            return Buf(a16, shape)

        def B32(*shape):
            return Buf(a32, shape)

        pa = [pst("pa0", [128, 512], F32), pst("pa1", [128, 512], F32)]
        ptr = pst("ptr", [128, 1024], BF16)
        psm = pst("psm", [128, 512], F32)
        pwa = pst("pwa", [128, 1024], F32)
        pwb = pst("pwb", [128, 1024], F32)
        PWA = ("pwa", 0, 2)
        PWB = ("pwb", 0, 2)
        pa_i = [0]
        pa6_i = [0]
        banks6 = [(pa[0], "pa0"), (pa[1], "pa1"), (pwa[:, 0:512], ("pwa", 0, 1)), (pwa[:, 512:1024], ("pwa", 1, 2)),
                  (pwb[:, 0:512], ("pwb", 0, 1)), (pwb[:, 512:1024], ("pwb", 1, 2))]

        def next_pa():
            i = pa_i[0]
            pa_i[0] ^= 1
            return pa[i], "pa%d" % i

        def next_pa6():
            i = pa6_i[0]
            pa6_i[0] = (i + 1) % 6
            return banks6[i]

        cst = B32(NCST)
        w2a = B32(512)
        idf = B32(128)
        tri = B32(128)
        stri = B32(128)
        trig = B32(128)
        strig = B32(128)
        strif = B32(128)
        strigf = B32(128)
        onesf = B32(128)
        negc = B32(2)
        onesc = B32(2, 128)
        cm = B32(2)
        idb = B16(128)
        ones_mean = B16(128)
        ones1 = B16(128)
        aneg = B32(16)

        S.add("sp", lambda e: [e.dma_start(out=cst.ap, in_=cst_d)], writes=[cst.k()], dma=True, dkey="cst")
        S.add("dve", lambda e: e.memset(w2a.ap, 0.0), writes=[w2a.k()])
        S.add("sp", lambda e: [e.dma_start(out=w2a.ap[0:17, :], in_=w2_d)], writes=[w2a.k()], dma=True, dkey="w2a")

        def mk_c1(e):
            e.memset(idf.ap, 0.0)
            e.memset(tri.ap, 1.0)
            e.memset(stri.ap, 1.0)
            e.memset(onesc.ap, 0.0)
            e.memset(cm.ap, 0.0)
            e.memset(ones_mean.ap, 1.0 / 1024.0)
            e.memset(strif.ap, 1.0)
            e.memset(onesf.ap, 1.0)
            e.memset(negc.ap, -1.0 / 16.0)
            return e.memset(ones1.ap, 1.0)

        def mk_c2(e):
            e.affine_select(out=strif.ap, in_=strif.ap, pattern=[[-1, 128]], compare_op=ALU.is_gt,
                            fill=0.0, base=0, channel_multiplier=1)
            e.affine_select(out=idf.ap, in_=idf.ap, pattern=[[-1, 128]], compare_op=ALU.not_equal,
                            fill=1.0, base=0, channel_multiplier=1)
            e.affine_select(out=tri.ap, in_=tri.ap, pattern=[[1, 128]], compare_op=ALU.is_ge,
                            fill=0.0, base=0, channel_multiplier=-1)
            e.affine_select(out=stri.ap, in_=stri.ap, pattern=[[-1, 128]], compare_op=ALU.is_gt,
                            fill=0.0, base=0, channel_multiplier=1)
            e.memset(onesc.ap[0:64, 0, :], 1.0)
            e.memset(onesc.ap[64:128, 1, :], 1.0)
            e.memset(cm.ap[0:64, 0:1], 1.0)
            return e.memset(cm.ap[64:128, 1:2], 1.0)

        def mk_c3(e):
            e.memset(tri.ap[0:64, 64:128], 0.0)
            return e.memset(stri.ap[64:128, 0:64], 0.0)
        chain("pool", [mk_c1, mk_c2, mk_c3], [], [idf.k(), tri.k(), stri.k(), onesc.k(), cm.k(), ones_mean.k(), ones1.k(),
                                                  strif.k(), onesf.k(), negc.k()])

        def mk_consts2(e):
            e.tensor_copy(out=idb.ap, in_=idf.ap)
            e.tensor_scalar(out=trig.ap, in0=tri.ap, scalar1=-1.0 / 16.0, scalar2=None, op0=ALU.mult)
            e.tensor_scalar(out=strigf.ap, in0=strif.ap, scalar1=-1.0 / 16.0, scalar2=None, op0=ALU.mult)
            return e.tensor_scalar(out=strig.ap, in0=stri.ap, scalar1=-1.0 / 16.0, scalar2=None, op0=ALU.mult)
        S.add("dve", mk_consts2, reads=[idf.k(), tri.k(), stri.k(), strif.k()], writes=[idb.k(), trig.k(), strig.k(), strigf.k()])

        def mk_aneg(e):
            return e.activation(out=aneg.ap, in_=cst.ap[:, C_ALOG:C_ALOG + 16], func=AF.Exp)
        S.add("act", mk_aneg, reads=[cst.k()], writes=[aneg.k()])
        S.add("dve", lambda e: e.tensor_scalar(out=aneg.ap, in0=aneg.ap, scalar1=-1.0, scalar2=None, op0=ALU.mult),
              reads=[aneg.k()], writes=[aneg.k()])

        def cvec(off, n):
            return cst.ap[:, off:off + n]

        NSLOT = 3
        WSZ = 4096
        wslots = [B16(WSZ) for _ in range(NSLOT)]
        wctr = [0]
        SSZ = 1024
        sslots = [B16(SSZ) for _ in range(2)]
        sctr = [0]

        def wload(wd, kc, c0, cw):
            if kc * cw <= SSZ:
                i = NSLOT + sctr[0] % 2
                sctr[0] += 1
                slot = sslots[i - NSLOT]
            else:
                i = wctr[0] % NSLOT
                wctr[0] += 1
                slot = wslots[i]
            assert kc * cw <= WSZ
            dst = slot.ap[:, 0:kc * cw].rearrange("p (k n) -> p k n", k=kc)
            src = wd.rearrange("(k p) n -> p k n", p=128)[:, :, c0:c0 + cw]
            if kc > 8:
                h = kc // 2
                S.add("pool", lambda e: [e.dma_start(out=dst[:, 0:h, :], in_=src[:, 0:h, :]),
                                         e.dma_start(out=dst[:, h:kc, :], in_=src[:, h:kc, :])],
                      writes=[slot.k()], dma=True, dkey="w%d" % i, ndma=2)
            else:
                S.add("pool", lambda e: [e.dma_start(out=dst, in_=src)], writes=[slot.k()], dma=True, dkey="w%d" % i)
            return dst, slot.k()

        def proj_fm(wd, c0, ncols, src, nt, epi, kc=8, blk=512):
            col = 0
            while col < ncols:
                cw = min(blk, ncols - col)
                wap, wkey = wload(wd, kc, c0 + col, cw)
                nch_all = (cw + 127) // 128
                gsz = max(1, 512 // nt)
                for cg in range(0, nch_all, gsz):
                    nch = min(gsz, nch_all - cg)
                    p, pk = next_pa6()
                    pv = p[:, 0:nch * nt].rearrange("p (c n) -> p c n", c=nch)

                    def mm(e, wap=wap, pv=pv, nch=nch, cw=cw, cg=cg):
                        r = None
                        for c in range(nch):
                            cc = cg + c
                            m = min(128, cw - cc * 128)
                            for k in range(kc):
                                r = e.matmul(out=pv[0:m, c, :], lhsT=wap[:, k, cc * 128:cc * 128 + m], rhs=src.ap[:, k, 0:nt],
                                             start=(k == 0), stop=(k == kc - 1))
                        return r
                    S.add("pe", mm, reads=[wkey, src.k()], writes=[pk])
                    epi(col // 128 + cg, nch, pv, pk)
                col += cw

        def proj_tm(wd, c0, ncols, src, ntiles, epi, kc=8):
            col = 0
            while col < ncols:
                cw = min(512, ncols - col)
                wap, wkey = wload(wd, kc, c0 + col, cw)
                for t in range(ntiles):
                    p, pk = next_pa6()

                    def mm(e, wap=wap, p=p, t=t, cw=cw):
                        r = None
                        for k in range(kc):
                            r = e.matmul(out=p[:, 0:cw], lhsT=src.ap[:, k, t * 128:(t + 1) * 128], rhs=wap[:, k, :],
                                         start=(k == 0), stop=(k == kc - 1))
                        return r
                    S.add("pe", mm, reads=[wkey, src.k()], writes=[pk])
                    epi(t, col, cw, p[:, 0:cw], pk)
                col += cw

        hT = B32(8, NT)
        nT = B16(8, NT)
        rstd = B32(NT)
        Sst = B32(1024)
        Sg = B32(1024)
        dtot = B32(20)
        halo3 = B16(12, 3)
        ctpad = [B16(2, 128), B16(2, 128)]
        qtpad = [B16(4, 128), B16(4, 128)]
        a1T = B32(NT)
        KT = B16(8, 256)
        Vx = B16(2, 1024)

        def init_state(e):
            e.memset(Sst.ap, 0.0)
            e.memset(Sg.ap, 0.0)
            e.memset(dtot.ap, 1.0)
            e.memset(ctpad[0].ap, 0.0)
            e.memset(ctpad[1].ap, 0.0)
            e.memset(qtpad[0].ap, 0.0)
            e.memset(qtpad[1].ap, 0.0)
            e.memset(halo3.ap, 0.0)
            return e.memset(a1T.ap, 0.0)
        S.add("pool", init_state, writes=[Sst.k(), Sg.k(), dtot.k(), ctpad[0].k(), ctpad[1].k(), qtpad[0].k(),
                                          qtpad[1].k(), a1T.k(), halo3.k()])
        S.add("dve", lambda e: e.memset(a1T.ap[0:32, :], 1.0), reads=[a1T.k()], writes=[a1T.k()])

        xs_i = [0]

        def load_xT(row0, nt, xreads=()):
            xsb = [B32(1024), B32(1024)]
            for t in range(nt // 128):
                j = xs_i[0]
                xs_i[0] ^= 1
                xs = xsb[j]
                pw, pwk = (pwa, PWA) if j == 0 else (pwb, PWB)
                r0 = row0 + t * 128
                S.add("sp", lambda e, xs=xs, r0=r0: [e.dma_start(out=xs.ap, in_=x_d[r0:r0 + 128, :])],
                      writes=[xs.k()], reads=list(xreads), dma=True, dkey="xs%d" % xs.off)

                def tr(e, xs=xs, pw=pw):
                    r = None
                    for c in range(8):
                        r = e.transpose(out=pw[:, c * 128:(c + 1) * 128], in_=xs.ap[:, c * 128:(c + 1) * 128], identity=idf.ap)
                    return r
                S.add("pe", tr, reads=[xs.k(), idf.k()], writes=[pwk])
                S.add("act", lambda e, t=t, pw=pw: e.activation(out=hT.ap[:, :, t * 128:(t + 1) * 128],
                                                               in_=pw[:, :].rearrange("p (c n) -> p c n", c=8), func=AF.Copy),
                      reads=[pwk], writes=[hT.k()])
            xsb[0].free(); xsb[1].free()

        def norm_fm(goff, nt, stats_only=False):
            sqb = B16(8, NT)
            S.add("act", lambda e: e.activation(out=sqb.ap[:, :, 0:nt], in_=hT.ap[:, :, 0:nt], func=AF.Square),
                  reads=[hT.k()], writes=[sqb.k()])

            def mm(e):
                r = None
                for k in range(8):
                    r = e.matmul(out=psm[:, 0:nt], lhsT=ones_mean.ap, rhs=sqb.ap[:, k, 0:nt], start=(k == 0), stop=(k == 7))
                return r
            S.add("pe", mm, reads=[sqb.k(), ones_mean.k()], writes=["psm"])
            sqb.free()
            S.add("act", lambda e: e.activation(out=rstd.ap[:, 0:nt], in_=psm[:, 0:nt], func=AF.Ln, bias=EPS), reads=["psm"], writes=[rstd.k()])
            S.add("act", lambda e: e.activation(out=rstd.ap[:, 0:nt], in_=rstd.ap[:, 0:nt], func=AF.Exp, scale=-0.5), reads=[rstd.k()], writes=[rstd.k()])
            if stats_only:
                return
            tgt = nT

            def nrm(e):
                r = None
                for k in range(8):
                    r = e.scalar_tensor_tensor(out=tgt.ap[:, k, 0:nt], in0=hT.ap[:, k, 0:nt], scalar=cst.ap[:, goff + k:goff + k + 1],
                                               in1=rstd.ap[:, 0:nt], op0=ALU.mult, op1=ALU.mult)
                return r
            S.add("dve", nrm, reads=[hT.k(), rstd.k(), cst.k()], writes=[tgt.k()])

        def resid_epi(c0, nch, pv, pk):
            S.add("dve", lambda e: e.tensor_tensor(out=hT.ap[:, c0:c0 + nch, :], in0=hT.ap[:, c0:c0 + nch, :], in1=pv, op=ALU.add),
                  reads=[pk, hT.k(c0, c0 + nch)], writes=[hT.k(c0, c0 + nch)])

        def dbg_dump(name, buf, n, dt=F32):
            if not DBG or name not in os.environ.get('KDBG', '').split(','):
                return
            d = nc.dram_tensor("dbg_" + name, [128, n], dt, kind="ExternalOutput").ap()
            dbg_outs[name] = d
            flat = buf.arena.t[:, buf.off:buf.off + n]
            S.add("sp", lambda e: [e.dma_start(out=d, in_=flat)], reads=[buf.k()], dma=True, dkey="dbg_" + name, name="OUTdbg")

        def interleave(gens):
            gens = list(gens)
            while gens:
                for g in list(gens):
                    try:
                        next(g)
                    except StopIteration:
                        gens.remove(g)

        def step(row0, nt, mode, orow0=None, first_dbg=False, xreads=()):
            full = mode == "p2"
            tps = nt // 128
            load_xT(row0, nt, xreads)
            norm_fm(C_GMIX, nt)
            if first_dbg:
                dbg_dump("nT", nT, 8 * NT, BF16)
            dtb = B32(tps, 16)
            ab = B32(tps, 16)

            def dt_epi(t, col, cw, p, pk):
                tmp = B32(16)
                S.add("dve", lambda e: e.tensor_tensor(out=tmp.ap, in0=p, in1=cvec(C_DTB, 16), op=ALU.add),
                      reads=[pk, cst.k()], writes=[tmp.k()])

                S.add("act", lambda e: e.activation(out=tmp.ap, in_=tmp.ap, func=AF.Exp), reads=[tmp.k()], writes=[tmp.k()])
                S.add("act", lambda e: e.activation(out=dtb.ap[:, t, :], in_=tmp.ap, func=AF.Ln, bias=1.0), reads=[tmp.k()], writes=[dtb.k(t)])
                S.add("dve", lambda e: e.tensor_tensor(out=ab.ap[:, t, :], in0=dtb.ap[:, t, :], in1=aneg.ap, op=ALU.mult),
                      reads=[dtb.k(t), aneg.k()], writes=[ab.k(t)])
                tmp.free()
            proj_tm(w_in, O_DT, 16, nT, tps, dt_epi)
            xbr = B16(12, nt + 3)

            def xbc_epi(c0, nch, pv, pk):
                S.add("act", lambda e: e.activation(out=xbr.ap[:, c0:c0 + nch, 3:3 + nt], in_=pv, func=AF.Copy),
                      reads=[pk], writes=[xbr.k(c0, c0 + nch)])
            proj_fm(w_in, O_XBC, 1536, nT, nt, xbc_epi)
            if mode == "halo":
                S.add("dve", lambda e: e.tensor_copy(out=halo3.ap, in_=xbr.ap[:, :, nt:nt + 3]), reads=[xbr.k()], writes=[halo3.k()])
                xbr.free()
                return
            S.add("dve", lambda e: e.tensor_copy(out=xbr.ap[:, :, 0:3], in_=halo3.ap), reads=[halo3.k()], writes=[xbr.k()])
            xsT = B16(8, nt)
            BT = B16(2, nt)
            CT = B16(2, nt)
            CW = min(256, nt)
            NG = nt // CW
            caccs = {}

            def conv_start(hh):
                cacc = B32(12, CW)
                caccs[hh] = cacc
                o0 = hh * CW

                def conv0(e):
                    r = None
                    for c in range(12):
                        r = e.activation(out=cacc.ap[:, c, :], in_=xbr.ap[:, c, o0:o0 + CW], func=AF.Copy,
                                         scale=cst.ap[:, C_CONVW + 4 * c:C_CONVW + 4 * c + 1])
                    return r
                S.add("act", conv0, reads=[xbr.k(), cst.k()], writes=[cacc.k()])

                def convk(k):
                    def f(e):
                        r = None
                        for c in range(12):
                            r = e.scalar_tensor_tensor(out=cacc.ap[:, c, :], in0=xbr.ap[:, c, o0 + k:o0 + k + CW],
                                                       scalar=cst.ap[:, C_CONVW + 4 * c + k:C_CONVW + 4 * c + k + 1],
                                                       in1=cacc.ap[:, c, :], op0=ALU.mult, op1=ALU.add)
                        return r
                    return f
                chain("dve", [convk(1), convk(2), convk(3)], [xbr.k(), cst.k()], [cacc.k()])
                if hh == NG - 1:
                    S.add("dve", lambda e: e.tensor_copy(out=halo3.ap, in_=xbr.ap[:, :, nt:nt + 3]), reads=[xbr.k()], writes=[halo3.k()])
                    xbr.free()

            def conv_finish(hh):
                cacc = caccs[hh]
                o0 = hh * CW

                def csilu(e):
                    r = None
                    for c in range(12):
                        dst = xsT.ap[:, c, o0:o0 + CW] if c < 8 else (BT.ap[:, c - 8, o0:o0 + CW] if c < 10 else CT.ap[:, c - 10, o0:o0 + CW])
                        r = e.activation(out=dst, in_=cacc.ap[:, c, :], func=AF.Silu, bias=cst.ap[:, C_CONVB + c:C_CONVB + c + 1])
                    return r
                S.add("act", csilu, reads=[cacc.k(), cst.k()], writes=[xsT.k(), BT.k(), CT.k()])
                cacc.free()
            for hh_ in range(NG):
                conv_start(hh_)
            zs = None
            if full:
                zs = B16(tps, 1024)

                def z_epi(t, col, cw, p, pk):
                    S.add("act", lambda e: e.activation(out=zs.ap[:, t, col:col + cw], in_=p, func=AF.Silu),
                          reads=[pk], writes=[zs.kr(t * 1024 + col, t * 1024 + col + cw)])
                proj_tm(w_in, O_Z, 1024, nT, tps, z_epi)
            qT = B16(4, nt) if full else None
            kT = B16(4, nt) if full else None
            ktm = B16(tps, 512)
            vtm = B16(tps, 1024)
            rs = B16(tps, 1024) if full else None
            if full:
                def q_epi(c0, nch, pv, pk):
                    S.add("act", lambda e: e.activation(out=qT.ap[:, c0:c0 + nch, :], in_=pv, func=AF.Copy, scale=float(128 ** -0.5)),
                          reads=[pk], writes=[qT.k(c0, c0 + nch)])
                proj_fm(w_in, O_Q, 512, nT, nt, q_epi)
            wap, wkey = wload(w_in, 8, O_K, 512)
            if full:
                gsz = max(1, 512 // nt)
                for cg in range(0, 4, gsz):
                    p, pk = next_pa6()
                    pv = p[:, 0:gsz * nt].rearrange("p (c n) -> p c n", c=gsz)

                    def mmk(e, wap=wap, pv=pv, cg=cg, gsz=gsz):
                        r = None
                        for c in range(gsz):
                            for k in range(8):
                                r = e.matmul(out=pv[:, c, :], lhsT=wap[:, k, (cg + c) * 128:(cg + c + 1) * 128], rhs=nT.ap[:, k, 0:nt],
                                             start=(k == 0), stop=(k == 7))
                        return r
                    S.add("pe", mmk, reads=[wkey, nT.k()], writes=[pk])
                    S.add("act", lambda e, pv=pv, cg=cg, gsz=gsz: e.activation(out=kT.ap[:, cg:cg + gsz, :], in_=pv, func=AF.Copy),
                          reads=[pk], writes=[kT.k(cg, cg + gsz)])
            for t in range(tps):
                p, pk = next_pa6()

                def mmk2(e, wap=wap, p=p, t=t):
                    r = None
                    for k in range(8):
                        r = e.matmul(out=p[:, 0:512], lhsT=nT.ap[:, k, t * 128:(t + 1) * 128], rhs=wap[:, k, :], start=(k == 0), stop=(k == 7))
                    return r
                S.add("pe", mmk2, reads=[wkey, nT.k()], writes=[pk])
                S.add("act", lambda e, p=p, t=t: e.activation(out=ktm.ap[:, t, :], in_=p[:, 0:512], func=AF.Copy), reads=[pk], writes=[ktm.k(t)])

            def a1_epi(c0, nch, pv, pk):
                S.add("act", lambda e: e.activation(out=a1T.ap[0:16, 0:nt], in_=pv[0:16, 0, :], func=AF.Copy), reads=[pk], writes=[a1T.k()])
            proj_fm(w_in, O_A1, 128, nT, nt, a1_epi)

            def v_epi(t, col, cw, p, pk):
                S.add("act", lambda e: e.activation(out=vtm.ap[:, t, col:col + cw], in_=p, func=AF.Copy),
                      reads=[pk], writes=[vtm.kr(t * 1024 + col, t * 1024 + col + cw)])
            proj_tm(w_in, O_V, 1024, nT, tps, v_epi)
            if full:
                def r_epi(t, col, cw, p, pk):
                    S.add("act", lambda e: e.activation(out=rs.ap[:, t, col:col + cw], in_=p, func=AF.Silu),
                          reads=[pk], writes=[rs.kr(t * 1024 + col, t * 1024 + col + cw)])
                proj_tm(w_in, O_R, 1024, nT, tps, r_epi)
            for hh_ in range(NG):
                conv_finish(hh_)
            ynT = B16(8, nt) if full else None
            onT = B16(8, nt) if full else None
            for t in range(tps):
                interleave([ssd_tile(t, xsT, BT, CT, dtb, ab, zs, ynT, full, False),
                            gla_tile(t, qT, kT, ktm, vtm, rs, onT, full, False)])
            xsT.free(); BT.free(); CT.free(); dtb.free(); ab.free()
            if zs is not None:
                zs.free()
            ktm.free(); vtm.free()
            if not full:
                return
            qT.free(); kT.free(); rs.free()
            if first_dbg:
                dbg_dump("onT", onT, 8 * NT, BF16)
            gsT = B16(8, nt)
            ggT = B16(8, nt)
            mT = B16(8, nt)

            def gs_epi(c0, nch, pv, pk):
                S.add("act", lambda e: e.activation(out=gsT.ap[:, c0:c0 + nch, :], in_=pv, func=AF.Sigmoid), reads=[pk], writes=[gsT.k(c0, c0 + nch)])

            def gg_epi(c0, nch, pv, pk):
                S.add("act", lambda e: e.activation(out=ggT.ap[:, c0:c0 + nch, :], in_=pv, func=AF.Sigmoid), reads=[pk], writes=[ggT.k(c0, c0 + nch)])
            proj_fm(w_in, O_GS, 1024, nT, nt, gs_epi)
            proj_fm(w_in, O_GG, 1024, nT, nt, gg_epi)

            def ups_epi(c0, nch, pv, pk):
                S.add("dve", lambda e: e.tensor_tensor(out=gsT.ap[:, c0:c0 + nch, :], in0=gsT.ap[:, c0:c0 + nch, :], in1=pv, op=ALU.mult),
                      reads=[pk, gsT.k(c0, c0 + nch)], writes=[gsT.k(c0, c0 + nch)])
            proj_fm(w_ups, 0, 1024, ynT, nt, ups_epi)

            def upg_epi(c0, nch, pv, pk):
                S.add("dve", lambda e: e.tensor_tensor(out=ggT.ap[:, c0:c0 + nch, :], in0=ggT.ap[:, c0:c0 + nch, :], in1=pv, op=ALU.mult),
                      reads=[pk, ggT.k(c0, c0 + nch)], writes=[ggT.k(c0, c0 + nch)])
                S.add("dve", lambda e: e.tensor_tensor(out=mT.ap[:, c0:c0 + nch, :], in0=ggT.ap[:, c0:c0 + nch, :], in1=gsT.ap[:, c0:c0 + nch, :], op=ALU.add),
                      reads=[ggT.k(c0, c0 + nch), gsT.k(c0, c0 + nch)], writes=[mT.k(c0, c0 + nch)])
            proj_fm(w_upg, 0, 1024, onT, nt, upg_epi)
            ynT.free(); onT.free(); gsT.free(); ggT.free()
            proj_fm(w_o, 0, 1024, mT, nt, resid_epi)
            mT.free()
            if first_dbg:
                dbg_dump("h1", hT, 8 * NT)
            norm_fm(C_GXA, nt)
            qx = B16(8, nt)
            ox = B16(8, nt)

            def qx_epi(c0, nch, pv, pk):
                S.add("act", lambda e: e.activation(out=qx.ap[:, c0:c0 + nch, :], in_=pv, func=AF.Copy, scale=1.0 / 16.0),
                      reads=[pk], writes=[qx.k(c0, c0 + nch)])
            proj_fm(w_xq, 0, 1024, nT, nt, qx_epi)
            for hd in range(4):
                ET = B16(2, nt)
                for mc in range(2):
                    p, pk = next_pa()

                    def sc(e, p=p, hd=hd, mc=mc):
                        r = None
                        for dc in range(2):
                            r = e.matmul(out=p[:, 0:nt], lhsT=KT.ap[:, hd * 2 + dc, mc * 128:(mc + 1) * 128], rhs=qx.ap[:, hd * 2 + dc, :],
                                         start=(dc == 0), stop=(dc == 1))
                        return r
                    S.add("pe", sc, reads=[KT.k(), qx.k(hd * 2, hd * 2 + 2)], writes=[pk])
                    S.add("act", lambda e, p=p, mc=mc, ET=ET: e.activation(out=ET.ap[:, mc, :], in_=p[:, 0:nt], func=AF.Exp),
                          reads=[pk], writes=[ET.k(mc)])

                def den(e, ET=ET):
                    e.matmul(out=psm[:, 0:nt], lhsT=ones1.ap, rhs=ET.ap[:, 0, :], start=True, stop=False)
                    return e.matmul(out=psm[:, 0:nt], lhsT=ones1.ap, rhs=ET.ap[:, 1, :], start=False, stop=True)
                S.add("pe", den, reads=[ET.k(), ones1.k()], writes=["psm"])
                rden = B32(nt)
                S.add("dve", lambda e, rden=rden: e.reciprocal(out=rden.ap, in_=psm[:, 0:nt]), reads=["psm"], writes=[rden.k()])
                for dc in range(2):
                    p, pk = next_pa()

                    def pvm(e, p=p, hd=hd, dc=dc, ET=ET):
                        r = None
                        for mc in range(2):
                            r = e.matmul(out=p[:, 0:nt], lhsT=Vx.ap[:, mc, hd * 256 + dc * 128:hd * 256 + dc * 128 + 128], rhs=ET.ap[:, mc, :],
                                         start=(mc == 0), stop=(mc == 1))
                        return r
                    S.add("pe", pvm, reads=[Vx.k(), ET.k()], writes=[pk])
                    S.add("dve", lambda e, p=p, hd=hd, dc=dc, rden=rden: e.tensor_tensor(out=ox.ap[:, hd * 2 + dc, :], in0=p[:, 0:nt], in1=rden.ap, op=ALU.mult),
                          reads=[pk, rden.k()], writes=[ox.k(hd * 2 + dc)])
                ET.free(); rden.free()
            qx.free()
            proj_fm(w_xo, 0, 1024, ox, nt, resid_epi)
            ox.free()
            if first_dbg:
                dbg_dump("h2", hT, 8 * NT)
            norm_fm(C_GFFN, nt)
            aT = B16(22, nt)
            col = 0
            while col < DFF:
                cw = min(512, DFF - col)

                def g_epi(c0, nch, pv, pk, col=col):
                    cc = col // 128 + c0
                    S.add("act", lambda e: e.activation(out=aT.ap[:, cc:cc + nch, :], in_=pv, func=AF.Silu), reads=[pk], writes=[aT.k(cc, cc + nch)])

                def u_epi(c0, nch, pv, pk, col=col):
                    cc = col // 128 + c0
                    S.add("dve", lambda e: e.tensor_tensor(out=aT.ap[:, cc:cc + nch, :], in0=aT.ap[:, cc:cc + nch, :], in1=pv, op=ALU.mult),
                          reads=[pk, aT.k(cc, cc + nch)], writes=[aT.k(cc, cc + nch)])
                proj_fm(w_fi, col, cw, nT, nt, g_epi)
                proj_fm(w_fi, DFF + col, cw, nT, nt, u_epi)
                col += cw
            proj_fm(w_fo, 0, 1024, aT, nt, resid_epi, kc=22, blk=128)
            aT.free()
            if first_dbg:
                dbg_dump("h3", hT, 8 * NT)
            norm_fm(C_GFIN, nt, stats_only=True)
            for t in range(tps):
                of = B32(8, 128)

                def nrm_t(e, t=t, of=of):
                    r = None
                    for k in range(8):
                        r = e.scalar_tensor_tensor(out=of.ap[:, k, :], in0=hT.ap[:, k, t * 128:(t + 1) * 128], scalar=cst.ap[:, C_GFIN + k:C_GFIN + k + 1],
                                                   in1=rstd.ap[:, t * 128:(t + 1) * 128], op0=ALU.mult, op1=ALU.mult)
                    return r
                S.add("dve", nrm_t, reads=[hT.k(), rstd.k(), cst.k()], writes=[of.k()])

                def tr(e, t=t, of=of):
                    r = None
                    for c in range(8):
                        r = e.transpose(out=pwa[:, c * 128:(c + 1) * 128], in_=of.ap[:, c, :], identity=idf.ap)
                    return r
                S.add("pe", tr, reads=[of.k(), idf.k()], writes=[PWA])
                of.free()
                og = B32(1024)
                S.add("act", lambda e, og=og: e.activation(out=og.ap, in_=pwa[:, :], func=AF.Copy), reads=[PWA], writes=[og.k()])
                r0 = orow0 + t * 128
                S.add("sp", lambda e, og=og, r0=r0: [e.dma_start(out=out_d[r0:r0 + 128, :], in_=og.ap)], reads=[og.k()],
                      dma=True, dkey="og%d" % og.off, name="OUT")
                og.free()

        def ssd_tile_p1(t, xsT, BT, dtb, ab):
            cols = slice(t * 128, (t + 1) * 128)
            a_t = ab.ap[:, t, :]
            dt_t = dtb.ap[:, t, :]
            Btm = B16(2, 128)
            Xd = B16(1024)
            ex = B32(32)
            dd = B32(16)

            def trx(e):
                r = None
                for c in range(8):
                    r = e.transpose(out=ptr[:, c * 128:(c + 1) * 128], in_=xsT.ap[:, c, cols], identity=idb.ap)
                return r

            def smalls(e):
                e.matmul(out=psm[:, 0:16], lhsT=strif.ap, rhs=a_t, start=True, stop=True)
                return e.matmul(out=psm[:, 16:32], lhsT=onesf.ap, rhs=a_t, start=True, stop=True)
            S.add("pe", smalls, reads=[ab.k(t), strif.k(), onesf.k()], writes=["psm"])
            S.add("act", lambda e: e.activation(out=ex.ap, in_=psm[:, 0:32], func=AF.Exp), reads=["psm"], writes=[ex.k()])
            yield
            S.add("dve", lambda e: e.tensor_tensor(out=dd.ap, in0=ex.ap[:, 0:16], in1=dt_t, op=ALU.mult), reads=[ex.k(), dtb.k(t)], writes=[dd.k()])
            S.add("pe", trx, reads=[xsT.k(), idb.k()], writes=["ptr"])
            yield
            S.add("dve", lambda e: e.tensor_tensor(out=Xd.ap.rearrange("p (h q) -> p h q", h=16),
                                                   in0=ptr[:, :].rearrange("p (h q) -> p h q", h=16),
                                                   in1=dd.ap.unsqueeze(2).to_broadcast([128, 16, 64]), op=ALU.mult),
                  reads=["ptr", dd.k()], writes=[Xd.k()])
            yield

            def trb(e):
                e.transpose(out=ptr[:, 0:128], in_=BT.ap[:, 0, cols], identity=idb.ap)
                return e.transpose(out=ptr[:, 128:256], in_=BT.ap[:, 1, cols], identity=idb.ap)
            S.add("pe", trb, reads=[BT.k(), idb.k()], writes=["ptr"])
            S.add("act", lambda e: e.activation(out=Btm.ap, in_=ptr[:, 0:256].rearrange("p (g n) -> p g n", g=2), func=AF.Copy),
                  reads=["ptr"], writes=[Btm.k()])
            yield

            def sloc(e):
                e.matmul(out=pwb[:, 0:512], lhsT=Btm.ap[:, 0, :], rhs=Xd.ap[:, 0:512], start=True, stop=True)
                return e.matmul(out=pwb[:, 512:1024], lhsT=Btm.ap[:, 1, :], rhs=Xd.ap[:, 512:1024], start=True, stop=True)
            S.add("pe", sloc, reads=[Btm.k(), Xd.k()], writes=[PWB])
            S.add("dve", lambda e: e.tensor_tensor(out=Sst.ap.rearrange("p (h q) -> p h q", h=16), in0=Sst.ap.rearrange("p (h q) -> p h q", h=16),
                                                   in1=ex.ap[:, 16:32].unsqueeze(2).to_broadcast([128, 16, 64]), op=ALU.mult),
                  reads=[ex.k(), Sst.k()], writes=[Sst.k()])
            yield
            S.add("dve", lambda e: e.tensor_tensor(out=Sst.ap, in0=Sst.ap, in1=pwb[:, :], op=ALU.add), reads=[PWB, Sst.k()], writes=[Sst.k()])
            yield
            Btm.free(); Xd.free(); ex.free(); dd.free()

        def ssd_tile(t, xsT, BT, CT, dtb, ab, zs, ynT, full, dbg):
            cols = slice(t * 128, (t + 1) * 128)
            a_t = ab.ap[:, t, :]
            dt_t = dtb.ap[:, t, :]
            if not full:
                yield from ssd_tile_p1(t, xsT, BT, dtb, ab)
                return
            Xtm = B16(1024)
            xstm = B16(1024) if full else None
            Btm = B16(2, 128)

            def trx(e):
                r = None
                for c in range(8):
                    r = e.transpose(out=ptr[:, c * 128:(c + 1) * 128], in_=xsT.ap[:, c, cols], identity=idb.ap)
                return r
            S.add("pe", trx, reads=[xsT.k(), idb.k()], writes=["ptr"])
            yield
            S.add("dve", lambda e: e.tensor_tensor(out=Xtm.ap.rearrange("p (h q) -> p h q", h=16),
                                                   in0=ptr[:, :].rearrange("p (h q) -> p h q", h=16),
                                                   in1=dt_t.unsqueeze(2).to_broadcast([128, 16, 64]), op=ALU.mult),
                  reads=["ptr", dtb.k(t)], writes=[Xtm.k()])
            yield
            if full:
                S.add("dve", lambda e: e.tensor_tensor(out=xstm.ap.rearrange("p (h q) -> p h q", h=16),
                                                       in0=ptr[:, :].rearrange("p (h q) -> p h q", h=16),
                                                       in1=cvec(C_DSK, 16).unsqueeze(2).to_broadcast([128, 16, 64]), op=ALU.mult),
                      reads=["ptr", cst.k()], writes=[xstm.k()])
                yield

            def trb(e):
                e.transpose(out=ptr[:, 0:128], in_=BT.ap[:, 0, cols], identity=idb.ap)
                return e.transpose(out=ptr[:, 128:256], in_=BT.ap[:, 1, cols], identity=idb.ap)
            S.add("pe", trb, reads=[BT.k(), idb.k()], writes=["ptr"])
            yield
            S.add("act", lambda e: e.activation(out=Btm.ap, in_=ptr[:, 0:256].rearrange("p (g n) -> p g n", g=2), func=AF.Copy),
                  reads=["ptr"], writes=[Btm.k()])
            yield
            def smalls(e):
                e.matmul(out=psm[:, 0:16], lhsT=tri.ap, rhs=a_t, start=True, stop=True)
                e.matmul(out=psm[:, 16:32], lhsT=stri.ap, rhs=a_t, start=True, stop=True)
                e.matmul(out=psm[:, 32:48], lhsT=onesc.ap[:, 0, :], rhs=a_t, start=True, stop=True)
                return e.matmul(out=psm[:, 48:64], lhsT=onesc.ap[:, 1, :], rhs=a_t, start=True, stop=True)
            S.add("pe", smalls, reads=[ab.k(t), tri.k(), stri.k(), onesc.k()], writes=["psm"])
            yield
            ex = B32(64)
            S.add("act", lambda e: e.activation(out=ex.ap, in_=psm[:, 0:64], func=AF.Exp), reads=["psm"], writes=[ex.k()])
            yield
            eacs = ex.ap[:, 0:16]
            if dbg:
                dbg_dump("Xtm", Xtm, 1024, BF16)
                dbg_dump("ex", ex, 64)
                dbg_dump("dtb", dtb, 16)
            ds = B32(2, 16)

            def mkds(e):
                e.tensor_scalar(out=ds.ap[:, 0, :], in0=ex.ap[:, 16:32], scalar1=cm.ap[:, 0:1], scalar2=None, op0=ALU.mult)
                return e.tensor_scalar(out=ds.ap[:, 1, :], in0=ex.ap[:, 16:32], scalar1=cm.ap[:, 1:2], scalar2=None, op0=ALU.mult)
            S.add("dve", mkds, reads=[ex.k(), cm.k()], writes=[ds.k()])
            yield
            Xd = [B16(1024), B16(1024)]
            for c in range(2):
                S.add("dve", lambda e, c=c: e.tensor_tensor(out=Xd[c].ap.rearrange("p (h q) -> p h q", h=16),
                                                            in0=Xtm.ap.rearrange("p (h q) -> p h q", h=16),
                                                            in1=ds.ap[:, c, :].unsqueeze(2).to_broadcast([128, 16, 64]), op=ALU.mult),
                      reads=[Xtm.k(), ds.k()], writes=[Xd[c].k()])
                yield
            MT = None
            if full:
                MT = B16(16, 128)
                for g in range(2):
                    rhs_all = B32(8, 128)
                    S.add("dve", lambda e, g=g, rhs_all=rhs_all: e.tensor_tensor(
                        out=rhs_all.ap, in0=tri.ap.unsqueeze(1).to_broadcast([128, 8, 128]),
                        in1=a_t[:, g * 8:(g + 1) * 8].unsqueeze(2).to_broadcast([128, 8, 128]), op=ALU.mult),
                        reads=[tri.k(), ab.k(t)], writes=[rhs_all.k()])
                    yield

                    def dmm(e, rhs_all=rhs_all):
                        e.matmul(out=pwa[:, 0:512], lhsT=stri.ap, rhs=rhs_all.ap[:, 0:4, :], start=True, stop=True)
                        return e.matmul(out=pwa[:, 512:1024], lhsT=stri.ap, rhs=rhs_all.ap[:, 4:8, :], start=True, stop=True)
                    S.add("pe", dmm, reads=[rhs_all.k(), stri.k()], writes=[PWA])
                    yield
                    E = B16(8, 128)

                    def eexp(e, E=E):
                        e.activation(out=E.ap[:, 0:4, :], in_=pwa[:, 0:512].rearrange("p (h l) -> p h l", h=4), func=AF.Exp)
                        return e.activation(out=E.ap[:, 4:8, :], in_=pwa[:, 512:1024].rearrange("p (h l) -> p h l", h=4), func=AF.Exp)
                    S.add("act", eexp, reads=[PWA], writes=[E.k()])
                    yield
                    S.add("pe", lambda e, g=g: e.matmul(out=psm[:, 128 + g * 128:256 + g * 128], lhsT=BT.ap[:, g, cols], rhs=CT.ap[:, g, cols],
                                                        start=True, stop=True), reads=[BT.k(), CT.k()], writes=["psm"])
                    yield
                    cbm = B32(128)
                    S.add("dve", lambda e, g=g, cbm=cbm: e.tensor_tensor(out=cbm.ap, in0=psm[:, 128 + g * 128:256 + g * 128], in1=tri.ap, op=ALU.mult),
                          reads=["psm", tri.k()], writes=[cbm.k()])
                    yield
                    S.add("dve", lambda e, g=g, cbm=cbm, E=E: e.tensor_tensor(out=MT.ap[:, g * 8:(g + 1) * 8, :], in0=E.ap,
                                                                               in1=cbm.ap.unsqueeze(1).to_broadcast([128, 8, 128]), op=ALU.mult),
                          reads=[E.k(), cbm.k()], writes=[MT.k(g * 8, (g + 1) * 8)])
                    yield
                    rhs_all.free(); E.free(); cbm.free()
                for c in range(2):
                    S.add("act", lambda e, c=c: e.activation(out=ctpad[c].ap[:, :, c * 64:(c + 1) * 64],
                                                             in_=CT.ap[:, :, t * 128 + c * 64:t * 128 + (c + 1) * 64], func=AF.Copy),
                          reads=[CT.k()], writes=[ctpad[c].k()])
                    yield
            Sbf = [B16(1024), B16(1024)] if full else None
            for c in range(2):
                def sloc(e, c=c):
                    e.matmul(out=pwb[:, 0:512], lhsT=Btm.ap[:, 0, :], rhs=Xd[c].ap[:, 0:512], start=True, stop=True)
                    return e.matmul(out=pwb[:, 512:1024], lhsT=Btm.ap[:, 1, :], rhs=Xd[c].ap[:, 512:1024], start=True, stop=True)
                S.add("pe", sloc, reads=[Btm.k(), Xd[c].k()], writes=[PWB])
                yield
                if full:
                    S.add("act", lambda e, c=c: e.activation(out=Sbf[c].ap, in_=Sst.ap, func=AF.Copy), reads=[Sst.k()], writes=[Sbf[c].k()])
                    yield

                def supd(e, c=c):
                    e.tensor_tensor(out=dtot.ap[:, 0:16], in0=dtot.ap[:, 0:16], in1=ex.ap[:, 32 + 16 * c:48 + 16 * c], op=ALU.mult)
                    return e.tensor_tensor(out=Sst.ap.rearrange("p (h q) -> p h q", h=16), in0=Sst.ap.rearrange("p (h q) -> p h q", h=16),
                                           in1=ex.ap[:, 32 + 16 * c:48 + 16 * c].unsqueeze(2).to_broadcast([128, 16, 64]), op=ALU.mult)
                S.add("dve", supd, reads=[ex.k(), Sst.k(), dtot.kr(0, 16)], writes=[Sst.k(), dtot.kr(0, 16)])
                yield
                S.add("dve", lambda e: e.tensor_tensor(out=Sst.ap, in0=Sst.ap, in1=pwb[:, :], op=ALU.add), reads=[PWB, Sst.k()], writes=[Sst.k()])
                yield
            Xd[0].free(); Xd[1].free(); Btm.free(); ds.free()
            if not full:
                Xtm.free(); ex.free()
                return
            def ymm(e):
                r = None
                for b in range(2):
                    e.matmul(out=pwa[:, b * 512:(b + 1) * 512], lhsT=idb.ap, rhs=xstm.ap[:, b * 512:(b + 1) * 512], start=True, stop=False)
                    for hh in range(8):
                        h = b * 8 + hh
                        r = e.matmul(out=pwa[:, h * 64:(h + 1) * 64], lhsT=MT.ap[:, h, :], rhs=Xtm.ap[:, h * 64:(h + 1) * 64],
                                     start=False, stop=(hh == 7))
                return r
            S.add("pe", ymm, reads=[idb.k(), xstm.k(), MT.k(), Xtm.k()], writes=[PWA])
            yield

            def yoff(e):
                r = None
                for g in range(2):
                    for c in range(2):
                        r = e.matmul(out=pwb[:, g * 512:(g + 1) * 512], lhsT=ctpad[c].ap[:, g, :], rhs=Sbf[c].ap[:, g * 512:(g + 1) * 512],
                                     start=(c == 0), stop=(c == 1))
                return r
            S.add("pe", yoff, reads=[ctpad[0].k(), ctpad[1].k(), Sbf[0].k(), Sbf[1].k()], writes=[PWB])
            yield
            yt = B32(1024)

            S.add("dve", lambda e: e.tensor_tensor(out=yt.ap.rearrange("p (h q) -> p h q", h=16), in0=pwb[:, :].rearrange("p (h q) -> p h q", h=16),
                                                   in1=eacs.unsqueeze(2).to_broadcast([128, 16, 64]), op=ALU.mult),
                  reads=[PWB, ex.k()], writes=[yt.k()])
            yield
            S.add("dve", lambda e: e.tensor_tensor(out=yt.ap, in0=yt.ap, in1=pwa[:, :], op=ALU.add), reads=[PWA, yt.k()], writes=[yt.k()])
            yield
            S.add("dve", lambda e: e.tensor_tensor(out=yt.ap, in0=yt.ap, in1=zs.ap[:, t, :], op=ALU.mult), reads=[yt.k(), zs.k(t)], writes=[yt.k()])
            yield
            if dbg:
                dbg_dump("yt", yt, 1024)
                dbg_dump("MT", MT, 2048, BF16)
                dbg_dump("Sbf1", Sbf[1], 1024, BF16)
            Xtm.free(); xstm.free(); MT.free(); Sbf[0].free(); Sbf[1].free(); ex.free()
            junk = B16(1024)
            ss = B32(2)

            def ysq(e):
                e.activation(out=junk.ap[:, 0:512], in_=yt.ap[:, 0:512], func=AF.Square, accum_out=ss.ap[:, 0:1])
                return e.activation(out=junk.ap[:, 512:1024], in_=yt.ap[:, 512:1024], func=AF.Square, accum_out=ss.ap[:, 1:2])
            S.add("act", ysq, reads=[yt.k()], writes=[junk.k(), ss.k()])
            yield
            chain("act", [lambda e: e.activation(out=ss.ap, in_=ss.ap, func=AF.Ln, bias=EPS, scale=1.0 / 512.0),
                          lambda e: e.activation(out=ss.ap, in_=ss.ap, func=AF.Exp, scale=-0.5)], [], [ss.k()])
            yield
            yn = B16(1024)

            def ynorm(e):
                e.scalar_tensor_tensor(out=yn.ap[:, 0:512], in0=yt.ap[:, 0:512], scalar=ss.ap[:, 0:1], in1=cvec(C_SSDN, 512), op0=ALU.mult, op1=ALU.mult)
                return e.scalar_tensor_tensor(out=yn.ap[:, 512:1024], in0=yt.ap[:, 512:1024], scalar=ss.ap[:, 1:2], in1=cvec(C_SSDN + 512, 512),
                                              op0=ALU.mult, op1=ALU.mult)
            S.add("dve", ynorm, reads=[yt.k(), ss.k(), cst.k()], writes=[yn.k()])
            yield

            def try_(e):
                r = None
                for c in range(8):
                    r = e.transpose(out=ptr[:, c * 128:(c + 1) * 128], in_=yn.ap[:, c * 128:(c + 1) * 128], identity=idb.ap)
                return r
            S.add("pe", try_, reads=[yn.k(), idb.k()], writes=["ptr"])
            yield
            S.add("act", lambda e: e.activation(out=ynT.ap[:, :, cols], in_=ptr[:, :].rearrange("p (c n) -> p c n", c=8), func=AF.Copy),
                  reads=["ptr"], writes=[ynT.k()])
            yield
            yt.free(); junk.free(); ss.free(); yn.free()

        def gla_tile(t, qT, kT, ktm, vtm, rs, onT, full, dbg=False):
            cols = slice(t * 128, (t + 1) * 128)
            p0, pk0 = next_pa()
            S.add("pe", lambda e: e.matmul(out=p0[:, 0:512], lhsT=a1T.ap[:, cols], rhs=w2a.ap, start=True, stop=True),
                  reads=[a1T.k(), w2a.k()], writes=[pk0])
            yield
            la = B32(512)

            S.add("act", lambda e: e.activation(out=la.ap, in_=p0[:, 0:512], func=AF.Exp, scale=-1.0), reads=[pk0], writes=[la.k()])
            yield
            S.add("act", lambda e: e.activation(out=la.ap, in_=la.ap, func=AF.Ln, bias=1.0), reads=[la.k()], writes=[la.k()])
            yield
            if dbg:
                dbg_dump("la", la, 512)
            if not full:
                pd, pkd = next_pa()

                def dmm_(e):
                    r = None
                    for hd in range(4):
                        r = e.matmul(out=pd[:, hd * 2:hd * 2 + 2], lhsT=la.ap[:, hd * 128:(hd + 1) * 128], rhs=negc.ap, start=True, stop=True)
                    return r
                S.add("pe", dmm_, reads=[la.k(), negc.k()], writes=[pkd])
                dec8 = B32(8)
                S.add("act", lambda e: e.activation(out=dec8.ap, in_=pd[:, 0:8], func=AF.Exp), reads=[pkd], writes=[dec8.k()])
                yield
                pe_, pke = next_pa()
                S.add("pe", lambda e: e.matmul(out=pe_[:, 0:512], lhsT=strigf.ap, rhs=la.ap, start=True, stop=True), reads=[strigf.k(), la.k()], writes=[pke])
                khf = B16(512)
                Ekf = B32(512)
                S.add("act", lambda e: e.activation(out=Ekf.ap, in_=pe_[:, 0:512], func=AF.Exp), reads=[pke], writes=[Ekf.k()])
                yield
                S.add("dve", lambda e: e.tensor_tensor(out=khf.ap, in0=Ekf.ap, in1=ktm.ap[:, t, :], op=ALU.mult), reads=[Ekf.k(), ktm.k(t)], writes=[khf.k()])
                yield
                for hp in range(2):
                    pu, pku = next_pa()

                    def umm_(e, hp=hp, pu=pu):
                        r = None
                        for h2 in range(2):
                            hd = hp * 2 + h2
                            r = e.matmul(out=pu[:, h2 * 256:(h2 + 1) * 256], lhsT=khf.ap[:, hd * 128:(hd + 1) * 128],
                                         rhs=vtm.ap[:, t, hd * 256:(hd + 1) * 256], start=True, stop=True)
                        return r
                    S.add("pe", umm_, reads=[khf.k(), vtm.k(t)], writes=[pku])

                    def gupd_(e, hp=hp, pu=pu):
                        r = None
                        for h2 in range(2):
                            hd = hp * 2 + h2
                            r = e.scalar_tensor_tensor(out=Sg.ap[:, hd * 256:(hd + 1) * 256], in0=Sg.ap[:, hd * 256:(hd + 1) * 256],
                                                       scalar=dec8.ap[:, hd * 2:hd * 2 + 1], in1=pu[:, h2 * 256:(h2 + 1) * 256],
                                                       op0=ALU.mult, op1=ALU.add)
                        return r
                    S.add("dve", gupd_, reads=[pku, dec8.k(), Sg.kr(hp * 512, hp * 512 + 512)], writes=[Sg.kr(hp * 512, hp * 512 + 512)])
                    yield
                la.free(); dec8.free(); khf.free(); Ekf.free()
                return
            p1, pk1 = next_pa()

            def bc(e):
                r = None
                for hd in range(4):
                    r = e.matmul(out=p1[:, hd * 128:(hd + 1) * 128], lhsT=la.ap[:, hd * 128:(hd + 1) * 128], rhs=trig.ap, start=True, stop=True)
                return r
            S.add("pe", bc, reads=[la.k(), trig.k()], writes=[pk1])
            yield
            EqT = B32(4, 128)
            S.add("act", lambda e: e.activation(out=EqT.ap, in_=p1[:, 0:512].rearrange("p (h l) -> p h l", h=4), func=AF.Exp),
                  reads=[pk1], writes=[EqT.k()])
            yield
            ktT = None
            if full:
                EkT = B32(4, 128)
                S.add("act", lambda e: e.activation(out=EkT.ap, in_=p1[:, 0:512].rearrange("p (h l) -> p h l", h=4), func=AF.Exp, scale=-1.0),
                      reads=[pk1], writes=[EkT.k()])
                yield
                ktT = B16(4, 128)
                S.add("dve", lambda e: e.tensor_tensor(out=ktT.ap, in0=kT.ap[:, :, cols], in1=EkT.ap, op=ALU.mult),
                      reads=[kT.k(), EkT.k()], writes=[ktT.k()])
                yield
                for c in range(2):
                    S.add("dve", lambda e, c=c: e.tensor_tensor(out=qtpad[c].ap[:, :, c * 64:(c + 1) * 64],
                                                                in0=qT.ap[:, :, t * 128 + c * 64:t * 128 + (c + 1) * 64],
                                                                in1=EqT.ap[:, :, c * 64:(c + 1) * 64], op=ALU.mult),
                          reads=[qT.k(), EqT.k()], writes=[qtpad[c].k()])
                    yield
                EkT.free()
            p2, pk2 = next_pa()
            S.add("pe", lambda e: e.matmul(out=p2[:, 0:512], lhsT=strig.ap, rhs=la.ap, start=True, stop=True), reads=[strig.k(), la.k()], writes=[pk2])
            yield
            Ekh = B32(512)
            S.add("act", lambda e: e.activation(out=Ekh.ap, in_=p2[:, 0:512], func=AF.Exp), reads=[pk2], writes=[Ekh.k()])
            yield
            kh = [B16(512), B16(512)]
            for c in range(2):
                S.add("dve", lambda e, c=c: e.scalar_tensor_tensor(out=kh[c].ap, in0=Ekh.ap, scalar=cm.ap[:, c:c + 1], in1=ktm.ap[:, t, :],
                                                                   op0=ALU.mult, op1=ALU.mult),
                      reads=[Ekh.k(), cm.k(), ktm.k(t)], writes=[kh[c].k()])
                yield
            la.free(); Ekh.free()
            attT = None
            if full:
                p3, pk3 = next_pa()

                def att(e):
                    r = None
                    for hd in range(4):
                        e.matmul(out=p3[:, hd * 128:(hd + 1) * 128], lhsT=ktT.ap[:, hd, :], rhs=qtpad[0].ap[:, hd, :], start=True, stop=False)
                        r = e.matmul(out=p3[:, hd * 128:(hd + 1) * 128], lhsT=ktT.ap[:, hd, :], rhs=qtpad[1].ap[:, hd, :], start=False, stop=True)
                    return r
                S.add("pe", att, reads=[ktT.k(), qtpad[0].k(), qtpad[1].k()], writes=[pk3])
                yield
                attT = B16(4, 128)
                S.add("dve", lambda e: e.tensor_tensor(out=attT.ap, in0=p3[:, 0:512].rearrange("p (h l) -> p h l", h=4),
                                                       in1=tri.ap.unsqueeze(1).to_broadcast([128, 4, 128]), op=ALU.mult),
                      reads=[pk3, tri.k()], writes=[attT.k()])
                yield
                ktT.free()
            Sgb = [B16(1024), B16(1024)] if full else None
            for c in range(2):
                if full:
                    S.add("act", lambda e, c=c: e.activation(out=Sgb[c].ap, in_=Sg.ap, func=AF.Copy), reads=[Sg.k()], writes=[Sgb[c].k()])
                    yield
                for hp in range(2):
                    pu, pku = next_pa()

                    def umm(e, c=c, hp=hp, pu=pu):
                        r = None
                        for h2 in range(2):
                            hd = hp * 2 + h2
                            r = e.matmul(out=pu[:, h2 * 256:(h2 + 1) * 256], lhsT=kh[c].ap[:, hd * 128:(hd + 1) * 128],
                                         rhs=vtm.ap[:, t, hd * 256:(hd + 1) * 256], start=True, stop=True)
                        return r
                    S.add("pe", umm, reads=[kh[c].k(), vtm.k(t)], writes=[pku])
                    yield

                    def gupd(e, c=c, hp=hp, pu=pu):
                        r = None
                        for h2 in range(2):
                            hd = hp * 2 + h2
                            r = e.scalar_tensor_tensor(out=Sg.ap[:, hd * 256:(hd + 1) * 256], in0=Sg.ap[:, hd * 256:(hd + 1) * 256],
                                                       scalar=EqT.ap[:, hd, c * 64 + 63:c * 64 + 64], in1=pu[:, h2 * 256:(h2 + 1) * 256],
                                                       op0=ALU.mult, op1=ALU.add)
                        return r
                    S.add("dve", gupd, reads=[pku, EqT.k(), Sg.kr(hp * 512, hp * 512 + 512)], writes=[Sg.kr(hp * 512, hp * 512 + 512)])
                    yield
                    yield
            kh[0].free(); kh[1].free(); EqT.free()
            if not full:
                return

            junk = B16(1024)
            ss = B32(4)
            pos = []
            for hp in range(2):
                po, pko = next_pa()
                pos.append((po, pko))

                def omm(e, hp=hp, po=po):
                    r = None
                    for h2 in range(2):
                        hd = hp * 2 + h2
                        o = po[:, h2 * 256:(h2 + 1) * 256]
                        e.matmul(out=o, lhsT=attT.ap[:, hd, :], rhs=vtm.ap[:, t, hd * 256:(hd + 1) * 256], start=True, stop=False)
                        e.matmul(out=o, lhsT=qtpad[0].ap[:, hd, :], rhs=Sgb[0].ap[:, hd * 256:(hd + 1) * 256], start=False, stop=False)
                        r = e.matmul(out=o, lhsT=qtpad[1].ap[:, hd, :], rhs=Sgb[1].ap[:, hd * 256:(hd + 1) * 256], start=False, stop=True)
                    return r
                S.add("pe", omm, reads=[attT.k(), vtm.k(t), qtpad[0].k(), qtpad[1].k(), Sgb[0].k(), Sgb[1].k()], writes=[pko])
                yield

                def osq(e, hp=hp, po=po):
                    r = None
                    for h2 in range(2):
                        hd = hp * 2 + h2
                        r = e.activation(out=junk.ap[:, hd * 256:(hd + 1) * 256], in_=po[:, h2 * 256:(h2 + 1) * 256], func=AF.Square,
                                         accum_out=ss.ap[:, hd:hd + 1])
                    return r
                S.add("act", osq, reads=[pko], writes=[junk.kr(hp * 512, hp * 512 + 512), ss.k()])
                yield
                yield
            attT.free(); Sgb[0].free(); Sgb[1].free()
            chain("act", [lambda e: e.activation(out=ss.ap, in_=ss.ap, func=AF.Ln, bias=EPS, scale=1.0 / 256.0),
                          lambda e: e.activation(out=ss.ap, in_=ss.ap, func=AF.Exp, scale=-0.5)], [], [ss.k()])
            yield
            yield
            on = B32(1024)
            onb = B16(1024)
            for hp in range(2):
                po, pko = pos[hp]

                def onorm(e, hp=hp, po=po):
                    r = None
                    for h2 in range(2):
                        hd = hp * 2 + h2
                        r = e.scalar_tensor_tensor(out=on.ap[:, hd * 256:(hd + 1) * 256], in0=po[:, h2 * 256:(h2 + 1) * 256], scalar=ss.ap[:, hd:hd + 1],
                                                   in1=cvec(C_GLAN, 256), op0=ALU.mult, op1=ALU.mult)
                    return r
                S.add("dve", onorm, reads=[pko, ss.k(), cst.k()], writes=[on.kr(hp * 512, hp * 512 + 512)])
                yield
            S.add("dve", lambda e: e.tensor_tensor(out=onb.ap, in0=on.ap, in1=rs.ap[:, t, :], op=ALU.mult), reads=[on.k(), rs.k(t)], writes=[onb.k()])
            yield
            yield

            def tro(e):
                r = None
                for c in range(8):
                    r = e.transpose(out=ptr[:, c * 128:(c + 1) * 128], in_=onb.ap[:, c * 128:(c + 1) * 128], identity=idb.ap)
                return r
            S.add("pe", tro, reads=[onb.k(), idb.k()], writes=["ptr"])
            yield
            S.add("act", lambda e: e.activation(out=onT.ap[:, :, cols], in_=ptr[:, :].rearrange("p (c n) -> p c n", c=8), func=AF.Copy),
                  reads=["ptr"], writes=[onT.k()])
            yield
            junk.free(); ss.free(); on.free(); onb.free()

        def prologue_mem():
            mn = B16(2, 1024)
            for mc in range(2):
                ms = B32(1024)
                S.add("sp", lambda e, ms=ms, mc=mc: [e.dma_start(out=ms.ap, in_=mem_d[mc * 128:(mc + 1) * 128, :])], writes=[ms.k()], dma=True, dkey="ms%d" % ms.off)
                junk = B16(1024)
                ss = B32(1)
                S.add("act", lambda e, ms=ms, junk=junk, ss=ss: e.activation(out=junk.ap, in_=ms.ap, func=AF.Square, accum_out=ss.ap),
                      reads=[ms.k()], writes=[junk.k(), ss.k()])
                chain("act", [lambda e, ss=ss: e.activation(out=ss.ap, in_=ss.ap, func=AF.Ln, bias=EPS, scale=1.0 / 1024.0),
                              lambda e, ss=ss: e.activation(out=ss.ap, in_=ss.ap, func=AF.Exp, scale=-0.5)], [], [ss.k()])
                S.add("dve", lambda e, ms=ms, ss=ss, mc=mc: e.scalar_tensor_tensor(out=mn.ap[:, mc, :], in0=ms.ap, scalar=ss.ap[:, 0:1], in1=cvec(C_MEMN, 1024),
                                                                                   op0=ALU.mult, op1=ALU.mult),
                      reads=[ms.k(), ss.k(), cst.k()], writes=[mn.k(mc)])
                ms.free(); junk.free(); ss.free()
            mnT = B16(8, 256)
            for mc in range(2):
                def trm(e, mc=mc):
                    r = None
                    for c in range(8):
                        r = e.transpose(out=ptr[:, c * 128:(c + 1) * 128], in_=mn.ap[:, mc, c * 128:(c + 1) * 128], identity=idb.ap)
                    return r
                S.add("pe", trm, reads=[mn.k(mc), idb.k()], writes=["ptr"])
                S.add("act", lambda e, mc=mc: e.activation(out=mnT.ap[:, :, mc * 128:(mc + 1) * 128], in_=ptr[:, :].rearrange("p (c n) -> p c n", c=8), func=AF.Copy),
                      reads=["ptr"], writes=[mnT.k()])
            mn.free()

            def k_epi(c0, nch, pv, pk):
                S.add("act", lambda e: e.activation(out=KT.ap[:, c0:c0 + nch, :], in_=pv, func=AF.Copy), reads=[pk], writes=[KT.k(c0, c0 + nch)])
            proj_fm(w_xkv, 0, 1024, mnT, 256, k_epi, blk=256)

            def v_epi(t, col, cw, p, pk):
                S.add("act", lambda e: e.activation(out=Vx.ap[:, t, col:col + cw], in_=p, func=AF.Copy), reads=[pk],
                      writes=[Vx.kr(t * 1024 + col, t * 1024 + col + cw)])
            proj_tm(w_xkv, 1024, 1024, mnT, 2, v_epi)
            mnT.free()

        prologue_mem()
        for s_ in range(NPRE // NT):
            step(s_ * NT, NT, "p1")
            if (s_ + 1) % NSTEP == 0:
                zi = cst.ap[:, C_CMASK + (s_ + 1) // NSTEP - 1:C_CMASK + (s_ + 1) // NSTEP]

                def zs_(e, zi=zi):
                    e.tensor_scalar(out=Sst.ap, in0=Sst.ap, scalar1=zi, scalar2=None, op0=ALU.mult)
                    return e.tensor_scalar(out=Sg.ap, in0=Sg.ap, scalar1=zi, scalar2=None, op0=ALU.mult)
                S.add("dve", zs_, reads=[Sst.k(), Sg.k(), cst.k()], writes=[Sst.k(), Sg.k()])
        for s_ in range(NSTEP):
            step(NPRE + s_ * NT, NT, "p2", orow0=s_ * NT, first_dbg=(s_ == 0))
        print("arena peaks: A16 %d / %d, A32 %d / %d; ops %d" % (a16.peak, N16, a32.peak, N32, len(S.ops)))
        S.emit(nc, st)
    return nc, dbg_outs


_CACHE = {}


def host_inputs(x, mem, norm_mix, w_in, ssd_conv_w, ssd_conv_b, ssd_dt_bias, ssd_A_log, ssd_D, ssd_norm,
                gla_w_a2, gla_b_a, gla_norm, w_up_ssd, w_up_gla, w_o, norm_xattn, norm_mem, w_xq, w_xkv,
                w_xo, norm_ffn, w_ffn_in, w_ffn_out, norm_final):
    f = lambda a: np.ascontiguousarray(np.asarray(a, dtype=np.float32))
    x = f(x); mem = f(mem)

    def fm(g):
        return f(g).reshape(8, 128).T

    def rep(v):
        v = f(v).reshape(1, -1)
        return np.broadcast_to(v, (128, v.shape[1]))
    cst = np.zeros((128, NCST), np.float32)
    cst[:, C_GMIX:C_GMIX + 8] = fm(norm_mix[0])
    cst[:, C_GXA:C_GXA + 8] = fm(norm_xattn[0])
    cst[:, C_GFFN:C_GFFN + 8] = fm(norm_ffn[0])
    cst[:, C_GFIN:C_GFIN + 8] = fm(norm_final)
    cw = f(ssd_conv_w[0])[:, 0, :]
    cst[:, C_CONVW:C_CONVW + 48] = cw.reshape(4, 12, 128).transpose(2, 1, 0).reshape(128, 48)
    cst[:, C_CONVB:C_CONVB + 12] = f(ssd_conv_b[0]).reshape(12, 128).T
    cst[:, C_DTB:C_DTB + 16] = rep(ssd_dt_bias[0])
    cst[:, C_ALOG:C_ALOG + 16] = rep(ssd_A_log[0])
    cst[:, C_DSK:C_DSK + 16] = rep(ssd_D[0])
    cst[:, C_SSDN:C_SSDN + 1024] = rep(ssd_norm[0])
    cst[:, C_GLAN:C_GLAN + 256] = rep(gla_norm[0])
    cst[:, C_MEMN:C_MEMN + 1024] = rep(norm_mem[0])
    w2aug = np.concatenate([f(gla_w_a2[0]), f(gla_b_a[0]).reshape(1, 512)], 0)
    shared = {"w2aug": f(w2aug), "w_in": f(w_in[0]), "w_up_ssd": f(w_up_ssd[0]), "w_up_gla": f(w_up_gla[0]), "w_o": f(w_o[0]),
              "w_xq": f(w_xq[0]), "w_xkv": f(w_xkv[0]), "w_xo": f(w_xo[0]), "w_ffn_in": f(w_ffn_in[0]), "w_ffn_out": f(w_ffn_out[0])}
    in_maps = []
    for c in range(NCORES):
        b, j = divmod(c, 4)
        xe = np.zeros((4 * SEG, D), np.float32)
        xe[(3 - j) * SEG:3 * SEG] = x[b, 0:j * SEG]
        xe[3 * SEG:] = x[b, j * SEG:(j + 1) * SEG]
        cc = cst.copy()
        for i in range(3):
            cc[:, C_CMASK + i] = 1.0 if i >= 3 - j else 0.0
        m = {"x_ext": xe, "mem_b": mem[b], "cst": cc}
        m.update(shared)
        in_maps.append(m)
    return in_maps


def kernel(**inputs):
    if "nc" not in _CACHE:
        _CACHE["nc"] = build_program()
    nc, dbg = _CACHE["nc"]
    in_maps = host_inputs(**inputs)
    res = run_bass_kernel_spmd(nc, in_maps, core_ids=list(range(NCORES)))
    _CACHE["last"] = res
    out = np.zeros((2, 4 * SEG, D), np.float32)
    for c in range(NCORES):
        b, j = divmod(c, 4)
        out[b, j * SEG:(j + 1) * SEG] = res.results[c]["out"]
    return out
```

```python
import numpy as np
from contextlib import ExitStack
import concourse.bass as bass
import concourse.mybir as mybir
from concourse.bass_utils import run_bass_kernel_spmd

F32 = mybir.dt.float32
BF16 = mybir.dt.bfloat16
AF = mybir.ActivationFunctionType
ALU = mybir.AluOpType

NCORES = 8
D = 1024
SEG = 2048
NT = 512
HALO = 128
EPS = 1e-6
EPOCH = 8000
import os
DBG = bool(os.environ.get('KDBG'))

O_Z, O_XBC, O_DT, O_Q, O_K, O_V, O_R, O_A1, O_GS, O_GG = 0, 1024, 2560, 2576, 3088, 3600, 4624, 5648, 5664, 6688
DFF = 2816

C_GMIX, C_GXA, C_GFFN, C_GFIN = 0, 8, 16, 24
C_CONVW = 32
C_CONVB = 80
C_DTB, C_ALOG, C_DSK = 92, 108, 124
C_SSDN = 140
C_GLAN = 1164
C_MEMN = 1420
C_CMASK = 2444
NCST = 2452


class Op:
    __slots__ = ("eng", "fn", "deps", "sig", "signal", "is_dma", "dkey", "ndma", "name", "inc")


class Sched:
    ENGS = ("pe", "act", "dve", "pool", "sp")

    def __init__(self):
        self.ops = []
        self.recs = {}

    @staticmethod
    def _k(k):
        return (k, 0, 1) if isinstance(k, str) else k

    def add(self, eng, fn, reads=(), writes=(), dma=False, dkey=None, ndma=1, name="", inc=16):
        op = Op()
        op.eng, op.fn, op.is_dma, op.dkey, op.ndma, op.inc, op.name = eng, fn, dma, dkey, ndma, inc, name
        op.sig = False
        op.signal = None
        deps = []
        seen = set()

        def push(d, raw):
            if d is None or d is op or id(d) in seen:
                return
            if not raw and not (d.is_dma or dma or d.eng != eng):
                return
            seen.add(id(d))
            deps.append(d)

        for k in reads:
            a, lo, hi = self._k(k)
            for r in self.recs.get(a, ()):
                if r[0] < hi and lo < r[1]:
                    push(r[2], True)
                    r[3].append(op)
        for k in writes:
            a, lo, hi = self._k(k)
            lst = self.recs.setdefault(a, [])
            new = []
            for r in lst:
                if r[0] < hi and lo < r[1]:
                    push(r[2], False)
                    for rd in r[3]:
                        push(rd, False)
                    if r[0] < lo:
                        new.append([r[0], lo, r[2], list(r[3])])
                    if hi < r[1]:
                        new.append([hi, r[1], r[2], list(r[3])])
                else:
                    new.append(r)
            new.append([lo, hi, op, []])
            self.recs[a] = new
        if eng == "pe" and not dma:
            deps = [d for d in deps if d.is_dma or d.eng != "pe"]
        op.deps = deps
        for d in deps:
            d.sig = True
        self.ops.append(op)
        return op

    def emit(self, nc, stack):
        eng_count = {e: 0 for e in self.ENGS}
        eng_sems = {e: [] for e in self.ENGS}
        dma_sems = {}
        dma_vals = {}
        for op in self.ops:
            if op.is_dma:
                if op.dkey not in dma_sems:
                    dma_sems[op.dkey] = stack.enter_context(nc.semaphore("d%d" % len(dma_sems)))
                    dma_vals[op.dkey] = 0
                dma_vals[op.dkey] += op.inc * op.ndma
                op.signal = (dma_sems[op.dkey], dma_vals[op.dkey])
            elif op.sig:
                c = eng_count[op.eng]
                ep = c // EPOCH
                if ep >= len(eng_sems[op.eng]):
                    eng_sems[op.eng].append(stack.enter_context(nc.semaphore("e%s%d" % (op.eng, ep))))
                op.signal = (eng_sems[op.eng][ep], c % EPOCH + 1)
                eng_count[op.eng] = c + 1
        by_eng = {e: [o for o in self.ops if o.eng == e] for e in self.ENGS}
        finals = {}
        for op in self.ops:
            if op.is_dma and op.name.startswith("OUT"):
                sem, val = op.signal
                if finals.get(id(sem), (None, 0))[1] < val:
                    finals[id(sem)] = (sem, val)

        def run(engh, ename):
            waited = {}
            for op in by_eng[ename]:
                for d in op.deps:
                    sem, val = d.signal
                    if waited.get(id(sem), 0) < val:
                        engh.wait_ge(sem, val)
                        waited[id(sem)] = val
                r = op.fn(engh)
                if op.is_dma:
                    assert len(r) == op.ndma, (op.name, len(r), op.ndma)
                    for ins in r:
                        ins.then_inc(op.signal[0], op.inc)
                elif op.sig:
                    r.then_inc(op.signal[0], 1)
            if ename == "sp":
                for sem, val in finals.values():
                    engh.wait_ge(sem, val)

        with nc.Block() as block:
            @block.tensor
            def _(e):
                run(e, "pe")

            @block.scalar
            def _(e):
                run(e, "act")

            @block.vector
            def _(e):
                run(e, "dve")

            @block.gpsimd
            def _(e):
                run(e, "pool")

            @block.sync
            def _(e):
                run(e, "sp")


class Arena:
    def __init__(self, name, tensor, n):
        self.name, self.t, self.n = name, tensor, n
        self.used = []
        self.peak = 0

    def alloc(self, n):
        n = (n + 15) // 16 * 16
        self.used.sort()
        pos = 0
        for off, sz in self.used:
            if off - pos >= n:
                break
            pos = off + sz
        if pos + n > self.n:
            raise RuntimeError("arena %s full: need %d at %d of %d" % (self.name, n, pos, self.n))
        self.used.append((pos, n))
        self.peak = max(self.peak, pos + n)
        return pos

    def free(self, off):
        self.used = [u for u in self.used if u[0] != off]


class Buf:
    def __init__(self, arena, shape):
        self.arena = arena
        self.shape = list(shape)
        self.n = int(np.prod(shape))
        self.off = arena.alloc(self.n)
        self.inner = self.n // self.shape[0] if len(shape) == 2 else self.n

    def free(self):
        self.arena.free(self.off)

    @property
    def ap(self):
        a = self.arena.t[:, self.off:self.off + self.n]
        if len(self.shape) == 2:
            return a.rearrange("p (a b) -> p a b", a=self.shape[0])
        return a

    def k(self, i=None, j=None):
        if i is None:
            return (self.arena.name, self.off, self.off + self.n)
        if j is None:
            j = i + 1
        return (self.arena.name, self.off + i * self.inner, self.off + j * self.inner)

    def kr(self, lo, hi):
        return (self.arena.name, self.off + lo, self.off + hi)


def build_program():
    nc = bass.Bass("TRN2", target_bir_lowering=False)
    TPS = NT // 128
    NSTEP = SEG // NT

    def din(name, shape):
        return nc.dram_tensor(name, list(shape), F32, kind="ExternalInput").ap()

    NPRE = 3 * SEG
    x_d = din("x_ext", [NPRE + SEG, D])
    mem_d = din("mem_b", [256, D])
    cst_d = din("cst", [128, NCST])
    w2_d = din("w2aug", [17, 512])
    w_in = din("w_in", [D, 7712])
    w_ups = din("w_up_ssd", [D, D])
    w_upg = din("w_up_gla", [D, D])
    w_o = din("w_o", [D, D])
    w_xq = din("w_xq", [D, D])
    w_xkv = din("w_xkv", [D, 2 * D])
    w_xo = din("w_xo", [D, D])
    w_fi = din("w_ffn_in", [D, 2 * DFF])
    w_fo = din("w_ffn_out", [DFF, D])
    out_d = nc.dram_tensor("out", [SEG, D], F32, kind="ExternalOutput").ap()
    dbg_outs = {}

    S = Sched()

    def chain(eng, fns, reads, writes):
        for f in fns:
            S.add(eng, f, reads=list(reads) + list(writes), writes=writes)
    st = ExitStack()
    with st:
        def sb(name, shape, dt):
            return st.enter_context(nc.sbuf_tensor(name, list(shape), dt))

        def pst(name, shape, dt):
            return st.enter_context(nc.psum_tensor(name, list(shape), dt))

        N16 = 70000
        N32 = 18000
        a16 = Arena("A16", sb("A16", [128, N16], BF16), N16)
        a32 = Arena("A32", sb("A32", [128, N32], F32), N32)

        def B16(*shape):
            return Buf(a16, shape)

        def B32(*shape):
            return Buf(a32, shape)

        pa = [pst("pa0", [128, 512], F32), pst("pa1", [128, 512], F32)]
        ptr = pst("ptr", [128, 1024], BF16)
        psm = pst("psm", [128, 512], F32)
        pwa = pst("pwa", [128, 1024], F32)
        pwb = pst("pwb", [128, 1024], F32)
        PWA = ("pwa", 0, 2)
        PWB = ("pwb", 0, 2)
        pa_i = [0]
        pa6_i = [0]
        banks6 = [(pa[0], "pa0"), (pa[1], "pa1"), (pwa[:, 0:512], ("pwa", 0, 1)), (pwa[:, 512:1024], ("pwa", 1, 2)),
                  (pwb[:, 0:512], ("pwb", 0, 1)), (pwb[:, 512:1024], ("pwb", 1, 2))]

        def next_pa():
            i = pa_i[0]
            pa_i[0] ^= 1
            return pa[i], "pa%d" % i

        def next_pa6():
            i = pa6_i[0]
            pa6_i[0] = (i + 1) % 6
            return banks6[i]

        cst = B32(NCST)
        w2a = B32(512)
        idf = B32(128)
        tri = B32(128)
        stri = B32(128)
        trig = B32(128)
        strig = B32(128)
        strif = B32(128)
        strigf = B32(128)
        onesf = B32(128)
        negc = B32(2)
        onesc = B32(2, 128)
        cm = B32(2)
        idb = B16(128)
        ones_mean = B16(128)
        ones1 = B16(128)
        aneg = B32(16)

        S.add("sp", lambda e: [e.dma_start(out=cst.ap, in_=cst_d)], writes=[cst.k()], dma=True, dkey="cst")
        S.add("dve", lambda e: e.memset(w2a.ap, 0.0), writes=[w2a.k()])
        S.add("sp", lambda e: [e.dma_start(out=w2a.ap[0:17, :], in_=w2_d)], writes=[w2a.k()], dma=True, dkey="w2a")

        def mk_c1(e):
            e.memset(idf.ap, 0.0)
            e.memset(tri.ap, 1.0)
            e.memset(stri.ap, 1.0)
            e.memset(onesc.ap, 0.0)
            e.memset(cm.ap, 0.0)
            e.memset(ones_mean.ap, 1.0 / 1024.0)
            e.memset(strif.ap, 1.0)
            e.memset(onesf.ap, 1.0)
            e.memset(negc.ap, -1.0 / 16.0)
            return e.memset(ones1.ap, 1.0)

        def mk_c2(e):
            e.affine_select(out=strif.ap, in_=strif.ap, pattern=[[-1, 128]], compare_op=ALU.is_gt,
                            fill=0.0, base=0, channel_multiplier=1)
            e.affine_select(out=idf.ap, in_=idf.ap, pattern=[[-1, 128]], compare_op=ALU.not_equal,
                            fill=1.0, base=0, channel_multiplier=1)
            e.affine_select(out=tri.ap, in_=tri.ap, pattern=[[1, 128]], compare_op=ALU.is_ge,
                            fill=0.0, base=0, channel_multiplier=-1)
            e.affine_select(out=stri.ap, in_=stri.ap, pattern=[[-1, 128]], compare_op=ALU.is_gt,
                            fill=0.0, base=0, channel_multiplier=1)
            e.memset(onesc.ap[0:64, 0, :], 1.0)
            e.memset(onesc.ap[64:128, 1, :], 1.0)
            e.memset(cm.ap[0:64, 0:1], 1.0)
            return e.memset(cm.ap[64:128, 1:2], 1.0)

        def mk_c3(e):
            e.memset(tri.ap[0:64, 64:128], 0.0)
            return e.memset(stri.ap[64:128, 0:64], 0.0)
        chain("pool", [mk_c1, mk_c2, mk_c3], [], [idf.k(), tri.k(), stri.k(), onesc.k(), cm.k(), ones_mean.k(), ones1.k(),
                                                  strif.k(), onesf.k(), negc.k()])

        def mk_consts2(e):
            e.tensor_copy(out=idb.ap, in_=idf.ap)
            e.tensor_scalar(out=trig.ap, in0=tri.ap, scalar1=-1.0 / 16.0, scalar2=None, op0=ALU.mult)
            e.tensor_scalar(out=strigf.ap, in0=strif.ap, scalar1=-1.0 / 16.0, scalar2=None, op0=ALU.mult)
            return e.tensor_scalar(out=strig.ap, in0=stri.ap, scalar1=-1.0 / 16.0, scalar2=None, op0=ALU.mult)
        S.add("dve", mk_consts2, reads=[idf.k(), tri.k(), stri.k(), strif.k()], writes=[idb.k(), trig.k(), strig.k(), strigf.k()])

        def mk_aneg(e):
            return e.activation(out=aneg.ap, in_=cst.ap[:, C_ALOG:C_ALOG + 16], func=AF.Exp)
        S.add("act", mk_aneg, reads=[cst.k()], writes=[aneg.k()])
        S.add("dve", lambda e: e.tensor_scalar(out=aneg.ap, in0=aneg.ap, scalar1=-1.0, scalar2=None, op0=ALU.mult),
              reads=[aneg.k()], writes=[aneg.k()])

        def cvec(off, n):
            return cst.ap[:, off:off + n]

        NSLOT = 3
        WSZ = 4096
        wslots = [B16(WSZ) for _ in range(NSLOT)]
        wctr = [0]
        SSZ = 1024
        sslots = [B16(SSZ) for _ in range(2)]
        sctr = [0]

        def wload(wd, kc, c0, cw):
            if kc * cw <= SSZ:
                i = NSLOT + sctr[0] % 2
                sctr[0] += 1
                slot = sslots[i - NSLOT]
            else:
                i = wctr[0] % NSLOT
                wctr[0] += 1
                slot = wslots[i]
            assert kc * cw <= WSZ
            dst = slot.ap[:, 0:kc * cw].rearrange("p (k n) -> p k n", k=kc)
            src = wd.rearrange("(k p) n -> p k n", p=128)[:, :, c0:c0 + cw]
            if kc > 8:
                h = kc // 2
                S.add("pool", lambda e: [e.dma_start(out=dst[:, 0:h, :], in_=src[:, 0:h, :]),
                                         e.dma_start(out=dst[:, h:kc, :], in_=src[:, h:kc, :])],
                      writes=[slot.k()], dma=True, dkey="w%d" % i, ndma=2)
            else:
                S.add("pool", lambda e: [e.dma_start(out=dst, in_=src)], writes=[slot.k()], dma=True, dkey="w%d" % i)
            return dst, slot.k()

        def proj_fm(wd, c0, ncols, src, nt, epi, kc=8, blk=512):
            col = 0
            while col < ncols:
                cw = min(blk, ncols - col)
                wap, wkey = wload(wd, kc, c0 + col, cw)
                nch_all = (cw + 127) // 128
                gsz = max(1, 512 // nt)
                for cg in range(0, nch_all, gsz):
                    nch = min(gsz, nch_all - cg)
                    p, pk = next_pa6()
                    pv = p[:, 0:nch * nt].rearrange("p (c n) -> p c n", c=nch)

                    def mm(e, wap=wap, pv=pv, nch=nch, cw=cw, cg=cg):
                        r = None
                        for c in range(nch):
                            cc = cg + c
                            m = min(128, cw - cc * 128)
                            for k in range(kc):
                                r = e.matmul(out=pv[0:m, c, :], lhsT=wap[:, k, cc * 128:cc * 128 + m], rhs=src.ap[:, k, 0:nt],
                                             start=(k == 0), stop=(k == kc - 1))
                        return r
                    S.add("pe", mm, reads=[wkey, src.k()], writes=[pk])
                    epi(col // 128 + cg, nch, pv, pk)
                col += cw

        def proj_tm(wd, c0, ncols, src, ntiles, epi, kc=8):
            col = 0
            while col < ncols:
                cw = min(512, ncols - col)
                wap, wkey = wload(wd, kc, c0 + col, cw)
                for t in range(ntiles):
                    p, pk = next_pa6()

                    def mm(e, wap=wap, p=p, t=t, cw=cw):
                        r = None
                        for k in range(kc):
                            r = e.matmul(out=p[:, 0:cw], lhsT=src.ap[:, k, t * 128:(t + 1) * 128], rhs=wap[:, k, :],
                                         start=(k == 0), stop=(k == kc - 1))
                        return r
                    S.add("pe", mm, reads=[wkey, src.k()], writes=[pk])
                    epi(t, col, cw, p[:, 0:cw], pk)
                col += cw

        hT = B32(8, NT)
        nT = B16(8, NT)
        rstd = B32(NT)
        Sst = B32(1024)
        Sg = B32(1024)
        dtot = B32(20)
        halo3 = B16(12, 3)
        ctpad = [B16(2, 128), B16(2, 128)]
        qtpad = [B16(4, 128), B16(4, 128)]
        a1T = B32(NT)
        KT = B16(8, 256)
        Vx = B16(2, 1024)

        def init_state(e):
            e.memset(Sst.ap, 0.0)
            e.memset(Sg.ap, 0.0)
            e.memset(dtot.ap, 1.0)
            e.memset(ctpad[0].ap, 0.0)
            e.memset(ctpad[1].ap, 0.0)
            e.memset(qtpad[0].ap, 0.0)
            e.memset(qtpad[1].ap, 0.0)
            e.memset(halo3.ap, 0.0)
            return e.memset(a1T.ap, 0.0)
        S.add("pool", init_state, writes=[Sst.k(), Sg.k(), dtot.k(), ctpad[0].k(), ctpad[1].k(), qtpad[0].k(),
                                          qtpad[1].k(), a1T.k(), halo3.k()])
        S.add("dve", lambda e: e.memset(a1T.ap[0:32, :], 1.0), reads=[a1T.k()], writes=[a1T.k()])

        xs_i = [0]

        def load_xT(row0, nt, xreads=()):
            xsb = [B32(1024), B32(1024)]
            for t in range(nt // 128):
                j = xs_i[0]
                xs_i[0] ^= 1
                xs = xsb[j]
                pw, pwk = (pwa, PWA) if j == 0 else (pwb, PWB)
                r0 = row0 + t * 128
                S.add("sp", lambda e, xs=xs, r0=r0: [e.dma_start(out=xs.ap, in_=x_d[r0:r0 + 128, :])],
                      writes=[xs.k()], reads=list(xreads), dma=True, dkey="xs%d" % xs.off)

                def tr(e, xs=xs, pw=pw):
                    r = None
                    for c in range(8):
                        r = e.transpose(out=pw[:, c * 128:(c + 1) * 128], in_=xs.ap[:, c * 128:(c + 1) * 128], identity=idf.ap)
                    return r
                S.add("pe", tr, reads=[xs.k(), idf.k()], writes=[pwk])
                S.add("act", lambda e, t=t, pw=pw: e.activation(out=hT.ap[:, :, t * 128:(t + 1) * 128],
                                                               in_=pw[:, :].rearrange("p (c n) -> p c n", c=8), func=AF.Copy),
                      reads=[pwk], writes=[hT.k()])
            xsb[0].free(); xsb[1].free()

        def norm_fm(goff, nt, stats_only=False):
            sqb = B16(8, NT)
            S.add("act", lambda e: e.activation(out=sqb.ap[:, :, 0:nt], in_=hT.ap[:, :, 0:nt], func=AF.Square),
                  reads=[hT.k()], writes=[sqb.k()])

            def mm(e):
                r = None
                for k in range(8):
                    r = e.matmul(out=psm[:, 0:nt], lhsT=ones_mean.ap, rhs=sqb.ap[:, k, 0:nt], start=(k == 0), stop=(k == 7))
                return r
            S.add("pe", mm, reads=[sqb.k(), ones_mean.k()], writes=["psm"])
            sqb.free()
            S.add("act", lambda e: e.activation(out=rstd.ap[:, 0:nt], in_=psm[:, 0:nt], func=AF.Ln, bias=EPS), reads=["psm"], writes=[rstd.k()])
            S.add("act", lambda e: e.activation(out=rstd.ap[:, 0:nt], in_=rstd.ap[:, 0:nt], func=AF.Exp, scale=-0.5), reads=[rstd.k()], writes=[rstd.k()])
            if stats_only:
                return
            tgt = nT

            def nrm(e):
                r = None
                for k in range(8):
                    r = e.scalar_tensor_tensor(out=tgt.ap[:, k, 0:nt], in0=hT.ap[:, k, 0:nt], scalar=cst.ap[:, goff + k:goff + k + 1],
                                               in1=rstd.ap[:, 0:nt], op0=ALU.mult, op1=ALU.mult)
                return r
            S.add("dve", nrm, reads=[hT.k(), rstd.k(), cst.k()], writes=[tgt.k()])

        def resid_epi(c0, nch, pv, pk):
            S.add("dve", lambda e: e.tensor_tensor(out=hT.ap[:, c0:c0 + nch, :], in0=hT.ap[:, c0:c0 + nch, :], in1=pv, op=ALU.add),
                  reads=[pk, hT.k(c0, c0 + nch)], writes=[hT.k(c0, c0 + nch)])

        def dbg_dump(name, buf, n, dt=F32):
            if not DBG or name not in os.environ.get('KDBG', '').split(','):
                return
            d = nc.dram_tensor("dbg_" + name, [128, n], dt, kind="ExternalOutput").ap()
            dbg_outs[name] = d
            flat = buf.arena.t[:, buf.off:buf.off + n]
            S.add("sp", lambda e: [e.dma_start(out=d, in_=flat)], reads=[buf.k()], dma=True, dkey="dbg_" + name, name="OUTdbg")

        def interleave(gens):
            gens = list(gens)
            while gens:
                for g in list(gens):
                    try:
                        next(g)
                    except StopIteration:
                        gens.remove(g)

        def step(row0, nt, mode, orow0=None, first_dbg=False, xreads=()):
            full = mode == "p2"
            tps = nt // 128
            load_xT(row0, nt, xreads)
            norm_fm(C_GMIX, nt)
            if first_dbg:
                dbg_dump("nT", nT, 8 * NT, BF16)
            dtb = B32(tps, 16)
            ab = B32(tps, 16)

            def dt_epi(t, col, cw, p, pk):
                tmp = B32(16)
                S.add("dve", lambda e: e.tensor_tensor(out=tmp.ap, in0=p, in1=cvec(C_DTB, 16), op=ALU.add),
                      reads=[pk, cst.k()], writes=[tmp.k()])

                S.add("act", lambda e: e.activation(out=tmp.ap, in_=tmp.ap, func=AF.Exp), reads=[tmp.k()], writes=[tmp.k()])
                S.add("act", lambda e: e.activation(out=dtb.ap[:, t, :], in_=tmp.ap, func=AF.Ln, bias=1.0), reads=[tmp.k()], writes=[dtb.k(t)])
                S.add("dve", lambda e: e.tensor_tensor(out=ab.ap[:, t, :], in0=dtb.ap[:, t, :], in1=aneg.ap, op=ALU.mult),
                      reads=[dtb.k(t), aneg.k()], writes=[ab.k(t)])
                tmp.free()
            proj_tm(w_in, O_DT, 16, nT, tps, dt_epi)
            xbr = B16(12, nt + 3)
            xsT = B16(8, nt)
            BT = B16(2, nt)
            CT = B16(2, nt)
            S.add("dve", lambda e: e.tensor_copy(out=xbr.ap[:, :, 0:3], in_=halo3.ap), reads=[halo3.k()], writes=[xbr.k()])

            def conv_chunk(c):
                cacc = B32(nt)
                S.add("act", lambda e: e.activation(out=cacc.ap, in_=xbr.ap[:, c, 0:nt], func=AF.Copy,
                                                    scale=cst.ap[:, C_CONVW + 4 * c:C_CONVW + 4 * c + 1]),
                      reads=[xbr.k(c), cst.k()], writes=[cacc.k()])

                def tap(k):
                    return lambda e: e.scalar_tensor_tensor(out=cacc.ap, in0=xbr.ap[:, c, k:k + nt],
                                                            scalar=cst.ap[:, C_CONVW + 4 * c + k:C_CONVW + 4 * c + k + 1],
                                                            in1=cacc.ap, op0=ALU.mult, op1=ALU.add)
                chain("dve", [tap(1), tap(2), tap(3)], [xbr.k(c), cst.k()], [cacc.k()])
                dstb, di = (xsT, c) if c < 8 else ((BT, c - 8) if c < 10 else (CT, c - 10))
                S.add("act", lambda e: e.activation(out=dstb.ap[:, di, :], in_=cacc.ap, func=AF.Silu, bias=cst.ap[:, C_CONVB + c:C_CONVB + c + 1]),
                      reads=[cacc.k(), cst.k()], writes=[dstb.k(di)])
                cacc.free()

            def xbc_epi(c0, nch, pv, pk):
                S.add("act", lambda e: e.activation(out=xbr.ap[:, c0:c0 + nch, 3:3 + nt], in_=pv, func=AF.Copy),
                      reads=[pk], writes=[xbr.k(c0, c0 + nch)])
                for c in range(c0, c0 + nch):
                    if full or c < 10:
                        conv_chunk(c)
            proj_fm(w_in, O_XBC, 1536, nT, nt, xbc_epi)
            S.add("dve", lambda e: e.tensor_copy(out=halo3.ap, in_=xbr.ap[:, :, nt:nt + 3]), reads=[xbr.k()], writes=[halo3.k()])
            xbr.free()
            zs = None
            if full:
                zs = B16(tps, 1024)

                def z_epi(t, col, cw, p, pk):
                    S.add("act", lambda e: e.activation(out=zs.ap[:, t, col:col + cw], in_=p, func=AF.Silu),
                          reads=[pk], writes=[zs.kr(t * 1024 + col, t * 1024 + col + cw)])
                proj_tm(w_in, O_Z, 1024, nT, tps, z_epi)
            qT = B16(4, nt) if full else None
            kT = B16(4, nt) if full else None
            ktm = B16(tps, 512)
            vtm = B16(tps, 1024)
            rs = B16(tps, 1024) if full else None
            if full:
                def q_epi(c0, nch, pv, pk):
                    S.add("act", lambda e: e.activation(out=qT.ap[:, c0:c0 + nch, :], in_=pv, func=AF.Copy, scale=float(128 ** -0.5)),
                          reads=[pk], writes=[qT.k(c0, c0 + nch)])
                proj_fm(w_in, O_Q, 512, nT, nt, q_epi)
            wap, wkey = wload(w_in, 8, O_K, 512)
            if full:
                gsz = max(1, 512 // nt)
                for cg in range(0, 4, gsz):
                    p, pk = next_pa6()
                    pv = p[:, 0:gsz * nt].rearrange("p (c n) -> p c n", c=gsz)

                    def mmk(e, wap=wap, pv=pv, cg=cg, gsz=gsz):
                        r = None
                        for c in range(gsz):
                            for k in range(8):
                                r = e.matmul(out=pv[:, c, :], lhsT=wap[:, k, (cg + c) * 128:(cg + c + 1) * 128], rhs=nT.ap[:, k, 0:nt],
                                             start=(k == 0), stop=(k == 7))
                        return r
                    S.add("pe", mmk, reads=[wkey, nT.k()], writes=[pk])
                    S.add("act", lambda e, pv=pv, cg=cg, gsz=gsz: e.activation(out=kT.ap[:, cg:cg + gsz, :], in_=pv, func=AF.Copy),
                          reads=[pk], writes=[kT.k(cg, cg + gsz)])
            for t in range(tps):
                p, pk = next_pa6()

                def mmk2(e, wap=wap, p=p, t=t):
                    r = None
                    for k in range(8):
                        r = e.matmul(out=p[:, 0:512], lhsT=nT.ap[:, k, t * 128:(t + 1) * 128], rhs=wap[:, k, :], start=(k == 0), stop=(k == 7))
                    return r
                S.add("pe", mmk2, reads=[wkey, nT.k()], writes=[pk])
                S.add("act", lambda e, p=p, t=t: e.activation(out=ktm.ap[:, t, :], in_=p[:, 0:512], func=AF.Copy), reads=[pk], writes=[ktm.k(t)])

            def a1_epi(c0, nch, pv, pk):
                S.add("act", lambda e: e.activation(out=a1T.ap[0:16, 0:nt], in_=pv[0:16, 0, :], func=AF.Copy), reads=[pk], writes=[a1T.k()])
            proj_fm(w_in, O_A1, 128, nT, nt, a1_epi)

            def v_epi(t, col, cw, p, pk):
                S.add("act", lambda e: e.activation(out=vtm.ap[:, t, col:col + cw], in_=p, func=AF.Copy),
                      reads=[pk], writes=[vtm.kr(t * 1024 + col, t * 1024 + col + cw)])
            proj_tm(w_in, O_V, 1024, nT, tps, v_epi)
            if full:
                def r_epi(t, col, cw, p, pk):
                    S.add("act", lambda e: e.activation(out=rs.ap[:, t, col:col + cw], in_=p, func=AF.Silu),
                          reads=[pk], writes=[rs.kr(t * 1024 + col, t * 1024 + col + cw)])
                proj_tm(w_in, O_R, 1024, nT, tps, r_epi)
            ynT = B16(8, nt) if full else None
            onT = B16(8, nt) if full else None
            for t in range(tps):
                interleave([ssd_tile(t, xsT, BT, CT, dtb, ab, zs, ynT, full, False),
                            gla_tile(t, qT, kT, ktm, vtm, rs, onT, full, False)])
            xsT.free(); BT.free(); CT.free(); dtb.free(); ab.free()
            if zs is not None:
                zs.free()
            ktm.free(); vtm.free()
            if not full:
                return
            qT.free(); kT.free(); rs.free()
            if first_dbg:
                dbg_dump("onT", onT, 8 * NT, BF16)
            gsT = B16(8, nt)
            ggT = B16(8, nt)
            mT = B16(8, nt)

            def gs_epi(c0, nch, pv, pk):
                S.add("act", lambda e: e.activation(out=gsT.ap[:, c0:c0 + nch, :], in_=pv, func=AF.Sigmoid), reads=[pk], writes=[gsT.k(c0, c0 + nch)])

            def gg_epi(c0, nch, pv, pk):
                S.add("act", lambda e: e.activation(out=ggT.ap[:, c0:c0 + nch, :], in_=pv, func=AF.Sigmoid), reads=[pk], writes=[ggT.k(c0, c0 + nch)])
            proj_fm(w_in, O_GS, 1024, nT, nt, gs_epi)
            proj_fm(w_in, O_GG, 1024, nT, nt, gg_epi)

            def ups_epi(c0, nch, pv, pk):
                S.add("dve", lambda e: e.tensor_tensor(out=gsT.ap[:, c0:c0 + nch, :], in0=gsT.ap[:, c0:c0 + nch, :], in1=pv, op=ALU.mult),
                      reads=[pk, gsT.k(c0, c0 + nch)], writes=[gsT.k(c0, c0 + nch)])
            proj_fm(w_ups, 0, 1024, ynT, nt, ups_epi)

            def upg_epi(c0, nch, pv, pk):
                S.add("dve", lambda e: e.tensor_tensor(out=ggT.ap[:, c0:c0 + nch, :], in0=ggT.ap[:, c0:c0 + nch, :], in1=pv, op=ALU.mult),
                      reads=[pk, ggT.k(c0, c0 + nch)], writes=[ggT.k(c0, c0 + nch)])
                S.add("dve", lambda e: e.tensor_tensor(out=mT.ap[:, c0:c0 + nch, :], in0=ggT.ap[:, c0:c0 + nch, :], in1=gsT.ap[:, c0:c0 + nch, :], op=ALU.add),
                      reads=[ggT.k(c0, c0 + nch), gsT.k(c0, c0 + nch)], writes=[mT.k(c0, c0 + nch)])
            proj_fm(w_upg, 0, 1024, onT, nt, upg_epi)
            ynT.free(); onT.free(); gsT.free(); ggT.free()
            proj_fm(w_o, 0, 1024, mT, nt, resid_epi)
            mT.free()
            if first_dbg:
                dbg_dump("h1", hT, 8 * NT)
            norm_fm(C_GXA, nt)
            qx = B16(8, nt)
            ox = B16(8, nt)

            def qx_epi(c0, nch, pv, pk):
                S.add("act", lambda e: e.activation(out=qx.ap[:, c0:c0 + nch, :], in_=pv, func=AF.Copy, scale=1.0 / 16.0),
                      reads=[pk], writes=[qx.k(c0, c0 + nch)])
            proj_fm(w_xq, 0, 1024, nT, nt, qx_epi)
            for hd in range(4):
                ET = B16(2, nt)
                for mc in range(2):
                    p, pk = next_pa()

                    def sc(e, p=p, hd=hd, mc=mc):
                        r = None
                        for dc in range(2):
                            r = e.matmul(out=p[:, 0:nt], lhsT=KT.ap[:, hd * 2 + dc, mc * 128:(mc + 1) * 128], rhs=qx.ap[:, hd * 2 + dc, :],
                                         start=(dc == 0), stop=(dc == 1))
                        return r
                    S.add("pe", sc, reads=[KT.k(), qx.k(hd * 2, hd * 2 + 2)], writes=[pk])
                    S.add("act", lambda e, p=p, mc=mc, ET=ET: e.activation(out=ET.ap[:, mc, :], in_=p[:, 0:nt], func=AF.Exp),
                          reads=[pk], writes=[ET.k(mc)])

                def den(e, ET=ET):
                    e.matmul(out=psm[:, 0:nt], lhsT=ones1.ap, rhs=ET.ap[:, 0, :], start=True, stop=False)
                    return e.matmul(out=psm[:, 0:nt], lhsT=ones1.ap, rhs=ET.ap[:, 1, :], start=False, stop=True)
                S.add("pe", den, reads=[ET.k(), ones1.k()], writes=["psm"])
                rden = B32(nt)
                S.add("dve", lambda e, rden=rden: e.reciprocal(out=rden.ap, in_=psm[:, 0:nt]), reads=["psm"], writes=[rden.k()])
                for dc in range(2):
                    p, pk = next_pa()

                    def pvm(e, p=p, hd=hd, dc=dc, ET=ET):
                        r = None
                        for mc in range(2):
                            r = e.matmul(out=p[:, 0:nt], lhsT=Vx.ap[:, mc, hd * 256 + dc * 128:hd * 256 + dc * 128 + 128], rhs=ET.ap[:, mc, :],
                                         start=(mc == 0), stop=(mc == 1))
                        return r
                    S.add("pe", pvm, reads=[Vx.k(), ET.k()], writes=[pk])
                    S.add("dve", lambda e, p=p, hd=hd, dc=dc, rden=rden: e.tensor_tensor(out=ox.ap[:, hd * 2 + dc, :], in0=p[:, 0:nt], in1=rden.ap, op=ALU.mult),
                          reads=[pk, rden.k()], writes=[ox.k(hd * 2 + dc)])
                ET.free(); rden.free()
            qx.free()
            proj_fm(w_xo, 0, 1024, ox, nt, resid_epi)
            ox.free()
            if first_dbg:
                dbg_dump("h2", hT, 8 * NT)
            norm_fm(C_GFFN, nt)
            aT = B16(22, nt)
            col = 0
            while col < DFF:
                cw = min(512, DFF - col)

                def g_epi(c0, nch, pv, pk, col=col):
                    cc = col // 128 + c0
                    S.add("act", lambda e: e.activation(out=aT.ap[:, cc:cc + nch, :], in_=pv, func=AF.Silu), reads=[pk], writes=[aT.k(cc, cc + nch)])

                def u_epi(c0, nch, pv, pk, col=col):
                    cc = col // 128 + c0
                    S.add("dve", lambda e: e.tensor_tensor(out=aT.ap[:, cc:cc + nch, :], in0=aT.ap[:, cc:cc + nch, :], in1=pv, op=ALU.mult),
                          reads=[pk, aT.k(cc, cc + nch)], writes=[aT.k(cc, cc + nch)])
                proj_fm(w_fi, col, cw, nT, nt, g_epi)
                proj_fm(w_fi, DFF + col, cw, nT, nt, u_epi)
                col += cw
            proj_fm(w_fo, 0, 1024, aT, nt, resid_epi, kc=22, blk=128)
            aT.free()
            if first_dbg:
                dbg_dump("h3", hT, 8 * NT)
            norm_fm(C_GFIN, nt, stats_only=True)
            for t in range(tps):
                of = B32(8, 128)

                def nrm_t(e, t=t, of=of):
                    r = None
                    for k in range(8):
                        r = e.scalar_tensor_tensor(out=of.ap[:, k, :], in0=hT.ap[:, k, t * 128:(t + 1) * 128], scalar=cst.ap[:, C_GFIN + k:C_GFIN + k + 1],
                                                   in1=rstd.ap[:, t * 128:(t + 1) * 128], op0=ALU.mult, op1=ALU.mult)
                    return r
                S.add("dve", nrm_t, reads=[hT.k(), rstd.k(), cst.k()], writes=[of.k()])

                def tr(e, t=t, of=of):
                    r = None
                    for c in range(8):
                        r = e.transpose(out=pwa[:, c * 128:(c + 1) * 128], in_=of.ap[:, c, :], identity=idf.ap)
                    return r
                S.add("pe", tr, reads=[of.k(), idf.k()], writes=[PWA])
                of.free()
                og = B32(1024)
                S.add("act", lambda e, og=og: e.activation(out=og.ap, in_=pwa[:, :], func=AF.Copy), reads=[PWA], writes=[og.k()])
                r0 = orow0 + t * 128
                S.add("sp", lambda e, og=og, r0=r0: [e.dma_start(out=out_d[r0:r0 + 128, :], in_=og.ap)], reads=[og.k()],
                      dma=True, dkey="og%d" % og.off, name="OUT")
                og.free()

        def ssd_tile_p1(t, xsT, BT, dtb, ab):
            cols = slice(t * 128, (t + 1) * 128)
            a_t = ab.ap[:, t, :]
            dt_t = dtb.ap[:, t, :]
            Btm = B16(2, 128)
            Xd = B16(1024)
            ex = B32(32)
            dd = B32(16)

            def trx(e):
                r = None
                for c in range(8):
                    r = e.transpose(out=ptr[:, c * 128:(c + 1) * 128], in_=xsT.ap[:, c, cols], identity=idb.ap)
                return r

            def smalls(e):
                e.matmul(out=psm[:, 0:16], lhsT=strif.ap, rhs=a_t, start=True, stop=True)
                return e.matmul(out=psm[:, 16:32], lhsT=onesf.ap, rhs=a_t, start=True, stop=True)
            S.add("pe", smalls, reads=[ab.k(t), strif.k(), onesf.k()], writes=["psm"])
            S.add("act", lambda e: e.activation(out=ex.ap, in_=psm[:, 0:32], func=AF.Exp), reads=["psm"], writes=[ex.k()])
            yield
            S.add("dve", lambda e: e.tensor_tensor(out=dd.ap, in0=ex.ap[:, 0:16], in1=dt_t, op=ALU.mult), reads=[ex.k(), dtb.k(t)], writes=[dd.k()])
            S.add("pe", trx, reads=[xsT.k(), idb.k()], writes=["ptr"])
            yield
            S.add("dve", lambda e: e.tensor_tensor(out=Xd.ap.rearrange("p (h q) -> p h q", h=16),
                                                   in0=ptr[:, :].rearrange("p (h q) -> p h q", h=16),
                                                   in1=dd.ap.unsqueeze(2).to_broadcast([128, 16, 64]), op=ALU.mult),
                  reads=["ptr", dd.k()], writes=[Xd.k()])
            yield

            def trb(e):
                e.transpose(out=ptr[:, 0:128], in_=BT.ap[:, 0, cols], identity=idb.ap)
                return e.transpose(out=ptr[:, 128:256], in_=BT.ap[:, 1, cols], identity=idb.ap)
            S.add("pe", trb, reads=[BT.k(), idb.k()], writes=["ptr"])
            S.add("act", lambda e: e.activation(out=Btm.ap, in_=ptr[:, 0:256].rearrange("p (g n) -> p g n", g=2), func=AF.Copy),
                  reads=["ptr"], writes=[Btm.k()])
            yield

            def sloc(e):
                e.matmul(out=pwb[:, 0:512], lhsT=Btm.ap[:, 0, :], rhs=Xd.ap[:, 0:512], start=True, stop=True)
                return e.matmul(out=pwb[:, 512:1024], lhsT=Btm.ap[:, 1, :], rhs=Xd.ap[:, 512:1024], start=True, stop=True)
            S.add("pe", sloc, reads=[Btm.k(), Xd.k()], writes=[PWB])
            S.add("dve", lambda e: e.tensor_tensor(out=Sst.ap.rearrange("p (h q) -> p h q", h=16), in0=Sst.ap.rearrange("p (h q) -> p h q", h=16),
                                                   in1=ex.ap[:, 16:32].unsqueeze(2).to_broadcast([128, 16, 64]), op=ALU.mult),
                  reads=[ex.k(), Sst.k()], writes=[Sst.k()])
            yield
            S.add("dve", lambda e: e.tensor_tensor(out=Sst.ap, in0=Sst.ap, in1=pwb[:, :], op=ALU.add), reads=[PWB, Sst.k()], writes=[Sst.k()])
            yield
            Btm.free(); Xd.free(); ex.free(); dd.free()

        def ssd_tile(t, xsT, BT, CT, dtb, ab, zs, ynT, full, dbg):
            cols = slice(t * 128, (t + 1) * 128)
            a_t = ab.ap[:, t, :]
            dt_t = dtb.ap[:, t, :]
            if not full:
                yield from ssd_tile_p1(t, xsT, BT, dtb, ab)
                return
            Xtm = B16(1024)
            xstm = B16(1024) if full else None
            Btm = B16(2, 128)

            def trx(e):
                r = None
                for c in range(8):
                    r = e.transpose(out=ptr[:, c * 128:(c + 1) * 128], in_=xsT.ap[:, c, cols], identity=idb.ap)
                return r
            S.add("pe", trx, reads=[xsT.k(), idb.k()], writes=["ptr"])
            yield
            S.add("dve", lambda e: e.tensor_tensor(out=Xtm.ap.rearrange("p (h q) -> p h q", h=16),
                                                   in0=ptr[:, :].rearrange("p (h q) -> p h q", h=16),
                                                   in1=dt_t.unsqueeze(2).to_broadcast([128, 16, 64]), op=ALU.mult),
                  reads=["ptr", dtb.k(t)], writes=[Xtm.k()])
            yield
            if full:
                S.add("dve", lambda e: e.tensor_tensor(out=xstm.ap.rearrange("p (h q) -> p h q", h=16),
                                                       in0=ptr[:, :].rearrange("p (h q) -> p h q", h=16),
                                                       in1=cvec(C_DSK, 16).unsqueeze(2).to_broadcast([128, 16, 64]), op=ALU.mult),
                      reads=["ptr", cst.k()], writes=[xstm.k()])
                yield

            def trb(e):
                e.transpose(out=ptr[:, 0:128], in_=BT.ap[:, 0, cols], identity=idb.ap)
                return e.transpose(out=ptr[:, 128:256], in_=BT.ap[:, 1, cols], identity=idb.ap)
            S.add("pe", trb, reads=[BT.k(), idb.k()], writes=["ptr"])
            yield
            S.add("act", lambda e: e.activation(out=Btm.ap, in_=ptr[:, 0:256].rearrange("p (g n) -> p g n", g=2), func=AF.Copy),
                  reads=["ptr"], writes=[Btm.k()])
            yield
            def smalls(e):
                e.matmul(out=psm[:, 0:16], lhsT=tri.ap, rhs=a_t, start=True, stop=True)
                e.matmul(out=psm[:, 16:32], lhsT=stri.ap, rhs=a_t, start=True, stop=True)
                e.matmul(out=psm[:, 32:48], lhsT=onesc.ap[:, 0, :], rhs=a_t, start=True, stop=True)
                return e.matmul(out=psm[:, 48:64], lhsT=onesc.ap[:, 1, :], rhs=a_t, start=True, stop=True)
            S.add("pe", smalls, reads=[ab.k(t), tri.k(), stri.k(), onesc.k()], writes=["psm"])
            yield
            ex = B32(64)
            S.add("act", lambda e: e.activation(out=ex.ap, in_=psm[:, 0:64], func=AF.Exp), reads=["psm"], writes=[ex.k()])
            yield
            eacs = ex.ap[:, 0:16]
            if dbg:
                dbg_dump("Xtm", Xtm, 1024, BF16)
                dbg_dump("ex", ex, 64)
                dbg_dump("dtb", dtb, 16)
            ds = B32(2, 16)

            def mkds(e):
                e.tensor_scalar(out=ds.ap[:, 0, :], in0=ex.ap[:, 16:32], scalar1=cm.ap[:, 0:1], scalar2=None, op0=ALU.mult)
                return e.tensor_scalar(out=ds.ap[:, 1, :], in0=ex.ap[:, 16:32], scalar1=cm.ap[:, 1:2], scalar2=None, op0=ALU.mult)
            S.add("dve", mkds, reads=[ex.k(), cm.k()], writes=[ds.k()])
            yield
            Xd = [B16(1024), B16(1024)]
            for c in range(2):
                S.add("dve", lambda e, c=c: e.tensor_tensor(out=Xd[c].ap.rearrange("p (h q) -> p h q", h=16),
                                                            in0=Xtm.ap.rearrange("p (h q) -> p h q", h=16),
                                                            in1=ds.ap[:, c, :].unsqueeze(2).to_broadcast([128, 16, 64]), op=ALU.mult),
                      reads=[Xtm.k(), ds.k()], writes=[Xd[c].k()])
                yield
            MT = None
            if full:
                MT = B16(16, 128)
                for g in range(2):
                    rhs_all = B32(8, 128)
                    S.add("dve", lambda e, g=g, rhs_all=rhs_all: e.tensor_tensor(
                        out=rhs_all.ap, in0=tri.ap.unsqueeze(1).to_broadcast([128, 8, 128]),
                        in1=a_t[:, g * 8:(g + 1) * 8].unsqueeze(2).to_broadcast([128, 8, 128]), op=ALU.mult),
                        reads=[tri.k(), ab.k(t)], writes=[rhs_all.k()])
                    yield

                    def dmm(e, rhs_all=rhs_all):
                        e.matmul(out=pwa[:, 0:512], lhsT=stri.ap, rhs=rhs_all.ap[:, 0:4, :], start=True, stop=True)
                        return e.matmul(out=pwa[:, 512:1024], lhsT=stri.ap, rhs=rhs_all.ap[:, 4:8, :], start=True, stop=True)
                    S.add("pe", dmm, reads=[rhs_all.k(), stri.k()], writes=[PWA])
                    yield
                    E = B16(8, 128)

                    def eexp(e, E=E):
                        e.activation(out=E.ap[:, 0:4, :], in_=pwa[:, 0:512].rearrange("p (h l) -> p h l", h=4), func=AF.Exp)
                        return e.activation(out=E.ap[:, 4:8, :], in_=pwa[:, 512:1024].rearrange("p (h l) -> p h l", h=4), func=AF.Exp)
                    S.add("act", eexp, reads=[PWA], writes=[E.k()])
                    yield
                    S.add("pe", lambda e, g=g: e.matmul(out=psm[:, 128 + g * 128:256 + g * 128], lhsT=BT.ap[:, g, cols], rhs=CT.ap[:, g, cols],
                                                        start=True, stop=True), reads=[BT.k(), CT.k()], writes=["psm"])
                    yield
                    cbm = B32(128)
                    S.add("dve", lambda e, g=g, cbm=cbm: e.tensor_tensor(out=cbm.ap, in0=psm[:, 128 + g * 128:256 + g * 128], in1=tri.ap, op=ALU.mult),
                          reads=["psm", tri.k()], writes=[cbm.k()])
                    yield
                    S.add("dve", lambda e, g=g, cbm=cbm, E=E: e.tensor_tensor(out=MT.ap[:, g * 8:(g + 1) * 8, :], in0=E.ap,
                                                                               in1=cbm.ap.unsqueeze(1).to_broadcast([128, 8, 128]), op=ALU.mult),
                          reads=[E.k(), cbm.k()], writes=[MT.k(g * 8, (g + 1) * 8)])
                    yield
                    rhs_all.free(); E.free(); cbm.free()
                for c in range(2):
                    S.add("act", lambda e, c=c: e.activation(out=ctpad[c].ap[:, :, c * 64:(c + 1) * 64],
                                                             in_=CT.ap[:, :, t * 128 + c * 64:t * 128 + (c + 1) * 64], func=AF.Copy),
                          reads=[CT.k()], writes=[ctpad[c].k()])
                    yield
            Sbf = [B16(1024), B16(1024)] if full else None
            for c in range(2):
                def sloc(e, c=c):
                    e.matmul(out=pwb[:, 0:512], lhsT=Btm.ap[:, 0, :], rhs=Xd[c].ap[:, 0:512], start=True, stop=True)
                    return e.matmul(out=pwb[:, 512:1024], lhsT=Btm.ap[:, 1, :], rhs=Xd[c].ap[:, 512:1024], start=True, stop=True)
                S.add("pe", sloc, reads=[Btm.k(), Xd[c].k()], writes=[PWB])
                yield
                if full:
                    S.add("act", lambda e, c=c: e.activation(out=Sbf[c].ap, in_=Sst.ap, func=AF.Copy), reads=[Sst.k()], writes=[Sbf[c].k()])
                    yield

                def supd(e, c=c):
                    e.tensor_tensor(out=dtot.ap[:, 0:16], in0=dtot.ap[:, 0:16], in1=ex.ap[:, 32 + 16 * c:48 + 16 * c], op=ALU.mult)
                    return e.tensor_tensor(out=Sst.ap.rearrange("p (h q) -> p h q", h=16), in0=Sst.ap.rearrange("p (h q) -> p h q", h=16),
                                           in1=ex.ap[:, 32 + 16 * c:48 + 16 * c].unsqueeze(2).to_broadcast([128, 16, 64]), op=ALU.mult)
                S.add("dve", supd, reads=[ex.k(), Sst.k(), dtot.kr(0, 16)], writes=[Sst.k(), dtot.kr(0, 16)])
                yield
                S.add("dve", lambda e: e.tensor_tensor(out=Sst.ap, in0=Sst.ap, in1=pwb[:, :], op=ALU.add), reads=[PWB, Sst.k()], writes=[Sst.k()])
                yield
            Xd[0].free(); Xd[1].free(); Btm.free(); ds.free()
            if not full:
                Xtm.free(); ex.free()
                return
            def ymm(e):
                r = None
                for b in range(2):
                    e.matmul(out=pwa[:, b * 512:(b + 1) * 512], lhsT=idb.ap, rhs=xstm.ap[:, b * 512:(b + 1) * 512], start=True, stop=False)
                    for hh in range(8):
                        h = b * 8 + hh
                        r = e.matmul(out=pwa[:, h * 64:(h + 1) * 64], lhsT=MT.ap[:, h, :], rhs=Xtm.ap[:, h * 64:(h + 1) * 64],
                                     start=False, stop=(hh == 7))
                return r
            S.add("pe", ymm, reads=[idb.k(), xstm.k(), MT.k(), Xtm.k()], writes=[PWA])
            yield

            def yoff(e):
                r = None
                for g in range(2):
                    for c in range(2):
                        r = e.matmul(out=pwb[:, g * 512:(g + 1) * 512], lhsT=ctpad[c].ap[:, g, :], rhs=Sbf[c].ap[:, g * 512:(g + 1) * 512],
                                     start=(c == 0), stop=(c == 1))
                return r
            S.add("pe", yoff, reads=[ctpad[0].k(), ctpad[1].k(), Sbf[0].k(), Sbf[1].k()], writes=[PWB])
            yield
            yt = B32(1024)

            S.add("dve", lambda e: e.tensor_tensor(out=yt.ap.rearrange("p (h q) -> p h q", h=16), in0=pwb[:, :].rearrange("p (h q) -> p h q", h=16),
                                                   in1=eacs.unsqueeze(2).to_broadcast([128, 16, 64]), op=ALU.mult),
                  reads=[PWB, ex.k()], writes=[yt.k()])
            yield
            S.add("dve", lambda e: e.tensor_tensor(out=yt.ap, in0=yt.ap, in1=pwa[:, :], op=ALU.add), reads=[PWA, yt.k()], writes=[yt.k()])
            yield
            S.add("dve", lambda e: e.tensor_tensor(out=yt.ap, in0=yt.ap, in1=zs.ap[:, t, :], op=ALU.mult), reads=[yt.k(), zs.k(t)], writes=[yt.k()])
            yield
            if dbg:
                dbg_dump("yt", yt, 1024)
                dbg_dump("MT", MT, 2048, BF16)
                dbg_dump("Sbf1", Sbf[1], 1024, BF16)
            Xtm.free(); xstm.free(); MT.free(); Sbf[0].free(); Sbf[1].free(); ex.free()
            junk = B16(1024)
            ss = B32(2)

            def ysq(e):
                e.activation(out=junk.ap[:, 0:512], in_=yt.ap[:, 0:512], func=AF.Square, accum_out=ss.ap[:, 0:1])
                return e.activation(out=junk.ap[:, 512:1024], in_=yt.ap[:, 512:1024], func=AF.Square, accum_out=ss.ap[:, 1:2])
            S.add("act", ysq, reads=[yt.k()], writes=[junk.k(), ss.k()])
            yield
            chain("act", [lambda e: e.activation(out=ss.ap, in_=ss.ap, func=AF.Ln, bias=EPS, scale=1.0 / 512.0),
                          lambda e: e.activation(out=ss.ap, in_=ss.ap, func=AF.Exp, scale=-0.5)], [], [ss.k()])
            yield
            yn = B16(1024)

            def ynorm(e):
                e.scalar_tensor_tensor(out=yn.ap[:, 0:512], in0=yt.ap[:, 0:512], scalar=ss.ap[:, 0:1], in1=cvec(C_SSDN, 512), op0=ALU.mult, op1=ALU.mult)
                return e.scalar_tensor_tensor(out=yn.ap[:, 512:1024], in0=yt.ap[:, 512:1024], scalar=ss.ap[:, 1:2], in1=cvec(C_SSDN + 512, 512),
                                              op0=ALU.mult, op1=ALU.mult)
            S.add("dve", ynorm, reads=[yt.k(), ss.k(), cst.k()], writes=[yn.k()])
            yield

            def try_(e):
                r = None
                for c in range(8):
                    r = e.transpose(out=ptr[:, c * 128:(c + 1) * 128], in_=yn.ap[:, c * 128:(c + 1) * 128], identity=idb.ap)
                return r
            S.add("pe", try_, reads=[yn.k(), idb.k()], writes=["ptr"])
            yield
            S.add("act", lambda e: e.activation(out=ynT.ap[:, :, cols], in_=ptr[:, :].rearrange("p (c n) -> p c n", c=8), func=AF.Copy),
                  reads=["ptr"], writes=[ynT.k()])
            yield
            yt.free(); junk.free(); ss.free(); yn.free()

        def gla_tile(t, qT, kT, ktm, vtm, rs, onT, full, dbg=False):
            cols = slice(t * 128, (t + 1) * 128)
            p0, pk0 = next_pa()
            S.add("pe", lambda e: e.matmul(out=p0[:, 0:512], lhsT=a1T.ap[:, cols], rhs=w2a.ap, start=True, stop=True),
                  reads=[a1T.k(), w2a.k()], writes=[pk0])
            yield
            la = B32(512)

            S.add("act", lambda e: e.activation(out=la.ap, in_=p0[:, 0:512], func=AF.Exp, scale=-1.0), reads=[pk0], writes=[la.k()])
            yield
            S.add("act", lambda e: e.activation(out=la.ap, in_=la.ap, func=AF.Ln, bias=1.0), reads=[la.k()], writes=[la.k()])
            yield
            if dbg:
                dbg_dump("la", la, 512)
            if not full:
                pd, pkd = next_pa()

                def dmm_(e):
                    r = None
                    for hd in range(4):
                        r = e.matmul(out=pd[:, hd * 2:hd * 2 + 2], lhsT=la.ap[:, hd * 128:(hd + 1) * 128], rhs=negc.ap, start=True, stop=True)
                    return r
                S.add("pe", dmm_, reads=[la.k(), negc.k()], writes=[pkd])
                dec8 = B32(8)
                S.add("act", lambda e: e.activation(out=dec8.ap, in_=pd[:, 0:8], func=AF.Exp), reads=[pkd], writes=[dec8.k()])
                yield
                pe_, pke = next_pa()
                S.add("pe", lambda e: e.matmul(out=pe_[:, 0:512], lhsT=strigf.ap, rhs=la.ap, start=True, stop=True), reads=[strigf.k(), la.k()], writes=[pke])
                khf = B16(512)
                Ekf = B32(512)
                S.add("act", lambda e: e.activation(out=Ekf.ap, in_=pe_[:, 0:512], func=AF.Exp), reads=[pke], writes=[Ekf.k()])
                yield
                S.add("dve", lambda e: e.tensor_tensor(out=khf.ap, in0=Ekf.ap, in1=ktm.ap[:, t, :], op=ALU.mult), reads=[Ekf.k(), ktm.k(t)], writes=[khf.k()])
                yield
                for hp in range(2):
                    pu, pku = next_pa()

                    def umm_(e, hp=hp, pu=pu):
                        r = None
                        for h2 in range(2):
                            hd = hp * 2 + h2
                            r = e.matmul(out=pu[:, h2 * 256:(h2 + 1) * 256], lhsT=khf.ap[:, hd * 128:(hd + 1) * 128],
                                         rhs=vtm.ap[:, t, hd * 256:(hd + 1) * 256], start=True, stop=True)
                        return r
                    S.add("pe", umm_, reads=[khf.k(), vtm.k(t)], writes=[pku])

                    def gupd_(e, hp=hp, pu=pu):
                        r = None
                        for h2 in range(2):
                            hd = hp * 2 + h2
                            r = e.scalar_tensor_tensor(out=Sg.ap[:, hd * 256:(hd + 1) * 256], in0=Sg.ap[:, hd * 256:(hd + 1) * 256],
                                                       scalar=dec8.ap[:, hd * 2:hd * 2 + 1], in1=pu[:, h2 * 256:(h2 + 1) * 256],
                                                       op0=ALU.mult, op1=ALU.add)
                        return r
                    S.add("dve", gupd_, reads=[pku, dec8.k(), Sg.kr(hp * 512, hp * 512 + 512)], writes=[Sg.kr(hp * 512, hp * 512 + 512)])
                    yield
                la.free(); dec8.free(); khf.free(); Ekf.free()
                return
            p1, pk1 = next_pa()

            def bc(e):
                r = None
                for hd in range(4):
                    r = e.matmul(out=p1[:, hd * 128:(hd + 1) * 128], lhsT=la.ap[:, hd * 128:(hd + 1) * 128], rhs=trig.ap, start=True, stop=True)
                return r
            S.add("pe", bc, reads=[la.k(), trig.k()], writes=[pk1])
            yield
            EqT = B32(4, 128)
            S.add("act", lambda e: e.activation(out=EqT.ap, in_=p1[:, 0:512].rearrange("p (h l) -> p h l", h=4), func=AF.Exp),
                  reads=[pk1], writes=[EqT.k()])
            yield
            ktT = None
            if full:
                EkT = B32(4, 128)
                S.add("act", lambda e: e.activation(out=EkT.ap, in_=p1[:, 0:512].rearrange("p (h l) -> p h l", h=4), func=AF.Exp, scale=-1.0),
                      reads=[pk1], writes=[EkT.k()])
                yield
                ktT = B16(4, 128)
                S.add("dve", lambda e: e.tensor_tensor(out=ktT.ap, in0=kT.ap[:, :, cols], in1=EkT.ap, op=ALU.mult),
                      reads=[kT.k(), EkT.k()], writes=[ktT.k()])
                yield
                for c in range(2):
                    S.add("dve", lambda e, c=c: e.tensor_tensor(out=qtpad[c].ap[:, :, c * 64:(c + 1) * 64],
                                                                in0=qT.ap[:, :, t * 128 + c * 64:t * 128 + (c + 1) * 64],
                                                                in1=EqT.ap[:, :, c * 64:(c + 1) * 64], op=ALU.mult),
                          reads=[qT.k(), EqT.k()], writes=[qtpad[c].k()])
                    yield
                EkT.free()
            p2, pk2 = next_pa()
            S.add("pe", lambda e: e.matmul(out=p2[:, 0:512], lhsT=strig.ap, rhs=la.ap, start=True, stop=True), reads=[strig.k(), la.k()], writes=[pk2])
            yield
            Ekh = B32(512)
            S.add("act", lambda e: e.activation(out=Ekh.ap, in_=p2[:, 0:512], func=AF.Exp), reads=[pk2], writes=[Ekh.k()])
            yield
            kh = [B16(512), B16(512)]
            for c in range(2):
                S.add("dve", lambda e, c=c: e.scalar_tensor_tensor(out=kh[c].ap, in0=Ekh.ap, scalar=cm.ap[:, c:c + 1], in1=ktm.ap[:, t, :],
                                                                   op0=ALU.mult, op1=ALU.mult),
                      reads=[Ekh.k(), cm.k(), ktm.k(t)], writes=[kh[c].k()])
                yield
            la.free(); Ekh.free()
            attT = None
            if full:
                p3, pk3 = next_pa()

                def att(e):
                    r = None
                    for hd in range(4):
                        e.matmul(out=p3[:, hd * 128:(hd + 1) * 128], lhsT=ktT.ap[:, hd, :], rhs=qtpad[0].ap[:, hd, :], start=True, stop=False)
                        r = e.matmul(out=p3[:, hd * 128:(hd + 1) * 128], lhsT=ktT.ap[:, hd, :], rhs=qtpad[1].ap[:, hd, :], start=False, stop=True)
                    return r
                S.add("pe", att, reads=[ktT.k(), qtpad[0].k(), qtpad[1].k()], writes=[pk3])
                yield
                attT = B16(4, 128)
                S.add("dve", lambda e: e.tensor_tensor(out=attT.ap, in0=p3[:, 0:512].rearrange("p (h l) -> p h l", h=4),
                                                       in1=tri.ap.unsqueeze(1).to_broadcast([128, 4, 128]), op=ALU.mult),
                      reads=[pk3, tri.k()], writes=[attT.k()])
                yield
                ktT.free()
            Sgb = [B16(1024), B16(1024)] if full else None
            for c in range(2):
                if full:
                    S.add("act", lambda e, c=c: e.activation(out=Sgb[c].ap, in_=Sg.ap, func=AF.Copy), reads=[Sg.k()], writes=[Sgb[c].k()])
                    yield
                for hp in range(2):
                    pu, pku = next_pa()

                    def umm(e, c=c, hp=hp, pu=pu):
                        r = None
                        for h2 in range(2):
                            hd = hp * 2 + h2
                            r = e.matmul(out=pu[:, h2 * 256:(h2 + 1) * 256], lhsT=kh[c].ap[:, hd * 128:(hd + 1) * 128],
                                         rhs=vtm.ap[:, t, hd * 256:(hd + 1) * 256], start=True, stop=True)
                        return r
                    S.add("pe", umm, reads=[kh[c].k(), vtm.k(t)], writes=[pku])
                    yield

                    def gupd(e, c=c, hp=hp, pu=pu):
                        r = None
                        for h2 in range(2):
                            hd = hp * 2 + h2
                            r = e.scalar_tensor_tensor(out=Sg.ap[:, hd * 256:(hd + 1) * 256], in0=Sg.ap[:, hd * 256:(hd + 1) * 256],
                                                       scalar=EqT.ap[:, hd, c * 64 + 63:c * 64 + 64], in1=pu[:, h2 * 256:(h2 + 1) * 256],
                                                       op0=ALU.mult, op1=ALU.add)
                        return r
                    S.add("dve", gupd, reads=[pku, EqT.k(), Sg.kr(hp * 512, hp * 512 + 512)], writes=[Sg.kr(hp * 512, hp * 512 + 512)])
                    yield
                    yield
            kh[0].free(); kh[1].free(); EqT.free()
            if not full:
                return

            junk = B16(1024)
            ss = B32(4)
            pos = []
            for hp in range(2):
                po, pko = next_pa()
                pos.append((po, pko))

                def omm(e, hp=hp, po=po):
                    r = None
                    for h2 in range(2):
                        hd = hp * 2 + h2
                        o = po[:, h2 * 256:(h2 + 1) * 256]
                        e.matmul(out=o, lhsT=attT.ap[:, hd, :], rhs=vtm.ap[:, t, hd * 256:(hd + 1) * 256], start=True, stop=False)
                        e.matmul(out=o, lhsT=qtpad[0].ap[:, hd, :], rhs=Sgb[0].ap[:, hd * 256:(hd + 1) * 256], start=False, stop=False)
                        r = e.matmul(out=o, lhsT=qtpad[1].ap[:, hd, :], rhs=Sgb[1].ap[:, hd * 256:(hd + 1) * 256], start=False, stop=True)
                    return r
                S.add("pe", omm, reads=[attT.k(), vtm.k(t), qtpad[0].k(), qtpad[1].k(), Sgb[0].k(), Sgb[1].k()], writes=[pko])
                yield

                def osq(e, hp=hp, po=po):
                    r = None
                    for h2 in range(2):
                        hd = hp * 2 + h2
                        r = e.activation(out=junk.ap[:, hd * 256:(hd + 1) * 256], in_=po[:, h2 * 256:(h2 + 1) * 256], func=AF.Square,
                                         accum_out=ss.ap[:, hd:hd + 1])
                    return r
                S.add("act", osq, reads=[pko], writes=[junk.kr(hp * 512, hp * 512 + 512), ss.k()])
                yield
                yield
            attT.free(); Sgb[0].free(); Sgb[1].free()
            chain("act", [lambda e: e.activation(out=ss.ap, in_=ss.ap, func=AF.Ln, bias=EPS, scale=1.0 / 256.0),
                          lambda e: e.activation(out=ss.ap, in_=ss.ap, func=AF.Exp, scale=-0.5)], [], [ss.k()])
            yield
            yield
            on = B32(1024)
            onb = B16(1024)
            for hp in range(2):
                po, pko = pos[hp]

                def onorm(e, hp=hp, po=po):
                    r = None
                    for h2 in range(2):
                        hd = hp * 2 + h2
                        r = e.scalar_tensor_tensor(out=on.ap[:, hd * 256:(hd + 1) * 256], in0=po[:, h2 * 256:(h2 + 1) * 256], scalar=ss.ap[:, hd:hd + 1],
                                                   in1=cvec(C_GLAN, 256), op0=ALU.mult, op1=ALU.mult)
                    return r
                S.add("dve", onorm, reads=[pko, ss.k(), cst.k()], writes=[on.kr(hp * 512, hp * 512 + 512)])
                yield
            S.add("dve", lambda e: e.tensor_tensor(out=onb.ap, in0=on.ap, in1=rs.ap[:, t, :], op=ALU.mult), reads=[on.k(), rs.k(t)], writes=[onb.k()])
            yield
            yield

            def tro(e):
                r = None
                for c in range(8):
                    r = e.transpose(out=ptr[:, c * 128:(c + 1) * 128], in_=onb.ap[:, c * 128:(c + 1) * 128], identity=idb.ap)
                return r
            S.add("pe", tro, reads=[onb.k(), idb.k()], writes=["ptr"])
            yield
            S.add("act", lambda e: e.activation(out=onT.ap[:, :, cols], in_=ptr[:, :].rearrange("p (c n) -> p c n", c=8), func=AF.Copy),
                  reads=["ptr"], writes=[onT.k()])
            yield
            junk.free(); ss.free(); on.free(); onb.free()

        def prologue_mem():
            mn = B16(2, 1024)
            for mc in range(2):
                ms = B32(1024)
                S.add("sp", lambda e, ms=ms, mc=mc: [e.dma_start(out=ms.ap, in_=mem_d[mc * 128:(mc + 1) * 128, :])], writes=[ms.k()], dma=True, dkey="ms%d" % ms.off)
                junk = B16(1024)
                ss = B32(1)
                S.add("act", lambda e, ms=ms, junk=junk, ss=ss: e.activation(out=junk.ap, in_=ms.ap, func=AF.Square, accum_out=ss.ap),
                      reads=[ms.k()], writes=[junk.k(), ss.k()])
                chain("act", [lambda e, ss=ss: e.activation(out=ss.ap, in_=ss.ap, func=AF.Ln, bias=EPS, scale=1.0 / 1024.0),
                              lambda e, ss=ss: e.activation(out=ss.ap, in_=ss.ap, func=AF.Exp, scale=-0.5)], [], [ss.k()])
                S.add("dve", lambda e, ms=ms, ss=ss, mc=mc: e.scalar_tensor_tensor(out=mn.ap[:, mc, :], in0=ms.ap, scalar=ss.ap[:, 0:1], in1=cvec(C_MEMN, 1024),
                                                                                   op0=ALU.mult, op1=ALU.mult),
                      reads=[ms.k(), ss.k(), cst.k()], writes=[mn.k(mc)])
                ms.free(); junk.free(); ss.free()
            mnT = B16(8, 256)
            for mc in range(2):
                def trm(e, mc=mc):
                    r = None
                    for c in range(8):
                        r = e.transpose(out=ptr[:, c * 128:(c + 1) * 128], in_=mn.ap[:, mc, c * 128:(c + 1) * 128], identity=idb.ap)
                    return r
                S.add("pe", trm, reads=[mn.k(mc), idb.k()], writes=["ptr"])
                S.add("act", lambda e, mc=mc: e.activation(out=mnT.ap[:, :, mc * 128:(mc + 1) * 128], in_=ptr[:, :].rearrange("p (c n) -> p c n", c=8), func=AF.Copy),
                      reads=["ptr"], writes=[mnT.k()])
            mn.free()

            def k_epi(c0, nch, pv, pk):
                S.add("act", lambda e: e.activation(out=KT.ap[:, c0:c0 + nch, :], in_=pv, func=AF.Copy), reads=[pk], writes=[KT.k(c0, c0 + nch)])
            proj_fm(w_xkv, 0, 1024, mnT, 256, k_epi, blk=256)

            def v_epi(t, col, cw, p, pk):
                S.add("act", lambda e: e.activation(out=Vx.ap[:, t, col:col + cw], in_=p, func=AF.Copy), reads=[pk],
                      writes=[Vx.kr(t * 1024 + col, t * 1024 + col + cw)])
            proj_tm(w_xkv, 1024, 1024, mnT, 2, v_epi)
            mnT.free()

        prologue_mem()
        for s_ in range(NPRE // NT):
            step(s_ * NT, NT, "p1")
            if (s_ + 1) % NSTEP == 0:
                zi = cst.ap[:, C_CMASK + (s_ + 1) // NSTEP - 1:C_CMASK + (s_ + 1) // NSTEP]

                def zs_(e, zi=zi):
                    e.tensor_scalar(out=Sst.ap, in0=Sst.ap, scalar1=zi, scalar2=None, op0=ALU.mult)
                    return e.tensor_scalar(out=Sg.ap, in0=Sg.ap, scalar1=zi, scalar2=None, op0=ALU.mult)
                S.add("dve", zs_, reads=[Sst.k(), Sg.k(), cst.k()], writes=[Sst.k(), Sg.k()])
        for s_ in range(NSTEP):
            step(NPRE + s_ * NT, NT, "p2", orow0=s_ * NT, first_dbg=(s_ == 0))
        print("arena peaks: A16 %d / %d, A32 %d / %d; ops %d" % (a16.peak, N16, a32.peak, N32, len(S.ops)))
        S.emit(nc, st)
    return nc, dbg_outs


_CACHE = {}


def host_inputs(x, mem, norm_mix, w_in, ssd_conv_w, ssd_conv_b, ssd_dt_bias, ssd_A_log, ssd_D, ssd_norm,
                gla_w_a2, gla_b_a, gla_norm, w_up_ssd, w_up_gla, w_o, norm_xattn, norm_mem, w_xq, w_xkv,
                w_xo, norm_ffn, w_ffn_in, w_ffn_out, norm_final):
    f = lambda a: np.ascontiguousarray(np.asarray(a, dtype=np.float32))
    x = f(x); mem = f(mem)

    def fm(g):
        return f(g).reshape(8, 128).T

    def rep(v):
        v = f(v).reshape(1, -1)
        return np.broadcast_to(v, (128, v.shape[1]))
    cst = np.zeros((128, NCST), np.float32)
    cst[:, C_GMIX:C_GMIX + 8] = fm(norm_mix[0])
    cst[:, C_GXA:C_GXA + 8] = fm(norm_xattn[0])
    cst[:, C_GFFN:C_GFFN + 8] = fm(norm_ffn[0])
    cst[:, C_GFIN:C_GFIN + 8] = fm(norm_final)
    cw = f(ssd_conv_w[0])[:, 0, :]
    cst[:, C_CONVW:C_CONVW + 48] = cw.reshape(4, 12, 128).transpose(2, 1, 0).reshape(128, 48)
    cst[:, C_CONVB:C_CONVB + 12] = f(ssd_conv_b[0]).reshape(12, 128).T
    cst[:, C_DTB:C_DTB + 16] = rep(ssd_dt_bias[0])
    cst[:, C_ALOG:C_ALOG + 16] = rep(ssd_A_log[0])
    cst[:, C_DSK:C_DSK + 16] = rep(ssd_D[0])
    cst[:, C_SSDN:C_SSDN + 1024] = rep(ssd_norm[0])
    cst[:, C_GLAN:C_GLAN + 256] = rep(gla_norm[0])
    cst[:, C_MEMN:C_MEMN + 1024] = rep(norm_mem[0])
    w2aug = np.concatenate([f(gla_w_a2[0]), f(gla_b_a[0]).reshape(1, 512)], 0)
    shared = {"w2aug": f(w2aug), "w_in": f(w_in[0]), "w_up_ssd": f(w_up_ssd[0]), "w_up_gla": f(w_up_gla[0]), "w_o": f(w_o[0]),
              "w_xq": f(w_xq[0]), "w_xkv": f(w_xkv[0]), "w_xo": f(w_xo[0]), "w_ffn_in": f(w_ffn_in[0]), "w_ffn_out": f(w_ffn_out[0])}
    in_maps = []
    for c in range(NCORES):
        b, j = divmod(c, 4)
        xe = np.zeros((4 * SEG, D), np.float32)
        xe[(3 - j) * SEG:3 * SEG] = x[b, 0:j * SEG]
        xe[3 * SEG:] = x[b, j * SEG:(j + 1) * SEG]
        cc = cst.copy()
        for i in range(3):
            cc[:, C_CMASK + i] = 1.0 if i >= 3 - j else 0.0
        m = {"x_ext": xe, "mem_b": mem[b], "cst": cc}
        m.update(shared)
        in_maps.append(m)
    return in_maps


def kernel(**inputs):
    if "nc" not in _CACHE:
        _CACHE["nc"] = build_program()
    nc, dbg = _CACHE["nc"]
    in_maps = host_inputs(**inputs)
    res = run_bass_kernel_spmd(nc, in_maps, core_ids=list(range(NCORES)))
    _CACHE["last"] = res
    out = np.zeros((2, 4 * SEG, D), np.float32)
    for c in range(NCORES):
        b, j = divmod(c, 4)
        out[b, j * SEG:(j + 1) * SEG] = res.results[c]["out"]
    return out
```

```python
import numpy as np
from contextlib import ExitStack
import concourse.bass as bass
import concourse.mybir as mybir
from concourse.bass_utils import run_bass_kernel_spmd

F32 = mybir.dt.float32
BF16 = mybir.dt.bfloat16
AF = mybir.ActivationFunctionType
ALU = mybir.AluOpType

NCORES = 8
D = 1024
SEG = 2048
NT = 512
HALO = 128
EPS = 1e-6
EPOCH = 8000
DBG = False

O_Z, O_XBC, O_DT, O_Q, O_K, O_V, O_R, O_A1, O_GS, O_GG = 0, 1024, 2560, 2576, 3088, 3600, 4624, 5648, 5664, 6688
DFF = 2816

C_GMIX, C_GXA, C_GFFN, C_GFIN = 0, 8, 16, 24
C_CONVW = 32
C_CONVB = 80
C_DTB, C_ALOG, C_DSK = 92, 108, 124
C_SSDN = 140
C_GLAN = 1164
C_MEMN = 1420
C_CMASK = 2444
NCST = 2452


class Op:
    __slots__ = ("eng", "fn", "deps", "sig", "signal", "is_dma", "dkey", "ndma", "name", "inc")


class Sched:
    ENGS = ("pe", "act", "dve", "pool", "sp")

    def __init__(self):
        self.ops = []
        self.recs = {}

    @staticmethod
    def _k(k):
        return (k, 0, 1) if isinstance(k, str) else k

    def add(self, eng, fn, reads=(), writes=(), dma=False, dkey=None, ndma=1, name="", inc=16):
        op = Op()
        op.eng, op.fn, op.is_dma, op.dkey, op.ndma, op.inc, op.name = eng, fn, dma, dkey, ndma, inc, name
        op.sig = False
        op.signal = None
        deps = []
        seen = set()

        def push(d, raw):
            if d is None or d is op or id(d) in seen:
                return
            if not raw and not (d.is_dma or dma or d.eng != eng):
                return
            seen.add(id(d))
            deps.append(d)

        for k in reads:
            a, lo, hi = self._k(k)
            for r in self.recs.get(a, ()):
                if r[0] < hi and lo < r[1]:
                    push(r[2], True)
                    r[3].append(op)
        for k in writes:
            a, lo, hi = self._k(k)
            lst = self.recs.setdefault(a, [])
            new = []
            for r in lst:
                if r[0] < hi and lo < r[1]:
                    push(r[2], False)
                    for rd in r[3]:
                        push(rd, False)
                    if r[0] < lo:
                        new.append([r[0], lo, r[2], list(r[3])])
                    if hi < r[1]:
                        new.append([hi, r[1], r[2], list(r[3])])
                else:
                    new.append(r)
            new.append([lo, hi, op, []])
            self.recs[a] = new
        if eng == "pe" and not dma:
            deps = [d for d in deps if d.is_dma or d.eng != "pe"]
        op.deps = deps
        for d in deps:
            d.sig = True
        self.ops.append(op)
        return op

    def emit(self, nc, stack):
        eng_count = {e: 0 for e in self.ENGS}
        eng_sems = {e: [] for e in self.ENGS}
        dma_sems = {}
        dma_vals = {}
        for op in self.ops:
            if op.is_dma:
                if op.dkey not in dma_sems:
                    dma_sems[op.dkey] = stack.enter_context(nc.semaphore("d%d" % len(dma_sems)))
                    dma_vals[op.dkey] = 0
                dma_vals[op.dkey] += op.inc * op.ndma
                op.signal = (dma_sems[op.dkey], dma_vals[op.dkey])
            elif op.sig:
                c = eng_count[op.eng]
                ep = c // EPOCH
                if ep >= len(eng_sems[op.eng]):
                    eng_sems[op.eng].append(stack.enter_context(nc.semaphore("e%s%d" % (op.eng, ep))))
                op.signal = (eng_sems[op.eng][ep], c % EPOCH + 1)
                eng_count[op.eng] = c + 1
        by_eng = {e: [o for o in self.ops if o.eng == e] for e in self.ENGS}
        finals = {}
        for op in self.ops:
            if op.is_dma and op.name.startswith("OUT"):
                sem, val = op.signal
                if finals.get(id(sem), (None, 0))[1] < val:
                    finals[id(sem)] = (sem, val)

        def run(engh, ename):
            waited = {}
            for op in by_eng[ename]:
                for d in op.deps:
                    sem, val = d.signal
                    if waited.get(id(sem), 0) < val:
                        engh.wait_ge(sem, val)
                        waited[id(sem)] = val
                r = op.fn(engh)
                if op.is_dma:
                    assert len(r) == op.ndma, (op.name, len(r), op.ndma)
                    for ins in r:
                        ins.then_inc(op.signal[0], op.inc)
                elif op.sig:
                    r.then_inc(op.signal[0], 1)
            if ename == "sp":
                for sem, val in finals.values():
                    engh.wait_ge(sem, val)

        with nc.Block() as block:
            @block.tensor
            def _(e):
                run(e, "pe")

            @block.scalar
            def _(e):
                run(e, "act")

            @block.vector
            def _(e):
                run(e, "dve")

            @block.gpsimd
            def _(e):
                run(e, "pool")

            @block.sync
            def _(e):
                run(e, "sp")


class Arena:
    def __init__(self, name, tensor, n):
        self.name, self.t, self.n = name, tensor, n
        self.used = []
        self.peak = 0

    def alloc(self, n):
        n = (n + 15) // 16 * 16
        self.used.sort()
        pos = 0
        for off, sz in self.used:
            if off - pos >= n:
                break
            pos = off + sz
        if pos + n > self.n:
            raise RuntimeError("arena %s full: need %d at %d of %d" % (self.name, n, pos, self.n))
        self.used.append((pos, n))
        self.peak = max(self.peak, pos + n)
        return pos

    def free(self, off):
        self.used = [u for u in self.used if u[0] != off]


class Buf:
    def __init__(self, arena, shape):
        self.arena = arena
        self.shape = list(shape)
        self.n = int(np.prod(shape))
        self.off = arena.alloc(self.n)
        self.inner = self.n // self.shape[0] if len(shape) == 2 else self.n

    def free(self):
        self.arena.free(self.off)

    @property
    def ap(self):
        a = self.arena.t[:, self.off:self.off + self.n]
        if len(self.shape) == 2:
            return a.rearrange("p (a b) -> p a b", a=self.shape[0])
        return a

    def k(self, i=None, j=None):
        if i is None:
            return (self.arena.name, self.off, self.off + self.n)
        if j is None:
            j = i + 1
        return (self.arena.name, self.off + i * self.inner, self.off + j * self.inner)

    def kr(self, lo, hi):
        return (self.arena.name, self.off + lo, self.off + hi)


def build_program():
    nc = bass.Bass("TRN2", target_bir_lowering=False)
    TPS = NT // 128
    NSTEP = SEG // NT

    def din(name, shape):
        return nc.dram_tensor(name, list(shape), F32, kind="ExternalInput").ap()

    NPRE = 3 * SEG
    x_d = din("x_ext", [NPRE + SEG, D])
    mem_d = din("mem_b", [256, D])
    cst_d = din("cst", [128, NCST])
    w2_d = din("w2aug", [17, 512])
    w_in = din("w_in", [D, 7712])
    w_ups = din("w_up_ssd", [D, D])
    w_upg = din("w_up_gla", [D, D])
    w_o = din("w_o", [D, D])
    w_xq = din("w_xq", [D, D])
    w_xkv = din("w_xkv", [D, 2 * D])
    w_xo = din("w_xo", [D, D])
    w_fi = din("w_ffn_in", [D, 2 * DFF])
    w_fo = din("w_ffn_out", [DFF, D])
    out_d = nc.dram_tensor("out", [SEG, D], F32, kind="ExternalOutput").ap()
    dbg_outs = {}

    S = Sched()

    def chain(eng, fns, reads, writes):
        for f in fns:
            S.add(eng, f, reads=list(reads) + list(writes), writes=writes)
    st = ExitStack()
    with st:
        def sb(name, shape, dt):
            return st.enter_context(nc.sbuf_tensor(name, list(shape), dt))

        def pst(name, shape, dt):
            return st.enter_context(nc.psum_tensor(name, list(shape), dt))

        N16 = 70000
        N32 = 18000
        a16 = Arena("A16", sb("A16", [128, N16], BF16), N16)
        a32 = Arena("A32", sb("A32", [128, N32], F32), N32)

        def B16(*shape):
            return Buf(a16, shape)

        def B32(*shape):
            return Buf(a32, shape)

        pa = [pst("pa0", [128, 512], F32), pst("pa1", [128, 512], F32)]
        ptr = pst("ptr", [128, 1024], BF16)
        psm = pst("psm", [128, 512], F32)
        pwa = pst("pwa", [128, 1024], F32)
        pwb = pst("pwb", [128, 1024], F32)
        PWA = ("pwa", 0, 2)
        PWB = ("pwb", 0, 2)
        pa_i = [0]
        pa6_i = [0]
        banks6 = [(pa[0], "pa0"), (pa[1], "pa1"), (pwa[:, 0:512], ("pwa", 0, 1)), (pwa[:, 512:1024], ("pwa", 1, 2)),
                  (pwb[:, 0:512], ("pwb", 0, 1)), (pwb[:, 512:1024], ("pwb", 1, 2))]

        def next_pa():
            i = pa_i[0]
            pa_i[0] ^= 1
            return pa[i], "pa%d" % i

        def next_pa6():
            i = pa6_i[0]
            pa6_i[0] = (i + 1) % 6
            return banks6[i]

        cst = B32(NCST)
        w2a = B32(512)
        idf = B32(128)
        tri = B32(128)
        stri = B32(128)
        trig = B32(128)
        strig = B32(128)
        strif = B32(128)
        strigf = B32(128)
        onesf = B32(128)
        negc = B32(2)
        onesc = B32(2, 128)
        cm = B32(2)
        idb = B16(128)
        ones_mean = B16(128)
        ones1 = B16(128)
        aneg = B32(16)

        S.add("sp", lambda e: [e.dma_start(out=cst.ap, in_=cst_d)], writes=[cst.k()], dma=True, dkey="cst")
        S.add("dve", lambda e: e.memset(w2a.ap, 0.0), writes=[w2a.k()])
        S.add("sp", lambda e: [e.dma_start(out=w2a.ap[0:17, :], in_=w2_d)], writes=[w2a.k()], dma=True, dkey="w2a")

        def mk_c1(e):
            e.memset(idf.ap, 0.0)
            e.memset(tri.ap, 1.0)
            e.memset(stri.ap, 1.0)
            e.memset(onesc.ap, 0.0)
            e.memset(cm.ap, 0.0)
            e.memset(ones_mean.ap, 1.0 / 1024.0)
            e.memset(strif.ap, 1.0)
            e.memset(onesf.ap, 1.0)
            e.memset(negc.ap, -1.0 / 16.0)
            return e.memset(ones1.ap, 1.0)

        def mk_c2(e):
            e.affine_select(out=strif.ap, in_=strif.ap, pattern=[[-1, 128]], compare_op=ALU.is_gt,
                            fill=0.0, base=0, channel_multiplier=1)
            e.affine_select(out=idf.ap, in_=idf.ap, pattern=[[-1, 128]], compare_op=ALU.not_equal,
                            fill=1.0, base=0, channel_multiplier=1)
            e.affine_select(out=tri.ap, in_=tri.ap, pattern=[[1, 128]], compare_op=ALU.is_ge,
                            fill=0.0, base=0, channel_multiplier=-1)
            e.affine_select(out=stri.ap, in_=stri.ap, pattern=[[-1, 128]], compare_op=ALU.is_gt,
                            fill=0.0, base=0, channel_multiplier=1)
            e.memset(onesc.ap[0:64, 0, :], 1.0)
            e.memset(onesc.ap[64:128, 1, :], 1.0)
            e.memset(cm.ap[0:64, 0:1], 1.0)
            return e.memset(cm.ap[64:128, 1:2], 1.0)

        def mk_c3(e):
            e.memset(tri.ap[0:64, 64:128], 0.0)
            return e.memset(stri.ap[64:128, 0:64], 0.0)
        chain("pool", [mk_c1, mk_c2, mk_c3], [], [idf.k(), tri.k(), stri.k(), onesc.k(), cm.k(), ones_mean.k(), ones1.k(),
                                                  strif.k(), onesf.k(), negc.k()])

        def mk_consts2(e):
            e.tensor_copy(out=idb.ap, in_=idf.ap)
            e.tensor_scalar(out=trig.ap, in0=tri.ap, scalar1=-1.0 / 16.0, scalar2=None, op0=ALU.mult)
            e.tensor_scalar(out=strigf.ap, in0=strif.ap, scalar1=-1.0 / 16.0, scalar2=None, op0=ALU.mult)
            return e.tensor_scalar(out=strig.ap, in0=stri.ap, scalar1=-1.0 / 16.0, scalar2=None, op0=ALU.mult)
        S.add("dve", mk_consts2, reads=[idf.k(), tri.k(), stri.k(), strif.k()], writes=[idb.k(), trig.k(), strig.k(), strigf.k()])

        def mk_aneg(e):
            return e.activation(out=aneg.ap, in_=cst.ap[:, C_ALOG:C_ALOG + 16], func=AF.Exp)
        S.add("act", mk_aneg, reads=[cst.k()], writes=[aneg.k()])
        S.add("dve", lambda e: e.tensor_scalar(out=aneg.ap, in0=aneg.ap, scalar1=-1.0, scalar2=None, op0=ALU.mult),
              reads=[aneg.k()], writes=[aneg.k()])

        def cvec(off, n):
            return cst.ap[:, off:off + n]

        NSLOT = 3
        WSZ = 4096
        wslots = [B16(WSZ) for _ in range(NSLOT)]
        wctr = [0]
        SSZ = 1024
        sslots = [B16(SSZ) for _ in range(2)]
        sctr = [0]

        def wload(wd, kc, c0, cw):
            if kc * cw <= SSZ:
                i = NSLOT + sctr[0] % 2
                sctr[0] += 1
                slot = sslots[i - NSLOT]
            else:
                i = wctr[0] % NSLOT
                wctr[0] += 1
                slot = wslots[i]
            assert kc * cw <= WSZ
            dst = slot.ap[:, 0:kc * cw].rearrange("p (k n) -> p k n", k=kc)
            src = wd.rearrange("(k p) n -> p k n", p=128)[:, :, c0:c0 + cw]
            if kc > 8:
                h = kc // 2
                S.add("pool", lambda e: [e.dma_start(out=dst[:, 0:h, :], in_=src[:, 0:h, :]),
                                         e.dma_start(out=dst[:, h:kc, :], in_=src[:, h:kc, :])],
                      writes=[slot.k()], dma=True, dkey="w%d" % i, ndma=2)
            else:
                S.add("pool", lambda e: [e.dma_start(out=dst, in_=src)], writes=[slot.k()], dma=True, dkey="w%d" % i)
            return dst, slot.k()

        def proj_fm(wd, c0, ncols, src, nt, epi, kc=8, blk=512):
            col = 0
            while col < ncols:
                cw = min(blk, ncols - col)
                wap, wkey = wload(wd, kc, c0 + col, cw)
                nch_all = (cw + 127) // 128
                gsz = max(1, 512 // nt)
                for cg in range(0, nch_all, gsz):
                    nch = min(gsz, nch_all - cg)
                    p, pk = next_pa6()
                    pv = p[:, 0:nch * nt].rearrange("p (c n) -> p c n", c=nch)

                    def mm(e, wap=wap, pv=pv, nch=nch, cw=cw, cg=cg):
                        r = None
                        for c in range(nch):
                            cc = cg + c
                            m = min(128, cw - cc * 128)
                            for k in range(kc):
                                r = e.matmul(out=pv[0:m, c, :], lhsT=wap[:, k, cc * 128:cc * 128 + m], rhs=src.ap[:, k, 0:nt],
                                             start=(k == 0), stop=(k == kc - 1))
                        return r
                    S.add("pe", mm, reads=[wkey, src.k()], writes=[pk])
                    epi(col // 128 + cg, nch, pv, pk)
                col += cw

        def proj_tm(wd, c0, ncols, src, ntiles, epi, kc=8):
            col = 0
            while col < ncols:
                cw = min(512, ncols - col)
                wap, wkey = wload(wd, kc, c0 + col, cw)
                for t in range(ntiles):
                    p, pk = next_pa6()

                    def mm(e, wap=wap, p=p, t=t, cw=cw):
                        r = None
                        for k in range(kc):
                            r = e.matmul(out=p[:, 0:cw], lhsT=src.ap[:, k, t * 128:(t + 1) * 128], rhs=wap[:, k, :],
                                         start=(k == 0), stop=(k == kc - 1))
                        return r
                    S.add("pe", mm, reads=[wkey, src.k()], writes=[pk])
                    epi(t, col, cw, p[:, 0:cw], pk)
                col += cw

        hT = B32(8, NT)
        nT = B16(8, NT)
        rstd = B32(NT)
        Sst = B32(1024)
        Sg = B32(1024)
        dtot = B32(20)
        halo3 = B16(12, 3)
        ctpad = [B16(2, 128), B16(2, 128)]
        qtpad = [B16(4, 128), B16(4, 128)]
        a1T = B32(NT)
        KT = B16(8, 256)
        Vx = B16(2, 1024)

        def init_state(e):
            e.memset(Sst.ap, 0.0)
            e.memset(Sg.ap, 0.0)
            e.memset(dtot.ap, 1.0)
            e.memset(ctpad[0].ap, 0.0)
            e.memset(ctpad[1].ap, 0.0)
            e.memset(qtpad[0].ap, 0.0)
            e.memset(qtpad[1].ap, 0.0)
            e.memset(halo3.ap, 0.0)
            return e.memset(a1T.ap, 0.0)
        S.add("pool", init_state, writes=[Sst.k(), Sg.k(), dtot.k(), ctpad[0].k(), ctpad[1].k(), qtpad[0].k(),
                                          qtpad[1].k(), a1T.k(), halo3.k()])
        S.add("dve", lambda e: e.memset(a1T.ap[0:32, :], 1.0), reads=[a1T.k()], writes=[a1T.k()])

        xs_i = [0]

        def load_xT(row0, nt, xreads=()):
            xsb = [B32(1024), B32(1024)]
            for t in range(nt // 128):
                j = xs_i[0]
                xs_i[0] ^= 1
                xs = xsb[j]
                pw, pwk = (pwa, PWA) if j == 0 else (pwb, PWB)
                r0 = row0 + t * 128
                S.add("sp", lambda e, xs=xs, r0=r0: [e.dma_start(out=xs.ap, in_=x_d[r0:r0 + 128, :])],
                      writes=[xs.k()], reads=list(xreads), dma=True, dkey="xs%d" % xs.off)

                def tr(e, xs=xs, pw=pw):
                    r = None
                    for c in range(8):
                        r = e.transpose(out=pw[:, c * 128:(c + 1) * 128], in_=xs.ap[:, c * 128:(c + 1) * 128], identity=idf.ap)
                    return r
                S.add("pe", tr, reads=[xs.k(), idf.k()], writes=[pwk])
                S.add("act", lambda e, t=t, pw=pw: e.activation(out=hT.ap[:, :, t * 128:(t + 1) * 128],
                                                               in_=pw[:, :].rearrange("p (c n) -> p c n", c=8), func=AF.Copy),
                      reads=[pwk], writes=[hT.k()])
            xsb[0].free(); xsb[1].free()

        def norm_fm(goff, nt, stats_only=False):
            sqb = B16(8, NT)
            S.add("act", lambda e: e.activation(out=sqb.ap[:, :, 0:nt], in_=hT.ap[:, :, 0:nt], func=AF.Square),
                  reads=[hT.k()], writes=[sqb.k()])

            def mm(e):
                r = None
                for k in range(8):
                    r = e.matmul(out=psm[:, 0:nt], lhsT=ones_mean.ap, rhs=sqb.ap[:, k, 0:nt], start=(k == 0), stop=(k == 7))
                return r
            S.add("pe", mm, reads=[sqb.k(), ones_mean.k()], writes=["psm"])
            sqb.free()
            S.add("act", lambda e: e.activation(out=rstd.ap[:, 0:nt], in_=psm[:, 0:nt], func=AF.Ln, bias=EPS), reads=["psm"], writes=[rstd.k()])
            S.add("act", lambda e: e.activation(out=rstd.ap[:, 0:nt], in_=rstd.ap[:, 0:nt], func=AF.Exp, scale=-0.5), reads=[rstd.k()], writes=[rstd.k()])
            if stats_only:
                return
            tgt = nT

            def nrm(e):
                r = None
                for k in range(8):
                    r = e.scalar_tensor_tensor(out=tgt.ap[:, k, 0:nt], in0=hT.ap[:, k, 0:nt], scalar=cst.ap[:, goff + k:goff + k + 1],
                                               in1=rstd.ap[:, 0:nt], op0=ALU.mult, op1=ALU.mult)
                return r
            S.add("dve", nrm, reads=[hT.k(), rstd.k(), cst.k()], writes=[tgt.k()])

        def resid_epi(c0, nch, pv, pk):
            S.add("dve", lambda e: e.tensor_tensor(out=hT.ap[:, c0:c0 + nch, :], in0=hT.ap[:, c0:c0 + nch, :], in1=pv, op=ALU.add),
                  reads=[pk, hT.k(c0, c0 + nch)], writes=[hT.k(c0, c0 + nch)])

        def dbg_dump(name, buf, n, dt=F32):
            if not DBG:
                return
            d = nc.dram_tensor("dbg_" + name, [128, n], dt, kind="ExternalOutput").ap()
            dbg_outs[name] = d
            flat = buf.arena.t[:, buf.off:buf.off + n]
            S.add("sp", lambda e: [e.dma_start(out=d, in_=flat)], reads=[buf.k()], dma=True, dkey="dbg_" + name, name="OUTdbg")

        def interleave(gens):
            gens = list(gens)
            while gens:
                for g in list(gens):
                    try:
                        next(g)
                    except StopIteration:
                        gens.remove(g)

        def step(row0, nt, mode, orow0=None, first_dbg=False, xreads=()):
            full = mode == "p2"
            tps = nt // 128
            load_xT(row0, nt, xreads)
            norm_fm(C_GMIX, nt)
            if first_dbg:
                dbg_dump("nT", nT, 8 * NT, BF16)
            dtb = B32(tps, 16)
            ab = B32(tps, 16)

            def dt_epi(t, col, cw, p, pk):
                tmp = B32(16)
                S.add("dve", lambda e: e.tensor_tensor(out=tmp.ap, in0=p, in1=cvec(C_DTB, 16), op=ALU.add),
                      reads=[pk, cst.k()], writes=[tmp.k()])

                S.add("act", lambda e: e.activation(out=tmp.ap, in_=tmp.ap, func=AF.Exp), reads=[tmp.k()], writes=[tmp.k()])
                S.add("act", lambda e: e.activation(out=dtb.ap[:, t, :], in_=tmp.ap, func=AF.Ln, bias=1.0), reads=[tmp.k()], writes=[dtb.k(t)])
                S.add("dve", lambda e: e.tensor_tensor(out=ab.ap[:, t, :], in0=dtb.ap[:, t, :], in1=aneg.ap, op=ALU.mult),
                      reads=[dtb.k(t), aneg.k()], writes=[ab.k(t)])
                tmp.free()
            proj_tm(w_in, O_DT, 16, nT, tps, dt_epi)
            xbr = B16(12, nt + 3)
            xsT = B16(8, nt)
            BT = B16(2, nt)
            CT = B16(2, nt)
            S.add("dve", lambda e: e.tensor_copy(out=xbr.ap[:, :, 0:3], in_=halo3.ap), reads=[halo3.k()], writes=[xbr.k()])

            def conv_chunk(c):
                cacc = B32(nt)
                S.add("act", lambda e: e.activation(out=cacc.ap, in_=xbr.ap[:, c, 0:nt], func=AF.Copy,
                                                    scale=cst.ap[:, C_CONVW + 4 * c:C_CONVW + 4 * c + 1]),
                      reads=[xbr.k(c), cst.k()], writes=[cacc.k()])

                def tap(k):
                    return lambda e: e.scalar_tensor_tensor(out=cacc.ap, in0=xbr.ap[:, c, k:k + nt],
                                                            scalar=cst.ap[:, C_CONVW + 4 * c + k:C_CONVW + 4 * c + k + 1],
                                                            in1=cacc.ap, op0=ALU.mult, op1=ALU.add)
                chain("dve", [tap(1), tap(2), tap(3)], [xbr.k(c), cst.k()], [cacc.k()])
                dstb, di = (xsT, c) if c < 8 else ((BT, c - 8) if c < 10 else (CT, c - 10))
                S.add("act", lambda e: e.activation(out=dstb.ap[:, di, :], in_=cacc.ap, func=AF.Silu, bias=cst.ap[:, C_CONVB + c:C_CONVB + c + 1]),
                      reads=[cacc.k(), cst.k()], writes=[dstb.k(di)])
                cacc.free()

            def xbc_epi(c0, nch, pv, pk):
                S.add("act", lambda e: e.activation(out=xbr.ap[:, c0:c0 + nch, 3:3 + nt], in_=pv, func=AF.Copy),
                      reads=[pk], writes=[xbr.k(c0, c0 + nch)])
                for c in range(c0, c0 + nch):
                    if full or c < 10:
                        conv_chunk(c)
            proj_fm(w_in, O_XBC, 1536, nT, nt, xbc_epi)
            S.add("dve", lambda e: e.tensor_copy(out=halo3.ap, in_=xbr.ap[:, :, nt:nt + 3]), reads=[xbr.k()], writes=[halo3.k()])
            xbr.free()
            zs = None
            if full:
                zs = B16(tps, 1024)

                def z_epi(t, col, cw, p, pk):
                    S.add("act", lambda e: e.activation(out=zs.ap[:, t, col:col + cw], in_=p, func=AF.Silu),
                          reads=[pk], writes=[zs.kr(t * 1024 + col, t * 1024 + col + cw)])
                proj_tm(w_in, O_Z, 1024, nT, tps, z_epi)
            qT = B16(4, nt) if full else None
            kT = B16(4, nt) if full else None
            ktm = B16(tps, 512)
            vtm = B16(tps, 1024)
            rs = B16(tps, 1024) if full else None
            if full:
                def q_epi(c0, nch, pv, pk):
                    S.add("act", lambda e: e.activation(out=qT.ap[:, c0:c0 + nch, :], in_=pv, func=AF.Copy, scale=float(128 ** -0.5)),
                          reads=[pk], writes=[qT.k(c0, c0 + nch)])
                proj_fm(w_in, O_Q, 512, nT, nt, q_epi)
            wap, wkey = wload(w_in, 8, O_K, 512)
            if full:
                gsz = max(1, 512 // nt)
                for cg in range(0, 4, gsz):
                    p, pk = next_pa6()
                    pv = p[:, 0:gsz * nt].rearrange("p (c n) -> p c n", c=gsz)

                    def mmk(e, wap=wap, pv=pv, cg=cg, gsz=gsz):
                        r = None
                        for c in range(gsz):
                            for k in range(8):
                                r = e.matmul(out=pv[:, c, :], lhsT=wap[:, k, (cg + c) * 128:(cg + c + 1) * 128], rhs=nT.ap[:, k, 0:nt],
                                             start=(k == 0), stop=(k == 7))
                        return r
                    S.add("pe", mmk, reads=[wkey, nT.k()], writes=[pk])
                    S.add("act", lambda e, pv=pv, cg=cg, gsz=gsz: e.activation(out=kT.ap[:, cg:cg + gsz, :], in_=pv, func=AF.Copy),
                          reads=[pk], writes=[kT.k(cg, cg + gsz)])
            for t in range(tps):
                p, pk = next_pa6()

                def mmk2(e, wap=wap, p=p, t=t):
                    r = None
                    for k in range(8):
                        r = e.matmul(out=p[:, 0:512], lhsT=nT.ap[:, k, t * 128:(t + 1) * 128], rhs=wap[:, k, :], start=(k == 0), stop=(k == 7))
                    return r
                S.add("pe", mmk2, reads=[wkey, nT.k()], writes=[pk])
                S.add("act", lambda e, p=p, t=t: e.activation(out=ktm.ap[:, t, :], in_=p[:, 0:512], func=AF.Copy), reads=[pk], writes=[ktm.k(t)])

            def a1_epi(c0, nch, pv, pk):
                S.add("act", lambda e: e.activation(out=a1T.ap[0:16, 0:nt], in_=pv[0:16, 0, :], func=AF.Copy), reads=[pk], writes=[a1T.k()])
            proj_fm(w_in, O_A1, 128, nT, nt, a1_epi)

            def v_epi(t, col, cw, p, pk):
                S.add("act", lambda e: e.activation(out=vtm.ap[:, t, col:col + cw], in_=p, func=AF.Copy),
                      reads=[pk], writes=[vtm.kr(t * 1024 + col, t * 1024 + col + cw)])
            proj_tm(w_in, O_V, 1024, nT, tps, v_epi)
            if full:
                def r_epi(t, col, cw, p, pk):
                    S.add("act", lambda e: e.activation(out=rs.ap[:, t, col:col + cw], in_=p, func=AF.Silu),
                          reads=[pk], writes=[rs.kr(t * 1024 + col, t * 1024 + col + cw)])
                proj_tm(w_in, O_R, 1024, nT, tps, r_epi)
            ynT = B16(8, nt) if full else None
            onT = B16(8, nt) if full else None
            for t in range(tps):
                interleave([ssd_tile(t, xsT, BT, CT, dtb, ab, zs, ynT, full, False),
                            gla_tile(t, qT, kT, ktm, vtm, rs, onT, full, False)])
            xsT.free(); BT.free(); CT.free(); dtb.free(); ab.free()
            if zs is not None:
                zs.free()
            ktm.free(); vtm.free()
            if not full:
                return
            qT.free(); kT.free(); rs.free()
            if first_dbg:
                dbg_dump("onT", onT, 8 * NT, BF16)
            gsT = B16(8, nt)
            ggT = B16(8, nt)
            mT = B16(8, nt)

            def gs_epi(c0, nch, pv, pk):
                S.add("act", lambda e: e.activation(out=gsT.ap[:, c0:c0 + nch, :], in_=pv, func=AF.Sigmoid), reads=[pk], writes=[gsT.k(c0, c0 + nch)])

            def gg_epi(c0, nch, pv, pk):
                S.add("act", lambda e: e.activation(out=ggT.ap[:, c0:c0 + nch, :], in_=pv, func=AF.Sigmoid), reads=[pk], writes=[ggT.k(c0, c0 + nch)])
            proj_fm(w_in, O_GS, 1024, nT, nt, gs_epi)
            proj_fm(w_in, O_GG, 1024, nT, nt, gg_epi)

            def ups_epi(c0, nch, pv, pk):
                S.add("dve", lambda e: e.tensor_tensor(out=gsT.ap[:, c0:c0 + nch, :], in0=gsT.ap[:, c0:c0 + nch, :], in1=pv, op=ALU.mult),
                      reads=[pk, gsT.k(c0, c0 + nch)], writes=[gsT.k(c0, c0 + nch)])
            proj_fm(w_ups, 0, 1024, ynT, nt, ups_epi)

            def upg_epi(c0, nch, pv, pk):
                S.add("dve", lambda e: e.tensor_tensor(out=ggT.ap[:, c0:c0 + nch, :], in0=ggT.ap[:, c0:c0 + nch, :], in1=pv, op=ALU.mult),
                      reads=[pk, ggT.k(c0, c0 + nch)], writes=[ggT.k(c0, c0 + nch)])
                S.add("dve", lambda e: e.tensor_tensor(out=mT.ap[:, c0:c0 + nch, :], in0=ggT.ap[:, c0:c0 + nch, :], in1=gsT.ap[:, c0:c0 + nch, :], op=ALU.add),
                      reads=[ggT.k(c0, c0 + nch), gsT.k(c0, c0 + nch)], writes=[mT.k(c0, c0 + nch)])
            proj_fm(w_upg, 0, 1024, onT, nt, upg_epi)
            ynT.free(); onT.free(); gsT.free(); ggT.free()
            proj_fm(w_o, 0, 1024, mT, nt, resid_epi)
            mT.free()
            if first_dbg:
                dbg_dump("h1", hT, 8 * NT)
            norm_fm(C_GXA, nt)
            qx = B16(8, nt)
            ox = B16(8, nt)

            def qx_epi(c0, nch, pv, pk):
                S.add("act", lambda e: e.activation(out=qx.ap[:, c0:c0 + nch, :], in_=pv, func=AF.Copy, scale=1.0 / 16.0),
                      reads=[pk], writes=[qx.k(c0, c0 + nch)])
            proj_fm(w_xq, 0, 1024, nT, nt, qx_epi)
            for hd in range(4):
                ET = B16(2, nt)
                for mc in range(2):
                    p, pk = next_pa6()

                    def sc(e, p=p, hd=hd, mc=mc):
                        r = None
                        for dc in range(2):
                            r = e.matmul(out=p[:, 0:nt], lhsT=KT.ap[:, hd * 2 + dc, mc * 128:(mc + 1) * 128], rhs=qx.ap[:, hd * 2 + dc, :],
                                         start=(dc == 0), stop=(dc == 1))
                        return r
                    S.add("pe", sc, reads=[KT.k(), qx.k(hd * 2, hd * 2 + 2)], writes=[pk])
                    S.add("act", lambda e, p=p, mc=mc, ET=ET: e.activation(out=ET.ap[:, mc, :], in_=p[:, 0:nt], func=AF.Exp),
                          reads=[pk], writes=[ET.k(mc)])

                def den(e, ET=ET):
                    e.matmul(out=psm[:, 0:nt], lhsT=ones1.ap, rhs=ET.ap[:, 0, :], start=True, stop=False)
                    return e.matmul(out=psm[:, 0:nt], lhsT=ones1.ap, rhs=ET.ap[:, 1, :], start=False, stop=True)
                S.add("pe", den, reads=[ET.k(), ones1.k()], writes=["psm"])
                rden = B32(nt)
                S.add("dve", lambda e, rden=rden: e.reciprocal(out=rden.ap, in_=psm[:, 0:nt]), reads=["psm"], writes=[rden.k()])
                for dc in range(2):
                    p, pk = next_pa6()

                    def pvm(e, p=p, hd=hd, dc=dc, ET=ET):
                        r = None
                        for mc in range(2):
                            r = e.matmul(out=p[:, 0:nt], lhsT=Vx.ap[:, mc, hd * 256 + dc * 128:hd * 256 + dc * 128 + 128], rhs=ET.ap[:, mc, :],
                                         start=(mc == 0), stop=(mc == 1))
                        return r
                    S.add("pe", pvm, reads=[Vx.k(), ET.k()], writes=[pk])
                    S.add("dve", lambda e, p=p, hd=hd, dc=dc, rden=rden: e.tensor_tensor(out=ox.ap[:, hd * 2 + dc, :], in0=p[:, 0:nt], in1=rden.ap, op=ALU.mult),
                          reads=[pk, rden.k()], writes=[ox.k(hd * 2 + dc)])
                ET.free(); rden.free()
            qx.free()
            proj_fm(w_xo, 0, 1024, ox, nt, resid_epi)
            ox.free()
            if first_dbg:
                dbg_dump("h2", hT, 8 * NT)
            norm_fm(C_GFFN, nt)
            aT = B16(22, nt)
            col = 0
            while col < DFF:
                cw = min(512, DFF - col)

                def g_epi(c0, nch, pv, pk, col=col):
                    cc = col // 128 + c0
                    S.add("act", lambda e: e.activation(out=aT.ap[:, cc:cc + nch, :], in_=pv, func=AF.Silu), reads=[pk], writes=[aT.k(cc, cc + nch)])

                def u_epi(c0, nch, pv, pk, col=col):
                    cc = col // 128 + c0
                    S.add("dve", lambda e: e.tensor_tensor(out=aT.ap[:, cc:cc + nch, :], in0=aT.ap[:, cc:cc + nch, :], in1=pv, op=ALU.mult),
                          reads=[pk, aT.k(cc, cc + nch)], writes=[aT.k(cc, cc + nch)])
                proj_fm(w_fi, col, cw, nT, nt, g_epi)
                proj_fm(w_fi, DFF + col, cw, nT, nt, u_epi)
                col += cw
            proj_fm(w_fo, 0, 1024, aT, nt, resid_epi, kc=22, blk=128)
            aT.free()
            if first_dbg:
                dbg_dump("h3", hT, 8 * NT)
            norm_fm(C_GFIN, nt, stats_only=True)
            for t in range(tps):
                of = B32(8, 128)

                def nrm_t(e, t=t, of=of):
                    r = None
                    for k in range(8):
                        r = e.scalar_tensor_tensor(out=of.ap[:, k, :], in0=hT.ap[:, k, t * 128:(t + 1) * 128], scalar=cst.ap[:, C_GFIN + k:C_GFIN + k + 1],
                                                   in1=rstd.ap[:, t * 128:(t + 1) * 128], op0=ALU.mult, op1=ALU.mult)
                    return r
                S.add("dve", nrm_t, reads=[hT.k(), rstd.k(), cst.k()], writes=[of.k()])

                def tr(e, t=t, of=of):
                    r = None
                    for c in range(8):
                        r = e.transpose(out=pwa[:, c * 128:(c + 1) * 128], in_=of.ap[:, c, :], identity=idf.ap)
                    return r
                S.add("pe", tr, reads=[of.k(), idf.k()], writes=[PWA])
                of.free()
                og = B32(1024)
                S.add("act", lambda e, og=og: e.activation(out=og.ap, in_=pwa[:, :], func=AF.Copy), reads=[PWA], writes=[og.k()])
                r0 = orow0 + t * 128
                S.add("sp", lambda e, og=og, r0=r0: [e.dma_start(out=out_d[r0:r0 + 128, :], in_=og.ap)], reads=[og.k()],
                      dma=True, dkey="og%d" % og.off, name="OUT")
                og.free()

        def ssd_tile_p1(t, xsT, BT, dtb, ab):
            cols = slice(t * 128, (t + 1) * 128)
            a_t = ab.ap[:, t, :]
            dt_t = dtb.ap[:, t, :]
            Btm = B16(2, 128)
            Xd = B16(1024)
            ex = B32(32)
            dd = B32(16)

            def trx(e):
                r = None
                for c in range(8):
                    r = e.transpose(out=ptr[:, c * 128:(c + 1) * 128], in_=xsT.ap[:, c, cols], identity=idb.ap)
                return r

            def smalls(e):
                e.matmul(out=psm[:, 0:16], lhsT=strif.ap, rhs=a_t, start=True, stop=True)
                return e.matmul(out=psm[:, 16:32], lhsT=onesf.ap, rhs=a_t, start=True, stop=True)
            S.add("pe", smalls, reads=[ab.k(t), strif.k(), onesf.k()], writes=["psm"])
            S.add("act", lambda e: e.activation(out=ex.ap, in_=psm[:, 0:32], func=AF.Exp), reads=["psm"], writes=[ex.k()])
            yield
            S.add("dve", lambda e: e.tensor_tensor(out=dd.ap, in0=ex.ap[:, 0:16], in1=dt_t, op=ALU.mult), reads=[ex.k(), dtb.k(t)], writes=[dd.k()])
            S.add("pe", trx, reads=[xsT.k(), idb.k()], writes=["ptr"])
            yield
            S.add("dve", lambda e: e.tensor_tensor(out=Xd.ap.rearrange("p (h q) -> p h q", h=16),
                                                   in0=ptr[:, :].rearrange("p (h q) -> p h q", h=16),
                                                   in1=dd.ap.unsqueeze(2).to_broadcast([128, 16, 64]), op=ALU.mult),
                  reads=["ptr", dd.k()], writes=[Xd.k()])
            yield

            def trb(e):
                e.transpose(out=ptr[:, 0:128], in_=BT.ap[:, 0, cols], identity=idb.ap)
                return e.transpose(out=ptr[:, 128:256], in_=BT.ap[:, 1, cols], identity=idb.ap)
            S.add("pe", trb, reads=[BT.k(), idb.k()], writes=["ptr"])
            S.add("act", lambda e: e.activation(out=Btm.ap, in_=ptr[:, 0:256].rearrange("p (g n) -> p g n", g=2), func=AF.Copy),
                  reads=["ptr"], writes=[Btm.k()])
            yield

            def sloc(e):
                e.matmul(out=pwb[:, 0:512], lhsT=Btm.ap[:, 0, :], rhs=Xd.ap[:, 0:512], start=True, stop=True)
                return e.matmul(out=pwb[:, 512:1024], lhsT=Btm.ap[:, 1, :], rhs=Xd.ap[:, 512:1024], start=True, stop=True)
            S.add("pe", sloc, reads=[Btm.k(), Xd.k()], writes=[PWB])
            S.add("dve", lambda e: e.tensor_tensor(out=Sst.ap.rearrange("p (h q) -> p h q", h=16), in0=Sst.ap.rearrange("p (h q) -> p h q", h=16),
                                                   in1=ex.ap[:, 16:32].unsqueeze(2).to_broadcast([128, 16, 64]), op=ALU.mult),
                  reads=[ex.k(), Sst.k()], writes=[Sst.k()])
            yield
            S.add("dve", lambda e: e.tensor_tensor(out=Sst.ap, in0=Sst.ap, in1=pwb[:, :], op=ALU.add), reads=[PWB, Sst.k()], writes=[Sst.k()])
            yield
            Btm.free(); Xd.free(); ex.free(); dd.free()

        def ssd_tile(t, xsT, BT, CT, dtb, ab, zs, ynT, full, dbg):
            cols = slice(t * 128, (t + 1) * 128)
            a_t = ab.ap[:, t, :]
            dt_t = dtb.ap[:, t, :]
            if not full:
                yield from ssd_tile_p1(t, xsT, BT, dtb, ab)
                return
            Xtm = B16(1024)
            xstm = B16(1024) if full else None
            Btm = B16(2, 128)

            def trx(e):
                r = None
                for c in range(8):
                    r = e.transpose(out=ptr[:, c * 128:(c + 1) * 128], in_=xsT.ap[:, c, cols], identity=idb.ap)
                return r
            S.add("pe", trx, reads=[xsT.k(), idb.k()], writes=["ptr"])
            yield
            S.add("dve", lambda e: e.tensor_tensor(out=Xtm.ap.rearrange("p (h q) -> p h q", h=16),
                                                   in0=ptr[:, :].rearrange("p (h q) -> p h q", h=16),
                                                   in1=dt_t.unsqueeze(2).to_broadcast([128, 16, 64]), op=ALU.mult),
                  reads=["ptr", dtb.k(t)], writes=[Xtm.k()])
            yield
            if full:
                S.add("dve", lambda e: e.tensor_tensor(out=xstm.ap.rearrange("p (h q) -> p h q", h=16),
                                                       in0=ptr[:, :].rearrange("p (h q) -> p h q", h=16),
                                                       in1=cvec(C_DSK, 16).unsqueeze(2).to_broadcast([128, 16, 64]), op=ALU.mult),
                      reads=["ptr", cst.k()], writes=[xstm.k()])
                yield

            def trb(e):
                e.transpose(out=ptr[:, 0:128], in_=BT.ap[:, 0, cols], identity=idb.ap)
                return e.transpose(out=ptr[:, 128:256], in_=BT.ap[:, 1, cols], identity=idb.ap)
            S.add("pe", trb, reads=[BT.k(), idb.k()], writes=["ptr"])
            yield
            S.add("act", lambda e: e.activation(out=Btm.ap, in_=ptr[:, 0:256].rearrange("p (g n) -> p g n", g=2), func=AF.Copy),
                  reads=["ptr"], writes=[Btm.k()])
            yield
            def smalls(e):
                e.matmul(out=psm[:, 0:16], lhsT=tri.ap, rhs=a_t, start=True, stop=True)
                e.matmul(out=psm[:, 16:32], lhsT=stri.ap, rhs=a_t, start=True, stop=True)
                e.matmul(out=psm[:, 32:48], lhsT=onesc.ap[:, 0, :], rhs=a_t, start=True, stop=True)
                return e.matmul(out=psm[:, 48:64], lhsT=onesc.ap[:, 1, :], rhs=a_t, start=True, stop=True)
            S.add("pe", smalls, reads=[ab.k(t), tri.k(), stri.k(), onesc.k()], writes=["psm"])
            yield
            ex = B32(64)
            S.add("act", lambda e: e.activation(out=ex.ap, in_=psm[:, 0:64], func=AF.Exp), reads=["psm"], writes=[ex.k()])
            yield
            eacs = ex.ap[:, 0:16]
            if dbg:
                dbg_dump("Xtm", Xtm, 1024, BF16)
                dbg_dump("ex", ex, 64)
                dbg_dump("dtb", dtb, 16)
            ds = B32(2, 16)

            def mkds(e):
                e.tensor_scalar(out=ds.ap[:, 0, :], in0=ex.ap[:, 16:32], scalar1=cm.ap[:, 0:1], scalar2=None, op0=ALU.mult)
                return e.tensor_scalar(out=ds.ap[:, 1, :], in0=ex.ap[:, 16:32], scalar1=cm.ap[:, 1:2], scalar2=None, op0=ALU.mult)
            S.add("dve", mkds, reads=[ex.k(), cm.k()], writes=[ds.k()])
            yield
            Xd = [B16(1024), B16(1024)]
            for c in range(2):
                S.add("dve", lambda e, c=c: e.tensor_tensor(out=Xd[c].ap.rearrange("p (h q) -> p h q", h=16),
                                                            in0=Xtm.ap.rearrange("p (h q) -> p h q", h=16),
                                                            in1=ds.ap[:, c, :].unsqueeze(2).to_broadcast([128, 16, 64]), op=ALU.mult),
                      reads=[Xtm.k(), ds.k()], writes=[Xd[c].k()])
                yield
            MT = None
            if full:
                MT = B16(16, 128)
                for g in range(2):
                    rhs_all = B32(8, 128)
                    S.add("dve", lambda e, g=g, rhs_all=rhs_all: e.tensor_tensor(
                        out=rhs_all.ap, in0=tri.ap.unsqueeze(1).to_broadcast([128, 8, 128]),
                        in1=a_t[:, g * 8:(g + 1) * 8].unsqueeze(2).to_broadcast([128, 8, 128]), op=ALU.mult),
                        reads=[tri.k(), ab.k(t)], writes=[rhs_all.k()])
                    yield

                    def dmm(e, rhs_all=rhs_all):
                        e.matmul(out=pwa[:, 0:512], lhsT=stri.ap, rhs=rhs_all.ap[:, 0:4, :], start=True, stop=True)
                        return e.matmul(out=pwa[:, 512:1024], lhsT=stri.ap, rhs=rhs_all.ap[:, 4:8, :], start=True, stop=True)
                    S.add("pe", dmm, reads=[rhs_all.k(), stri.k()], writes=[PWA])
                    yield
                    E = B16(8, 128)

                    def eexp(e, E=E):
                        e.activation(out=E.ap[:, 0:4, :], in_=pwa[:, 0:512].rearrange("p (h l) -> p h l", h=4), func=AF.Exp)
                        return e.activation(out=E.ap[:, 4:8, :], in_=pwa[:, 512:1024].rearrange("p (h l) -> p h l", h=4), func=AF.Exp)
                    S.add("act", eexp, reads=[PWA], writes=[E.k()])
                    yield
                    S.add("pe", lambda e, g=g: e.matmul(out=psm[:, 128 + g * 128:256 + g * 128], lhsT=BT.ap[:, g, cols], rhs=CT.ap[:, g, cols],
                                                        start=True, stop=True), reads=[BT.k(), CT.k()], writes=["psm"])
                    yield
                    cbm = B32(128)
                    S.add("dve", lambda e, g=g, cbm=cbm: e.tensor_tensor(out=cbm.ap, in0=psm[:, 128 + g * 128:256 + g * 128], in1=tri.ap, op=ALU.mult),
                          reads=["psm", tri.k()], writes=[cbm.k()])
                    yield
                    S.add("dve", lambda e, g=g, cbm=cbm, E=E: e.tensor_tensor(out=MT.ap[:, g * 8:(g + 1) * 8, :], in0=E.ap,
                                                                               in1=cbm.ap.unsqueeze(1).to_broadcast([128, 8, 128]), op=ALU.mult),
                          reads=[E.k(), cbm.k()], writes=[MT.k(g * 8, (g + 1) * 8)])
                    yield
                    rhs_all.free(); E.free(); cbm.free()
                for c in range(2):
                    S.add("act", lambda e, c=c: e.activation(out=ctpad[c].ap[:, :, c * 64:(c + 1) * 64],
                                                             in_=CT.ap[:, :, t * 128 + c * 64:t * 128 + (c + 1) * 64], func=AF.Copy),
                          reads=[CT.k()], writes=[ctpad[c].k()])
                    yield
            Sbf = [B16(1024), B16(1024)] if full else None
            for c in range(2):
                def sloc(e, c=c):
                    e.matmul(out=pwb[:, 0:512], lhsT=Btm.ap[:, 0, :], rhs=Xd[c].ap[:, 0:512], start=True, stop=True)
                    return e.matmul(out=pwb[:, 512:1024], lhsT=Btm.ap[:, 1, :], rhs=Xd[c].ap[:, 512:1024], start=True, stop=True)
                S.add("pe", sloc, reads=[Btm.k(), Xd[c].k()], writes=[PWB])
                yield
                if full:
                    S.add("act", lambda e, c=c: e.activation(out=Sbf[c].ap, in_=Sst.ap, func=AF.Copy), reads=[Sst.k()], writes=[Sbf[c].k()])
                    yield

                def supd(e, c=c):
                    e.tensor_tensor(out=dtot.ap[:, 0:16], in0=dtot.ap[:, 0:16], in1=ex.ap[:, 32 + 16 * c:48 + 16 * c], op=ALU.mult)
                    return e.tensor_tensor(out=Sst.ap.rearrange("p (h q) -> p h q", h=16), in0=Sst.ap.rearrange("p (h q) -> p h q", h=16),
                                           in1=ex.ap[:, 32 + 16 * c:48 + 16 * c].unsqueeze(2).to_broadcast([128, 16, 64]), op=ALU.mult)
                S.add("dve", supd, reads=[ex.k(), Sst.k(), dtot.kr(0, 16)], writes=[Sst.k(), dtot.kr(0, 16)])
                yield
                S.add("dve", lambda e: e.tensor_tensor(out=Sst.ap, in0=Sst.ap, in1=pwb[:, :], op=ALU.add), reads=[PWB, Sst.k()], writes=[Sst.k()])
                yield
            Xd[0].free(); Xd[1].free(); Btm.free(); ds.free()
            if not full:
                Xtm.free(); ex.free()
                return
            def ymm(e):
                r = None
                for b in range(2):
                    e.matmul(out=pwa[:, b * 512:(b + 1) * 512], lhsT=idb.ap, rhs=xstm.ap[:, b * 512:(b + 1) * 512], start=True, stop=False)
                    for hh in range(8):
                        h = b * 8 + hh
                        r = e.matmul(out=pwa[:, h * 64:(h + 1) * 64], lhsT=MT.ap[:, h, :], rhs=Xtm.ap[:, h * 64:(h + 1) * 64],
                                     start=False, stop=(hh == 7))
                return r
            S.add("pe", ymm, reads=[idb.k(), xstm.k(), MT.k(), Xtm.k()], writes=[PWA])
            yield

            def yoff(e):
                r = None
                for g in range(2):
                    for c in range(2):
                        r = e.matmul(out=pwb[:, g * 512:(g + 1) * 512], lhsT=ctpad[c].ap[:, g, :], rhs=Sbf[c].ap[:, g * 512:(g + 1) * 512],
                                     start=(c == 0), stop=(c == 1))
                return r
            S.add("pe", yoff, reads=[ctpad[0].k(), ctpad[1].k(), Sbf[0].k(), Sbf[1].k()], writes=[PWB])
            yield
            yt = B32(1024)

            S.add("dve", lambda e: e.tensor_tensor(out=yt.ap.rearrange("p (h q) -> p h q", h=16), in0=pwb[:, :].rearrange("p (h q) -> p h q", h=16),
                                                   in1=eacs.unsqueeze(2).to_broadcast([128, 16, 64]), op=ALU.mult),
                  reads=[PWB, ex.k()], writes=[yt.k()])
            yield
            S.add("dve", lambda e: e.tensor_tensor(out=yt.ap, in0=yt.ap, in1=pwa[:, :], op=ALU.add), reads=[PWA, yt.k()], writes=[yt.k()])
            yield
            S.add("dve", lambda e: e.tensor_tensor(out=yt.ap, in0=yt.ap, in1=zs.ap[:, t, :], op=ALU.mult), reads=[yt.k(), zs.k(t)], writes=[yt.k()])
            yield
            if dbg:
                dbg_dump("yt", yt, 1024)
                dbg_dump("MT", MT, 2048, BF16)
                dbg_dump("Sbf1", Sbf[1], 1024, BF16)
            Xtm.free(); xstm.free(); MT.free(); Sbf[0].free(); Sbf[1].free(); ex.free()
            junk = B16(1024)
            ss = B32(2)

            def ysq(e):
                e.activation(out=junk.ap[:, 0:512], in_=yt.ap[:, 0:512], func=AF.Square, accum_out=ss.ap[:, 0:1])
                return e.activation(out=junk.ap[:, 512:1024], in_=yt.ap[:, 512:1024], func=AF.Square, accum_out=ss.ap[:, 1:2])
            S.add("act", ysq, reads=[yt.k()], writes=[junk.k(), ss.k()])
            yield
            chain("act", [lambda e: e.activation(out=ss.ap, in_=ss.ap, func=AF.Ln, bias=EPS, scale=1.0 / 512.0),
                          lambda e: e.activation(out=ss.ap, in_=ss.ap, func=AF.Exp, scale=-0.5)], [], [ss.k()])
            yield
            yn = B16(1024)

            def ynorm(e):
                e.scalar_tensor_tensor(out=yn.ap[:, 0:512], in0=yt.ap[:, 0:512], scalar=ss.ap[:, 0:1], in1=cvec(C_SSDN, 512), op0=ALU.mult, op1=ALU.mult)
                return e.scalar_tensor_tensor(out=yn.ap[:, 512:1024], in0=yt.ap[:, 512:1024], scalar=ss.ap[:, 1:2], in1=cvec(C_SSDN + 512, 512),
                                              op0=ALU.mult, op1=ALU.mult)
            S.add("dve", ynorm, reads=[yt.k(), ss.k(), cst.k()], writes=[yn.k()])
            yield

            def try_(e):
                r = None
                for c in range(8):
                    r = e.transpose(out=ptr[:, c * 128:(c + 1) * 128], in_=yn.ap[:, c * 128:(c + 1) * 128], identity=idb.ap)
                return r
            S.add("pe", try_, reads=[yn.k(), idb.k()], writes=["ptr"])
            yield
            S.add("act", lambda e: e.activation(out=ynT.ap[:, :, cols], in_=ptr[:, :].rearrange("p (c n) -> p c n", c=8), func=AF.Copy),
                  reads=["ptr"], writes=[ynT.k()])
            yield
            yt.free(); junk.free(); ss.free(); yn.free()

        def gla_tile(t, qT, kT, ktm, vtm, rs, onT, full, dbg=False):
            cols = slice(t * 128, (t + 1) * 128)
            p0, pk0 = next_pa()
            S.add("pe", lambda e: e.matmul(out=p0[:, 0:512], lhsT=a1T.ap[:, cols], rhs=w2a.ap, start=True, stop=True),
                  reads=[a1T.k(), w2a.k()], writes=[pk0])
            yield
            la = B32(512)

            S.add("act", lambda e: e.activation(out=la.ap, in_=p0[:, 0:512], func=AF.Exp, scale=-1.0), reads=[pk0], writes=[la.k()])
            yield
            S.add("act", lambda e: e.activation(out=la.ap, in_=la.ap, func=AF.Ln, bias=1.0), reads=[la.k()], writes=[la.k()])
            yield
            if dbg:
                dbg_dump("la", la, 512)
            if not full:
                pd, pkd = next_pa()

                def dmm_(e):
                    r = None
                    for hd in range(4):
                        r = e.matmul(out=pd[:, hd * 2:hd * 2 + 2], lhsT=la.ap[:, hd * 128:(hd + 1) * 128], rhs=negc.ap, start=True, stop=True)
                    return r
                S.add("pe", dmm_, reads=[la.k(), negc.k()], writes=[pkd])
                dec8 = B32(8)
                S.add("act", lambda e: e.activation(out=dec8.ap, in_=pd[:, 0:8], func=AF.Exp), reads=[pkd], writes=[dec8.k()])
                yield
                pe_, pke = next_pa()
                S.add("pe", lambda e: e.matmul(out=pe_[:, 0:512], lhsT=strigf.ap, rhs=la.ap, start=True, stop=True), reads=[strigf.k(), la.k()], writes=[pke])
                khf = B16(512)
                Ekf = B32(512)
                S.add("act", lambda e: e.activation(out=Ekf.ap, in_=pe_[:, 0:512], func=AF.Exp), reads=[pke], writes=[Ekf.k()])
                yield
                S.add("dve", lambda e: e.tensor_tensor(out=khf.ap, in0=Ekf.ap, in1=ktm.ap[:, t, :], op=ALU.mult), reads=[Ekf.k(), ktm.k(t)], writes=[khf.k()])
                yield
                for hp in range(2):
                    pu, pku = next_pa()

                    def umm_(e, hp=hp, pu=pu):
                        r = None
                        for h2 in range(2):
                            hd = hp * 2 + h2
                            r = e.matmul(out=pu[:, h2 * 256:(h2 + 1) * 256], lhsT=khf.ap[:, hd * 128:(hd + 1) * 128],
                                         rhs=vtm.ap[:, t, hd * 256:(hd + 1) * 256], start=True, stop=True)
                        return r
                    S.add("pe", umm_, reads=[khf.k(), vtm.k(t)], writes=[pku])

                    def gupd_(e, hp=hp, pu=pu):
                        r = None
                        for h2 in range(2):
                            hd = hp * 2 + h2
                            r = e.scalar_tensor_tensor(out=Sg.ap[:, hd * 256:(hd + 1) * 256], in0=Sg.ap[:, hd * 256:(hd + 1) * 256],
                                                       scalar=dec8.ap[:, hd * 2:hd * 2 + 1], in1=pu[:, h2 * 256:(h2 + 1) * 256],
                                                       op0=ALU.mult, op1=ALU.add)
                        return r
                    S.add("dve", gupd_, reads=[pku, dec8.k(), Sg.kr(hp * 512, hp * 512 + 512)], writes=[Sg.kr(hp * 512, hp * 512 + 512)])
                    yield
                la.free(); dec8.free(); khf.free(); Ekf.free()
                return
            p1, pk1 = next_pa()

            def bc(e):
                r = None
                for hd in range(4):
                    r = e.matmul(out=p1[:, hd * 128:(hd + 1) * 128], lhsT=la.ap[:, hd * 128:(hd + 1) * 128], rhs=trig.ap, start=True, stop=True)
                return r
            S.add("pe", bc, reads=[la.k(), trig.k()], writes=[pk1])
            yield
            EqT = B32(4, 128)
            S.add("act", lambda e: e.activation(out=EqT.ap, in_=p1[:, 0:512].rearrange("p (h l) -> p h l", h=4), func=AF.Exp),
                  reads=[pk1], writes=[EqT.k()])
            yield
            ktT = None
            if full:
                EkT = B32(4, 128)
                S.add("act", lambda e: e.activation(out=EkT.ap, in_=p1[:, 0:512].rearrange("p (h l) -> p h l", h=4), func=AF.Exp, scale=-1.0),
                      reads=[pk1], writes=[EkT.k()])
                yield
                ktT = B16(4, 128)
                S.add("dve", lambda e: e.tensor_tensor(out=ktT.ap, in0=kT.ap[:, :, cols], in1=EkT.ap, op=ALU.mult),
                      reads=[kT.k(), EkT.k()], writes=[ktT.k()])
                yield
                for c in range(2):
                    S.add("dve", lambda e, c=c: e.tensor_tensor(out=qtpad[c].ap[:, :, c * 64:(c + 1) * 64],
                                                                in0=qT.ap[:, :, t * 128 + c * 64:t * 128 + (c + 1) * 64],
                                                                in1=EqT.ap[:, :, c * 64:(c + 1) * 64], op=ALU.mult),
                          reads=[qT.k(), EqT.k()], writes=[qtpad[c].k()])
                    yield
                EkT.free()
            p2, pk2 = next_pa()
            S.add("pe", lambda e: e.matmul(out=p2[:, 0:512], lhsT=strig.ap, rhs=la.ap, start=True, stop=True), reads=[strig.k(), la.k()], writes=[pk2])
            yield
            Ekh = B32(512)
            S.add("act", lambda e: e.activation(out=Ekh.ap, in_=p2[:, 0:512], func=AF.Exp), reads=[pk2], writes=[Ekh.k()])
            yield
            kh = [B16(512), B16(512)]
            for c in range(2):
                S.add("dve", lambda e, c=c: e.scalar_tensor_tensor(out=kh[c].ap, in0=Ekh.ap, scalar=cm.ap[:, c:c + 1], in1=ktm.ap[:, t, :],
                                                                   op0=ALU.mult, op1=ALU.mult),
                      reads=[Ekh.k(), cm.k(), ktm.k(t)], writes=[kh[c].k()])
                yield
            la.free(); Ekh.free()
            attT = None
            if full:
                p3, pk3 = next_pa()

                def att(e):
                    r = None
                    for hd in range(4):
                        e.matmul(out=p3[:, hd * 128:(hd + 1) * 128], lhsT=ktT.ap[:, hd, :], rhs=qtpad[0].ap[:, hd, :], start=True, stop=False)
                        r = e.matmul(out=p3[:, hd * 128:(hd + 1) * 128], lhsT=ktT.ap[:, hd, :], rhs=qtpad[1].ap[:, hd, :], start=False, stop=True)
                    return r
                S.add("pe", att, reads=[ktT.k(), qtpad[0].k(), qtpad[1].k()], writes=[pk3])
                yield
                attT = B16(4, 128)
                S.add("dve", lambda e: e.tensor_tensor(out=attT.ap, in0=p3[:, 0:512].rearrange("p (h l) -> p h l", h=4),
                                                       in1=tri.ap.unsqueeze(1).to_broadcast([128, 4, 128]), op=ALU.mult),
                      reads=[pk3, tri.k()], writes=[attT.k()])
                yield
                ktT.free()
            Sgb = [B16(1024), B16(1024)] if full else None
            for c in range(2):
                if full:
                    S.add("act", lambda e, c=c: e.activation(out=Sgb[c].ap, in_=Sg.ap, func=AF.Copy), reads=[Sg.k()], writes=[Sgb[c].k()])
                    yield
                for hp in range(2):
                    pu, pku = next_pa()

                    def umm(e, c=c, hp=hp, pu=pu):
                        r = None
                        for h2 in range(2):
                            hd = hp * 2 + h2
                            r = e.matmul(out=pu[:, h2 * 256:(h2 + 1) * 256], lhsT=kh[c].ap[:, hd * 128:(hd + 1) * 128],
                                         rhs=vtm.ap[:, t, hd * 256:(hd + 1) * 256], start=True, stop=True)
                        return r
                    S.add("pe", umm, reads=[kh[c].k(), vtm.k(t)], writes=[pku])
                    yield

                    def gupd(e, c=c, hp=hp, pu=pu):
                        r = None
                        for h2 in range(2):
                            hd = hp * 2 + h2
                            r = e.scalar_tensor_tensor(out=Sg.ap[:, hd * 256:(hd + 1) * 256], in0=Sg.ap[:, hd * 256:(hd + 1) * 256],
                                                       scalar=EqT.ap[:, hd, c * 64 + 63:c * 64 + 64], in1=pu[:, h2 * 256:(h2 + 1) * 256],
                                                       op0=ALU.mult, op1=ALU.add)
                        return r
                    S.add("dve", gupd, reads=[pku, EqT.k(), Sg.kr(hp * 512, hp * 512 + 512)], writes=[Sg.kr(hp * 512, hp * 512 + 512)])
                    yield
                    yield
            kh[0].free(); kh[1].free(); EqT.free()
            if not full:
                return

            junk = B16(1024)
            ss = B32(4)
            pos = []
            for hp in range(2):
                po, pko = next_pa()
                pos.append((po, pko))

                def omm(e, hp=hp, po=po):
                    r = None
                    for h2 in range(2):
                        hd = hp * 2 + h2
                        o = po[:, h2 * 256:(h2 + 1) * 256]
                        e.matmul(out=o, lhsT=attT.ap[:, hd, :], rhs=vtm.ap[:, t, hd * 256:(hd + 1) * 256], start=True, stop=False)
                        e.matmul(out=o, lhsT=qtpad[0].ap[:, hd, :], rhs=Sgb[0].ap[:, hd * 256:(hd + 1) * 256], start=False, stop=False)
                        r = e.matmul(out=o, lhsT=qtpad[1].ap[:, hd, :], rhs=Sgb[1].ap[:, hd * 256:(hd + 1) * 256], start=False, stop=True)
                    return r
                S.add("pe", omm, reads=[attT.k(), vtm.k(t), qtpad[0].k(), qtpad[1].k(), Sgb[0].k(), Sgb[1].k()], writes=[pko])
                yield

                def osq(e, hp=hp, po=po):
                    r = None
                    for h2 in range(2):
                        hd = hp * 2 + h2
                        r = e.activation(out=junk.ap[:, hd * 256:(hd + 1) * 256], in_=po[:, h2 * 256:(h2 + 1) * 256], func=AF.Square,
                                         accum_out=ss.ap[:, hd:hd + 1])
                    return r
                S.add("act", osq, reads=[pko], writes=[junk.kr(hp * 512, hp * 512 + 512), ss.k()])
                yield
                yield
            attT.free(); Sgb[0].free(); Sgb[1].free()
            chain("act", [lambda e: e.activation(out=ss.ap, in_=ss.ap, func=AF.Ln, bias=EPS, scale=1.0 / 256.0),
                          lambda e: e.activation(out=ss.ap, in_=ss.ap, func=AF.Exp, scale=-0.5)], [], [ss.k()])
            yield
            yield
            on = B32(1024)
            onb = B16(1024)
            for hp in range(2):
                po, pko = pos[hp]

                def onorm(e, hp=hp, po=po):
                    r = None
                    for h2 in range(2):
                        hd = hp * 2 + h2
                        r = e.scalar_tensor_tensor(out=on.ap[:, hd * 256:(hd + 1) * 256], in0=po[:, h2 * 256:(h2 + 1) * 256], scalar=ss.ap[:, hd:hd + 1],
                                                   in1=cvec(C_GLAN, 256), op0=ALU.mult, op1=ALU.mult)
                    return r
                S.add("dve", onorm, reads=[pko, ss.k(), cst.k()], writes=[on.kr(hp * 512, hp * 512 + 512)])
                yield
            S.add("dve", lambda e: e.tensor_tensor(out=onb.ap, in0=on.ap, in1=rs.ap[:, t, :], op=ALU.mult), reads=[on.k(), rs.k(t)], writes=[onb.k()])
            yield
            yield

            def tro(e):
                r = None
                for c in range(8):
                    r = e.transpose(out=ptr[:, c * 128:(c + 1) * 128], in_=onb.ap[:, c * 128:(c + 1) * 128], identity=idb.ap)
                return r
            S.add("pe", tro, reads=[onb.k(), idb.k()], writes=["ptr"])
            yield
            S.add("act", lambda e: e.activation(out=onT.ap[:, :, cols], in_=ptr[:, :].rearrange("p (c n) -> p c n", c=8), func=AF.Copy),
                  reads=["ptr"], writes=[onT.k()])
            yield
            junk.free(); ss.free(); on.free(); onb.free()

        def prologue_mem():
            mn = B16(2, 1024)
            for mc in range(2):
                ms = B32(1024)
                S.add("sp", lambda e, ms=ms, mc=mc: [e.dma_start(out=ms.ap, in_=mem_d[mc * 128:(mc + 1) * 128, :])], writes=[ms.k()], dma=True, dkey="ms%d" % ms.off)
                junk = B16(1024)
                ss = B32(1)
                S.add("act", lambda e, ms=ms, junk=junk, ss=ss: e.activation(out=junk.ap, in_=ms.ap, func=AF.Square, accum_out=ss.ap),
                      reads=[ms.k()], writes=[junk.k(), ss.k()])
                chain("act", [lambda e, ss=ss: e.activation(out=ss.ap, in_=ss.ap, func=AF.Ln, bias=EPS, scale=1.0 / 1024.0),
                              lambda e, ss=ss: e.activation(out=ss.ap, in_=ss.ap, func=AF.Exp, scale=-0.5)], [], [ss.k()])
                S.add("dve", lambda e, ms=ms, ss=ss, mc=mc: e.scalar_tensor_tensor(out=mn.ap[:, mc, :], in0=ms.ap, scalar=ss.ap[:, 0:1], in1=cvec(C_MEMN, 1024),
                                                                                   op0=ALU.mult, op1=ALU.mult),
                      reads=[ms.k(), ss.k(), cst.k()], writes=[mn.k(mc)])
                ms.free(); junk.free(); ss.free()
            mnT = B16(8, 256)
            for mc in range(2):
                def trm(e, mc=mc):
                    r = None
                    for c in range(8):
                        r = e.transpose(out=ptr[:, c * 128:(c + 1) * 128], in_=mn.ap[:, mc, c * 128:(c + 1) * 128], identity=idb.ap)
                    return r
                S.add("pe", trm, reads=[mn.k(mc), idb.k()], writes=["ptr"])
                S.add("act", lambda e, mc=mc: e.activation(out=mnT.ap[:, :, mc * 128:(mc + 1) * 128], in_=ptr[:, :].rearrange("p (c n) -> p c n", c=8), func=AF.Copy),
                      reads=["ptr"], writes=[mnT.k()])
            mn.free()

            def k_epi(c0, nch, pv, pk):
                S.add("act", lambda e: e.activation(out=KT.ap[:, c0:c0 + nch, :], in_=pv, func=AF.Copy), reads=[pk], writes=[KT.k(c0, c0 + nch)])
            proj_fm(w_xkv, 0, 1024, mnT, 256, k_epi, blk=256)

            def v_epi(t, col, cw, p, pk):
                S.add("act", lambda e: e.activation(out=Vx.ap[:, t, col:col + cw], in_=p, func=AF.Copy), reads=[pk],
                      writes=[Vx.kr(t * 1024 + col, t * 1024 + col + cw)])
            proj_tm(w_xkv, 1024, 1024, mnT, 2, v_epi)
            mnT.free()

        prologue_mem()
        for s_ in range(NPRE // NT):
            step(s_ * NT, NT, "p1")
            if (s_ + 1) % NSTEP == 0:
                zi = cst.ap[:, C_CMASK + (s_ + 1) // NSTEP - 1:C_CMASK + (s_ + 1) // NSTEP]

                def zs_(e, zi=zi):
                    e.tensor_scalar(out=Sst.ap, in0=Sst.ap, scalar1=zi, scalar2=None, op0=ALU.mult)
                    return e.tensor_scalar(out=Sg.ap, in0=Sg.ap, scalar1=zi, scalar2=None, op0=ALU.mult)
                S.add("dve", zs_, reads=[Sst.k(), Sg.k(), cst.k()], writes=[Sst.k(), Sg.k()])
        for s_ in range(NSTEP):
            step(NPRE + s_ * NT, NT, "p2", orow0=s_ * NT, first_dbg=(s_ == 0))
        print("arena peaks: A16 %d / %d, A32 %d / %d; ops %d" % (a16.peak, N16, a32.peak, N32, len(S.ops)))
        S.emit(nc, st)
    return nc, dbg_outs


_CACHE = {}


def host_inputs(x, mem, norm_mix, w_in, ssd_conv_w, ssd_conv_b, ssd_dt_bias, ssd_A_log, ssd_D, ssd_norm,
                gla_w_a2, gla_b_a, gla_norm, w_up_ssd, w_up_gla, w_o, norm_xattn, norm_mem, w_xq, w_xkv,
                w_xo, norm_ffn, w_ffn_in, w_ffn_out, norm_final):
    f = lambda a: np.ascontiguousarray(np.asarray(a, dtype=np.float32))
    x = f(x); mem = f(mem)

    def fm(g):
        return f(g).reshape(8, 128).T

    def rep(v):
        v = f(v).reshape(1, -1)
        return np.broadcast_to(v, (128, v.shape[1]))
    cst = np.zeros((128, NCST), np.float32)
    cst[:, C_GMIX:C_GMIX + 8] = fm(norm_mix[0])
    cst[:, C_GXA:C_GXA + 8] = fm(norm_xattn[0])
    cst[:, C_GFFN:C_GFFN + 8] = fm(norm_ffn[0])
    cst[:, C_GFIN:C_GFIN + 8] = fm(norm_final)
    cw = f(ssd_conv_w[0])[:, 0, :]
    cst[:, C_CONVW:C_CONVW + 48] = cw.reshape(4, 12, 128).transpose(2, 1, 0).reshape(128, 48)
    cst[:, C_CONVB:C_CONVB + 12] = f(ssd_conv_b[0]).reshape(12, 128).T
    cst[:, C_DTB:C_DTB + 16] = rep(ssd_dt_bias[0])
    cst[:, C_ALOG:C_ALOG + 16] = rep(ssd_A_log[0])
    cst[:, C_DSK:C_DSK + 16] = rep(ssd_D[0])
    cst[:, C_SSDN:C_SSDN + 1024] = rep(ssd_norm[0])
    cst[:, C_GLAN:C_GLAN + 256] = rep(gla_norm[0])
    cst[:, C_MEMN:C_MEMN + 1024] = rep(norm_mem[0])
    w2aug = np.concatenate([f(gla_w_a2[0]), f(gla_b_a[0]).reshape(1, 512)], 0)
    shared = {"w2aug": f(w2aug), "w_in": f(w_in[0]), "w_up_ssd": f(w_up_ssd[0]), "w_up_gla": f(w_up_gla[0]), "w_o": f(w_o[0]),
              "w_xq": f(w_xq[0]), "w_xkv": f(w_xkv[0]), "w_xo": f(w_xo[0]), "w_ffn_in": f(w_ffn_in[0]), "w_ffn_out": f(w_ffn_out[0])}
    in_maps = []
    for c in range(NCORES):
        b, j = divmod(c, 4)
        xe = np.zeros((4 * SEG, D), np.float32)
        xe[(3 - j) * SEG:3 * SEG] = x[b, 0:j * SEG]
        xe[3 * SEG:] = x[b, j * SEG:(j + 1) * SEG]
        cc = cst.copy()
        for i in range(3):
            cc[:, C_CMASK + i] = 1.0 if i >= 3 - j else 0.0
        m = {"x_ext": xe, "mem_b": mem[b], "cst": cc}
        m.update(shared)
        in_maps.append(m)
    return in_maps


def kernel(**inputs):
    if "nc" not in _CACHE:
        _CACHE["nc"] = build_program()
    nc, dbg = _CACHE["nc"]
    in_maps = host_inputs(**inputs)
    res = run_bass_kernel_spmd(nc, in_maps, core_ids=list(range(NCORES)))
    _CACHE["last"] = res
    out = np.zeros((2, 4 * SEG, D), np.float32)
    for c in range(NCORES):
        b, j = divmod(c, 4)
        out[b, j * SEG:(j + 1) * SEG] = res.results[c]["out"]
    return out
```

```python
import numpy as np
from contextlib import ExitStack
import concourse.bass as bass
import concourse.mybir as mybir
from concourse.bass_utils import run_bass_kernel_spmd

F32 = mybir.dt.float32
BF16 = mybir.dt.bfloat16
AF = mybir.ActivationFunctionType
ALU = mybir.AluOpType

NCORES = 8
D = 1024
SEG = 2048
NT = 512
HALO = 128
EPS = 1e-6
EPOCH = 8000
DBG = False

O_Z, O_XBC, O_DT, O_Q, O_K, O_V, O_R, O_A1, O_GS, O_GG = 0, 1024, 2560, 2576, 3088, 3600, 4624, 5648, 5664, 6688
DFF = 2816

C_GMIX, C_GXA, C_GFFN, C_GFIN = 0, 8, 16, 24
C_CONVW = 32
C_CONVB = 80
C_DTB, C_ALOG, C_DSK = 92, 108, 124
C_SSDN = 140
C_GLAN = 1164
C_MEMN = 1420
C_CMASK = 2444
NCST = 2452


class Op:
    __slots__ = ("eng", "fn", "deps", "sig", "signal", "is_dma", "dkey", "ndma", "name", "inc")


class Sched:
    ENGS = ("pe", "act", "dve", "pool", "sp")

    def __init__(self):
        self.ops = []
        self.recs = {}

    @staticmethod
    def _k(k):
        return (k, 0, 1) if isinstance(k, str) else k

    def add(self, eng, fn, reads=(), writes=(), dma=False, dkey=None, ndma=1, name="", inc=16):
        op = Op()
        op.eng, op.fn, op.is_dma, op.dkey, op.ndma, op.inc, op.name = eng, fn, dma, dkey, ndma, inc, name
        op.sig = False
        op.signal = None
        deps = []
        seen = set()

        def push(d, raw):
            if d is None or d is op or id(d) in seen:
                return
            if not raw and not (d.is_dma or dma or d.eng != eng):
                return
            seen.add(id(d))
            deps.append(d)

        for k in reads:
            a, lo, hi = self._k(k)
            for r in self.recs.get(a, ()):
                if r[0] < hi and lo < r[1]:
                    push(r[2], True)
                    r[3].append(op)
        for k in writes:
            a, lo, hi = self._k(k)
            lst = self.recs.setdefault(a, [])
            new = []
            for r in lst:
                if r[0] < hi and lo < r[1]:
                    push(r[2], False)
                    for rd in r[3]:
                        push(rd, False)
                    if r[0] < lo:
                        new.append([r[0], lo, r[2], list(r[3])])
                    if hi < r[1]:
                        new.append([hi, r[1], r[2], list(r[3])])
                else:
                    new.append(r)
            new.append([lo, hi, op, []])
            self.recs[a] = new
        if eng == "pe" and not dma:
            deps = [d for d in deps if d.is_dma or d.eng != "pe"]
        op.deps = deps
        for d in deps:
            d.sig = True
        self.ops.append(op)
        return op

    def emit(self, nc, stack):
        eng_count = {e: 0 for e in self.ENGS}
        eng_sems = {e: [] for e in self.ENGS}
        dma_sems = {}
        dma_vals = {}
        for op in self.ops:
            if op.is_dma:
                if op.dkey not in dma_sems:
                    dma_sems[op.dkey] = stack.enter_context(nc.semaphore("d%d" % len(dma_sems)))
                    dma_vals[op.dkey] = 0
                dma_vals[op.dkey] += op.inc * op.ndma
                op.signal = (dma_sems[op.dkey], dma_vals[op.dkey])
            elif op.sig:
                c = eng_count[op.eng]
                ep = c // EPOCH
                if ep >= len(eng_sems[op.eng]):
                    eng_sems[op.eng].append(stack.enter_context(nc.semaphore("e%s%d" % (op.eng, ep))))
                op.signal = (eng_sems[op.eng][ep], c % EPOCH + 1)
                eng_count[op.eng] = c + 1
        by_eng = {e: [o for o in self.ops if o.eng == e] for e in self.ENGS}
        finals = {}
        for op in self.ops:
            if op.is_dma and op.name.startswith("OUT"):
                sem, val = op.signal
                if finals.get(id(sem), (None, 0))[1] < val:
                    finals[id(sem)] = (sem, val)

        def run(engh, ename):
            waited = {}
            for op in by_eng[ename]:
                for d in op.deps:
                    sem, val = d.signal
                    if waited.get(id(sem), 0) < val:
                        engh.wait_ge(sem, val)
                        waited[id(sem)] = val
                r = op.fn(engh)
                if op.is_dma:
                    assert len(r) == op.ndma, (op.name, len(r), op.ndma)
                    for ins in r:
                        ins.then_inc(op.signal[0], op.inc)
                elif op.sig:
                    r.then_inc(op.signal[0], 1)
            if ename == "sp":
                for sem, val in finals.values():
                    engh.wait_ge(sem, val)

        with nc.Block() as block:
            @block.tensor
            def _(e):
                run(e, "pe")

            @block.scalar
            def _(e):
                run(e, "act")

            @block.vector
            def _(e):
                run(e, "dve")

            @block.gpsimd
            def _(e):
                run(e, "pool")

            @block.sync
            def _(e):
                run(e, "sp")


class Arena:
    def __init__(self, name, tensor, n):
        self.name, self.t, self.n = name, tensor, n
        self.used = []
        self.peak = 0

    def alloc(self, n):
        n = (n + 15) // 16 * 16
        self.used.sort()
        pos = 0
        for off, sz in self.used:
            if off - pos >= n:
                break
            pos = off + sz
        if pos + n > self.n:
            raise RuntimeError("arena %s full: need %d at %d of %d" % (self.name, n, pos, self.n))
        self.used.append((pos, n))
        self.peak = max(self.peak, pos + n)
        return pos

    def free(self, off):
        self.used = [u for u in self.used if u[0] != off]


class Buf:
    def __init__(self, arena, shape):
        self.arena = arena
        self.shape = list(shape)
        self.n = int(np.prod(shape))
        self.off = arena.alloc(self.n)
        self.inner = self.n // self.shape[0] if len(shape) == 2 else self.n

    def free(self):
        self.arena.free(self.off)

    @property
    def ap(self):
        a = self.arena.t[:, self.off:self.off + self.n]
        if len(self.shape) == 2:
            return a.rearrange("p (a b) -> p a b", a=self.shape[0])
        return a

    def k(self, i=None, j=None):
        if i is None:
            return (self.arena.name, self.off, self.off + self.n)
        if j is None:
            j = i + 1
        return (self.arena.name, self.off + i * self.inner, self.off + j * self.inner)

    def kr(self, lo, hi):
        return (self.arena.name, self.off + lo, self.off + hi)


def build_program():
    nc = bass.Bass("TRN2", target_bir_lowering=False)
    TPS = NT // 128
    NSTEP = SEG // NT

    def din(name, shape):
        return nc.dram_tensor(name, list(shape), F32, kind="ExternalInput").ap()

    NPRE = 3 * SEG
    x_d = din("x_ext", [NPRE + SEG, D])
    mem_d = din("mem_b", [256, D])
    cst_d = din("cst", [128, NCST])
    w2_d = din("w2aug", [17, 512])
    w_in = din("w_in", [D, 7712])
    w_ups = din("w_up_ssd", [D, D])
    w_upg = din("w_up_gla", [D, D])
    w_o = din("w_o", [D, D])
    w_xq = din("w_xq", [D, D])
    w_xkv = din("w_xkv", [D, 2 * D])
    w_xo = din("w_xo", [D, D])
    w_fi = din("w_ffn_in", [D, 2 * DFF])
    w_fo = din("w_ffn_out", [DFF, D])
    out_d = nc.dram_tensor("out", [SEG, D], F32, kind="ExternalOutput").ap()
    dbg_outs = {}

    S = Sched()

    def chain(eng, fns, reads, writes):
        for f in fns:
            S.add(eng, f, reads=list(reads) + list(writes), writes=writes)
    st = ExitStack()
    with st:
        def sb(name, shape, dt):
            return st.enter_context(nc.sbuf_tensor(name, list(shape), dt))

        def pst(name, shape, dt):
            return st.enter_context(nc.psum_tensor(name, list(shape), dt))

        N16 = 70000
        N32 = 18000
        a16 = Arena("A16", sb("A16", [128, N16], BF16), N16)
        a32 = Arena("A32", sb("A32", [128, N32], F32), N32)

        def B16(*shape):
            return Buf(a16, shape)

        def B32(*shape):
            return Buf(a32, shape)

        pa = [pst("pa0", [128, 512], F32), pst("pa1", [128, 512], F32)]
        ptr = pst("ptr", [128, 1024], BF16)
        psm = pst("psm", [128, 512], F32)
        pwa = pst("pwa", [128, 1024], F32)
        pwb = pst("pwb", [128, 1024], F32)
        PWA = ("pwa", 0, 2)
        PWB = ("pwb", 0, 2)
        pa_i = [0]
        pa6_i = [0]
        banks6 = [(pa[0], "pa0"), (pa[1], "pa1"), (pwa[:, 0:512], ("pwa", 0, 1)), (pwa[:, 512:1024], ("pwa", 1, 2)),
                  (pwb[:, 0:512], ("pwb", 0, 1)), (pwb[:, 512:1024], ("pwb", 1, 2))]

        def next_pa():
            i = pa_i[0]
            pa_i[0] ^= 1
            return pa[i], "pa%d" % i

        def next_pa6():
            i = pa6_i[0]
            pa6_i[0] = (i + 1) % 6
            return banks6[i]

        cst = B32(NCST)
        w2a = B32(512)
        idf = B32(128)
        tri = B32(128)
        stri = B32(128)
        trig = B32(128)
        strig = B32(128)
        strif = B32(128)
        strigf = B32(128)
        onesf = B32(128)
        negc = B32(2)
        onesc = B32(2, 128)
        cm = B32(2)
        idb = B16(128)
        ones_mean = B16(128)
        ones1 = B16(128)
        aneg = B32(16)

        S.add("sp", lambda e: [e.dma_start(out=cst.ap, in_=cst_d)], writes=[cst.k()], dma=True, dkey="cst")
        S.add("dve", lambda e: e.memset(w2a.ap, 0.0), writes=[w2a.k()])
        S.add("sp", lambda e: [e.dma_start(out=w2a.ap[0:17, :], in_=w2_d)], writes=[w2a.k()], dma=True, dkey="w2a")

        def mk_c1(e):
            e.memset(idf.ap, 0.0)
            e.memset(tri.ap, 1.0)
            e.memset(stri.ap, 1.0)
            e.memset(onesc.ap, 0.0)
            e.memset(cm.ap, 0.0)
            e.memset(ones_mean.ap, 1.0 / 1024.0)
            e.memset(strif.ap, 1.0)
            e.memset(onesf.ap, 1.0)
            e.memset(negc.ap, -1.0 / 16.0)
            return e.memset(ones1.ap, 1.0)

        def mk_c2(e):
            e.affine_select(out=strif.ap, in_=strif.ap, pattern=[[-1, 128]], compare_op=ALU.is_gt,
                            fill=0.0, base=0, channel_multiplier=1)
            e.affine_select(out=idf.ap, in_=idf.ap, pattern=[[-1, 128]], compare_op=ALU.not_equal,
                            fill=1.0, base=0, channel_multiplier=1)
            e.affine_select(out=tri.ap, in_=tri.ap, pattern=[[1, 128]], compare_op=ALU.is_ge,
                            fill=0.0, base=0, channel_multiplier=-1)
            e.affine_select(out=stri.ap, in_=stri.ap, pattern=[[-1, 128]], compare_op=ALU.is_gt,
                            fill=0.0, base=0, channel_multiplier=1)
            e.memset(onesc.ap[0:64, 0, :], 1.0)
            e.memset(onesc.ap[64:128, 1, :], 1.0)
            e.memset(cm.ap[0:64, 0:1], 1.0)
            return e.memset(cm.ap[64:128, 1:2], 1.0)

        def mk_c3(e):
            e.memset(tri.ap[0:64, 64:128], 0.0)
            return e.memset(stri.ap[64:128, 0:64], 0.0)
        chain("pool", [mk_c1, mk_c2, mk_c3], [], [idf.k(), tri.k(), stri.k(), onesc.k(), cm.k(), ones_mean.k(), ones1.k(),
                                                  strif.k(), onesf.k(), negc.k()])

        def mk_consts2(e):
            e.tensor_copy(out=idb.ap, in_=idf.ap)
            e.tensor_scalar(out=trig.ap, in0=tri.ap, scalar1=-1.0 / 16.0, scalar2=None, op0=ALU.mult)
            e.tensor_scalar(out=strigf.ap, in0=strif.ap, scalar1=-1.0 / 16.0, scalar2=None, op0=ALU.mult)
            return e.tensor_scalar(out=strig.ap, in0=stri.ap, scalar1=-1.0 / 16.0, scalar2=None, op0=ALU.mult)
        S.add("dve", mk_consts2, reads=[idf.k(), tri.k(), stri.k(), strif.k()], writes=[idb.k(), trig.k(), strig.k(), strigf.k()])

        def mk_aneg(e):
            return e.activation(out=aneg.ap, in_=cst.ap[:, C_ALOG:C_ALOG + 16], func=AF.Exp)
        S.add("act", mk_aneg, reads=[cst.k()], writes=[aneg.k()])
        S.add("dve", lambda e: e.tensor_scalar(out=aneg.ap, in0=aneg.ap, scalar1=-1.0, scalar2=None, op0=ALU.mult),
              reads=[aneg.k()], writes=[aneg.k()])

        def cvec(off, n):
            return cst.ap[:, off:off + n]

        NSLOT = 3
        WSZ = 4096
        wslots = [B16(WSZ) for _ in range(NSLOT)]
        wctr = [0]
        SSZ = 1024
        sslots = [B16(SSZ) for _ in range(2)]
        sctr = [0]

        def wload(wd, kc, c0, cw):
            if kc * cw <= SSZ:
                i = NSLOT + sctr[0] % 2
                sctr[0] += 1
                slot = sslots[i - NSLOT]
            else:
                i = wctr[0] % NSLOT
                wctr[0] += 1
                slot = wslots[i]
            assert kc * cw <= WSZ
            dst = slot.ap[:, 0:kc * cw].rearrange("p (k n) -> p k n", k=kc)
            src = wd.rearrange("(k p) n -> p k n", p=128)[:, :, c0:c0 + cw]
            if kc > 8:
                h = kc // 2
                S.add("pool", lambda e: [e.dma_start(out=dst[:, 0:h, :], in_=src[:, 0:h, :]),
                                         e.dma_start(out=dst[:, h:kc, :], in_=src[:, h:kc, :])],
                      writes=[slot.k()], dma=True, dkey="w%d" % i, ndma=2)
            else:
                S.add("pool", lambda e: [e.dma_start(out=dst, in_=src)], writes=[slot.k()], dma=True, dkey="w%d" % i)
            return dst, slot.k()

        def proj_fm(wd, c0, ncols, src, nt, epi, kc=8, blk=512):
            col = 0
            while col < ncols:
                cw = min(blk, ncols - col)
                wap, wkey = wload(wd, kc, c0 + col, cw)
                nch_all = (cw + 127) // 128
                gsz = max(1, 512 // nt)
                for cg in range(0, nch_all, gsz):
                    nch = min(gsz, nch_all - cg)
                    p, pk = next_pa6()
                    pv = p[:, 0:nch * nt].rearrange("p (c n) -> p c n", c=nch)

                    def mm(e, wap=wap, pv=pv, nch=nch, cw=cw, cg=cg):
                        r = None
                        for c in range(nch):
                            cc = cg + c
                            m = min(128, cw - cc * 128)
                            for k in range(kc):
                                r = e.matmul(out=pv[0:m, c, :], lhsT=wap[:, k, cc * 128:cc * 128 + m], rhs=src.ap[:, k, 0:nt],
                                             start=(k == 0), stop=(k == kc - 1))
                        return r
                    S.add("pe", mm, reads=[wkey, src.k()], writes=[pk])
                    epi(col // 128 + cg, nch, pv, pk)
                col += cw

        def proj_tm(wd, c0, ncols, src, ntiles, epi, kc=8):
            col = 0
            while col < ncols:
                cw = min(512, ncols - col)
                wap, wkey = wload(wd, kc, c0 + col, cw)
                for t in range(ntiles):
                    p, pk = next_pa6()

                    def mm(e, wap=wap, p=p, t=t, cw=cw):
                        r = None
                        for k in range(kc):
                            r = e.matmul(out=p[:, 0:cw], lhsT=src.ap[:, k, t * 128:(t + 1) * 128], rhs=wap[:, k, :],
                                         start=(k == 0), stop=(k == kc - 1))
                        return r
                    S.add("pe", mm, reads=[wkey, src.k()], writes=[pk])
                    epi(t, col, cw, p[:, 0:cw], pk)
                col += cw

        hT = B32(8, NT)
        nT = B16(8, NT)
        rstd = B32(NT)
        Sst = B32(1024)
        Sg = B32(1024)
        dtot = B32(20)
        halo3 = B16(12, 3)
        ctpad = [B16(2, 128), B16(2, 128)]
        qtpad = [B16(4, 128), B16(4, 128)]
        a1T = B32(NT)
        KT = B16(8, 256)
        Vx = B16(2, 1024)

        def init_state(e):
            e.memset(Sst.ap, 0.0)
            e.memset(Sg.ap, 0.0)
            e.memset(dtot.ap, 1.0)
            e.memset(ctpad[0].ap, 0.0)
            e.memset(ctpad[1].ap, 0.0)
            e.memset(qtpad[0].ap, 0.0)
            e.memset(qtpad[1].ap, 0.0)
            e.memset(halo3.ap, 0.0)
            return e.memset(a1T.ap, 0.0)
        S.add("pool", init_state, writes=[Sst.k(), Sg.k(), dtot.k(), ctpad[0].k(), ctpad[1].k(), qtpad[0].k(),
                                          qtpad[1].k(), a1T.k(), halo3.k()])
        S.add("dve", lambda e: e.memset(a1T.ap[0:32, :], 1.0), reads=[a1T.k()], writes=[a1T.k()])

        xs_i = [0]

        def load_xT(row0, nt, xreads=()):
            xsb = [B32(1024), B32(1024)]
            for t in range(nt // 128):
                j = xs_i[0]
                xs_i[0] ^= 1
                xs = xsb[j]
                pw, pwk = (pwa, PWA) if j == 0 else (pwb, PWB)
                r0 = row0 + t * 128
                S.add("sp", lambda e, xs=xs, r0=r0: [e.dma_start(out=xs.ap, in_=x_d[r0:r0 + 128, :])],
                      writes=[xs.k()], reads=list(xreads), dma=True, dkey="xs%d" % xs.off)

                def tr(e, xs=xs, pw=pw):
                    r = None
                    for c in range(8):
                        r = e.transpose(out=pw[:, c * 128:(c + 1) * 128], in_=xs.ap[:, c * 128:(c + 1) * 128], identity=idf.ap)
                    return r
                S.add("pe", tr, reads=[xs.k(), idf.k()], writes=[pwk])
                S.add("act", lambda e, t=t, pw=pw: e.activation(out=hT.ap[:, :, t * 128:(t + 1) * 128],
                                                               in_=pw[:, :].rearrange("p (c n) -> p c n", c=8), func=AF.Copy),
                      reads=[pwk], writes=[hT.k()])
            xsb[0].free(); xsb[1].free()

        def norm_fm(goff, nt, stats_only=False):
            sqb = B16(8, NT)
            S.add("act", lambda e: e.activation(out=sqb.ap[:, :, 0:nt], in_=hT.ap[:, :, 0:nt], func=AF.Square),
                  reads=[hT.k()], writes=[sqb.k()])

            def mm(e):
                r = None
                for k in range(8):
                    r = e.matmul(out=psm[:, 0:nt], lhsT=ones_mean.ap, rhs=sqb.ap[:, k, 0:nt], start=(k == 0), stop=(k == 7))
                return r
            S.add("pe", mm, reads=[sqb.k(), ones_mean.k()], writes=["psm"])
            sqb.free()
            S.add("act", lambda e: e.activation(out=rstd.ap[:, 0:nt], in_=psm[:, 0:nt], func=AF.Ln, bias=EPS), reads=["psm"], writes=[rstd.k()])
            S.add("act", lambda e: e.activation(out=rstd.ap[:, 0:nt], in_=rstd.ap[:, 0:nt], func=AF.Exp, scale=-0.5), reads=[rstd.k()], writes=[rstd.k()])
            if stats_only:
                return
            tgt = nT

            def nrm(e):
                r = None
                for k in range(8):
                    r = e.scalar_tensor_tensor(out=tgt.ap[:, k, 0:nt], in0=hT.ap[:, k, 0:nt], scalar=cst.ap[:, goff + k:goff + k + 1],
                                               in1=rstd.ap[:, 0:nt], op0=ALU.mult, op1=ALU.mult)
                return r
            S.add("dve", nrm, reads=[hT.k(), rstd.k(), cst.k()], writes=[tgt.k()])

        def resid_epi(c0, nch, pv, pk):
            S.add("dve", lambda e: e.tensor_tensor(out=hT.ap[:, c0:c0 + nch, :], in0=hT.ap[:, c0:c0 + nch, :], in1=pv, op=ALU.add),
                  reads=[pk, hT.k(c0, c0 + nch)], writes=[hT.k(c0, c0 + nch)])

        def dbg_dump(name, buf, n, dt=F32):
            if not DBG:
                return
            d = nc.dram_tensor("dbg_" + name, [128, n], dt, kind="ExternalOutput").ap()
            dbg_outs[name] = d
            flat = buf.arena.t[:, buf.off:buf.off + n]
            S.add("sp", lambda e: [e.dma_start(out=d, in_=flat)], reads=[buf.k()], dma=True, dkey="dbg_" + name, name="OUTdbg")

        def interleave(gens):
            gens = list(gens)
            while gens:
                for g in list(gens):
                    try:
                        next(g)
                    except StopIteration:
                        gens.remove(g)

        def step(row0, nt, mode, orow0=None, first_dbg=False, xreads=()):
            full = mode == "p2"
            tps = nt // 128
            load_xT(row0, nt, xreads)
            norm_fm(C_GMIX, nt)
            if first_dbg:
                dbg_dump("nT", nT, 8 * NT, BF16)
            dtb = B32(tps, 16)
            ab = B32(tps, 16)

            def dt_epi(t, col, cw, p, pk):
                tmp = B32(16)
                S.add("dve", lambda e: e.tensor_tensor(out=tmp.ap, in0=p, in1=cvec(C_DTB, 16), op=ALU.add),
                      reads=[pk, cst.k()], writes=[tmp.k()])

                S.add("act", lambda e: e.activation(out=tmp.ap, in_=tmp.ap, func=AF.Exp), reads=[tmp.k()], writes=[tmp.k()])
                S.add("act", lambda e: e.activation(out=dtb.ap[:, t, :], in_=tmp.ap, func=AF.Ln, bias=1.0), reads=[tmp.k()], writes=[dtb.k(t)])
                S.add("dve", lambda e: e.tensor_tensor(out=ab.ap[:, t, :], in0=dtb.ap[:, t, :], in1=aneg.ap, op=ALU.mult),
                      reads=[dtb.k(t), aneg.k()], writes=[ab.k(t)])
                tmp.free()
            proj_tm(w_in, O_DT, 16, nT, tps, dt_epi)
            xbr = B16(12, nt + 3)
            xsT = B16(8, nt)
            BT = B16(2, nt)
            CT = B16(2, nt)
            S.add("dve", lambda e: e.tensor_copy(out=xbr.ap[:, :, 0:3], in_=halo3.ap), reads=[halo3.k()], writes=[xbr.k()])

            pend = []

            def conv_pair(cs):
                caccs_ = {}
                for c in cs:
                    cacc = B32(nt)
                    caccs_[c] = cacc
                    S.add("act", lambda e, c=c, cacc=cacc: e.activation(out=cacc.ap, in_=xbr.ap[:, c, 0:nt], func=AF.Copy,
                                                                        scale=cst.ap[:, C_CONVW + 4 * c:C_CONVW + 4 * c + 1]),
                          reads=[xbr.k(c), cst.k()], writes=[cacc.k()])
                for k in (1, 2, 3):
                    for c in cs:
                        cacc = caccs_[c]
                        S.add("dve", lambda e, c=c, cacc=cacc, k=k: e.scalar_tensor_tensor(
                            out=cacc.ap, in0=xbr.ap[:, c, k:k + nt], scalar=cst.ap[:, C_CONVW + 4 * c + k:C_CONVW + 4 * c + k + 1],
                            in1=cacc.ap, op0=ALU.mult, op1=ALU.add),
                            reads=[xbr.k(c), cst.k(), cacc.k()], writes=[cacc.k()])
                for c in cs:
                    cacc = caccs_[c]
                    dstb, di = (xsT, c) if c < 8 else ((BT, c - 8) if c < 10 else (CT, c - 10))
                    S.add("act", lambda e, c=c, cacc=cacc, dstb=dstb, di=di: e.activation(
                        out=dstb.ap[:, di, :], in_=cacc.ap, func=AF.Silu, bias=cst.ap[:, C_CONVB + c:C_CONVB + c + 1]),
                        reads=[cacc.k(), cst.k()], writes=[dstb.k(di)])
                    cacc.free()

            def xbc_epi(c0, nch, pv, pk):
                S.add("act", lambda e: e.activation(out=xbr.ap[:, c0:c0 + nch, 3:3 + nt], in_=pv, func=AF.Copy),
                      reads=[pk], writes=[xbr.k(c0, c0 + nch)])
                for c in range(c0, c0 + nch):
                    if full or c < 10:
                        pend.append(c)
                if len(pend) >= 2:
                    conv_pair(list(pend))
                    del pend[:]
            proj_fm(w_in, O_XBC, 1536, nT, nt, xbc_epi)
            if pend:
                conv_pair(list(pend))
                del pend[:]
            S.add("dve", lambda e: e.tensor_copy(out=halo3.ap, in_=xbr.ap[:, :, nt:nt + 3]), reads=[xbr.k()], writes=[halo3.k()])
            xbr.free()
            zs = None
            if full:
                zs = B16(tps, 1024)

                def z_epi(t, col, cw, p, pk):
                    S.add("act", lambda e: e.activation(out=zs.ap[:, t, col:col + cw], in_=p, func=AF.Silu),
                          reads=[pk], writes=[zs.kr(t * 1024 + col, t * 1024 + col + cw)])
                proj_tm(w_in, O_Z, 1024, nT, tps, z_epi)
            qT = B16(4, nt) if full else None
            kT = B16(4, nt) if full else None
            ktm = B16(tps, 512)
            vtm = B16(tps, 1024)
            rs = B16(tps, 1024) if full else None
            if full:
                def q_epi(c0, nch, pv, pk):
                    S.add("act", lambda e: e.activation(out=qT.ap[:, c0:c0 + nch, :], in_=pv, func=AF.Copy, scale=float(128 ** -0.5)),
                          reads=[pk], writes=[qT.k(c0, c0 + nch)])
                proj_fm(w_in, O_Q, 512, nT, nt, q_epi)
            wap, wkey = wload(w_in, 8, O_K, 512)
            if full:
                gsz = max(1, 512 // nt)
                for cg in range(0, 4, gsz):
                    p, pk = next_pa6()
                    pv = p[:, 0:gsz * nt].rearrange("p (c n) -> p c n", c=gsz)

                    def mmk(e, wap=wap, pv=pv, cg=cg, gsz=gsz):
                        r = None
                        for c in range(gsz):
                            for k in range(8):
                                r = e.matmul(out=pv[:, c, :], lhsT=wap[:, k, (cg + c) * 128:(cg + c + 1) * 128], rhs=nT.ap[:, k, 0:nt],
                                             start=(k == 0), stop=(k == 7))
                        return r
                    S.add("pe", mmk, reads=[wkey, nT.k()], writes=[pk])
                    S.add("act", lambda e, pv=pv, cg=cg, gsz=gsz: e.activation(out=kT.ap[:, cg:cg + gsz, :], in_=pv, func=AF.Copy),
                          reads=[pk], writes=[kT.k(cg, cg + gsz)])
            for t in range(tps):
                p, pk = next_pa6()

                def mmk2(e, wap=wap, p=p, t=t):
                    r = None
                    for k in range(8):
                        r = e.matmul(out=p[:, 0:512], lhsT=nT.ap[:, k, t * 128:(t + 1) * 128], rhs=wap[:, k, :], start=(k == 0), stop=(k == 7))
                    return r
                S.add("pe", mmk2, reads=[wkey, nT.k()], writes=[pk])
                S.add("act", lambda e, p=p, t=t: e.activation(out=ktm.ap[:, t, :], in_=p[:, 0:512], func=AF.Copy), reads=[pk], writes=[ktm.k(t)])

            def a1_epi(c0, nch, pv, pk):
                S.add("act", lambda e: e.activation(out=a1T.ap[0:16, 0:nt], in_=pv[0:16, 0, :], func=AF.Copy), reads=[pk], writes=[a1T.k()])
            proj_fm(w_in, O_A1, 128, nT, nt, a1_epi)

            def v_epi(t, col, cw, p, pk):
                S.add("act", lambda e: e.activation(out=vtm.ap[:, t, col:col + cw], in_=p, func=AF.Copy),
                      reads=[pk], writes=[vtm.kr(t * 1024 + col, t * 1024 + col + cw)])
            proj_tm(w_in, O_V, 1024, nT, tps, v_epi)
            if full:
                def r_epi(t, col, cw, p, pk):
                    S.add("act", lambda e: e.activation(out=rs.ap[:, t, col:col + cw], in_=p, func=AF.Silu),
                          reads=[pk], writes=[rs.kr(t * 1024 + col, t * 1024 + col + cw)])
                proj_tm(w_in, O_R, 1024, nT, tps, r_epi)
            ynT = B16(8, nt) if full else None
            onT = B16(8, nt) if full else None
            for t in range(tps):
                interleave([ssd_tile(t, xsT, BT, CT, dtb, ab, zs, ynT, full, False),
                            gla_tile(t, qT, kT, ktm, vtm, rs, onT, full, False)])
            xsT.free(); BT.free(); CT.free(); dtb.free(); ab.free()
            if zs is not None:
                zs.free()
            ktm.free(); vtm.free()
            if not full:
                return
            qT.free(); kT.free(); rs.free()
            if first_dbg:
                dbg_dump("onT", onT, 8 * NT, BF16)
            gsT = B16(8, nt)
            ggT = B16(8, nt)
            mT = B16(8, nt)

            def gs_epi(c0, nch, pv, pk):
                S.add("act", lambda e: e.activation(out=gsT.ap[:, c0:c0 + nch, :], in_=pv, func=AF.Sigmoid), reads=[pk], writes=[gsT.k(c0, c0 + nch)])

            def gg_epi(c0, nch, pv, pk):
                S.add("act", lambda e: e.activation(out=ggT.ap[:, c0:c0 + nch, :], in_=pv, func=AF.Sigmoid), reads=[pk], writes=[ggT.k(c0, c0 + nch)])
            proj_fm(w_in, O_GS, 1024, nT, nt, gs_epi)
            proj_fm(w_in, O_GG, 1024, nT, nt, gg_epi)

            def ups_epi(c0, nch, pv, pk):
                S.add("dve", lambda e: e.tensor_tensor(out=gsT.ap[:, c0:c0 + nch, :], in0=gsT.ap[:, c0:c0 + nch, :], in1=pv, op=ALU.mult),
                      reads=[pk, gsT.k(c0, c0 + nch)], writes=[gsT.k(c0, c0 + nch)])
            proj_fm(w_ups, 0, 1024, ynT, nt, ups_epi)

            def upg_epi(c0, nch, pv, pk):
                S.add("dve", lambda e: e.tensor_tensor(out=ggT.ap[:, c0:c0 + nch, :], in0=ggT.ap[:, c0:c0 + nch, :], in1=pv, op=ALU.mult),
                      reads=[pk, ggT.k(c0, c0 + nch)], writes=[ggT.k(c0, c0 + nch)])
                S.add("dve", lambda e: e.tensor_tensor(out=mT.ap[:, c0:c0 + nch, :], in0=ggT.ap[:, c0:c0 + nch, :], in1=gsT.ap[:, c0:c0 + nch, :], op=ALU.add),
                      reads=[ggT.k(c0, c0 + nch), gsT.k(c0, c0 + nch)], writes=[mT.k(c0, c0 + nch)])
            proj_fm(w_upg, 0, 1024, onT, nt, upg_epi)
            ynT.free(); onT.free(); gsT.free(); ggT.free()
            proj_fm(w_o, 0, 1024, mT, nt, resid_epi)
            mT.free()
            if first_dbg:
                dbg_dump("h1", hT, 8 * NT)
            norm_fm(C_GXA, nt)
            qx = B16(8, nt)
            ox = B16(8, nt)

            def qx_epi(c0, nch, pv, pk):
                S.add("act", lambda e: e.activation(out=qx.ap[:, c0:c0 + nch, :], in_=pv, func=AF.Copy, scale=1.0 / 16.0),
                      reads=[pk], writes=[qx.k(c0, c0 + nch)])
            proj_fm(w_xq, 0, 1024, nT, nt, qx_epi)
            for hd in range(4):
                ET = B16(2, nt)
                for mc in range(2):
                    p, pk = next_pa6()

                    def sc(e, p=p, hd=hd, mc=mc):
                        r = None
                        for dc in range(2):
                            r = e.matmul(out=p[:, 0:nt], lhsT=KT.ap[:, hd * 2 + dc, mc * 128:(mc + 1) * 128], rhs=qx.ap[:, hd * 2 + dc, :],
                                         start=(dc == 0), stop=(dc == 1))
                        return r
                    S.add("pe", sc, reads=[KT.k(), qx.k(hd * 2, hd * 2 + 2)], writes=[pk])
                    S.add("act", lambda e, p=p, mc=mc, ET=ET: e.activation(out=ET.ap[:, mc, :], in_=p[:, 0:nt], func=AF.Exp),
                          reads=[pk], writes=[ET.k(mc)])

                def den(e, ET=ET):
                    e.matmul(out=psm[:, 0:nt], lhsT=ones1.ap, rhs=ET.ap[:, 0, :], start=True, stop=False)
                    return e.matmul(out=psm[:, 0:nt], lhsT=ones1.ap, rhs=ET.ap[:, 1, :], start=False, stop=True)
                S.add("pe", den, reads=[ET.k(), ones1.k()], writes=["psm"])
                rden = B32(nt)
                S.add("dve", lambda e, rden=rden: e.reciprocal(out=rden.ap, in_=psm[:, 0:nt]), reads=["psm"], writes=[rden.k()])
                for dc in range(2):
                    p, pk = next_pa6()

                    def pvm(e, p=p, hd=hd, dc=dc, ET=ET):
                        r = None
                        for mc in range(2):
                            r = e.matmul(out=p[:, 0:nt], lhsT=Vx.ap[:, mc, hd * 256 + dc * 128:hd * 256 + dc * 128 + 128], rhs=ET.ap[:, mc, :],
                                         start=(mc == 0), stop=(mc == 1))
                        return r
                    S.add("pe", pvm, reads=[Vx.k(), ET.k()], writes=[pk])
                    S.add("dve", lambda e, p=p, hd=hd, dc=dc, rden=rden: e.tensor_tensor(out=ox.ap[:, hd * 2 + dc, :], in0=p[:, 0:nt], in1=rden.ap, op=ALU.mult),
                          reads=[pk, rden.k()], writes=[ox.k(hd * 2 + dc)])
                ET.free(); rden.free()
            qx.free()
            proj_fm(w_xo, 0, 1024, ox, nt, resid_epi)
            ox.free()
            if first_dbg:
                dbg_dump("h2", hT, 8 * NT)
            norm_fm(C_GFFN, nt)
            aT = B16(22, nt)
            col = 0
            while col < DFF:
                cw = min(512, DFF - col)

                def g_epi(c0, nch, pv, pk, col=col):
                    cc = col // 128 + c0
                    S.add("act", lambda e: e.activation(out=aT.ap[:, cc:cc + nch, :], in_=pv, func=AF.Silu), reads=[pk], writes=[aT.k(cc, cc + nch)])

                def u_epi(c0, nch, pv, pk, col=col):
                    cc = col // 128 + c0
                    S.add("dve", lambda e: e.tensor_tensor(out=aT.ap[:, cc:cc + nch, :], in0=aT.ap[:, cc:cc + nch, :], in1=pv, op=ALU.mult),
                          reads=[pk, aT.k(cc, cc + nch)], writes=[aT.k(cc, cc + nch)])
                proj_fm(w_fi, col, cw, nT, nt, g_epi)
                proj_fm(w_fi, DFF + col, cw, nT, nt, u_epi)
                col += cw
            proj_fm(w_fo, 0, 1024, aT, nt, resid_epi, kc=22, blk=128)
            aT.free()
            if first_dbg:
                dbg_dump("h3", hT, 8 * NT)
            norm_fm(C_GFIN, nt, stats_only=True)
            for t in range(tps):
                of = B32(8, 128)

                def nrm_t(e, t=t, of=of):
                    r = None
                    for k in range(8):
                        r = e.scalar_tensor_tensor(out=of.ap[:, k, :], in0=hT.ap[:, k, t * 128:(t + 1) * 128], scalar=cst.ap[:, C_GFIN + k:C_GFIN + k + 1],
                                                   in1=rstd.ap[:, t * 128:(t + 1) * 128], op0=ALU.mult, op1=ALU.mult)
                    return r
                S.add("dve", nrm_t, reads=[hT.k(), rstd.k(), cst.k()], writes=[of.k()])

                def tr(e, t=t, of=of):
                    r = None
                    for c in range(8):
                        r = e.transpose(out=pwa[:, c * 128:(c + 1) * 128], in_=of.ap[:, c, :], identity=idf.ap)
                    return r
                S.add("pe", tr, reads=[of.k(), idf.k()], writes=[PWA])
                of.free()
                og = B32(1024)
                S.add("act", lambda e, og=og: e.activation(out=og.ap, in_=pwa[:, :], func=AF.Copy), reads=[PWA], writes=[og.k()])
                r0 = orow0 + t * 128
                S.add("sp", lambda e, og=og, r0=r0: [e.dma_start(out=out_d[r0:r0 + 128, :], in_=og.ap)], reads=[og.k()],
                      dma=True, dkey="og%d" % og.off, name="OUT")
                og.free()

        def ssd_tile_p1(t, xsT, BT, dtb, ab):
            cols = slice(t * 128, (t + 1) * 128)
            a_t = ab.ap[:, t, :]
            dt_t = dtb.ap[:, t, :]
            Btm = B16(2, 128)
            Xd = B16(1024)
            ex = B32(32)
            dd = B32(16)

            def trx(e):
                r = None
                for c in range(8):
                    r = e.transpose(out=ptr[:, c * 128:(c + 1) * 128], in_=xsT.ap[:, c, cols], identity=idb.ap)
                return r

            def smalls(e):
                e.matmul(out=psm[:, 0:16], lhsT=strif.ap, rhs=a_t, start=True, stop=True)
                return e.matmul(out=psm[:, 16:32], lhsT=onesf.ap, rhs=a_t, start=True, stop=True)
            S.add("pe", smalls, reads=[ab.k(t), strif.k(), onesf.k()], writes=["psm"])
            S.add("act", lambda e: e.activation(out=ex.ap, in_=psm[:, 0:32], func=AF.Exp), reads=["psm"], writes=[ex.k()])
            yield
            S.add("dve", lambda e: e.tensor_tensor(out=dd.ap, in0=ex.ap[:, 0:16], in1=dt_t, op=ALU.mult), reads=[ex.k(), dtb.k(t)], writes=[dd.k()])
            S.add("pe", trx, reads=[xsT.k(), idb.k()], writes=["ptr"])
            yield
            S.add("dve", lambda e: e.tensor_tensor(out=Xd.ap.rearrange("p (h q) -> p h q", h=16),
                                                   in0=ptr[:, :].rearrange("p (h q) -> p h q", h=16),
                                                   in1=dd.ap.unsqueeze(2).to_broadcast([128, 16, 64]), op=ALU.mult),
                  reads=["ptr", dd.k()], writes=[Xd.k()])
            yield

            def trb(e):
                e.transpose(out=ptr[:, 0:128], in_=BT.ap[:, 0, cols], identity=idb.ap)
                return e.transpose(out=ptr[:, 128:256], in_=BT.ap[:, 1, cols], identity=idb.ap)
            S.add("pe", trb, reads=[BT.k(), idb.k()], writes=["ptr"])
            S.add("act", lambda e: e.activation(out=Btm.ap, in_=ptr[:, 0:256].rearrange("p (g n) -> p g n", g=2), func=AF.Copy),
                  reads=["ptr"], writes=[Btm.k()])
            yield

            def sloc(e):
                e.matmul(out=pwb[:, 0:512], lhsT=Btm.ap[:, 0, :], rhs=Xd.ap[:, 0:512], start=True, stop=True)
                return e.matmul(out=pwb[:, 512:1024], lhsT=Btm.ap[:, 1, :], rhs=Xd.ap[:, 512:1024], start=True, stop=True)
            S.add("pe", sloc, reads=[Btm.k(), Xd.k()], writes=[PWB])
            S.add("dve", lambda e: e.tensor_tensor(out=Sst.ap.rearrange("p (h q) -> p h q", h=16), in0=Sst.ap.rearrange("p (h q) -> p h q", h=16),
                                                   in1=ex.ap[:, 16:32].unsqueeze(2).to_broadcast([128, 16, 64]), op=ALU.mult),
                  reads=[ex.k(), Sst.k()], writes=[Sst.k()])
            yield
            S.add("dve", lambda e: e.tensor_tensor(out=Sst.ap, in0=Sst.ap, in1=pwb[:, :], op=ALU.add), reads=[PWB, Sst.k()], writes=[Sst.k()])
            yield
            Btm.free(); Xd.free(); ex.free(); dd.free()

        def ssd_tile(t, xsT, BT, CT, dtb, ab, zs, ynT, full, dbg):
            cols = slice(t * 128, (t + 1) * 128)
            a_t = ab.ap[:, t, :]
            dt_t = dtb.ap[:, t, :]
            if not full:
                yield from ssd_tile_p1(t, xsT, BT, dtb, ab)
                return
            Xtm = B16(1024)
            xstm = B16(1024) if full else None
            Btm = B16(2, 128)

            def trx(e):
                r = None
                for c in range(8):
                    r = e.transpose(out=ptr[:, c * 128:(c + 1) * 128], in_=xsT.ap[:, c, cols], identity=idb.ap)
                return r
            S.add("pe", trx, reads=[xsT.k(), idb.k()], writes=["ptr"])
            yield
            S.add("dve", lambda e: e.tensor_tensor(out=Xtm.ap.rearrange("p (h q) -> p h q", h=16),
                                                   in0=ptr[:, :].rearrange("p (h q) -> p h q", h=16),
                                                   in1=dt_t.unsqueeze(2).to_broadcast([128, 16, 64]), op=ALU.mult),
                  reads=["ptr", dtb.k(t)], writes=[Xtm.k()])
            yield
            if full:
                S.add("dve", lambda e: e.tensor_tensor(out=xstm.ap.rearrange("p (h q) -> p h q", h=16),
                                                       in0=ptr[:, :].rearrange("p (h q) -> p h q", h=16),
                                                       in1=cvec(C_DSK, 16).unsqueeze(2).to_broadcast([128, 16, 64]), op=ALU.mult),
                      reads=["ptr", cst.k()], writes=[xstm.k()])
                yield

            def trb(e):
                e.transpose(out=ptr[:, 0:128], in_=BT.ap[:, 0, cols], identity=idb.ap)
                return e.transpose(out=ptr[:, 128:256], in_=BT.ap[:, 1, cols], identity=idb.ap)
            S.add("pe", trb, reads=[BT.k(), idb.k()], writes=["ptr"])
            yield
            S.add("act", lambda e: e.activation(out=Btm.ap, in_=ptr[:, 0:256].rearrange("p (g n) -> p g n", g=2), func=AF.Copy),
                  reads=["ptr"], writes=[Btm.k()])
            yield
            def smalls(e):
                e.matmul(out=psm[:, 0:16], lhsT=tri.ap, rhs=a_t, start=True, stop=True)
                e.matmul(out=psm[:, 16:32], lhsT=stri.ap, rhs=a_t, start=True, stop=True)
                e.matmul(out=psm[:, 32:48], lhsT=onesc.ap[:, 0, :], rhs=a_t, start=True, stop=True)
                return e.matmul(out=psm[:, 48:64], lhsT=onesc.ap[:, 1, :], rhs=a_t, start=True, stop=True)
            S.add("pe", smalls, reads=[ab.k(t), tri.k(), stri.k(), onesc.k()], writes=["psm"])
            yield
            ex = B32(64)
            S.add("act", lambda e: e.activation(out=ex.ap, in_=psm[:, 0:64], func=AF.Exp), reads=["psm"], writes=[ex.k()])
            yield
            eacs = ex.ap[:, 0:16]
            if dbg:
                dbg_dump("Xtm", Xtm, 1024, BF16)
                dbg_dump("ex", ex, 64)
                dbg_dump("dtb", dtb, 16)
            ds = B32(2, 16)

            def mkds(e):
                e.tensor_scalar(out=ds.ap[:, 0, :], in0=ex.ap[:, 16:32], scalar1=cm.ap[:, 0:1], scalar2=None, op0=ALU.mult)
                return e.tensor_scalar(out=ds.ap[:, 1, :], in0=ex.ap[:, 16:32], scalar1=cm.ap[:, 1:2], scalar2=None, op0=ALU.mult)
            S.add("dve", mkds, reads=[ex.k(), cm.k()], writes=[ds.k()])
            yield
            Xd = [B16(1024), B16(1024)]
            for c in range(2):
                S.add("dve", lambda e, c=c: e.tensor_tensor(out=Xd[c].ap.rearrange("p (h q) -> p h q", h=16),
                                                            in0=Xtm.ap.rearrange("p (h q) -> p h q", h=16),
                                                            in1=ds.ap[:, c, :].unsqueeze(2).to_broadcast([128, 16, 64]), op=ALU.mult),
                      reads=[Xtm.k(), ds.k()], writes=[Xd[c].k()])
                yield
            MT = None
            if full:
                MT = B16(16, 128)
                for g in range(2):
                    rhs_all = B32(8, 128)
                    S.add("dve", lambda e, g=g, rhs_all=rhs_all: e.tensor_tensor(
                        out=rhs_all.ap, in0=tri.ap.unsqueeze(1).to_broadcast([128, 8, 128]),
                        in1=a_t[:, g * 8:(g + 1) * 8].unsqueeze(2).to_broadcast([128, 8, 128]), op=ALU.mult),
                        reads=[tri.k(), ab.k(t)], writes=[rhs_all.k()])
                    yield

                    def dmm(e, rhs_all=rhs_all):
                        e.matmul(out=pwa[:, 0:512], lhsT=stri.ap, rhs=rhs_all.ap[:, 0:4, :], start=True, stop=True)
                        return e.matmul(out=pwa[:, 512:1024], lhsT=stri.ap, rhs=rhs_all.ap[:, 4:8, :], start=True, stop=True)
                    S.add("pe", dmm, reads=[rhs_all.k(), stri.k()], writes=[PWA])
                    yield
                    E = B16(8, 128)

                    def eexp(e, E=E):
                        e.activation(out=E.ap[:, 0:4, :], in_=pwa[:, 0:512].rearrange("p (h l) -> p h l", h=4), func=AF.Exp)
                        return e.activation(out=E.ap[:, 4:8, :], in_=pwa[:, 512:1024].rearrange("p (h l) -> p h l", h=4), func=AF.Exp)
                    S.add("act", eexp, reads=[PWA], writes=[E.k()])
                    yield
                    S.add("pe", lambda e, g=g: e.matmul(out=psm[:, 128 + g * 128:256 + g * 128], lhsT=BT.ap[:, g, cols], rhs=CT.ap[:, g, cols],
                                                        start=True, stop=True), reads=[BT.k(), CT.k()], writes=["psm"])
                    yield
                    cbm = B32(128)
                    S.add("dve", lambda e, g=g, cbm=cbm: e.tensor_tensor(out=cbm.ap, in0=psm[:, 128 + g * 128:256 + g * 128], in1=tri.ap, op=ALU.mult),
                          reads=["psm", tri.k()], writes=[cbm.k()])
                    yield
                    S.add("dve", lambda e, g=g, cbm=cbm, E=E: e.tensor_tensor(out=MT.ap[:, g * 8:(g + 1) * 8, :], in0=E.ap,
                                                                               in1=cbm.ap.unsqueeze(1).to_broadcast([128, 8, 128]), op=ALU.mult),
                          reads=[E.k(), cbm.k()], writes=[MT.k(g * 8, (g + 1) * 8)])
                    yield
                    rhs_all.free(); E.free(); cbm.free()
                for c in range(2):
                    S.add("act", lambda e, c=c: e.activation(out=ctpad[c].ap[:, :, c * 64:(c + 1) * 64],
                                                             in_=CT.ap[:, :, t * 128 + c * 64:t * 128 + (c + 1) * 64], func=AF.Copy),
                          reads=[CT.k()], writes=[ctpad[c].k()])
                    yield
            Sbf = [B16(1024), B16(1024)] if full else None
            for c in range(2):
                def sloc(e, c=c):
                    e.matmul(out=pwb[:, 0:512], lhsT=Btm.ap[:, 0, :], rhs=Xd[c].ap[:, 0:512], start=True, stop=True)
                    return e.matmul(out=pwb[:, 512:1024], lhsT=Btm.ap[:, 1, :], rhs=Xd[c].ap[:, 512:1024], start=True, stop=True)
                S.add("pe", sloc, reads=[Btm.k(), Xd[c].k()], writes=[PWB])
                yield
                if full:
                    S.add("act", lambda e, c=c: e.activation(out=Sbf[c].ap, in_=Sst.ap, func=AF.Copy), reads=[Sst.k()], writes=[Sbf[c].k()])
                    yield

                def supd(e, c=c):
                    e.tensor_tensor(out=dtot.ap[:, 0:16], in0=dtot.ap[:, 0:16], in1=ex.ap[:, 32 + 16 * c:48 + 16 * c], op=ALU.mult)
                    return e.tensor_tensor(out=Sst.ap.rearrange("p (h q) -> p h q", h=16), in0=Sst.ap.rearrange("p (h q) -> p h q", h=16),
                                           in1=ex.ap[:, 32 + 16 * c:48 + 16 * c].unsqueeze(2).to_broadcast([128, 16, 64]), op=ALU.mult)
                S.add("dve", supd, reads=[ex.k(), Sst.k(), dtot.kr(0, 16)], writes=[Sst.k(), dtot.kr(0, 16)])
                yield
                S.add("dve", lambda e: e.tensor_tensor(out=Sst.ap, in0=Sst.ap, in1=pwb[:, :], op=ALU.add), reads=[PWB, Sst.k()], writes=[Sst.k()])
                yield
            Xd[0].free(); Xd[1].free(); Btm.free(); ds.free()
            if not full:
                Xtm.free(); ex.free()
                return
            def ymm(e):
                r = None
                for b in range(2):
                    e.matmul(out=pwa[:, b * 512:(b + 1) * 512], lhsT=idb.ap, rhs=xstm.ap[:, b * 512:(b + 1) * 512], start=True, stop=False)
                    for hh in range(8):
                        h = b * 8 + hh
                        r = e.matmul(out=pwa[:, h * 64:(h + 1) * 64], lhsT=MT.ap[:, h, :], rhs=Xtm.ap[:, h * 64:(h + 1) * 64],
                                     start=False, stop=(hh == 7))
                return r
            S.add("pe", ymm, reads=[idb.k(), xstm.k(), MT.k(), Xtm.k()], writes=[PWA])
            yield

            def yoff(e):
                r = None
                for g in range(2):
                    for c in range(2):
                        r = e.matmul(out=pwb[:, g * 512:(g + 1) * 512], lhsT=ctpad[c].ap[:, g, :], rhs=Sbf[c].ap[:, g * 512:(g + 1) * 512],
                                     start=(c == 0), stop=(c == 1))
                return r
            S.add("pe", yoff, reads=[ctpad[0].k(), ctpad[1].k(), Sbf[0].k(), Sbf[1].k()], writes=[PWB])
            yield
            yt = B32(1024)

            S.add("dve", lambda e: e.tensor_tensor(out=yt.ap.rearrange("p (h q) -> p h q", h=16), in0=pwb[:, :].rearrange("p (h q) -> p h q", h=16),
                                                   in1=eacs.unsqueeze(2).to_broadcast([128, 16, 64]), op=ALU.mult),
                  reads=[PWB, ex.k()], writes=[yt.k()])
            yield
            S.add("dve", lambda e: e.tensor_tensor(out=yt.ap, in0=yt.ap, in1=pwa[:, :], op=ALU.add), reads=[PWA, yt.k()], writes=[yt.k()])
            yield
            S.add("dve", lambda e: e.tensor_tensor(out=yt.ap, in0=yt.ap, in1=zs.ap[:, t, :], op=ALU.mult), reads=[yt.k(), zs.k(t)], writes=[yt.k()])
            yield
            if dbg:
                dbg_dump("yt", yt, 1024)
                dbg_dump("MT", MT, 2048, BF16)
                dbg_dump("Sbf1", Sbf[1], 1024, BF16)
            Xtm.free(); xstm.free(); MT.free(); Sbf[0].free(); Sbf[1].free(); ex.free()
            junk = B16(1024)
            ss = B32(2)

            def ysq(e):
                e.activation(out=junk.ap[:, 0:512], in_=yt.ap[:, 0:512], func=AF.Square, accum_out=ss.ap[:, 0:1])
                return e.activation(out=junk.ap[:, 512:1024], in_=yt.ap[:, 512:1024], func=AF.Square, accum_out=ss.ap[:, 1:2])
            S.add("act", ysq, reads=[yt.k()], writes=[junk.k(), ss.k()])
            yield
            chain("act", [lambda e: e.activation(out=ss.ap, in_=ss.ap, func=AF.Ln, bias=EPS, scale=1.0 / 512.0),
                          lambda e: e.activation(out=ss.ap, in_=ss.ap, func=AF.Exp, scale=-0.5)], [], [ss.k()])
            yield
            yn = B16(1024)

            def ynorm(e):
                e.scalar_tensor_tensor(out=yn.ap[:, 0:512], in0=yt.ap[:, 0:512], scalar=ss.ap[:, 0:1], in1=cvec(C_SSDN, 512), op0=ALU.mult, op1=ALU.mult)
                return e.scalar_tensor_tensor(out=yn.ap[:, 512:1024], in0=yt.ap[:, 512:1024], scalar=ss.ap[:, 1:2], in1=cvec(C_SSDN + 512, 512),
                                              op0=ALU.mult, op1=ALU.mult)
            S.add("dve", ynorm, reads=[yt.k(), ss.k(), cst.k()], writes=[yn.k()])
            yield

            def try_(e):
                r = None
                for c in range(8):
                    r = e.transpose(out=ptr[:, c * 128:(c + 1) * 128], in_=yn.ap[:, c * 128:(c + 1) * 128], identity=idb.ap)
                return r
            S.add("pe", try_, reads=[yn.k(), idb.k()], writes=["ptr"])
            yield
            S.add("act", lambda e: e.activation(out=ynT.ap[:, :, cols], in_=ptr[:, :].rearrange("p (c n) -> p c n", c=8), func=AF.Copy),
                  reads=["ptr"], writes=[ynT.k()])
            yield
            yt.free(); junk.free(); ss.free(); yn.free()

        def gla_tile(t, qT, kT, ktm, vtm, rs, onT, full, dbg=False):
            cols = slice(t * 128, (t + 1) * 128)
            p0, pk0 = next_pa()
            S.add("pe", lambda e: e.matmul(out=p0[:, 0:512], lhsT=a1T.ap[:, cols], rhs=w2a.ap, start=True, stop=True),
                  reads=[a1T.k(), w2a.k()], writes=[pk0])
            yield
            la = B32(512)

            S.add("act", lambda e: e.activation(out=la.ap, in_=p0[:, 0:512], func=AF.Exp, scale=-1.0), reads=[pk0], writes=[la.k()])
            yield
            S.add("act", lambda e: e.activation(out=la.ap, in_=la.ap, func=AF.Ln, bias=1.0), reads=[la.k()], writes=[la.k()])
            yield
            if dbg:
                dbg_dump("la", la, 512)
            if not full:
                pd, pkd = next_pa()

                def dmm_(e):
                    r = None
                    for hd in range(4):
                        r = e.matmul(out=pd[:, hd * 2:hd * 2 + 2], lhsT=la.ap[:, hd * 128:(hd + 1) * 128], rhs=negc.ap, start=True, stop=True)
                    return r
                S.add("pe", dmm_, reads=[la.k(), negc.k()], writes=[pkd])
                dec8 = B32(8)
                S.add("act", lambda e: e.activation(out=dec8.ap, in_=pd[:, 0:8], func=AF.Exp), reads=[pkd], writes=[dec8.k()])
                yield
                pe_, pke = next_pa()
                S.add("pe", lambda e: e.matmul(out=pe_[:, 0:512], lhsT=strigf.ap, rhs=la.ap, start=True, stop=True), reads=[strigf.k(), la.k()], writes=[pke])
                khf = B16(512)
                Ekf = B32(512)
                S.add("act", lambda e: e.activation(out=Ekf.ap, in_=pe_[:, 0:512], func=AF.Exp), reads=[pke], writes=[Ekf.k()])
                yield
                S.add("dve", lambda e: e.tensor_tensor(out=khf.ap, in0=Ekf.ap, in1=ktm.ap[:, t, :], op=ALU.mult), reads=[Ekf.k(), ktm.k(t)], writes=[khf.k()])
                yield
                for hp in range(2):
                    pu, pku = next_pa()

                    def umm_(e, hp=hp, pu=pu):
                        r = None
                        for h2 in range(2):
                            hd = hp * 2 + h2
                            r = e.matmul(out=pu[:, h2 * 256:(h2 + 1) * 256], lhsT=khf.ap[:, hd * 128:(hd + 1) * 128],
                                         rhs=vtm.ap[:, t, hd * 256:(hd + 1) * 256], start=True, stop=True)
                        return r
                    S.add("pe", umm_, reads=[khf.k(), vtm.k(t)], writes=[pku])

                    def gupd_(e, hp=hp, pu=pu):
                        r = None
                        for h2 in range(2):
                            hd = hp * 2 + h2
                            r = e.scalar_tensor_tensor(out=Sg.ap[:, hd * 256:(hd + 1) * 256], in0=Sg.ap[:, hd * 256:(hd + 1) * 256],
                                                       scalar=dec8.ap[:, hd * 2:hd * 2 + 1], in1=pu[:, h2 * 256:(h2 + 1) * 256],
                                                       op0=ALU.mult, op1=ALU.add)
                        return r
                    S.add("dve", gupd_, reads=[pku, dec8.k(), Sg.kr(hp * 512, hp * 512 + 512)], writes=[Sg.kr(hp * 512, hp * 512 + 512)])
                    yield
                la.free(); dec8.free(); khf.free(); Ekf.free()
                return
            p1, pk1 = next_pa()

            def bc(e):
                r = None
                for hd in range(4):
                    r = e.matmul(out=p1[:, hd * 128:(hd + 1) * 128], lhsT=la.ap[:, hd * 128:(hd + 1) * 128], rhs=trig.ap, start=True, stop=True)
                return r
            S.add("pe", bc, reads=[la.k(), trig.k()], writes=[pk1])
            yield
            EqT = B32(4, 128)
            S.add("act", lambda e: e.activation(out=EqT.ap, in_=p1[:, 0:512].rearrange("p (h l) -> p h l", h=4), func=AF.Exp),
                  reads=[pk1], writes=[EqT.k()])
            yield
            ktT = None
            if full:
                EkT = B32(4, 128)
                S.add("act", lambda e: e.activation(out=EkT.ap, in_=p1[:, 0:512].rearrange("p (h l) -> p h l", h=4), func=AF.Exp, scale=-1.0),
                      reads=[pk1], writes=[EkT.k()])
                yield
                ktT = B16(4, 128)
                S.add("dve", lambda e: e.tensor_tensor(out=ktT.ap, in0=kT.ap[:, :, cols], in1=EkT.ap, op=ALU.mult),
                      reads=[kT.k(), EkT.k()], writes=[ktT.k()])
                yield
                for c in range(2):
                    S.add("dve", lambda e, c=c: e.tensor_tensor(out=qtpad[c].ap[:, :, c * 64:(c + 1) * 64],
                                                                in0=qT.ap[:, :, t * 128 + c * 64:t * 128 + (c + 1) * 64],
                                                                in1=EqT.ap[:, :, c * 64:(c + 1) * 64], op=ALU.mult),
                          reads=[qT.k(), EqT.k()], writes=[qtpad[c].k()])
                    yield
                EkT.free()
            p2, pk2 = next_pa()
            S.add("pe", lambda e: e.matmul(out=p2[:, 0:512], lhsT=strig.ap, rhs=la.ap, start=True, stop=True), reads=[strig.k(), la.k()], writes=[pk2])
            yield
            Ekh = B32(512)
            S.add("act", lambda e: e.activation(out=Ekh.ap, in_=p2[:, 0:512], func=AF.Exp), reads=[pk2], writes=[Ekh.k()])
            yield
            kh = [B16(512), B16(512)]
            for c in range(2):
                S.add("dve", lambda e, c=c: e.scalar_tensor_tensor(out=kh[c].ap, in0=Ekh.ap, scalar=cm.ap[:, c:c + 1], in1=ktm.ap[:, t, :],
                                                                   op0=ALU.mult, op1=ALU.mult),
                      reads=[Ekh.k(), cm.k(), ktm.k(t)], writes=[kh[c].k()])
                yield
            la.free(); Ekh.free()
            attT = None
            if full:
                p3, pk3 = next_pa()

                def att(e):
                    r = None
                    for hd in range(4):
                        e.matmul(out=p3[:, hd * 128:(hd + 1) * 128], lhsT=ktT.ap[:, hd, :], rhs=qtpad[0].ap[:, hd, :], start=True, stop=False)
                        r = e.matmul(out=p3[:, hd * 128:(hd + 1) * 128], lhsT=ktT.ap[:, hd, :], rhs=qtpad[1].ap[:, hd, :], start=False, stop=True)
                    return r
                S.add("pe", att, reads=[ktT.k(), qtpad[0].k(), qtpad[1].k()], writes=[pk3])
                yield
                attT = B16(4, 128)
                S.add("dve", lambda e: e.tensor_tensor(out=attT.ap, in0=p3[:, 0:512].rearrange("p (h l) -> p h l", h=4),
                                                       in1=tri.ap.unsqueeze(1).to_broadcast([128, 4, 128]), op=ALU.mult),
                      reads=[pk3, tri.k()], writes=[attT.k()])
                yield
                ktT.free()
            Sgb = [B16(1024), B16(1024)] if full else None
            for c in range(2):
                if full:
                    S.add("act", lambda e, c=c: e.activation(out=Sgb[c].ap, in_=Sg.ap, func=AF.Copy), reads=[Sg.k()], writes=[Sgb[c].k()])
                    yield
                for hp in range(2):
                    pu, pku = next_pa()

                    def umm(e, c=c, hp=hp, pu=pu):
                        r = None
                        for h2 in range(2):
                            hd = hp * 2 + h2
                            r = e.matmul(out=pu[:, h2 * 256:(h2 + 1) * 256], lhsT=kh[c].ap[:, hd * 128:(hd + 1) * 128],
                                         rhs=vtm.ap[:, t, hd * 256:(hd + 1) * 256], start=True, stop=True)
                        return r
                    S.add("pe", umm, reads=[kh[c].k(), vtm.k(t)], writes=[pku])
                    yield

                    def gupd(e, c=c, hp=hp, pu=pu):
                        r = None
                        for h2 in range(2):
                            hd = hp * 2 + h2
                            r = e.scalar_tensor_tensor(out=Sg.ap[:, hd * 256:(hd + 1) * 256], in0=Sg.ap[:, hd * 256:(hd + 1) * 256],
                                                       scalar=EqT.ap[:, hd, c * 64 + 63:c * 64 + 64], in1=pu[:, h2 * 256:(h2 + 1) * 256],
                                                       op0=ALU.mult, op1=ALU.add)
                        return r
                    S.add("dve", gupd, reads=[pku, EqT.k(), Sg.kr(hp * 512, hp * 512 + 512)], writes=[Sg.kr(hp * 512, hp * 512 + 512)])
                    yield
                    yield
            kh[0].free(); kh[1].free(); EqT.free()
            if not full:
                return

            junk = B16(1024)
            ss = B32(4)
            pos = []
            for hp in range(2):
                po, pko = next_pa()
                pos.append((po, pko))

                def omm(e, hp=hp, po=po):
                    r = None
                    for h2 in range(2):
                        hd = hp * 2 + h2
                        o = po[:, h2 * 256:(h2 + 1) * 256]
                        e.matmul(out=o, lhsT=attT.ap[:, hd, :], rhs=vtm.ap[:, t, hd * 256:(hd + 1) * 256], start=True, stop=False)
                        e.matmul(out=o, lhsT=qtpad[0].ap[:, hd, :], rhs=Sgb[0].ap[:, hd * 256:(hd + 1) * 256], start=False, stop=False)
                        r = e.matmul(out=o, lhsT=qtpad[1].ap[:, hd, :], rhs=Sgb[1].ap[:, hd * 256:(hd + 1) * 256], start=False, stop=True)
                    return r
                S.add("pe", omm, reads=[attT.k(), vtm.k(t), qtpad[0].k(), qtpad[1].k(), Sgb[0].k(), Sgb[1].k()], writes=[pko])
                yield

                def osq(e, hp=hp, po=po):
                    r = None
                    for h2 in range(2):
                        hd = hp * 2 + h2
                        r = e.activation(out=junk.ap[:, hd * 256:(hd + 1) * 256], in_=po[:, h2 * 256:(h2 + 1) * 256], func=AF.Square,
                                         accum_out=ss.ap[:, hd:hd + 1])
                    return r
                S.add("act", osq, reads=[pko], writes=[junk.kr(hp * 512, hp * 512 + 512), ss.k()])
                yield
                yield
            attT.free(); Sgb[0].free(); Sgb[1].free()
            chain("act", [lambda e: e.activation(out=ss.ap, in_=ss.ap, func=AF.Ln, bias=EPS, scale=1.0 / 256.0),
                          lambda e: e.activation(out=ss.ap, in_=ss.ap, func=AF.Exp, scale=-0.5)], [], [ss.k()])
            yield
            yield
            on = B32(1024)
            onb = B16(1024)
            for hp in range(2):
                po, pko = pos[hp]

                def onorm(e, hp=hp, po=po):
                    r = None
                    for h2 in range(2):
                        hd = hp * 2 + h2
                        r = e.scalar_tensor_tensor(out=on.ap[:, hd * 256:(hd + 1) * 256], in0=po[:, h2 * 256:(h2 + 1) * 256], scalar=ss.ap[:, hd:hd + 1],
                                                   in1=cvec(C_GLAN, 256), op0=ALU.mult, op1=ALU.mult)
                    return r
                S.add("dve", onorm, reads=[pko, ss.k(), cst.k()], writes=[on.kr(hp * 512, hp * 512 + 512)])
                yield
            S.add("dve", lambda e: e.tensor_tensor(out=onb.ap, in0=on.ap, in1=rs.ap[:, t, :], op=ALU.mult), reads=[on.k(), rs.k(t)], writes=[onb.k()])
            yield
            yield

            def tro(e):
                r = None
                for c in range(8):
                    r = e.transpose(out=ptr[:, c * 128:(c + 1) * 128], in_=onb.ap[:, c * 128:(c + 1) * 128], identity=idb.ap)
                return r
            S.add("pe", tro, reads=[onb.k(), idb.k()], writes=["ptr"])
            yield
            S.add("act", lambda e: e.activation(out=onT.ap[:, :, cols], in_=ptr[:, :].rearrange("p (c n) -> p c n", c=8), func=AF.Copy),
                  reads=["ptr"], writes=[onT.k()])
            yield
            junk.free(); ss.free(); on.free(); onb.free()

        def prologue_mem():
            mn = B16(2, 1024)
            for mc in range(2):
                ms = B32(1024)
                S.add("sp", lambda e, ms=ms, mc=mc: [e.dma_start(out=ms.ap, in_=mem_d[mc * 128:(mc + 1) * 128, :])], writes=[ms.k()], dma=True, dkey="ms%d" % ms.off)
                junk = B16(1024)
                ss = B32(1)
                S.add("act", lambda e, ms=ms, junk=junk, ss=ss: e.activation(out=junk.ap, in_=ms.ap, func=AF.Square, accum_out=ss.ap),
                      reads=[ms.k()], writes=[junk.k(), ss.k()])
                chain("act", [lambda e, ss=ss: e.activation(out=ss.ap, in_=ss.ap, func=AF.Ln, bias=EPS, scale=1.0 / 1024.0),
                              lambda e, ss=ss: e.activation(out=ss.ap, in_=ss.ap, func=AF.Exp, scale=-0.5)], [], [ss.k()])
                S.add("dve", lambda e, ms=ms, ss=ss, mc=mc: e.scalar_tensor_tensor(out=mn.ap[:, mc, :], in0=ms.ap, scalar=ss.ap[:, 0:1], in1=cvec(C_MEMN, 1024),
                                                                                   op0=ALU.mult, op1=ALU.mult),
                      reads=[ms.k(), ss.k(), cst.k()], writes=[mn.k(mc)])
                ms.free(); junk.free(); ss.free()
            mnT = B16(8, 256)
            for mc in range(2):
                def trm(e, mc=mc):
                    r = None
                    for c in range(8):
                        r = e.transpose(out=ptr[:, c * 128:(c + 1) * 128], in_=mn.ap[:, mc, c * 128:(c + 1) * 128], identity=idb.ap)
                    return r
                S.add("pe", trm, reads=[mn.k(mc), idb.k()], writes=["ptr"])
                S.add("act", lambda e, mc=mc: e.activation(out=mnT.ap[:, :, mc * 128:(mc + 1) * 128], in_=ptr[:, :].rearrange("p (c n) -> p c n", c=8), func=AF.Copy),
                      reads=["ptr"], writes=[mnT.k()])
            mn.free()

            def k_epi(c0, nch, pv, pk):
                S.add("act", lambda e: e.activation(out=KT.ap[:, c0:c0 + nch, :], in_=pv, func=AF.Copy), reads=[pk], writes=[KT.k(c0, c0 + nch)])
            proj_fm(w_xkv, 0, 1024, mnT, 256, k_epi, blk=256)

            def v_epi(t, col, cw, p, pk):
                S.add("act", lambda e: e.activation(out=Vx.ap[:, t, col:col + cw], in_=p, func=AF.Copy), reads=[pk],
                      writes=[Vx.kr(t * 1024 + col, t * 1024 + col + cw)])
            proj_tm(w_xkv, 1024, 1024, mnT, 2, v_epi)
            mnT.free()

        prologue_mem()
        for s_ in range(NPRE // NT):
            step(s_ * NT, NT, "p1")
            if (s_ + 1) % NSTEP == 0:
                zi = cst.ap[:, C_CMASK + (s_ + 1) // NSTEP - 1:C_CMASK + (s_ + 1) // NSTEP]

                def zs_(e, zi=zi):
                    e.tensor_scalar(out=Sst.ap, in0=Sst.ap, scalar1=zi, scalar2=None, op0=ALU.mult)
                    return e.tensor_scalar(out=Sg.ap, in0=Sg.ap, scalar1=zi, scalar2=None, op0=ALU.mult)
                S.add("dve", zs_, reads=[Sst.k(), Sg.k(), cst.k()], writes=[Sst.k(), Sg.k()])
        for s_ in range(NSTEP):
            step(NPRE + s_ * NT, NT, "p2", orow0=s_ * NT, first_dbg=(s_ == 0))
        print("arena peaks: A16 %d / %d, A32 %d / %d; ops %d" % (a16.peak, N16, a32.peak, N32, len(S.ops)))
        S.emit(nc, st)
    return nc, dbg_outs


_CACHE = {}


def host_inputs(x, mem, norm_mix, w_in, ssd_conv_w, ssd_conv_b, ssd_dt_bias, ssd_A_log, ssd_D, ssd_norm,
                gla_w_a2, gla_b_a, gla_norm, w_up_ssd, w_up_gla, w_o, norm_xattn, norm_mem, w_xq, w_xkv,
                w_xo, norm_ffn, w_ffn_in, w_ffn_out, norm_final):
    f = lambda a: np.ascontiguousarray(np.asarray(a, dtype=np.float32))
    x = f(x); mem = f(mem)

    def fm(g):
        return f(g).reshape(8, 128).T

    def rep(v):
        v = f(v).reshape(1, -1)
        return np.broadcast_to(v, (128, v.shape[1]))
    cst = np.zeros((128, NCST), np.float32)
    cst[:, C_GMIX:C_GMIX + 8] = fm(norm_mix[0])
    cst[:, C_GXA:C_GXA + 8] = fm(norm_xattn[0])
    cst[:, C_GFFN:C_GFFN + 8] = fm(norm_ffn[0])
    cst[:, C_GFIN:C_GFIN + 8] = fm(norm_final)
    cw = f(ssd_conv_w[0])[:, 0, :]
    cst[:, C_CONVW:C_CONVW + 48] = cw.reshape(4, 12, 128).transpose(2, 1, 0).reshape(128, 48)
    cst[:, C_CONVB:C_CONVB + 12] = f(ssd_conv_b[0]).reshape(12, 128).T
    cst[:, C_DTB:C_DTB + 16] = rep(ssd_dt_bias[0])
    cst[:, C_ALOG:C_ALOG + 16] = rep(ssd_A_log[0])
    cst[:, C_DSK:C_DSK + 16] = rep(ssd_D[0])
    cst[:, C_SSDN:C_SSDN + 1024] = rep(ssd_norm[0])
    cst[:, C_GLAN:C_GLAN + 256] = rep(gla_norm[0])
    cst[:, C_MEMN:C_MEMN + 1024] = rep(norm_mem[0])
    w2aug = np.concatenate([f(gla_w_a2[0]), f(gla_b_a[0]).reshape(1, 512)], 0)
    shared = {"w2aug": f(w2aug), "w_in": f(w_in[0]), "w_up_ssd": f(w_up_ssd[0]), "w_up_gla": f(w_up_gla[0]), "w_o": f(w_o[0]),
              "w_xq": f(w_xq[0]), "w_xkv": f(w_xkv[0]), "w_xo": f(w_xo[0]), "w_ffn_in": f(w_ffn_in[0]), "w_ffn_out": f(w_ffn_out[0])}
    in_maps = []
    for c in range(NCORES):
        b, j = divmod(c, 4)
        xe = np.zeros((4 * SEG, D), np.float32)
        xe[(3 - j) * SEG:3 * SEG] = x[b, 0:j * SEG]
        xe[3 * SEG:] = x[b, j * SEG:(j + 1) * SEG]
        cc = cst.copy()
        for i in range(3):
            cc[:, C_CMASK + i] = 1.0 if i >= 3 - j else 0.0
        m = {"x_ext": xe, "mem_b": mem[b], "cst": cc}
        m.update(shared)
        in_maps.append(m)
    return in_maps


def kernel(**inputs):
    if "nc" not in _CACHE:
        _CACHE["nc"] = build_program()
    nc, dbg = _CACHE["nc"]
    in_maps = host_inputs(**inputs)
    res = run_bass_kernel_spmd(nc, in_maps, core_ids=list(range(NCORES)))
    _CACHE["last"] = res
    out = np.zeros((2, 4 * SEG, D), np.float32)
    for c in range(NCORES):
        b, j = divmod(c, 4)
        out[b, j * SEG:(j + 1) * SEG] = res.results[c]["out"]
    return out
```

```python
import numpy as np
from contextlib import ExitStack
import concourse.bass as bass
import concourse.mybir as mybir
from concourse.bass_utils import run_bass_kernel_spmd

F32 = mybir.dt.float32
BF16 = mybir.dt.bfloat16
AF = mybir.ActivationFunctionType
ALU = mybir.AluOpType

NCORES = 8
D = 1024
SEG = 2048
NT = 512
HALO = 128
EPS = 1e-6
EPOCH = 8000
DBG = False

O_Z, O_XBC, O_DT, O_Q, O_K, O_V, O_R, O_A1, O_GS, O_GG = 0, 1024, 2560, 2576, 3088, 3600, 4624, 5648, 5664, 6688
DFF = 2816

C_GMIX, C_GXA, C_GFFN, C_GFIN = 0, 8, 16, 24
C_CONVW = 32
C_CONVB = 80
C_DTB, C_ALOG, C_DSK = 92, 108, 124
C_SSDN = 140
C_GLAN = 1164
C_MEMN = 1420
C_CMASK = 2444
NCST = 2452


class Op:
    __slots__ = ("eng", "fn", "deps", "sig", "signal", "is_dma", "dkey", "ndma", "name", "inc")


class Sched:
    ENGS = ("pe", "act", "dve", "pool", "sp")

    def __init__(self):
        self.ops = []
        self.recs = {}

    @staticmethod
    def _k(k):
        return (k, 0, 1) if isinstance(k, str) else k

    def add(self, eng, fn, reads=(), writes=(), dma=False, dkey=None, ndma=1, name="", inc=16):
        op = Op()
        op.eng, op.fn, op.is_dma, op.dkey, op.ndma, op.inc, op.name = eng, fn, dma, dkey, ndma, inc, name
        op.sig = False
        op.signal = None
        deps = []
        seen = set()

        def push(d, raw):
            if d is None or d is op or id(d) in seen:
                return
            if not raw and not (d.is_dma or dma or d.eng != eng):
                return
            seen.add(id(d))
            deps.append(d)

        for k in reads:
            a, lo, hi = self._k(k)
            for r in self.recs.get(a, ()):
                if r[0] < hi and lo < r[1]:
                    push(r[2], True)
                    r[3].append(op)
        for k in writes:
            a, lo, hi = self._k(k)
            lst = self.recs.setdefault(a, [])
            new = []
            for r in lst:
                if r[0] < hi and lo < r[1]:
                    push(r[2], False)
                    for rd in r[3]:
                        push(rd, False)
                    if r[0] < lo:
                        new.append([r[0], lo, r[2], list(r[3])])
                    if hi < r[1]:
                        new.append([hi, r[1], r[2], list(r[3])])
                else:
                    new.append(r)
            new.append([lo, hi, op, []])
            self.recs[a] = new
        if eng == "pe" and not dma:
            deps = [d for d in deps if d.is_dma or d.eng != "pe"]
        op.deps = deps
        for d in deps:
            d.sig = True
        self.ops.append(op)
        return op

    def emit(self, nc, stack):
        eng_count = {e: 0 for e in self.ENGS}
        eng_sems = {e: [] for e in self.ENGS}
        dma_sems = {}
        dma_vals = {}
        for op in self.ops:
            if op.is_dma:
                if op.dkey not in dma_sems:
                    dma_sems[op.dkey] = stack.enter_context(nc.semaphore("d%d" % len(dma_sems)))
                    dma_vals[op.dkey] = 0
                dma_vals[op.dkey] += op.inc * op.ndma
                op.signal = (dma_sems[op.dkey], dma_vals[op.dkey])
            elif op.sig:
                c = eng_count[op.eng]
                ep = c // EPOCH
                if ep >= len(eng_sems[op.eng]):
                    eng_sems[op.eng].append(stack.enter_context(nc.semaphore("e%s%d" % (op.eng, ep))))
                op.signal = (eng_sems[op.eng][ep], c % EPOCH + 1)
                eng_count[op.eng] = c + 1
        by_eng = {e: [o for o in self.ops if o.eng == e] for e in self.ENGS}
        finals = {}
        for op in self.ops:
            if op.is_dma and op.name.startswith("OUT"):
                sem, val = op.signal
                if finals.get(id(sem), (None, 0))[1] < val:
                    finals[id(sem)] = (sem, val)

        def run(engh, ename):
            waited = {}
            for op in by_eng[ename]:
                for d in op.deps:
                    sem, val = d.signal
                    if waited.get(id(sem), 0) < val:
                        engh.wait_ge(sem, val)
                        waited[id(sem)] = val
                r = op.fn(engh)
                if op.is_dma:
                    assert len(r) == op.ndma, (op.name, len(r), op.ndma)
                    for ins in r:
                        ins.then_inc(op.signal[0], op.inc)
                elif op.sig:
                    r.then_inc(op.signal[0], 1)
            if ename == "sp":
                for sem, val in finals.values():
                    engh.wait_ge(sem, val)

        with nc.Block() as block:
            @block.tensor
            def _(e):
                run(e, "pe")

            @block.scalar
            def _(e):
                run(e, "act")

            @block.vector
            def _(e):
                run(e, "dve")

            @block.gpsimd
            def _(e):
                run(e, "pool")

            @block.sync
            def _(e):
                run(e, "sp")


class Arena:
    def __init__(self, name, tensor, n):
        self.name, self.t, self.n = name, tensor, n
        self.used = []
        self.peak = 0

    def alloc(self, n):
        n = (n + 15) // 16 * 16
        self.used.sort()
        pos = 0
        for off, sz in self.used:
            if off - pos >= n:
                break
            pos = off + sz
        if pos + n > self.n:
            raise RuntimeError("arena %s full: need %d at %d of %d" % (self.name, n, pos, self.n))
        self.used.append((pos, n))
        self.peak = max(self.peak, pos + n)
        return pos

    def free(self, off):
        self.used = [u for u in self.used if u[0] != off]


class Buf:
    def __init__(self, arena, shape):
        self.arena = arena
        self.shape = list(shape)
        self.n = int(np.prod(shape))
        self.off = arena.alloc(self.n)
        self.inner = self.n // self.shape[0] if len(shape) == 2 else self.n

    def free(self):
        self.arena.free(self.off)

    @property
    def ap(self):
        a = self.arena.t[:, self.off:self.off + self.n]
        if len(self.shape) == 2:
            return a.rearrange("p (a b) -> p a b", a=self.shape[0])
        return a

    def k(self, i=None, j=None):
        if i is None:
            return (self.arena.name, self.off, self.off + self.n)
        if j is None:
            j = i + 1
        return (self.arena.name, self.off + i * self.inner, self.off + j * self.inner)

    def kr(self, lo, hi):
        return (self.arena.name, self.off + lo, self.off + hi)


def build_program():
    nc = bass.Bass("TRN2", target_bir_lowering=False)
    TPS = NT // 128
    NSTEP = SEG // NT

    def din(name, shape):
        return nc.dram_tensor(name, list(shape), F32, kind="ExternalInput").ap()

    NPRE = 3 * SEG
    x_d = din("x_ext", [NPRE + SEG, D])
    mem_d = din("mem_b", [256, D])
    cst_d = din("cst", [128, NCST])
    w2_d = din("w2aug", [17, 512])
    w_in = din("w_in", [D, 7712])
    w_ups = din("w_up_ssd", [D, D])
    w_upg = din("w_up_gla", [D, D])
    w_o = din("w_o", [D, D])
    w_xq = din("w_xq", [D, D])
    w_xkv = din("w_xkv", [D, 2 * D])
    w_xo = din("w_xo", [D, D])
    w_fi = din("w_ffn_in", [D, 2 * DFF])
    w_fo = din("w_ffn_out", [DFF, D])
    out_d = nc.dram_tensor("out", [SEG, D], F32, kind="ExternalOutput").ap()
    dbg_outs = {}

    S = Sched()

    def chain(eng, fns, reads, writes):
        for f in fns:
            S.add(eng, f, reads=list(reads) + list(writes), writes=writes)
    st = ExitStack()
    with st:
        def sb(name, shape, dt):
            return st.enter_context(nc.sbuf_tensor(name, list(shape), dt))

        def pst(name, shape, dt):
            return st.enter_context(nc.psum_tensor(name, list(shape), dt))

        N16 = 70000
        N32 = 18000
        a16 = Arena("A16", sb("A16", [128, N16], BF16), N16)
        a32 = Arena("A32", sb("A32", [128, N32], F32), N32)

        def B16(*shape):
            return Buf(a16, shape)

        def B32(*shape):
            return Buf(a32, shape)

        pa = [pst("pa0", [128, 512], F32), pst("pa1", [128, 512], F32)]
        ptr = pst("ptr", [128, 1024], BF16)
        psm = pst("psm", [128, 512], F32)
        pwa = pst("pwa", [128, 1024], F32)
        pwb = pst("pwb", [128, 1024], F32)
        PWA = ("pwa", 0, 2)
        PWB = ("pwb", 0, 2)
        pa_i = [0]
        pa6_i = [0]
        banks6 = [(pa[0], "pa0"), (pa[1], "pa1"), (pwa[:, 0:512], ("pwa", 0, 1)), (pwa[:, 512:1024], ("pwa", 1, 2)),
                  (pwb[:, 0:512], ("pwb", 0, 1)), (pwb[:, 512:1024], ("pwb", 1, 2))]

        def next_pa():
            i = pa_i[0]
            pa_i[0] ^= 1
            return pa[i], "pa%d" % i

        def next_pa6():
            i = pa6_i[0]
            pa6_i[0] = (i + 1) % 6
            return banks6[i]

        cst = B32(NCST)
        w2a = B32(512)
        idf = B32(128)
        tri = B32(128)
        stri = B32(128)
        trig = B32(128)
        strig = B32(128)
        strif = B32(128)
        strigf = B32(128)
        onesf = B32(128)
        negc = B32(2)
        onesc = B32(2, 128)
        cm = B32(2)
        idb = B16(128)
        ones_mean = B16(128)
        ones1 = B16(128)
        aneg = B32(16)

        S.add("sp", lambda e: [e.dma_start(out=cst.ap, in_=cst_d)], writes=[cst.k()], dma=True, dkey="cst")
        S.add("dve", lambda e: e.memset(w2a.ap, 0.0), writes=[w2a.k()])
        S.add("sp", lambda e: [e.dma_start(out=w2a.ap[0:17, :], in_=w2_d)], writes=[w2a.k()], dma=True, dkey="w2a")

        def mk_c1(e):
            e.memset(idf.ap, 0.0)
            e.memset(tri.ap, 1.0)
            e.memset(stri.ap, 1.0)
            e.memset(onesc.ap, 0.0)
            e.memset(cm.ap, 0.0)
            e.memset(ones_mean.ap, 1.0 / 1024.0)
            e.memset(strif.ap, 1.0)
            e.memset(onesf.ap, 1.0)
            e.memset(negc.ap, -1.0 / 16.0)
            return e.memset(ones1.ap, 1.0)

        def mk_c2(e):
            e.affine_select(out=strif.ap, in_=strif.ap, pattern=[[-1, 128]], compare_op=ALU.is_gt,
                            fill=0.0, base=0, channel_multiplier=1)
            e.affine_select(out=idf.ap, in_=idf.ap, pattern=[[-1, 128]], compare_op=ALU.not_equal,
                            fill=1.0, base=0, channel_multiplier=1)
            e.affine_select(out=tri.ap, in_=tri.ap, pattern=[[1, 128]], compare_op=ALU.is_ge,
                            fill=0.0, base=0, channel_multiplier=-1)
            e.affine_select(out=stri.ap, in_=stri.ap, pattern=[[-1, 128]], compare_op=ALU.is_gt,
                            fill=0.0, base=0, channel_multiplier=1)
            e.memset(onesc.ap[0:64, 0, :], 1.0)
            e.memset(onesc.ap[64:128, 1, :], 1.0)
            e.memset(cm.ap[0:64, 0:1], 1.0)
            return e.memset(cm.ap[64:128, 1:2], 1.0)

        def mk_c3(e):
            e.memset(tri.ap[0:64, 64:128], 0.0)
            return e.memset(stri.ap[64:128, 0:64], 0.0)
        chain("pool", [mk_c1, mk_c2, mk_c3], [], [idf.k(), tri.k(), stri.k(), onesc.k(), cm.k(), ones_mean.k(), ones1.k(),
                                                  strif.k(), onesf.k(), negc.k()])

        def mk_consts2(e):
            e.tensor_copy(out=idb.ap, in_=idf.ap)
            e.tensor_scalar(out=trig.ap, in0=tri.ap, scalar1=-1.0 / 16.0, scalar2=None, op0=ALU.mult)
            e.tensor_scalar(out=strigf.ap, in0=strif.ap, scalar1=-1.0 / 16.0, scalar2=None, op0=ALU.mult)
            return e.tensor_scalar(out=strig.ap, in0=stri.ap, scalar1=-1.0 / 16.0, scalar2=None, op0=ALU.mult)
        S.add("dve", mk_consts2, reads=[idf.k(), tri.k(), stri.k(), strif.k()], writes=[idb.k(), trig.k(), strig.k(), strigf.k()])

        def mk_aneg(e):
            return e.activation(out=aneg.ap, in_=cst.ap[:, C_ALOG:C_ALOG + 16], func=AF.Exp)
        S.add("act", mk_aneg, reads=[cst.k()], writes=[aneg.k()])
        S.add("dve", lambda e: e.tensor_scalar(out=aneg.ap, in0=aneg.ap, scalar1=-1.0, scalar2=None, op0=ALU.mult),
              reads=[aneg.k()], writes=[aneg.k()])

        def cvec(off, n):
            return cst.ap[:, off:off + n]

        NSLOT = 3
        WSZ = 4096
        wslots = [B16(WSZ) for _ in range(NSLOT)]
        wctr = [0]
        SSZ = 1024
        sslots = [B16(SSZ) for _ in range(2)]
        sctr = [0]

        def wload(wd, kc, c0, cw):
            if kc * cw <= SSZ:
                i = NSLOT + sctr[0] % 2
                sctr[0] += 1
                slot = sslots[i - NSLOT]
            else:
                i = wctr[0] % NSLOT
                wctr[0] += 1
                slot = wslots[i]
            assert kc * cw <= WSZ
            dst = slot.ap[:, 0:kc * cw].rearrange("p (k n) -> p k n", k=kc)
            src = wd.rearrange("(k p) n -> p k n", p=128)[:, :, c0:c0 + cw]
            if kc > 8:
                h = kc // 2
                S.add("pool", lambda e: [e.dma_start(out=dst[:, 0:h, :], in_=src[:, 0:h, :]),
                                         e.dma_start(out=dst[:, h:kc, :], in_=src[:, h:kc, :])],
                      writes=[slot.k()], dma=True, dkey="w%d" % i, ndma=2)
            else:
                S.add("pool", lambda e: [e.dma_start(out=dst, in_=src)], writes=[slot.k()], dma=True, dkey="w%d" % i)
            return dst, slot.k()

        def proj_fm(wd, c0, ncols, src, nt, epi, kc=8, blk=512):
            col = 0
            while col < ncols:
                cw = min(blk, ncols - col)
                wap, wkey = wload(wd, kc, c0 + col, cw)
                nch_all = (cw + 127) // 128
                gsz = max(1, 512 // nt)
                for cg in range(0, nch_all, gsz):
                    nch = min(gsz, nch_all - cg)
                    p, pk = next_pa6()
                    pv = p[:, 0:nch * nt].rearrange("p (c n) -> p c n", c=nch)

                    def mm(e, wap=wap, pv=pv, nch=nch, cw=cw, cg=cg):
                        r = None
                        for c in range(nch):
                            cc = cg + c
                            m = min(128, cw - cc * 128)
                            for k in range(kc):
                                r = e.matmul(out=pv[0:m, c, :], lhsT=wap[:, k, cc * 128:cc * 128 + m], rhs=src.ap[:, k, 0:nt],
                                             start=(k == 0), stop=(k == kc - 1))
                        return r
                    S.add("pe", mm, reads=[wkey, src.k()], writes=[pk])
                    epi(col // 128 + cg, nch, pv, pk)
                col += cw

        def proj_tm(wd, c0, ncols, src, ntiles, epi, kc=8):
            col = 0
            while col < ncols:
                cw = min(512, ncols - col)
                wap, wkey = wload(wd, kc, c0 + col, cw)
                for t in range(ntiles):
                    p, pk = next_pa6()

                    def mm(e, wap=wap, p=p, t=t, cw=cw):
                        r = None
                        for k in range(kc):
                            r = e.matmul(out=p[:, 0:cw], lhsT=src.ap[:, k, t * 128:(t + 1) * 128], rhs=wap[:, k, :],
                                         start=(k == 0), stop=(k == kc - 1))
                        return r
                    S.add("pe", mm, reads=[wkey, src.k()], writes=[pk])
                    epi(t, col, cw, p[:, 0:cw], pk)
                col += cw

        hT = B32(8, NT)
        nT = B16(8, NT)
        rstd = B32(NT)
        Sst = B32(1024)
        Sg = B32(1024)
        dtot = B32(20)
        halo3 = B16(12, 3)
        ctpad = [B16(2, 128), B16(2, 128)]
        qtpad = [B16(4, 128), B16(4, 128)]
        a1T = B32(NT)
        KT = B16(8, 256)
        Vx = B16(2, 1024)

        def init_state(e):
            e.memset(Sst.ap, 0.0)
            e.memset(Sg.ap, 0.0)
            e.memset(dtot.ap, 1.0)
            e.memset(ctpad[0].ap, 0.0)
            e.memset(ctpad[1].ap, 0.0)
            e.memset(qtpad[0].ap, 0.0)
            e.memset(qtpad[1].ap, 0.0)
            e.memset(halo3.ap, 0.0)
            return e.memset(a1T.ap, 0.0)
        S.add("pool", init_state, writes=[Sst.k(), Sg.k(), dtot.k(), ctpad[0].k(), ctpad[1].k(), qtpad[0].k(),
                                          qtpad[1].k(), a1T.k(), halo3.k()])
        S.add("dve", lambda e: e.memset(a1T.ap[0:32, :], 1.0), reads=[a1T.k()], writes=[a1T.k()])

        xs_i = [0]

        def load_xT(row0, nt, xreads=()):
            xsb = [B32(1024), B32(1024)]
            for t in range(nt // 128):
                j = xs_i[0]
                xs_i[0] ^= 1
                xs = xsb[j]
                pw, pwk = (pwa, PWA) if j == 0 else (pwb, PWB)
                r0 = row0 + t * 128
                S.add("sp", lambda e, xs=xs, r0=r0: [e.dma_start(out=xs.ap, in_=x_d[r0:r0 + 128, :])],
                      writes=[xs.k()], reads=list(xreads), dma=True, dkey="xs%d" % xs.off)

                def tr(e, xs=xs, pw=pw):
                    r = None
                    for c in range(8):
                        r = e.transpose(out=pw[:, c * 128:(c + 1) * 128], in_=xs.ap[:, c * 128:(c + 1) * 128], identity=idf.ap)
                    return r
                S.add("pe", tr, reads=[xs.k(), idf.k()], writes=[pwk])
                S.add("act", lambda e, t=t, pw=pw: e.activation(out=hT.ap[:, :, t * 128:(t + 1) * 128],
                                                               in_=pw[:, :].rearrange("p (c n) -> p c n", c=8), func=AF.Copy),
                      reads=[pwk], writes=[hT.k()])
            xsb[0].free(); xsb[1].free()

        def norm_fm(goff, nt, stats_only=False):
            sqb = B16(8, NT)
            S.add("act", lambda e: e.activation(out=sqb.ap[:, :, 0:nt], in_=hT.ap[:, :, 0:nt], func=AF.Square),
                  reads=[hT.k()], writes=[sqb.k()])

            def mm(e):
                r = None
                for k in range(8):
                    r = e.matmul(out=psm[:, 0:nt], lhsT=ones_mean.ap, rhs=sqb.ap[:, k, 0:nt], start=(k == 0), stop=(k == 7))
                return r
            S.add("pe", mm, reads=[sqb.k(), ones_mean.k()], writes=["psm"])
            sqb.free()
            S.add("act", lambda e: e.activation(out=rstd.ap[:, 0:nt], in_=psm[:, 0:nt], func=AF.Ln, bias=EPS), reads=["psm"], writes=[rstd.k()])
            S.add("act", lambda e: e.activation(out=rstd.ap[:, 0:nt], in_=rstd.ap[:, 0:nt], func=AF.Exp, scale=-0.5), reads=[rstd.k()], writes=[rstd.k()])
            if stats_only:
                return
            tgt = nT

            def nrm(e):
                r = None
                for k in range(8):
                    r = e.scalar_tensor_tensor(out=tgt.ap[:, k, 0:nt], in0=hT.ap[:, k, 0:nt], scalar=cst.ap[:, goff + k:goff + k + 1],
                                               in1=rstd.ap[:, 0:nt], op0=ALU.mult, op1=ALU.mult)
                return r
            S.add("dve", nrm, reads=[hT.k(), rstd.k(), cst.k()], writes=[tgt.k()])

        def resid_epi(c0, nch, pv, pk):
            S.add("dve", lambda e: e.tensor_tensor(out=hT.ap[:, c0:c0 + nch, :], in0=hT.ap[:, c0:c0 + nch, :], in1=pv, op=ALU.add),
                  reads=[pk, hT.k(c0, c0 + nch)], writes=[hT.k(c0, c0 + nch)])

        def dbg_dump(name, buf, n, dt=F32):
            if not DBG:
                return
            d = nc.dram_tensor("dbg_" + name, [128, n], dt, kind="ExternalOutput").ap()
            dbg_outs[name] = d
            flat = buf.arena.t[:, buf.off:buf.off + n]
            S.add("sp", lambda e: [e.dma_start(out=d, in_=flat)], reads=[buf.k()], dma=True, dkey="dbg_" + name, name="OUTdbg")

        def interleave(gens):
            gens = list(gens)
            while gens:
                for g in list(gens):
                    try:
                        next(g)
                    except StopIteration:
                        gens.remove(g)

        def step(row0, nt, mode, orow0=None, first_dbg=False, xreads=(), keepc=True):
            full = mode == "p2"
            tps = nt // 128
            load_xT(row0, nt, xreads)
            norm_fm(C_GMIX, nt)
            if first_dbg:
                dbg_dump("nT", nT, 8 * NT, BF16)
            dtb = B32(tps, 16)
            ab = B32(tps, 16)

            def dt_epi(t, col, cw, p, pk):
                tmp = B32(16)
                S.add("dve", lambda e: e.tensor_tensor(out=tmp.ap, in0=p, in1=cvec(C_DTB, 16), op=ALU.add),
                      reads=[pk, cst.k()], writes=[tmp.k()])

                S.add("act", lambda e: e.activation(out=tmp.ap, in_=tmp.ap, func=AF.Exp), reads=[tmp.k()], writes=[tmp.k()])
                S.add("act", lambda e: e.activation(out=dtb.ap[:, t, :], in_=tmp.ap, func=AF.Ln, bias=1.0), reads=[tmp.k()], writes=[dtb.k(t)])
                S.add("dve", lambda e: e.tensor_tensor(out=ab.ap[:, t, :], in0=dtb.ap[:, t, :], in1=aneg.ap, op=ALU.mult),
                      reads=[dtb.k(t), aneg.k()], writes=[ab.k(t)])
                tmp.free()
            proj_tm(w_in, O_DT, 16, nT, tps, dt_epi)
            xbr = B16(12, nt + 3)
            xsT = B16(8, nt)
            BT = B16(2, nt)
            CT = B16(2, nt)
            S.add("dve", lambda e: e.tensor_copy(out=xbr.ap[:, :, 0:3], in_=halo3.ap), reads=[halo3.k()], writes=[xbr.k()])

            pend = []

            def conv_pair(cs):
                caccs_ = {}
                for c in cs:
                    cacc = B32(nt)
                    caccs_[c] = cacc
                    S.add("act", lambda e, c=c, cacc=cacc: e.activation(out=cacc.ap, in_=xbr.ap[:, c, 0:nt], func=AF.Copy,
                                                                        scale=cst.ap[:, C_CONVW + 4 * c:C_CONVW + 4 * c + 1]),
                          reads=[xbr.k(c), cst.k()], writes=[cacc.k()])
                for k in (1, 2, 3):
                    for c in cs:
                        cacc = caccs_[c]
                        S.add("dve", lambda e, c=c, cacc=cacc, k=k: e.scalar_tensor_tensor(
                            out=cacc.ap, in0=xbr.ap[:, c, k:k + nt], scalar=cst.ap[:, C_CONVW + 4 * c + k:C_CONVW + 4 * c + k + 1],
                            in1=cacc.ap, op0=ALU.mult, op1=ALU.add),
                            reads=[xbr.k(c), cst.k(), cacc.k()], writes=[cacc.k()])
                for c in cs:
                    cacc = caccs_[c]
                    dstb, di = (xsT, c) if c < 8 else ((BT, c - 8) if c < 10 else (CT, c - 10))
                    S.add("act", lambda e, c=c, cacc=cacc, dstb=dstb, di=di: e.activation(
                        out=dstb.ap[:, di, :], in_=cacc.ap, func=AF.Silu, bias=cst.ap[:, C_CONVB + c:C_CONVB + c + 1]),
                        reads=[cacc.k(), cst.k()], writes=[dstb.k(di)])
                    cacc.free()

            def xbc_epi(c0, nch, pv, pk):
                S.add("act", lambda e: e.activation(out=xbr.ap[:, c0:c0 + nch, 3:3 + nt], in_=pv, func=AF.Copy),
                      reads=[pk], writes=[xbr.k(c0, c0 + nch)])
                for c in range(c0, c0 + nch):
                    if full or c < 10:
                        pend.append(c)
                if len(pend) >= 2:
                    conv_pair(list(pend))
                    del pend[:]
            proj_fm(w_in, O_XBC, 1536 if (full or keepc) else 1280, nT, nt, xbc_epi)
            if pend:
                conv_pair(list(pend))
                del pend[:]
            S.add("dve", lambda e: e.tensor_copy(out=halo3.ap, in_=xbr.ap[:, :, nt:nt + 3]), reads=[xbr.k()], writes=[halo3.k()])
            xbr.free()
            zs = None
            if full:
                zs = B16(tps, 1024)

                def z_epi(t, col, cw, p, pk):
                    S.add("act", lambda e: e.activation(out=zs.ap[:, t, col:col + cw], in_=p, func=AF.Silu),
                          reads=[pk], writes=[zs.kr(t * 1024 + col, t * 1024 + col + cw)])
                proj_tm(w_in, O_Z, 1024, nT, tps, z_epi)
            qT = B16(4, nt) if full else None
            kT = B16(4, nt) if full else None
            ktm = B16(tps, 512)
            vtm = B16(tps, 1024)
            rs = B16(tps, 1024) if full else None
            if full:
                def q_epi(c0, nch, pv, pk):
                    S.add("act", lambda e: e.activation(out=qT.ap[:, c0:c0 + nch, :], in_=pv, func=AF.Copy, scale=float(128 ** -0.5)),
                          reads=[pk], writes=[qT.k(c0, c0 + nch)])
                proj_fm(w_in, O_Q, 512, nT, nt, q_epi)
            wap, wkey = wload(w_in, 8, O_K, 512)
            if full:
                gsz = max(1, 512 // nt)
                for cg in range(0, 4, gsz):
                    p, pk = next_pa6()
                    pv = p[:, 0:gsz * nt].rearrange("p (c n) -> p c n", c=gsz)

                    def mmk(e, wap=wap, pv=pv, cg=cg, gsz=gsz):
                        r = None
                        for c in range(gsz):
                            for k in range(8):
                                r = e.matmul(out=pv[:, c, :], lhsT=wap[:, k, (cg + c) * 128:(cg + c + 1) * 128], rhs=nT.ap[:, k, 0:nt],
                                             start=(k == 0), stop=(k == 7))
                        return r
                    S.add("pe", mmk, reads=[wkey, nT.k()], writes=[pk])
                    S.add("act", lambda e, pv=pv, cg=cg, gsz=gsz: e.activation(out=kT.ap[:, cg:cg + gsz, :], in_=pv, func=AF.Copy),
                          reads=[pk], writes=[kT.k(cg, cg + gsz)])
            for t in range(tps):
                p, pk = next_pa6()

                def mmk2(e, wap=wap, p=p, t=t):
                    r = None
                    for k in range(8):
                        r = e.matmul(out=p[:, 0:512], lhsT=nT.ap[:, k, t * 128:(t + 1) * 128], rhs=wap[:, k, :], start=(k == 0), stop=(k == 7))
                    return r
                S.add("pe", mmk2, reads=[wkey, nT.k()], writes=[pk])
                S.add("act", lambda e, p=p, t=t: e.activation(out=ktm.ap[:, t, :], in_=p[:, 0:512], func=AF.Copy), reads=[pk], writes=[ktm.k(t)])

            def a1_epi(c0, nch, pv, pk):
                S.add("act", lambda e: e.activation(out=a1T.ap[0:16, 0:nt], in_=pv[0:16, 0, :], func=AF.Copy), reads=[pk], writes=[a1T.k()])
            proj_fm(w_in, O_A1, 128, nT, nt, a1_epi)

            def v_epi(t, col, cw, p, pk):
                S.add("act", lambda e: e.activation(out=vtm.ap[:, t, col:col + cw], in_=p, func=AF.Copy),
                      reads=[pk], writes=[vtm.kr(t * 1024 + col, t * 1024 + col + cw)])
            proj_tm(w_in, O_V, 1024, nT, tps, v_epi)
            if full:
                def r_epi(t, col, cw, p, pk):
                    S.add("act", lambda e: e.activation(out=rs.ap[:, t, col:col + cw], in_=p, func=AF.Silu),
                          reads=[pk], writes=[rs.kr(t * 1024 + col, t * 1024 + col + cw)])
                proj_tm(w_in, O_R, 1024, nT, tps, r_epi)
            ynT = B16(8, nt) if full else None
            onT = B16(8, nt) if full else None
            for t in range(tps):
                interleave([ssd_tile(t, xsT, BT, CT, dtb, ab, zs, ynT, full, False),
                            gla_tile(t, qT, kT, ktm, vtm, rs, onT, full, False)])
            xsT.free(); BT.free(); CT.free(); dtb.free(); ab.free()
            if zs is not None:
                zs.free()
            ktm.free(); vtm.free()
            if not full:
                return
            qT.free(); kT.free(); rs.free()
            if first_dbg:
                dbg_dump("onT", onT, 8 * NT, BF16)
            gsT = B16(8, nt)
            ggT = B16(8, nt)
            mT = B16(8, nt)

            def gs_epi(c0, nch, pv, pk):
                S.add("act", lambda e: e.activation(out=gsT.ap[:, c0:c0 + nch, :], in_=pv, func=AF.Sigmoid), reads=[pk], writes=[gsT.k(c0, c0 + nch)])

            def gg_epi(c0, nch, pv, pk):
                S.add("act", lambda e: e.activation(out=ggT.ap[:, c0:c0 + nch, :], in_=pv, func=AF.Sigmoid), reads=[pk], writes=[ggT.k(c0, c0 + nch)])
            proj_fm(w_in, O_GS, 1024, nT, nt, gs_epi)
            proj_fm(w_in, O_GG, 1024, nT, nt, gg_epi)

            def ups_epi(c0, nch, pv, pk):
                S.add("dve", lambda e: e.tensor_tensor(out=gsT.ap[:, c0:c0 + nch, :], in0=gsT.ap[:, c0:c0 + nch, :], in1=pv, op=ALU.mult),
                      reads=[pk, gsT.k(c0, c0 + nch)], writes=[gsT.k(c0, c0 + nch)])
            proj_fm(w_ups, 0, 1024, ynT, nt, ups_epi)

            def upg_epi(c0, nch, pv, pk):
                S.add("dve", lambda e: e.tensor_tensor(out=ggT.ap[:, c0:c0 + nch, :], in0=ggT.ap[:, c0:c0 + nch, :], in1=pv, op=ALU.mult),
                      reads=[pk, ggT.k(c0, c0 + nch)], writes=[ggT.k(c0, c0 + nch)])
                S.add("dve", lambda e: e.tensor_tensor(out=mT.ap[:, c0:c0 + nch, :], in0=ggT.ap[:, c0:c0 + nch, :], in1=gsT.ap[:, c0:c0 + nch, :], op=ALU.add),
                      reads=[ggT.k(c0, c0 + nch), gsT.k(c0, c0 + nch)], writes=[mT.k(c0, c0 + nch)])
            proj_fm(w_upg, 0, 1024, onT, nt, upg_epi)
            ynT.free(); onT.free(); gsT.free(); ggT.free()
            proj_fm(w_o, 0, 1024, mT, nt, resid_epi)
            mT.free()
            if first_dbg:
                dbg_dump("h1", hT, 8 * NT)
            norm_fm(C_GXA, nt)
            qx = B16(8, nt)
            ox = B16(8, nt)

            def qx_epi(c0, nch, pv, pk):
                S.add("act", lambda e: e.activation(out=qx.ap[:, c0:c0 + nch, :], in_=pv, func=AF.Copy, scale=1.0 / 16.0),
                      reads=[pk], writes=[qx.k(c0, c0 + nch)])
            proj_fm(w_xq, 0, 1024, nT, nt, qx_epi)
            for hd in range(4):
                ET = B16(2, nt)
                for mc in range(2):
                    p, pk = next_pa6()

                    def sc(e, p=p, hd=hd, mc=mc):
                        r = None
                        for dc in range(2):
                            r = e.matmul(out=p[:, 0:nt], lhsT=KT.ap[:, hd * 2 + dc, mc * 128:(mc + 1) * 128], rhs=qx.ap[:, hd * 2 + dc, :],
                                         start=(dc == 0), stop=(dc == 1))
                        return r
                    S.add("pe", sc, reads=[KT.k(), qx.k(hd * 2, hd * 2 + 2)], writes=[pk])
                    S.add("act", lambda e, p=p, mc=mc, ET=ET: e.activation(out=ET.ap[:, mc, :], in_=p[:, 0:nt], func=AF.Exp),
                          reads=[pk], writes=[ET.k(mc)])

                def den(e, ET=ET):
                    e.matmul(out=psm[:, 0:nt], lhsT=ones1.ap, rhs=ET.ap[:, 0, :], start=True, stop=False)
                    return e.matmul(out=psm[:, 0:nt], lhsT=ones1.ap, rhs=ET.ap[:, 1, :], start=False, stop=True)
                S.add("pe", den, reads=[ET.k(), ones1.k()], writes=["psm"])
                rden = B32(nt)
                S.add("dve", lambda e, rden=rden: e.reciprocal(out=rden.ap, in_=psm[:, 0:nt]), reads=["psm"], writes=[rden.k()])
                for dc in range(2):
                    p, pk = next_pa6()

                    def pvm(e, p=p, hd=hd, dc=dc, ET=ET):
                        r = None
                        for mc in range(2):
                            r = e.matmul(out=p[:, 0:nt], lhsT=Vx.ap[:, mc, hd * 256 + dc * 128:hd * 256 + dc * 128 + 128], rhs=ET.ap[:, mc, :],
                                         start=(mc == 0), stop=(mc == 1))
                        return r
                    S.add("pe", pvm, reads=[Vx.k(), ET.k()], writes=[pk])
                    S.add("dve", lambda e, p=p, hd=hd, dc=dc, rden=rden: e.tensor_tensor(out=ox.ap[:, hd * 2 + dc, :], in0=p[:, 0:nt], in1=rden.ap, op=ALU.mult),
                          reads=[pk, rden.k()], writes=[ox.k(hd * 2 + dc)])
                ET.free(); rden.free()
            qx.free()
            proj_fm(w_xo, 0, 1024, ox, nt, resid_epi)
            ox.free()
            if first_dbg:
                dbg_dump("h2", hT, 8 * NT)
            norm_fm(C_GFFN, nt)
            aT = B16(22, nt)
            col = 0
            while col < DFF:
                cw = min(512, DFF - col)

                def g_epi(c0, nch, pv, pk, col=col):
                    cc = col // 128 + c0
                    S.add("act", lambda e: e.activation(out=aT.ap[:, cc:cc + nch, :], in_=pv, func=AF.Silu), reads=[pk], writes=[aT.k(cc, cc + nch)])

                def u_epi(c0, nch, pv, pk, col=col):
                    cc = col // 128 + c0
                    S.add("dve", lambda e: e.tensor_tensor(out=aT.ap[:, cc:cc + nch, :], in0=aT.ap[:, cc:cc + nch, :], in1=pv, op=ALU.mult),
                          reads=[pk, aT.k(cc, cc + nch)], writes=[aT.k(cc, cc + nch)])
                proj_fm(w_fi, col, cw, nT, nt, g_epi)
                proj_fm(w_fi, DFF + col, cw, nT, nt, u_epi)
                col += cw
            proj_fm(w_fo, 0, 1024, aT, nt, resid_epi, kc=22, blk=128)
            aT.free()
            if first_dbg:
                dbg_dump("h3", hT, 8 * NT)
            norm_fm(C_GFIN, nt, stats_only=True)
            for t in range(tps):
                of = B32(8, 128)

                def nrm_t(e, t=t, of=of):
                    r = None
                    for k in range(8):
                        r = e.scalar_tensor_tensor(out=of.ap[:, k, :], in0=hT.ap[:, k, t * 128:(t + 1) * 128], scalar=cst.ap[:, C_GFIN + k:C_GFIN + k + 1],
                                                   in1=rstd.ap[:, t * 128:(t + 1) * 128], op0=ALU.mult, op1=ALU.mult)
                    return r
                S.add("dve", nrm_t, reads=[hT.k(), rstd.k(), cst.k()], writes=[of.k()])

                def tr(e, t=t, of=of):
                    r = None
                    for c in range(8):
                        r = e.transpose(out=pwa[:, c * 128:(c + 1) * 128], in_=of.ap[:, c, :], identity=idf.ap)
                    return r
                S.add("pe", tr, reads=[of.k(), idf.k()], writes=[PWA])
                of.free()
                og = B32(1024)
                S.add("act", lambda e, og=og: e.activation(out=og.ap, in_=pwa[:, :], func=AF.Copy), reads=[PWA], writes=[og.k()])
                r0 = orow0 + t * 128
                S.add("sp", lambda e, og=og, r0=r0: [e.dma_start(out=out_d[r0:r0 + 128, :], in_=og.ap)], reads=[og.k()],
                      dma=True, dkey="og%d" % og.off, name="OUT")
                og.free()

        def ssd_tile_p1(t, xsT, BT, dtb, ab):
            cols = slice(t * 128, (t + 1) * 128)
            a_t = ab.ap[:, t, :]
            dt_t = dtb.ap[:, t, :]
            Btm = B16(2, 128)
            Xd = B16(1024)
            ex = B32(32)
            dd = B32(16)

            def trx(e):
                r = None
                for c in range(8):
                    r = e.transpose(out=ptr[:, c * 128:(c + 1) * 128], in_=xsT.ap[:, c, cols], identity=idb.ap)
                return r

            def smalls(e):
                e.matmul(out=psm[:, 0:16], lhsT=strif.ap, rhs=a_t, start=True, stop=True)
                return e.matmul(out=psm[:, 16:32], lhsT=onesf.ap, rhs=a_t, start=True, stop=True)
            S.add("pe", smalls, reads=[ab.k(t), strif.k(), onesf.k()], writes=["psm"])
            S.add("act", lambda e: e.activation(out=ex.ap, in_=psm[:, 0:32], func=AF.Exp), reads=["psm"], writes=[ex.k()])
            yield
            S.add("dve", lambda e: e.tensor_tensor(out=dd.ap, in0=ex.ap[:, 0:16], in1=dt_t, op=ALU.mult), reads=[ex.k(), dtb.k(t)], writes=[dd.k()])
            S.add("pe", trx, reads=[xsT.k(), idb.k()], writes=["ptr"])
            yield
            S.add("dve", lambda e: e.tensor_tensor(out=Xd.ap.rearrange("p (h q) -> p h q", h=16),
                                                   in0=ptr[:, :].rearrange("p (h q) -> p h q", h=16),
                                                   in1=dd.ap.unsqueeze(2).to_broadcast([128, 16, 64]), op=ALU.mult),
                  reads=["ptr", dd.k()], writes=[Xd.k()])
            yield

            def trb(e):
                e.transpose(out=ptr[:, 0:128], in_=BT.ap[:, 0, cols], identity=idb.ap)
                return e.transpose(out=ptr[:, 128:256], in_=BT.ap[:, 1, cols], identity=idb.ap)
            S.add("pe", trb, reads=[BT.k(), idb.k()], writes=["ptr"])
            S.add("act", lambda e: e.activation(out=Btm.ap, in_=ptr[:, 0:256].rearrange("p (g n) -> p g n", g=2), func=AF.Copy),
                  reads=["ptr"], writes=[Btm.k()])
            yield

            def sloc(e):
                e.matmul(out=pwb[:, 0:512], lhsT=Btm.ap[:, 0, :], rhs=Xd.ap[:, 0:512], start=True, stop=True)
                return e.matmul(out=pwb[:, 512:1024], lhsT=Btm.ap[:, 1, :], rhs=Xd.ap[:, 512:1024], start=True, stop=True)
            S.add("pe", sloc, reads=[Btm.k(), Xd.k()], writes=[PWB])
            S.add("dve", lambda e: e.tensor_tensor(out=Sst.ap.rearrange("p (h q) -> p h q", h=16), in0=Sst.ap.rearrange("p (h q) -> p h q", h=16),
                                                   in1=ex.ap[:, 16:32].unsqueeze(2).to_broadcast([128, 16, 64]), op=ALU.mult),
                  reads=[ex.k(), Sst.k()], writes=[Sst.k()])
            yield
            S.add("dve", lambda e: e.tensor_tensor(out=Sst.ap, in0=Sst.ap, in1=pwb[:, :], op=ALU.add), reads=[PWB, Sst.k()], writes=[Sst.k()])
            yield
            Btm.free(); Xd.free(); ex.free(); dd.free()

        def ssd_tile(t, xsT, BT, CT, dtb, ab, zs, ynT, full, dbg):
            cols = slice(t * 128, (t + 1) * 128)
            a_t = ab.ap[:, t, :]
            dt_t = dtb.ap[:, t, :]
            if not full:
                yield from ssd_tile_p1(t, xsT, BT, dtb, ab)
                return
            Xtm = B16(1024)
            xstm = B16(1024) if full else None
            Btm = B16(2, 128)

            def trx(e):
                r = None
                for c in range(8):
                    r = e.transpose(out=ptr[:, c * 128:(c + 1) * 128], in_=xsT.ap[:, c, cols], identity=idb.ap)
                return r
            S.add("pe", trx, reads=[xsT.k(), idb.k()], writes=["ptr"])
            yield
            S.add("dve", lambda e: e.tensor_tensor(out=Xtm.ap.rearrange("p (h q) -> p h q", h=16),
                                                   in0=ptr[:, :].rearrange("p (h q) -> p h q", h=16),
                                                   in1=dt_t.unsqueeze(2).to_broadcast([128, 16, 64]), op=ALU.mult),
                  reads=["ptr", dtb.k(t)], writes=[Xtm.k()])
            yield
            if full:
                S.add("dve", lambda e: e.tensor_tensor(out=xstm.ap.rearrange("p (h q) -> p h q", h=16),
                                                       in0=ptr[:, :].rearrange("p (h q) -> p h q", h=16),
                                                       in1=cvec(C_DSK, 16).unsqueeze(2).to_broadcast([128, 16, 64]), op=ALU.mult),
                      reads=["ptr", cst.k()], writes=[xstm.k()])
                yield

            def trb(e):
                e.transpose(out=ptr[:, 0:128], in_=BT.ap[:, 0, cols], identity=idb.ap)
                return e.transpose(out=ptr[:, 128:256], in_=BT.ap[:, 1, cols], identity=idb.ap)
            S.add("pe", trb, reads=[BT.k(), idb.k()], writes=["ptr"])
            yield
            S.add("act", lambda e: e.activation(out=Btm.ap, in_=ptr[:, 0:256].rearrange("p (g n) -> p g n", g=2), func=AF.Copy),
                  reads=["ptr"], writes=[Btm.k()])
            yield
            def smalls(e):
                e.matmul(out=psm[:, 0:16], lhsT=tri.ap, rhs=a_t, start=True, stop=True)
                e.matmul(out=psm[:, 16:32], lhsT=stri.ap, rhs=a_t, start=True, stop=True)
                e.matmul(out=psm[:, 32:48], lhsT=onesc.ap[:, 0, :], rhs=a_t, start=True, stop=True)
                return e.matmul(out=psm[:, 48:64], lhsT=onesc.ap[:, 1, :], rhs=a_t, start=True, stop=True)
            S.add("pe", smalls, reads=[ab.k(t), tri.k(), stri.k(), onesc.k()], writes=["psm"])
            yield
            ex = B32(64)
            S.add("act", lambda e: e.activation(out=ex.ap, in_=psm[:, 0:64], func=AF.Exp), reads=["psm"], writes=[ex.k()])
            yield
            eacs = ex.ap[:, 0:16]
            if dbg:
                dbg_dump("Xtm", Xtm, 1024, BF16)
                dbg_dump("ex", ex, 64)
                dbg_dump("dtb", dtb, 16)
            ds = B32(2, 16)

            def mkds(e):
                e.tensor_scalar(out=ds.ap[:, 0, :], in0=ex.ap[:, 16:32], scalar1=cm.ap[:, 0:1], scalar2=None, op0=ALU.mult)
                return e.tensor_scalar(out=ds.ap[:, 1, :], in0=ex.ap[:, 16:32], scalar1=cm.ap[:, 1:2], scalar2=None, op0=ALU.mult)
            S.add("dve", mkds, reads=[ex.k(), cm.k()], writes=[ds.k()])
            yield
            Xd = [B16(1024), B16(1024)]
            for c in range(2):
                S.add("dve", lambda e, c=c: e.tensor_tensor(out=Xd[c].ap.rearrange("p (h q) -> p h q", h=16),
                                                            in0=Xtm.ap.rearrange("p (h q) -> p h q", h=16),
                                                            in1=ds.ap[:, c, :].unsqueeze(2).to_broadcast([128, 16, 64]), op=ALU.mult),
                      reads=[Xtm.k(), ds.k()], writes=[Xd[c].k()])
                yield
            MT = None
            if full:
                MT = B16(16, 128)
                for g in range(2):
                    rhs_all = B32(8, 128)
                    S.add("dve", lambda e, g=g, rhs_all=rhs_all: e.tensor_tensor(
                        out=rhs_all.ap, in0=tri.ap.unsqueeze(1).to_broadcast([128, 8, 128]),
                        in1=a_t[:, g * 8:(g + 1) * 8].unsqueeze(2).to_broadcast([128, 8, 128]), op=ALU.mult),
                        reads=[tri.k(), ab.k(t)], writes=[rhs_all.k()])
                    yield

                    def dmm(e, rhs_all=rhs_all):
                        e.matmul(out=pwa[:, 0:512], lhsT=stri.ap, rhs=rhs_all.ap[:, 0:4, :], start=True, stop=True)
                        return e.matmul(out=pwa[:, 512:1024], lhsT=stri.ap, rhs=rhs_all.ap[:, 4:8, :], start=True, stop=True)
                    S.add("pe", dmm, reads=[rhs_all.k(), stri.k()], writes=[PWA])
                    yield
                    E = B16(8, 128)

                    def eexp(e, E=E):
                        e.activation(out=E.ap[:, 0:4, :], in_=pwa[:, 0:512].rearrange("p (h l) -> p h l", h=4), func=AF.Exp)
                        return e.activation(out=E.ap[:, 4:8, :], in_=pwa[:, 512:1024].rearrange("p (h l) -> p h l", h=4), func=AF.Exp)
                    S.add("act", eexp, reads=[PWA], writes=[E.k()])
                    yield
                    S.add("pe", lambda e, g=g: e.matmul(out=psm[:, 128 + g * 128:256 + g * 128], lhsT=BT.ap[:, g, cols], rhs=CT.ap[:, g, cols],
                                                        start=True, stop=True), reads=[BT.k(), CT.k()], writes=["psm"])
                    yield
                    cbm = B32(128)
                    S.add("dve", lambda e, g=g, cbm=cbm: e.tensor_tensor(out=cbm.ap, in0=psm[:, 128 + g * 128:256 + g * 128], in1=tri.ap, op=ALU.mult),
                          reads=["psm", tri.k()], writes=[cbm.k()])
                    yield
                    S.add("dve", lambda e, g=g, cbm=cbm, E=E: e.tensor_tensor(out=MT.ap[:, g * 8:(g + 1) * 8, :], in0=E.ap,
                                                                               in1=cbm.ap.unsqueeze(1).to_broadcast([128, 8, 128]), op=ALU.mult),
                          reads=[E.k(), cbm.k()], writes=[MT.k(g * 8, (g + 1) * 8)])
                    yield
                    rhs_all.free(); E.free(); cbm.free()
                for c in range(2):
                    S.add("act", lambda e, c=c: e.activation(out=ctpad[c].ap[:, :, c * 64:(c + 1) * 64],
                                                             in_=CT.ap[:, :, t * 128 + c * 64:t * 128 + (c + 1) * 64], func=AF.Copy),
                          reads=[CT.k()], writes=[ctpad[c].k()])
                    yield
            Sbf = [B16(1024), B16(1024)] if full else None
            for c in range(2):
                def sloc(e, c=c):
                    e.matmul(out=pwb[:, 0:512], lhsT=Btm.ap[:, 0, :], rhs=Xd[c].ap[:, 0:512], start=True, stop=True)
                    return e.matmul(out=pwb[:, 512:1024], lhsT=Btm.ap[:, 1, :], rhs=Xd[c].ap[:, 512:1024], start=True, stop=True)
                S.add("pe", sloc, reads=[Btm.k(), Xd[c].k()], writes=[PWB])
                yield
                if full:
                    S.add("act", lambda e, c=c: e.activation(out=Sbf[c].ap, in_=Sst.ap, func=AF.Copy), reads=[Sst.k()], writes=[Sbf[c].k()])
                    yield

                def supd(e, c=c):
                    e.tensor_tensor(out=dtot.ap[:, 0:16], in0=dtot.ap[:, 0:16], in1=ex.ap[:, 32 + 16 * c:48 + 16 * c], op=ALU.mult)
                    return e.tensor_tensor(out=Sst.ap.rearrange("p (h q) -> p h q", h=16), in0=Sst.ap.rearrange("p (h q) -> p h q", h=16),
                                           in1=ex.ap[:, 32 + 16 * c:48 + 16 * c].unsqueeze(2).to_broadcast([128, 16, 64]), op=ALU.mult)
                S.add("dve", supd, reads=[ex.k(), Sst.k(), dtot.kr(0, 16)], writes=[Sst.k(), dtot.kr(0, 16)])
                yield
                S.add("dve", lambda e: e.tensor_tensor(out=Sst.ap, in0=Sst.ap, in1=pwb[:, :], op=ALU.add), reads=[PWB, Sst.k()], writes=[Sst.k()])
                yield
            Xd[0].free(); Xd[1].free(); Btm.free(); ds.free()
            if not full:
                Xtm.free(); ex.free()
                return
            def ymm(e):
                r = None
                for b in range(2):
                    e.matmul(out=pwa[:, b * 512:(b + 1) * 512], lhsT=idb.ap, rhs=xstm.ap[:, b * 512:(b + 1) * 512], start=True, stop=False)
                    for hh in range(8):
                        h = b * 8 + hh
                        r = e.matmul(out=pwa[:, h * 64:(h + 1) * 64], lhsT=MT.ap[:, h, :], rhs=Xtm.ap[:, h * 64:(h + 1) * 64],
                                     start=False, stop=(hh == 7))
                return r
            S.add("pe", ymm, reads=[idb.k(), xstm.k(), MT.k(), Xtm.k()], writes=[PWA])
            yield

            def yoff(e):
                r = None
                for g in range(2):
                    for c in range(2):
                        r = e.matmul(out=pwb[:, g * 512:(g + 1) * 512], lhsT=ctpad[c].ap[:, g, :], rhs=Sbf[c].ap[:, g * 512:(g + 1) * 512],
                                     start=(c == 0), stop=(c == 1))
                return r
            S.add("pe", yoff, reads=[ctpad[0].k(), ctpad[1].k(), Sbf[0].k(), Sbf[1].k()], writes=[PWB])
            yield
            yt = B32(1024)

            S.add("dve", lambda e: e.tensor_tensor(out=yt.ap.rearrange("p (h q) -> p h q", h=16), in0=pwb[:, :].rearrange("p (h q) -> p h q", h=16),
                                                   in1=eacs.unsqueeze(2).to_broadcast([128, 16, 64]), op=ALU.mult),
                  reads=[PWB, ex.k()], writes=[yt.k()])
            yield
            S.add("dve", lambda e: e.tensor_tensor(out=yt.ap, in0=yt.ap, in1=pwa[:, :], op=ALU.add), reads=[PWA, yt.k()], writes=[yt.k()])
            yield
            S.add("dve", lambda e: e.tensor_tensor(out=yt.ap, in0=yt.ap, in1=zs.ap[:, t, :], op=ALU.mult), reads=[yt.k(), zs.k(t)], writes=[yt.k()])
            yield
            if dbg:
                dbg_dump("yt", yt, 1024)
                dbg_dump("MT", MT, 2048, BF16)
                dbg_dump("Sbf1", Sbf[1], 1024, BF16)
            Xtm.free(); xstm.free(); MT.free(); Sbf[0].free(); Sbf[1].free(); ex.free()
            junk = B16(1024)
            ss = B32(2)

            def ysq(e):
                e.activation(out=junk.ap[:, 0:512], in_=yt.ap[:, 0:512], func=AF.Square, accum_out=ss.ap[:, 0:1])
                return e.activation(out=junk.ap[:, 512:1024], in_=yt.ap[:, 512:1024], func=AF.Square, accum_out=ss.ap[:, 1:2])
            S.add("act", ysq, reads=[yt.k()], writes=[junk.k(), ss.k()])
            yield
            chain("act", [lambda e: e.activation(out=ss.ap, in_=ss.ap, func=AF.Ln, bias=EPS, scale=1.0 / 512.0),
                          lambda e: e.activation(out=ss.ap, in_=ss.ap, func=AF.Exp, scale=-0.5)], [], [ss.k()])
            yield
            yn = B16(1024)

            def ynorm(e):
                e.scalar_tensor_tensor(out=yn.ap[:, 0:512], in0=yt.ap[:, 0:512], scalar=ss.ap[:, 0:1], in1=cvec(C_SSDN, 512), op0=ALU.mult, op1=ALU.mult)
                return e.scalar_tensor_tensor(out=yn.ap[:, 512:1024], in0=yt.ap[:, 512:1024], scalar=ss.ap[:, 1:2], in1=cvec(C_SSDN + 512, 512),
                                              op0=ALU.mult, op1=ALU.mult)
            S.add("dve", ynorm, reads=[yt.k(), ss.k(), cst.k()], writes=[yn.k()])
            yield

            def try_(e):
                r = None
                for c in range(8):
                    r = e.transpose(out=ptr[:, c * 128:(c + 1) * 128], in_=yn.ap[:, c * 128:(c + 1) * 128], identity=idb.ap)
                return r
            S.add("pe", try_, reads=[yn.k(), idb.k()], writes=["ptr"])
            yield
            S.add("act", lambda e: e.activation(out=ynT.ap[:, :, cols], in_=ptr[:, :].rearrange("p (c n) -> p c n", c=8), func=AF.Copy),
                  reads=["ptr"], writes=[ynT.k()])
            yield
            yt.free(); junk.free(); ss.free(); yn.free()

        def gla_tile(t, qT, kT, ktm, vtm, rs, onT, full, dbg=False):
            cols = slice(t * 128, (t + 1) * 128)
            p0, pk0 = next_pa()
            S.add("pe", lambda e: e.matmul(out=p0[:, 0:512], lhsT=a1T.ap[:, cols], rhs=w2a.ap, start=True, stop=True),
                  reads=[a1T.k(), w2a.k()], writes=[pk0])
            yield
            la = B32(512)

            S.add("act", lambda e: e.activation(out=la.ap, in_=p0[:, 0:512], func=AF.Exp, scale=-1.0), reads=[pk0], writes=[la.k()])
            yield
            S.add("act", lambda e: e.activation(out=la.ap, in_=la.ap, func=AF.Ln, bias=1.0), reads=[la.k()], writes=[la.k()])
            yield
            if dbg:
                dbg_dump("la", la, 512)
            if not full:
                pd, pkd = next_pa()

                def dmm_(e):
                    r = None
                    for hd in range(4):
                        r = e.matmul(out=pd[:, hd * 2:hd * 2 + 2], lhsT=la.ap[:, hd * 128:(hd + 1) * 128], rhs=negc.ap, start=True, stop=True)
                    return r
                S.add("pe", dmm_, reads=[la.k(), negc.k()], writes=[pkd])
                dec8 = B32(8)
                S.add("act", lambda e: e.activation(out=dec8.ap, in_=pd[:, 0:8], func=AF.Exp), reads=[pkd], writes=[dec8.k()])
                yield
                pe_, pke = next_pa()
                S.add("pe", lambda e: e.matmul(out=pe_[:, 0:512], lhsT=strigf.ap, rhs=la.ap, start=True, stop=True), reads=[strigf.k(), la.k()], writes=[pke])
                khf = B16(512)
                Ekf = B32(512)
                S.add("act", lambda e: e.activation(out=Ekf.ap, in_=pe_[:, 0:512], func=AF.Exp), reads=[pke], writes=[Ekf.k()])
                yield
                S.add("dve", lambda e: e.tensor_tensor(out=khf.ap, in0=Ekf.ap, in1=ktm.ap[:, t, :], op=ALU.mult), reads=[Ekf.k(), ktm.k(t)], writes=[khf.k()])
                yield
                for hp in range(2):
                    pu, pku = next_pa()

                    def umm_(e, hp=hp, pu=pu):
                        r = None
                        for h2 in range(2):
                            hd = hp * 2 + h2
                            r = e.matmul(out=pu[:, h2 * 256:(h2 + 1) * 256], lhsT=khf.ap[:, hd * 128:(hd + 1) * 128],
                                         rhs=vtm.ap[:, t, hd * 256:(hd + 1) * 256], start=True, stop=True)
                        return r
                    S.add("pe", umm_, reads=[khf.k(), vtm.k(t)], writes=[pku])

                    def gupd_(e, hp=hp, pu=pu):
                        r = None
                        for h2 in range(2):
                            hd = hp * 2 + h2
                            r = e.scalar_tensor_tensor(out=Sg.ap[:, hd * 256:(hd + 1) * 256], in0=Sg.ap[:, hd * 256:(hd + 1) * 256],
                                                       scalar=dec8.ap[:, hd * 2:hd * 2 + 1], in1=pu[:, h2 * 256:(h2 + 1) * 256],
                                                       op0=ALU.mult, op1=ALU.add)
                        return r
                    S.add("dve", gupd_, reads=[pku, dec8.k(), Sg.kr(hp * 512, hp * 512 + 512)], writes=[Sg.kr(hp * 512, hp * 512 + 512)])
                    yield
                la.free(); dec8.free(); khf.free(); Ekf.free()
                return
            p1, pk1 = next_pa()

            def bc(e):
                r = None
                for hd in range(4):
                    r = e.matmul(out=p1[:, hd * 128:(hd + 1) * 128], lhsT=la.ap[:, hd * 128:(hd + 1) * 128], rhs=trig.ap, start=True, stop=True)
                return r
            S.add("pe", bc, reads=[la.k(), trig.k()], writes=[pk1])
            yield
            EqT = B32(4, 128)
            S.add("act", lambda e: e.activation(out=EqT.ap, in_=p1[:, 0:512].rearrange("p (h l) -> p h l", h=4), func=AF.Exp),
                  reads=[pk1], writes=[EqT.k()])
            yield
            ktT = None
            if full:
                EkT = B32(4, 128)
                S.add("act", lambda e: e.activation(out=EkT.ap, in_=p1[:, 0:512].rearrange("p (h l) -> p h l", h=4), func=AF.Exp, scale=-1.0),
                      reads=[pk1], writes=[EkT.k()])
                yield
                ktT = B16(4, 128)
                S.add("dve", lambda e: e.tensor_tensor(out=ktT.ap, in0=kT.ap[:, :, cols], in1=EkT.ap, op=ALU.mult),
                      reads=[kT.k(), EkT.k()], writes=[ktT.k()])
                yield
                for c in range(2):
                    S.add("dve", lambda e, c=c: e.tensor_tensor(out=qtpad[c].ap[:, :, c * 64:(c + 1) * 64],
                                                                in0=qT.ap[:, :, t * 128 + c * 64:t * 128 + (c + 1) * 64],
                                                                in1=EqT.ap[:, :, c * 64:(c + 1) * 64], op=ALU.mult),
                          reads=[qT.k(), EqT.k()], writes=[qtpad[c].k()])
                    yield
                EkT.free()
            p2, pk2 = next_pa()
            S.add("pe", lambda e: e.matmul(out=p2[:, 0:512], lhsT=strig.ap, rhs=la.ap, start=True, stop=True), reads=[strig.k(), la.k()], writes=[pk2])
            yield
            Ekh = B32(512)
            S.add("act", lambda e: e.activation(out=Ekh.ap, in_=p2[:, 0:512], func=AF.Exp), reads=[pk2], writes=[Ekh.k()])
            yield
            kh = [B16(512), B16(512)]
            for c in range(2):
                S.add("dve", lambda e, c=c: e.scalar_tensor_tensor(out=kh[c].ap, in0=Ekh.ap, scalar=cm.ap[:, c:c + 1], in1=ktm.ap[:, t, :],
                                                                   op0=ALU.mult, op1=ALU.mult),
                      reads=[Ekh.k(), cm.k(), ktm.k(t)], writes=[kh[c].k()])
                yield
            la.free(); Ekh.free()
            attT = None
            if full:
                p3, pk3 = next_pa()

                def att(e):
                    r = None
                    for hd in range(4):
                        e.matmul(out=p3[:, hd * 128:(hd + 1) * 128], lhsT=ktT.ap[:, hd, :], rhs=qtpad[0].ap[:, hd, :], start=True, stop=False)
                        r = e.matmul(out=p3[:, hd * 128:(hd + 1) * 128], lhsT=ktT.ap[:, hd, :], rhs=qtpad[1].ap[:, hd, :], start=False, stop=True)
                    return r
                S.add("pe", att, reads=[ktT.k(), qtpad[0].k(), qtpad[1].k()], writes=[pk3])
                yield
                attT = B16(4, 128)
                S.add("dve", lambda e: e.tensor_tensor(out=attT.ap, in0=p3[:, 0:512].rearrange("p (h l) -> p h l", h=4),
                                                       in1=tri.ap.unsqueeze(1).to_broadcast([128, 4, 128]), op=ALU.mult),
                      reads=[pk3, tri.k()], writes=[attT.k()])
                yield
                ktT.free()
            Sgb = [B16(1024), B16(1024)] if full else None
            for c in range(2):
                if full:
                    S.add("act", lambda e, c=c: e.activation(out=Sgb[c].ap, in_=Sg.ap, func=AF.Copy), reads=[Sg.k()], writes=[Sgb[c].k()])
                    yield
                for hp in range(2):
                    pu, pku = next_pa()

                    def umm(e, c=c, hp=hp, pu=pu):
                        r = None
                        for h2 in range(2):
                            hd = hp * 2 + h2
                            r = e.matmul(out=pu[:, h2 * 256:(h2 + 1) * 256], lhsT=kh[c].ap[:, hd * 128:(hd + 1) * 128],
                                         rhs=vtm.ap[:, t, hd * 256:(hd + 1) * 256], start=True, stop=True)
                        return r
                    S.add("pe", umm, reads=[kh[c].k(), vtm.k(t)], writes=[pku])
                    yield

                    def gupd(e, c=c, hp=hp, pu=pu):
                        r = None
                        for h2 in range(2):
                            hd = hp * 2 + h2
                            r = e.scalar_tensor_tensor(out=Sg.ap[:, hd * 256:(hd + 1) * 256], in0=Sg.ap[:, hd * 256:(hd + 1) * 256],
                                                       scalar=EqT.ap[:, hd, c * 64 + 63:c * 64 + 64], in1=pu[:, h2 * 256:(h2 + 1) * 256],
                                                       op0=ALU.mult, op1=ALU.add)
                        return r
                    S.add("dve", gupd, reads=[pku, EqT.k(), Sg.kr(hp * 512, hp * 512 + 512)], writes=[Sg.kr(hp * 512, hp * 512 + 512)])
                    yield
                    yield
            kh[0].free(); kh[1].free(); EqT.free()
            if not full:
                return

            junk = B16(1024)
            ss = B32(4)
            pos = []
            for hp in range(2):
                po, pko = next_pa()
                pos.append((po, pko))

                def omm(e, hp=hp, po=po):
                    r = None
                    for h2 in range(2):
                        hd = hp * 2 + h2
                        o = po[:, h2 * 256:(h2 + 1) * 256]
                        e.matmul(out=o, lhsT=attT.ap[:, hd, :], rhs=vtm.ap[:, t, hd * 256:(hd + 1) * 256], start=True, stop=False)
                        e.matmul(out=o, lhsT=qtpad[0].ap[:, hd, :], rhs=Sgb[0].ap[:, hd * 256:(hd + 1) * 256], start=False, stop=False)
                        r = e.matmul(out=o, lhsT=qtpad[1].ap[:, hd, :], rhs=Sgb[1].ap[:, hd * 256:(hd + 1) * 256], start=False, stop=True)
                    return r
                S.add("pe", omm, reads=[attT.k(), vtm.k(t), qtpad[0].k(), qtpad[1].k(), Sgb[0].k(), Sgb[1].k()], writes=[pko])
                yield

                def osq(e, hp=hp, po=po):
                    r = None
                    for h2 in range(2):
                        hd = hp * 2 + h2
                        r = e.activation(out=junk.ap[:, hd * 256:(hd + 1) * 256], in_=po[:, h2 * 256:(h2 + 1) * 256], func=AF.Square,
                                         accum_out=ss.ap[:, hd:hd + 1])
                    return r
                S.add("act", osq, reads=[pko], writes=[junk.kr(hp * 512, hp * 512 + 512), ss.k()])
                yield
                yield
            attT.free(); Sgb[0].free(); Sgb[1].free()
            chain("act", [lambda e: e.activation(out=ss.ap, in_=ss.ap, func=AF.Ln, bias=EPS, scale=1.0 / 256.0),
                          lambda e: e.activation(out=ss.ap, in_=ss.ap, func=AF.Exp, scale=-0.5)], [], [ss.k()])
            yield
            yield
            on = B32(1024)
            onb = B16(1024)
            for hp in range(2):
                po, pko = pos[hp]

                def onorm(e, hp=hp, po=po):
                    r = None
                    for h2 in range(2):
                        hd = hp * 2 + h2
                        r = e.scalar_tensor_tensor(out=on.ap[:, hd * 256:(hd + 1) * 256], in0=po[:, h2 * 256:(h2 + 1) * 256], scalar=ss.ap[:, hd:hd + 1],
                                                   in1=cvec(C_GLAN, 256), op0=ALU.mult, op1=ALU.mult)
                    return r
                S.add("dve", onorm, reads=[pko, ss.k(), cst.k()], writes=[on.kr(hp * 512, hp * 512 + 512)])
                yield
            S.add("dve", lambda e: e.tensor_tensor(out=onb.ap, in0=on.ap, in1=rs.ap[:, t, :], op=ALU.mult), reads=[on.k(), rs.k(t)], writes=[onb.k()])
            yield
            yield

            def tro(e):
                r = None
                for c in range(8):
                    r = e.transpose(out=ptr[:, c * 128:(c + 1) * 128], in_=onb.ap[:, c * 128:(c + 1) * 128], identity=idb.ap)
                return r
            S.add("pe", tro, reads=[onb.k(), idb.k()], writes=["ptr"])
            yield
            S.add("act", lambda e: e.activation(out=onT.ap[:, :, cols], in_=ptr[:, :].rearrange("p (c n) -> p c n", c=8), func=AF.Copy),
                  reads=["ptr"], writes=[onT.k()])
            yield
            junk.free(); ss.free(); on.free(); onb.free()

        def prologue_mem():
            mn = B16(2, 1024)
            for mc in range(2):
                ms = B32(1024)
                S.add("sp", lambda e, ms=ms, mc=mc: [e.dma_start(out=ms.ap, in_=mem_d[mc * 128:(mc + 1) * 128, :])], writes=[ms.k()], dma=True, dkey="ms%d" % ms.off)
                junk = B16(1024)
                ss = B32(1)
                S.add("act", lambda e, ms=ms, junk=junk, ss=ss: e.activation(out=junk.ap, in_=ms.ap, func=AF.Square, accum_out=ss.ap),
                      reads=[ms.k()], writes=[junk.k(), ss.k()])
                chain("act", [lambda e, ss=ss: e.activation(out=ss.ap, in_=ss.ap, func=AF.Ln, bias=EPS, scale=1.0 / 1024.0),
                              lambda e, ss=ss: e.activation(out=ss.ap, in_=ss.ap, func=AF.Exp, scale=-0.5)], [], [ss.k()])
                S.add("dve", lambda e, ms=ms, ss=ss, mc=mc: e.scalar_tensor_tensor(out=mn.ap[:, mc, :], in0=ms.ap, scalar=ss.ap[:, 0:1], in1=cvec(C_MEMN, 1024),
                                                                                   op0=ALU.mult, op1=ALU.mult),
                      reads=[ms.k(), ss.k(), cst.k()], writes=[mn.k(mc)])
                ms.free(); junk.free(); ss.free()
            mnT = B16(8, 256)
            for mc in range(2):
                def trm(e, mc=mc):
                    r = None
                    for c in range(8):
                        r = e.transpose(out=ptr[:, c * 128:(c + 1) * 128], in_=mn.ap[:, mc, c * 128:(c + 1) * 128], identity=idb.ap)
                    return r
                S.add("pe", trm, reads=[mn.k(mc), idb.k()], writes=["ptr"])
                S.add("act", lambda e, mc=mc: e.activation(out=mnT.ap[:, :, mc * 128:(mc + 1) * 128], in_=ptr[:, :].rearrange("p (c n) -> p c n", c=8), func=AF.Copy),
                      reads=["ptr"], writes=[mnT.k()])
            mn.free()

            def k_epi(c0, nch, pv, pk):
                S.add("act", lambda e: e.activation(out=KT.ap[:, c0:c0 + nch, :], in_=pv, func=AF.Copy), reads=[pk], writes=[KT.k(c0, c0 + nch)])
            proj_fm(w_xkv, 0, 1024, mnT, 256, k_epi, blk=256)

            def v_epi(t, col, cw, p, pk):
                S.add("act", lambda e: e.activation(out=Vx.ap[:, t, col:col + cw], in_=p, func=AF.Copy), reads=[pk],
                      writes=[Vx.kr(t * 1024 + col, t * 1024 + col + cw)])
            proj_tm(w_xkv, 1024, 1024, mnT, 2, v_epi)
            mnT.free()

        prologue_mem()
        for s_ in range(NPRE // NT):
            step(s_ * NT, NT, "p1", keepc=(s_ == NPRE // NT - 1))
            if (s_ + 1) % NSTEP == 0:
                zi = cst.ap[:, C_CMASK + (s_ + 1) // NSTEP - 1:C_CMASK + (s_ + 1) // NSTEP]

                def zs_(e, zi=zi):
                    e.tensor_scalar(out=Sst.ap, in0=Sst.ap, scalar1=zi, scalar2=None, op0=ALU.mult)
                    return e.tensor_scalar(out=Sg.ap, in0=Sg.ap, scalar1=zi, scalar2=None, op0=ALU.mult)
                S.add("dve", zs_, reads=[Sst.k(), Sg.k(), cst.k()], writes=[Sst.k(), Sg.k()])
        for s_ in range(NSTEP):
            step(NPRE + s_ * NT, NT, "p2", orow0=s_ * NT, first_dbg=(s_ == 0))
        print("arena peaks: A16 %d / %d, A32 %d / %d; ops %d" % (a16.peak, N16, a32.peak, N32, len(S.ops)))
        S.emit(nc, st)
    return nc, dbg_outs


_CACHE = {}


def host_inputs(x, mem, norm_mix, w_in, ssd_conv_w, ssd_conv_b, ssd_dt_bias, ssd_A_log, ssd_D, ssd_norm,
                gla_w_a2, gla_b_a, gla_norm, w_up_ssd, w_up_gla, w_o, norm_xattn, norm_mem, w_xq, w_xkv,
                w_xo, norm_ffn, w_ffn_in, w_ffn_out, norm_final):
    f = lambda a: np.ascontiguousarray(np.asarray(a, dtype=np.float32))
    x = f(x); mem = f(mem)

    def fm(g):
        return f(g).reshape(8, 128).T

    def rep(v):
        v = f(v).reshape(1, -1)
        return np.broadcast_to(v, (128, v.shape[1]))
    cst = np.zeros((128, NCST), np.float32)
    cst[:, C_GMIX:C_GMIX + 8] = fm(norm_mix[0])
    cst[:, C_GXA:C_GXA + 8] = fm(norm_xattn[0])
    cst[:, C_GFFN:C_GFFN + 8] = fm(norm_ffn[0])
    cst[:, C_GFIN:C_GFIN + 8] = fm(norm_final)
    cw = f(ssd_conv_w[0])[:, 0, :]
    cst[:, C_CONVW:C_CONVW + 48] = cw.reshape(4, 12, 128).transpose(2, 1, 0).reshape(128, 48)
    cst[:, C_CONVB:C_CONVB + 12] = f(ssd_conv_b[0]).reshape(12, 128).T
    cst[:, C_DTB:C_DTB + 16] = rep(ssd_dt_bias[0])
    cst[:, C_ALOG:C_ALOG + 16] = rep(ssd_A_log[0])
    cst[:, C_DSK:C_DSK + 16] = rep(ssd_D[0])
    cst[:, C_SSDN:C_SSDN + 1024] = rep(ssd_norm[0])
    cst[:, C_GLAN:C_GLAN + 256] = rep(gla_norm[0])
    cst[:, C_MEMN:C_MEMN + 1024] = rep(norm_mem[0])
    w2aug = np.concatenate([f(gla_w_a2[0]), f(gla_b_a[0]).reshape(1, 512)], 0)
    shared = {"w2aug": f(w2aug), "w_in": f(w_in[0]), "w_up_ssd": f(w_up_ssd[0]), "w_up_gla": f(w_up_gla[0]), "w_o": f(w_o[0]),
              "w_xq": f(w_xq[0]), "w_xkv": f(w_xkv[0]), "w_xo": f(w_xo[0]), "w_ffn_in": f(w_ffn_in[0]), "w_ffn_out": f(w_ffn_out[0])}
    in_maps = []
    for c in range(NCORES):
        b, j = divmod(c, 4)
        xe = np.zeros((4 * SEG, D), np.float32)
        xe[(3 - j) * SEG:3 * SEG] = x[b, 0:j * SEG]
        xe[3 * SEG:] = x[b, j * SEG:(j + 1) * SEG]
        cc = cst.copy()
        for i in range(3):
            cc[:, C_CMASK + i] = 1.0 if i >= 3 - j else 0.0
        m = {"x_ext": xe, "mem_b": mem[b], "cst": cc}
        m.update(shared)
        in_maps.append(m)
    return in_maps


def kernel(**inputs):
    if "nc" not in _CACHE:
        _CACHE["nc"] = build_program()
    nc, dbg = _CACHE["nc"]
    in_maps = host_inputs(**inputs)
    res = run_bass_kernel_spmd(nc, in_maps, core_ids=list(range(NCORES)))
    _CACHE["last"] = res
    out = np.zeros((2, 4 * SEG, D), np.float32)
    for c in range(NCORES):
        b, j = divmod(c, 4)
        out[b, j * SEG:(j + 1) * SEG] = res.results[c]["out"]
    return out
```

```python
import numpy as np
from contextlib import ExitStack
import concourse.bass as bass
import concourse.mybir as mybir
from concourse.bass_utils import run_bass_kernel_spmd

F32 = mybir.dt.float32
BF16 = mybir.dt.bfloat16
AF = mybir.ActivationFunctionType
ALU = mybir.AluOpType

NCORES = 8
D = 1024
SEG = 2048
NT = 512
HALO = 128
EPS = 1e-6
EPOCH = 8000
DBG = False

O_Z, O_XBC, O_DT, O_Q, O_K, O_V, O_R, O_A1, O_GS, O_GG = 0, 1024, 2560, 2576, 3088, 3600, 4624, 5648, 5664, 6688
DFF = 2816

C_GMIX, C_GXA, C_GFFN, C_GFIN = 0, 8, 16, 24
C_CONVW = 32
C_CONVB = 80
C_DTB, C_ALOG, C_DSK = 92, 108, 124
C_SSDN = 140
C_GLAN = 1164
C_MEMN = 1420
C_CMASK = 2444
NCST = 2452


class Op:
    __slots__ = ("eng", "fn", "deps", "sig", "signal", "is_dma", "dkey", "ndma", "name", "inc")


class Sched:
    ENGS = ("pe", "act", "dve", "pool", "sp")

    def __init__(self):
        self.ops = []
        self.recs = {}

    @staticmethod
    def _k(k):
        return (k, 0, 1) if isinstance(k, str) else k

    def add(self, eng, fn, reads=(), writes=(), dma=False, dkey=None, ndma=1, name="", inc=16):
        op = Op()
        op.eng, op.fn, op.is_dma, op.dkey, op.ndma, op.inc, op.name = eng, fn, dma, dkey, ndma, inc, name
        op.sig = False
        op.signal = None
        deps = []
        seen = set()

        def push(d, raw):
            if d is None or d is op or id(d) in seen:
                return
            if not raw and not (d.is_dma or dma or d.eng != eng):
                return
            seen.add(id(d))
            deps.append(d)

        for k in reads:
            a, lo, hi = self._k(k)
            for r in self.recs.get(a, ()):
                if r[0] < hi and lo < r[1]:
                    push(r[2], True)
                    r[3].append(op)
        for k in writes:
            a, lo, hi = self._k(k)
            lst = self.recs.setdefault(a, [])
            new = []
            for r in lst:
                if r[0] < hi and lo < r[1]:
                    push(r[2], False)
                    for rd in r[3]:
                        push(rd, False)
                    if r[0] < lo:
                        new.append([r[0], lo, r[2], list(r[3])])
                    if hi < r[1]:
                        new.append([hi, r[1], r[2], list(r[3])])
                else:
                    new.append(r)
            new.append([lo, hi, op, []])
            self.recs[a] = new
        if eng == "pe" and not dma:
            deps = [d for d in deps if d.is_dma or d.eng != "pe"]
        op.deps = deps
        for d in deps:
            d.sig = True
        self.ops.append(op)
        return op

    def emit(self, nc, stack):
        eng_count = {e: 0 for e in self.ENGS}
        eng_sems = {e: [] for e in self.ENGS}
        dma_sems = {}
        dma_vals = {}
        for op in self.ops:
            if op.is_dma:
                if op.dkey not in dma_sems:
                    dma_sems[op.dkey] = stack.enter_context(nc.semaphore("d%d" % len(dma_sems)))
                    dma_vals[op.dkey] = 0
                dma_vals[op.dkey] += op.inc * op.ndma
                op.signal = (dma_sems[op.dkey], dma_vals[op.dkey])
            elif op.sig:
                c = eng_count[op.eng]
                ep = c // EPOCH
                if ep >= len(eng_sems[op.eng]):
                    eng_sems[op.eng].append(stack.enter_context(nc.semaphore("e%s%d" % (op.eng, ep))))
                op.signal = (eng_sems[op.eng][ep], c % EPOCH + 1)
                eng_count[op.eng] = c + 1
        by_eng = {e: [o for o in self.ops if o.eng == e] for e in self.ENGS}
        finals = {}
        for op in self.ops:
            if op.is_dma and op.name.startswith("OUT"):
                sem, val = op.signal
                if finals.get(id(sem), (None, 0))[1] < val:
                    finals[id(sem)] = (sem, val)

        def run(engh, ename):
            waited = {}
            for op in by_eng[ename]:
                for d in op.deps:
                    sem, val = d.signal
                    if waited.get(id(sem), 0) < val:
                        engh.wait_ge(sem, val)
                        waited[id(sem)] = val
                r = op.fn(engh)
                if op.is_dma:
                    assert len(r) == op.ndma, (op.name, len(r), op.ndma)
                    for ins in r:
                        ins.then_inc(op.signal[0], op.inc)
                elif op.sig:
                    r.then_inc(op.signal[0], 1)
            if ename == "sp":
                for sem, val in finals.values():
                    engh.wait_ge(sem, val)

        with nc.Block() as block:
            @block.tensor
            def _(e):
                run(e, "pe")

            @block.scalar
            def _(e):
                run(e, "act")

            @block.vector
            def _(e):
                run(e, "dve")

            @block.gpsimd
            def _(e):
                run(e, "pool")

            @block.sync
            def _(e):
                run(e, "sp")


class Arena:
    def __init__(self, name, tensor, n):
        self.name, self.t, self.n = name, tensor, n
        self.used = []
        self.peak = 0

    def alloc(self, n):
        n = (n + 15) // 16 * 16
        self.used.sort()
        pos = 0
        for off, sz in self.used:
            if off - pos >= n:
                break
            pos = off + sz
        if pos + n > self.n:
            raise RuntimeError("arena %s full: need %d at %d of %d" % (self.name, n, pos, self.n))
        self.used.append((pos, n))
        self.peak = max(self.peak, pos + n)
        return pos

    def free(self, off):
        self.used = [u for u in self.used if u[0] != off]


class Buf:
    def __init__(self, arena, shape):
        self.arena = arena
        self.shape = list(shape)
        self.n = int(np.prod(shape))
        self.off = arena.alloc(self.n)
        self.inner = self.n // self.shape[0] if len(shape) == 2 else self.n

    def free(self):
        self.arena.free(self.off)

    @property
    def ap(self):
        a = self.arena.t[:, self.off:self.off + self.n]
        if len(self.shape) == 2:
            return a.rearrange("p (a b) -> p a b", a=self.shape[0])
        return a

    def k(self, i=None, j=None):
        if i is None:
            return (self.arena.name, self.off, self.off + self.n)
        if j is None:
            j = i + 1
        return (self.arena.name, self.off + i * self.inner, self.off + j * self.inner)

    def kr(self, lo, hi):
        return (self.arena.name, self.off + lo, self.off + hi)


def build_program():
    nc = bass.Bass("TRN2", target_bir_lowering=False)
    TPS = NT // 128
    NSTEP = SEG // NT

    def din(name, shape):
        return nc.dram_tensor(name, list(shape), F32, kind="ExternalInput").ap()

    NPRE = 3 * SEG
    x_d = din("x_ext", [NPRE + SEG, D])
    mem_d = din("mem_b", [256, D])
    cst_d = din("cst", [128, NCST])
    w2_d = din("w2aug", [17, 512])
    w_in = din("w_in", [D, 7712])
    w_ups = din("w_up_ssd", [D, D])
    w_upg = din("w_up_gla", [D, D])
    w_o = din("w_o", [D, D])
    w_xq = din("w_xq", [D, D])
    w_xkv = din("w_xkv", [D, 2 * D])
    w_xo = din("w_xo", [D, D])
    w_fi = din("w_ffn_in", [D, 2 * DFF])
    w_fo = din("w_ffn_out", [DFF, D])
    out_d = nc.dram_tensor("out", [SEG, D], F32, kind="ExternalOutput").ap()
    dbg_outs = {}

    S = Sched()

    def chain(eng, fns, reads, writes):
        for f in fns:
            S.add(eng, f, reads=list(reads) + list(writes), writes=writes)
    st = ExitStack()
    with st:
        def sb(name, shape, dt):
            return st.enter_context(nc.sbuf_tensor(name, list(shape), dt))

        def pst(name, shape, dt):
            return st.enter_context(nc.psum_tensor(name, list(shape), dt))

        N16 = 70000
        N32 = 18000
        a16 = Arena("A16", sb("A16", [128, N16], BF16), N16)
        a32 = Arena("A32", sb("A32", [128, N32], F32), N32)

        def B16(*shape):
            return Buf(a16, shape)

        def B32(*shape):
            return Buf(a32, shape)

        pa = [pst("pa0", [128, 512], F32), pst("pa1", [128, 512], F32)]
        ptr = pst("ptr", [128, 1024], BF16)
        psm = pst("psm", [128, 512], F32)
        pwa = pst("pwa", [128, 1024], F32)
        pwb = pst("pwb", [128, 1024], F32)
        PWA = ("pwa", 0, 2)
        PWB = ("pwb", 0, 2)
        pa_i = [0]
        pa6_i = [0]
        banks6 = [(pa[0], "pa0"), (pa[1], "pa1"), (pwa[:, 0:512], ("pwa", 0, 1)), (pwa[:, 512:1024], ("pwa", 1, 2)),
                  (pwb[:, 0:512], ("pwb", 0, 1)), (pwb[:, 512:1024], ("pwb", 1, 2))]

        def next_pa():
            i = pa_i[0]
            pa_i[0] ^= 1
            return pa[i], "pa%d" % i

        def next_pa6():
            i = pa6_i[0]
            pa6_i[0] = (i + 1) % 6
            return banks6[i]

        cst = B32(NCST)
        w2a = B32(512)
        idf = B32(128)
        tri = B32(128)
        stri = B32(128)
        trig = B32(128)
        strig = B32(128)
        strif = B32(128)
        strigf = B32(128)
        onesf = B32(128)
        negc = B32(2)
        onesc = B32(2, 128)
        cm = B32(2)
        idb = B16(128)
        ones_mean = B16(128)
        ones1 = B16(128)
        aneg = B32(16)

        S.add("sp", lambda e: [e.dma_start(out=cst.ap, in_=cst_d)], writes=[cst.k()], dma=True, dkey="cst")
        S.add("dve", lambda e: e.memset(w2a.ap, 0.0), writes=[w2a.k()])
        S.add("sp", lambda e: [e.dma_start(out=w2a.ap[0:17, :], in_=w2_d)], writes=[w2a.k()], dma=True, dkey="w2a")

        def mk_c1(e):
            e.memset(idf.ap, 0.0)
            e.memset(tri.ap, 1.0)
            e.memset(stri.ap, 1.0)
            e.memset(onesc.ap, 0.0)
            e.memset(cm.ap, 0.0)
            e.memset(ones_mean.ap, 1.0 / 1024.0)
            e.memset(strif.ap, 1.0)
            e.memset(onesf.ap, 1.0)
            e.memset(negc.ap, -1.0 / 16.0)
            return e.memset(ones1.ap, 1.0)

        def mk_c2(e):
            e.affine_select(out=strif.ap, in_=strif.ap, pattern=[[-1, 128]], compare_op=ALU.is_gt,
                            fill=0.0, base=0, channel_multiplier=1)
            e.affine_select(out=idf.ap, in_=idf.ap, pattern=[[-1, 128]], compare_op=ALU.not_equal,
                            fill=1.0, base=0, channel_multiplier=1)
            e.affine_select(out=tri.ap, in_=tri.ap, pattern=[[1, 128]], compare_op=ALU.is_ge,
                            fill=0.0, base=0, channel_multiplier=-1)
            e.affine_select(out=stri.ap, in_=stri.ap, pattern=[[-1, 128]], compare_op=ALU.is_gt,
                            fill=0.0, base=0, channel_multiplier=1)
            e.memset(onesc.ap[0:64, 0, :], 1.0)
            e.memset(onesc.ap[64:128, 1, :], 1.0)
            e.memset(cm.ap[0:64, 0:1], 1.0)
            return e.memset(cm.ap[64:128, 1:2], 1.0)

        def mk_c3(e):
            e.memset(tri.ap[0:64, 64:128], 0.0)
            return e.memset(stri.ap[64:128, 0:64], 0.0)
        chain("pool", [mk_c1, mk_c2, mk_c3], [], [idf.k(), tri.k(), stri.k(), onesc.k(), cm.k(), ones_mean.k(), ones1.k(),
                                                  strif.k(), onesf.k(), negc.k()])

        def mk_consts2(e):
            e.tensor_copy(out=idb.ap, in_=idf.ap)
            e.tensor_scalar(out=trig.ap, in0=tri.ap, scalar1=-1.0 / 16.0, scalar2=None, op0=ALU.mult)
            e.tensor_scalar(out=strigf.ap, in0=strif.ap, scalar1=-1.0 / 16.0, scalar2=None, op0=ALU.mult)
            return e.tensor_scalar(out=strig.ap, in0=stri.ap, scalar1=-1.0 / 16.0, scalar2=None, op0=ALU.mult)
        S.add("dve", mk_consts2, reads=[idf.k(), tri.k(), stri.k(), strif.k()], writes=[idb.k(), trig.k(), strig.k(), strigf.k()])

        def mk_aneg(e):
            return e.activation(out=aneg.ap, in_=cst.ap[:, C_ALOG:C_ALOG + 16], func=AF.Exp)
        S.add("act", mk_aneg, reads=[cst.k()], writes=[aneg.k()])
        S.add("dve", lambda e: e.tensor_scalar(out=aneg.ap, in0=aneg.ap, scalar1=-1.0, scalar2=None, op0=ALU.mult),
              reads=[aneg.k()], writes=[aneg.k()])

        def cvec(off, n):
            return cst.ap[:, off:off + n]

        NSLOT = 3
        WSZ = 4096
        wslots = [B16(WSZ) for _ in range(NSLOT)]
        wctr = [0]
        SSZ = 1024
        sslots = [B16(SSZ) for _ in range(2)]
        sctr = [0]

        def wload(wd, kc, c0, cw):
            if kc * cw <= SSZ:
                i = NSLOT + sctr[0] % 2
                sctr[0] += 1
                slot = sslots[i - NSLOT]
            else:
                i = wctr[0] % NSLOT
                wctr[0] += 1
                slot = wslots[i]
            assert kc * cw <= WSZ
            dst = slot.ap[:, 0:kc * cw].rearrange("p (k n) -> p k n", k=kc)
            src = wd.rearrange("(k p) n -> p k n", p=128)[:, :, c0:c0 + cw]
            if kc > 8:
                h = kc // 2
                S.add("pool", lambda e: [e.dma_start(out=dst[:, 0:h, :], in_=src[:, 0:h, :]),
                                         e.dma_start(out=dst[:, h:kc, :], in_=src[:, h:kc, :])],
                      writes=[slot.k()], dma=True, dkey="w%d" % i, ndma=2)
            else:
                S.add("pool", lambda e: [e.dma_start(out=dst, in_=src)], writes=[slot.k()], dma=True, dkey="w%d" % i)
            return dst, slot.k()

        def proj_fm(wd, c0, ncols, src, nt, epi, kc=8, blk=512):
            col = 0
            while col < ncols:
                cw = min(blk, ncols - col)
                wap, wkey = wload(wd, kc, c0 + col, cw)
                nch_all = (cw + 127) // 128
                gsz = max(1, 512 // nt)
                for cg in range(0, nch_all, gsz):
                    nch = min(gsz, nch_all - cg)
                    p, pk = next_pa6()
                    pv = p[:, 0:nch * nt].rearrange("p (c n) -> p c n", c=nch)

                    def mm(e, wap=wap, pv=pv, nch=nch, cw=cw, cg=cg):
                        r = None
                        for c in range(nch):
                            cc = cg + c
                            m = min(128, cw - cc * 128)
                            for k in range(kc):
                                r = e.matmul(out=pv[0:m, c, :], lhsT=wap[:, k, cc * 128:cc * 128 + m], rhs=src.ap[:, k, 0:nt],
                                             start=(k == 0), stop=(k == kc - 1))
                        return r
                    S.add("pe", mm, reads=[wkey, src.k()], writes=[pk])
                    epi(col // 128 + cg, nch, pv, pk)
                col += cw

        def proj_tm(wd, c0, ncols, src, ntiles, epi, kc=8):
            col = 0
            while col < ncols:
                cw = min(512, ncols - col)
                wap, wkey = wload(wd, kc, c0 + col, cw)
                for t in range(ntiles):
                    p, pk = next_pa6()

                    def mm(e, wap=wap, p=p, t=t, cw=cw):
                        r = None
                        for k in range(kc):
                            r = e.matmul(out=p[:, 0:cw], lhsT=src.ap[:, k, t * 128:(t + 1) * 128], rhs=wap[:, k, :],
                                         start=(k == 0), stop=(k == kc - 1))
                        return r
                    S.add("pe", mm, reads=[wkey, src.k()], writes=[pk])
                    epi(t, col, cw, p[:, 0:cw], pk)
                col += cw

        hT = B32(8, NT)
        nT = B16(8, NT)
        rstd = B32(NT)
        Sst = B32(1024)
        Sg = B32(1024)
        dtot = B32(20)
        halo3 = B16(12, 3)
        ctpad = [B16(2, 128), B16(2, 128)]
        qtpad = [B16(4, 128), B16(4, 128)]
        a1T = B32(NT)
        KT = B16(8, 256)
        Vx = B16(2, 1024)

        def init_state(e):
            e.memset(Sst.ap, 0.0)
            e.memset(Sg.ap, 0.0)
            e.memset(dtot.ap, 1.0)
            e.memset(ctpad[0].ap, 0.0)
            e.memset(ctpad[1].ap, 0.0)
            e.memset(qtpad[0].ap, 0.0)
            e.memset(qtpad[1].ap, 0.0)
            e.memset(halo3.ap, 0.0)
            return e.memset(a1T.ap, 0.0)
        S.add("pool", init_state, writes=[Sst.k(), Sg.k(), dtot.k(), ctpad[0].k(), ctpad[1].k(), qtpad[0].k(),
                                          qtpad[1].k(), a1T.k(), halo3.k()])
        S.add("dve", lambda e: e.memset(a1T.ap[0:32, :], 1.0), reads=[a1T.k()], writes=[a1T.k()])

        xs_i = [0]

        def load_xT(row0, nt, xreads=()):
            xsb = [B32(1024), B32(1024)]
            for t in range(nt // 128):
                j = xs_i[0]
                xs_i[0] ^= 1
                xs = xsb[j]
                pw, pwk = (pwa, PWA) if j == 0 else (pwb, PWB)
                r0 = row0 + t * 128
                S.add("sp", lambda e, xs=xs, r0=r0: [e.dma_start(out=xs.ap, in_=x_d[r0:r0 + 128, :])],
                      writes=[xs.k()], reads=list(xreads), dma=True, dkey="xs%d" % xs.off)

                def tr(e, xs=xs, pw=pw):
                    r = None
                    for c in range(8):
                        r = e.transpose(out=pw[:, c * 128:(c + 1) * 128], in_=xs.ap[:, c * 128:(c + 1) * 128], identity=idf.ap)
                    return r
                S.add("pe", tr, reads=[xs.k(), idf.k()], writes=[pwk])
                S.add("act", lambda e, t=t, pw=pw: e.activation(out=hT.ap[:, :, t * 128:(t + 1) * 128],
                                                               in_=pw[:, :].rearrange("p (c n) -> p c n", c=8), func=AF.Copy),
                      reads=[pwk], writes=[hT.k()])
            xsb[0].free(); xsb[1].free()

        def norm_fm(goff, nt, stats_only=False):
            sqb = B16(8, NT)
            S.add("act", lambda e: e.activation(out=sqb.ap[:, :, 0:nt], in_=hT.ap[:, :, 0:nt], func=AF.Square),
                  reads=[hT.k()], writes=[sqb.k()])

            def mm(e):
                r = None
                for k in range(8):
                    r = e.matmul(out=psm[:, 0:nt], lhsT=ones_mean.ap, rhs=sqb.ap[:, k, 0:nt], start=(k == 0), stop=(k == 7))
                return r
            S.add("pe", mm, reads=[sqb.k(), ones_mean.k()], writes=["psm"])
            sqb.free()
            S.add("act", lambda e: e.activation(out=rstd.ap[:, 0:nt], in_=psm[:, 0:nt], func=AF.Ln, bias=EPS), reads=["psm"], writes=[rstd.k()])
            S.add("act", lambda e: e.activation(out=rstd.ap[:, 0:nt], in_=rstd.ap[:, 0:nt], func=AF.Exp, scale=-0.5), reads=[rstd.k()], writes=[rstd.k()])
            if stats_only:
                return
            tgt = nT

            def nrm(e):
                r = None
                for k in range(8):
                    r = e.scalar_tensor_tensor(out=tgt.ap[:, k, 0:nt], in0=hT.ap[:, k, 0:nt], scalar=cst.ap[:, goff + k:goff + k + 1],
                                               in1=rstd.ap[:, 0:nt], op0=ALU.mult, op1=ALU.mult)
                return r
            S.add("dve", nrm, reads=[hT.k(), rstd.k(), cst.k()], writes=[tgt.k()])

        def resid_epi(c0, nch, pv, pk):
            S.add("dve", lambda e: e.tensor_tensor(out=hT.ap[:, c0:c0 + nch, :], in0=hT.ap[:, c0:c0 + nch, :], in1=pv, op=ALU.add),
                  reads=[pk, hT.k(c0, c0 + nch)], writes=[hT.k(c0, c0 + nch)])

        def dbg_dump(name, buf, n, dt=F32):
            if not DBG:
                return
            d = nc.dram_tensor("dbg_" + name, [128, n], dt, kind="ExternalOutput").ap()
            dbg_outs[name] = d
            flat = buf.arena.t[:, buf.off:buf.off + n]
            S.add("sp", lambda e: [e.dma_start(out=d, in_=flat)], reads=[buf.k()], dma=True, dkey="dbg_" + name, name="OUTdbg")

        def interleave(gens):
            gens = list(gens)
            while gens:
                for g in list(gens):
                    try:
                        next(g)
                    except StopIteration:
                        gens.remove(g)

        def step(row0, nt, mode, orow0=None, first_dbg=False, xreads=(), keepc=True):
            full = mode == "p2"
            tps = nt // 128
            load_xT(row0, nt, xreads)
            norm_fm(C_GMIX, nt)
            if first_dbg:
                dbg_dump("nT", nT, 8 * NT, BF16)
            dtb = B32(tps, 16)
            ab = B32(tps, 16)

            def dt_epi(t, col, cw, p, pk):
                tmp = B32(16)
                S.add("dve", lambda e: e.tensor_tensor(out=tmp.ap, in0=p, in1=cvec(C_DTB, 16), op=ALU.add),
                      reads=[pk, cst.k()], writes=[tmp.k()])

                S.add("act", lambda e: e.activation(out=tmp.ap, in_=tmp.ap, func=AF.Exp), reads=[tmp.k()], writes=[tmp.k()])
                S.add("act", lambda e: e.activation(out=dtb.ap[:, t, :], in_=tmp.ap, func=AF.Ln, bias=1.0), reads=[tmp.k()], writes=[dtb.k(t)])
                S.add("dve", lambda e: e.tensor_tensor(out=ab.ap[:, t, :], in0=dtb.ap[:, t, :], in1=aneg.ap, op=ALU.mult),
                      reads=[dtb.k(t), aneg.k()], writes=[ab.k(t)])
                tmp.free()
            proj_tm(w_in, O_DT, 16, nT, tps, dt_epi)
            xbr = B16(12, nt + 3)
            xsT = B16(8, nt)
            BT = B16(2, nt)
            CT = B16(2, nt)
            S.add("dve", lambda e: e.tensor_copy(out=xbr.ap[:, :, 0:3], in_=halo3.ap), reads=[halo3.k()], writes=[xbr.k()])

            pend = []

            def conv_pair(cs):
                caccs_ = {}
                for c in cs:
                    cacc = B32(nt)
                    caccs_[c] = cacc
                    S.add("act", lambda e, c=c, cacc=cacc: e.activation(out=cacc.ap, in_=xbr.ap[:, c, 0:nt], func=AF.Copy,
                                                                        scale=cst.ap[:, C_CONVW + 4 * c:C_CONVW + 4 * c + 1]),
                          reads=[xbr.k(c), cst.k()], writes=[cacc.k()])
                for k in (1, 2, 3):
                    for c in cs:
                        cacc = caccs_[c]
                        S.add("dve", lambda e, c=c, cacc=cacc, k=k: e.scalar_tensor_tensor(
                            out=cacc.ap, in0=xbr.ap[:, c, k:k + nt], scalar=cst.ap[:, C_CONVW + 4 * c + k:C_CONVW + 4 * c + k + 1],
                            in1=cacc.ap, op0=ALU.mult, op1=ALU.add),
                            reads=[xbr.k(c), cst.k(), cacc.k()], writes=[cacc.k()])
                for c in cs:
                    cacc = caccs_[c]
                    dstb, di = (xsT, c) if c < 8 else ((BT, c - 8) if c < 10 else (CT, c - 10))
                    S.add("act", lambda e, c=c, cacc=cacc, dstb=dstb, di=di: e.activation(
                        out=dstb.ap[:, di, :], in_=cacc.ap, func=AF.Silu, bias=cst.ap[:, C_CONVB + c:C_CONVB + c + 1]),
                        reads=[cacc.k(), cst.k()], writes=[dstb.k(di)])
                    cacc.free()

            def xbc_epi(c0, nch, pv, pk):
                S.add("act", lambda e: e.activation(out=xbr.ap[:, c0:c0 + nch, 3:3 + nt], in_=pv, func=AF.Copy),
                      reads=[pk], writes=[xbr.k(c0, c0 + nch)])
                for c in range(c0, c0 + nch):
                    if full or c < 10:
                        pend.append(c)
                if len(pend) >= 2:
                    conv_pair(list(pend))
                    del pend[:]
            proj_fm(w_in, O_XBC, 1536 if (full or keepc) else 1280, nT, nt, xbc_epi)
            if pend:
                conv_pair(list(pend))
                del pend[:]
            S.add("dve", lambda e: e.tensor_copy(out=halo3.ap, in_=xbr.ap[:, :, nt:nt + 3]), reads=[xbr.k()], writes=[halo3.k()])
            xbr.free()
            zs = None
            if full:
                zs = B16(tps, 1024)

                def z_epi(t, col, cw, p, pk):
                    S.add("act", lambda e: e.activation(out=zs.ap[:, t, col:col + cw], in_=p, func=AF.Silu),
                          reads=[pk], writes=[zs.kr(t * 1024 + col, t * 1024 + col + cw)])
                proj_tm(w_in, O_Z, 1024, nT, tps, z_epi)
            qT = B16(4, nt) if full else None
            kT = B16(4, nt) if full else None
            ktm = B16(tps, 512)
            vtm = B16(tps, 1024)
            rs = B16(tps, 1024) if full else None
            if full:
                def q_epi(c0, nch, pv, pk):
                    S.add("act", lambda e: e.activation(out=qT.ap[:, c0:c0 + nch, :], in_=pv, func=AF.Copy, scale=float(128 ** -0.5)),
                          reads=[pk], writes=[qT.k(c0, c0 + nch)])
                proj_fm(w_in, O_Q, 512, nT, nt, q_epi)
            wap, wkey = wload(w_in, 8, O_K, 512)
            if full:
                gsz = max(1, 512 // nt)
                for cg in range(0, 4, gsz):
                    p, pk = next_pa6()
                    pv = p[:, 0:gsz * nt].rearrange("p (c n) -> p c n", c=gsz)

                    def mmk(e, wap=wap, pv=pv, cg=cg, gsz=gsz):
                        r = None
                        for c in range(gsz):
                            for k in range(8):
                                r = e.matmul(out=pv[:, c, :], lhsT=wap[:, k, (cg + c) * 128:(cg + c + 1) * 128], rhs=nT.ap[:, k, 0:nt],
                                             start=(k == 0), stop=(k == 7))
                        return r
                    S.add("pe", mmk, reads=[wkey, nT.k()], writes=[pk])
                    S.add("act", lambda e, pv=pv, cg=cg, gsz=gsz: e.activation(out=kT.ap[:, cg:cg + gsz, :], in_=pv, func=AF.Copy),
                          reads=[pk], writes=[kT.k(cg, cg + gsz)])
            for t in range(tps):
                p, pk = next_pa6()

                def mmk2(e, wap=wap, p=p, t=t):
                    r = None
                    for k in range(8):
                        r = e.matmul(out=p[:, 0:512], lhsT=nT.ap[:, k, t * 128:(t + 1) * 128], rhs=wap[:, k, :], start=(k == 0), stop=(k == 7))
                    return r
                S.add("pe", mmk2, reads=[wkey, nT.k()], writes=[pk])
                S.add("act", lambda e, p=p, t=t: e.activation(out=ktm.ap[:, t, :], in_=p[:, 0:512], func=AF.Copy), reads=[pk], writes=[ktm.k(t)])

            def a1_epi(c0, nch, pv, pk):
                S.add("act", lambda e: e.activation(out=a1T.ap[0:16, 0:nt], in_=pv[0:16, 0, :], func=AF.Copy), reads=[pk], writes=[a1T.k()])
            proj_fm(w_in, O_A1, 16, nT, nt, a1_epi)

            def v_epi(t, col, cw, p, pk):
                S.add("act", lambda e: e.activation(out=vtm.ap[:, t, col:col + cw], in_=p, func=AF.Copy),
                      reads=[pk], writes=[vtm.kr(t * 1024 + col, t * 1024 + col + cw)])
            proj_tm(w_in, O_V, 1024, nT, tps, v_epi)
            if full:
                def r_epi(t, col, cw, p, pk):
                    S.add("act", lambda e: e.activation(out=rs.ap[:, t, col:col + cw], in_=p, func=AF.Silu),
                          reads=[pk], writes=[rs.kr(t * 1024 + col, t * 1024 + col + cw)])
                proj_tm(w_in, O_R, 1024, nT, tps, r_epi)
            ynT = B16(8, nt) if full else None
            onT = B16(8, nt) if full else None
            for t in range(tps):
                interleave([ssd_tile(t, xsT, BT, CT, dtb, ab, zs, ynT, full, False),
                            gla_tile(t, qT, kT, ktm, vtm, rs, onT, full, False)])
            xsT.free(); BT.free(); CT.free(); dtb.free(); ab.free()
            if zs is not None:
                zs.free()
            ktm.free(); vtm.free()
            if not full:
                return
            qT.free(); kT.free(); rs.free()
            if first_dbg:
                dbg_dump("onT", onT, 8 * NT, BF16)
            gsT = B16(8, nt)
            ggT = B16(8, nt)
            mT = B16(8, nt)

            def gs_epi(c0, nch, pv, pk):
                S.add("act", lambda e: e.activation(out=gsT.ap[:, c0:c0 + nch, :], in_=pv, func=AF.Sigmoid), reads=[pk], writes=[gsT.k(c0, c0 + nch)])

            def gg_epi(c0, nch, pv, pk):
                S.add("act", lambda e: e.activation(out=ggT.ap[:, c0:c0 + nch, :], in_=pv, func=AF.Sigmoid), reads=[pk], writes=[ggT.k(c0, c0 + nch)])
            proj_fm(w_in, O_GS, 1024, nT, nt, gs_epi)
            proj_fm(w_in, O_GG, 1024, nT, nt, gg_epi)

            def ups_epi(c0, nch, pv, pk):
                S.add("dve", lambda e: e.tensor_tensor(out=gsT.ap[:, c0:c0 + nch, :], in0=gsT.ap[:, c0:c0 + nch, :], in1=pv, op=ALU.mult),
                      reads=[pk, gsT.k(c0, c0 + nch)], writes=[gsT.k(c0, c0 + nch)])
            proj_fm(w_ups, 0, 1024, ynT, nt, ups_epi)

            def upg_epi(c0, nch, pv, pk):
                S.add("dve", lambda e: e.tensor_tensor(out=ggT.ap[:, c0:c0 + nch, :], in0=ggT.ap[:, c0:c0 + nch, :], in1=pv, op=ALU.mult),
                      reads=[pk, ggT.k(c0, c0 + nch)], writes=[ggT.k(c0, c0 + nch)])
                S.add("dve", lambda e: e.tensor_tensor(out=mT.ap[:, c0:c0 + nch, :], in0=ggT.ap[:, c0:c0 + nch, :], in1=gsT.ap[:, c0:c0 + nch, :], op=ALU.add),
                      reads=[ggT.k(c0, c0 + nch), gsT.k(c0, c0 + nch)], writes=[mT.k(c0, c0 + nch)])
            proj_fm(w_upg, 0, 1024, onT, nt, upg_epi)
            ynT.free(); onT.free(); gsT.free(); ggT.free()
            proj_fm(w_o, 0, 1024, mT, nt, resid_epi)
            mT.free()
            if first_dbg:
                dbg_dump("h1", hT, 8 * NT)
            norm_fm(C_GXA, nt)
            qx = B16(8, nt)
            ox = B16(8, nt)

            def qx_epi(c0, nch, pv, pk):
                S.add("act", lambda e: e.activation(out=qx.ap[:, c0:c0 + nch, :], in_=pv, func=AF.Copy, scale=1.0 / 16.0),
                      reads=[pk], writes=[qx.k(c0, c0 + nch)])
            proj_fm(w_xq, 0, 1024, nT, nt, qx_epi)
            for hd in range(4):
                ET = B16(2, nt)
                for mc in range(2):
                    p, pk = next_pa6()

                    def sc(e, p=p, hd=hd, mc=mc):
                        r = None
                        for dc in range(2):
                            r = e.matmul(out=p[:, 0:nt], lhsT=KT.ap[:, hd * 2 + dc, mc * 128:(mc + 1) * 128], rhs=qx.ap[:, hd * 2 + dc, :],
                                         start=(dc == 0), stop=(dc == 1))
                        return r
                    S.add("pe", sc, reads=[KT.k(), qx.k(hd * 2, hd * 2 + 2)], writes=[pk])
                    S.add("act", lambda e, p=p, mc=mc, ET=ET: e.activation(out=ET.ap[:, mc, :], in_=p[:, 0:nt], func=AF.Exp),
                          reads=[pk], writes=[ET.k(mc)])

                def den(e, ET=ET):
                    e.matmul(out=psm[:, 0:nt], lhsT=ones1.ap, rhs=ET.ap[:, 0, :], start=True, stop=False)
                    return e.matmul(out=psm[:, 0:nt], lhsT=ones1.ap, rhs=ET.ap[:, 1, :], start=False, stop=True)
                S.add("pe", den, reads=[ET.k(), ones1.k()], writes=["psm"])
                rden = B32(nt)
                S.add("dve", lambda e, rden=rden: e.reciprocal(out=rden.ap, in_=psm[:, 0:nt]), reads=["psm"], writes=[rden.k()])
                for dc in range(2):
                    p, pk = next_pa6()

                    def pvm(e, p=p, hd=hd, dc=dc, ET=ET):
                        r = None
                        for mc in range(2):
                            r = e.matmul(out=p[:, 0:nt], lhsT=Vx.ap[:, mc, hd * 256 + dc * 128:hd * 256 + dc * 128 + 128], rhs=ET.ap[:, mc, :],
                                         start=(mc == 0), stop=(mc == 1))
                        return r
                    S.add("pe", pvm, reads=[Vx.k(), ET.k()], writes=[pk])
                    S.add("dve", lambda e, p=p, hd=hd, dc=dc, rden=rden: e.tensor_tensor(out=ox.ap[:, hd * 2 + dc, :], in0=p[:, 0:nt], in1=rden.ap, op=ALU.mult),
                          reads=[pk, rden.k()], writes=[ox.k(hd * 2 + dc)])
                ET.free(); rden.free()
            qx.free()
            proj_fm(w_xo, 0, 1024, ox, nt, resid_epi)
            ox.free()
            if first_dbg:
                dbg_dump("h2", hT, 8 * NT)
            norm_fm(C_GFFN, nt)
            aT = B16(22, nt)
            col = 0
            while col < DFF:
                cw = min(512, DFF - col)

                def g_epi(c0, nch, pv, pk, col=col):
                    cc = col // 128 + c0
                    S.add("act", lambda e: e.activation(out=aT.ap[:, cc:cc + nch, :], in_=pv, func=AF.Silu), reads=[pk], writes=[aT.k(cc, cc + nch)])

                def u_epi(c0, nch, pv, pk, col=col):
                    cc = col // 128 + c0
                    S.add("dve", lambda e: e.tensor_tensor(out=aT.ap[:, cc:cc + nch, :], in0=aT.ap[:, cc:cc + nch, :], in1=pv, op=ALU.mult),
                          reads=[pk, aT.k(cc, cc + nch)], writes=[aT.k(cc, cc + nch)])
                proj_fm(w_fi, col, cw, nT, nt, g_epi)
                proj_fm(w_fi, DFF + col, cw, nT, nt, u_epi)
                col += cw
            proj_fm(w_fo, 0, 1024, aT, nt, resid_epi, kc=22, blk=128)
            aT.free()
            if first_dbg:
                dbg_dump("h3", hT, 8 * NT)
            norm_fm(C_GFIN, nt, stats_only=True)
            for t in range(tps):
                of = B32(8, 128)

                def nrm_t(e, t=t, of=of):
                    r = None
                    for k in range(8):
                        r = e.scalar_tensor_tensor(out=of.ap[:, k, :], in0=hT.ap[:, k, t * 128:(t + 1) * 128], scalar=cst.ap[:, C_GFIN + k:C_GFIN + k + 1],
                                                   in1=rstd.ap[:, t * 128:(t + 1) * 128], op0=ALU.mult, op1=ALU.mult)
                    return r
                S.add("dve", nrm_t, reads=[hT.k(), rstd.k(), cst.k()], writes=[of.k()])

                def tr(e, t=t, of=of):
                    r = None
                    for c in range(8):
                        r = e.transpose(out=pwa[:, c * 128:(c + 1) * 128], in_=of.ap[:, c, :], identity=idf.ap)
                    return r
                S.add("pe", tr, reads=[of.k(), idf.k()], writes=[PWA])
                of.free()
                og = B32(1024)
                S.add("act", lambda e, og=og: e.activation(out=og.ap, in_=pwa[:, :], func=AF.Copy), reads=[PWA], writes=[og.k()])
                r0 = orow0 + t * 128
                S.add("sp", lambda e, og=og, r0=r0: [e.dma_start(out=out_d[r0:r0 + 128, :], in_=og.ap)], reads=[og.k()],
                      dma=True, dkey="og%d" % og.off, name="OUT")
                og.free()

        def ssd_tile_p1(t, xsT, BT, dtb, ab):
            cols = slice(t * 128, (t + 1) * 128)
            a_t = ab.ap[:, t, :]
            dt_t = dtb.ap[:, t, :]
            Btm = B16(2, 128)
            Xd = B16(1024)
            ex = B32(32)
            dd = B32(16)

            def trx(e):
                r = None
                for c in range(8):
                    r = e.transpose(out=ptr[:, c * 128:(c + 1) * 128], in_=xsT.ap[:, c, cols], identity=idb.ap)
                return r

            def smalls(e):
                e.matmul(out=psm[:, 0:16], lhsT=strif.ap, rhs=a_t, start=True, stop=True)
                return e.matmul(out=psm[:, 16:32], lhsT=onesf.ap, rhs=a_t, start=True, stop=True)
            S.add("pe", smalls, reads=[ab.k(t), strif.k(), onesf.k()], writes=["psm"])
            S.add("act", lambda e: e.activation(out=ex.ap, in_=psm[:, 0:32], func=AF.Exp), reads=["psm"], writes=[ex.k()])
            yield
            S.add("dve", lambda e: e.tensor_tensor(out=dd.ap, in0=ex.ap[:, 0:16], in1=dt_t, op=ALU.mult), reads=[ex.k(), dtb.k(t)], writes=[dd.k()])
            S.add("pe", trx, reads=[xsT.k(), idb.k()], writes=["ptr"])
            yield
            S.add("dve", lambda e: e.tensor_tensor(out=Xd.ap.rearrange("p (h q) -> p h q", h=16),
                                                   in0=ptr[:, :].rearrange("p (h q) -> p h q", h=16),
                                                   in1=dd.ap.unsqueeze(2).to_broadcast([128, 16, 64]), op=ALU.mult),
                  reads=["ptr", dd.k()], writes=[Xd.k()])
            yield

            def trb(e):
                e.transpose(out=ptr[:, 0:128], in_=BT.ap[:, 0, cols], identity=idb.ap)
                return e.transpose(out=ptr[:, 128:256], in_=BT.ap[:, 1, cols], identity=idb.ap)
            S.add("pe", trb, reads=[BT.k(), idb.k()], writes=["ptr"])
            S.add("act", lambda e: e.activation(out=Btm.ap, in_=ptr[:, 0:256].rearrange("p (g n) -> p g n", g=2), func=AF.Copy),
                  reads=["ptr"], writes=[Btm.k()])
            yield

            def sloc(e):
                e.matmul(out=pwb[:, 0:512], lhsT=Btm.ap[:, 0, :], rhs=Xd.ap[:, 0:512], start=True, stop=True)
                return e.matmul(out=pwb[:, 512:1024], lhsT=Btm.ap[:, 1, :], rhs=Xd.ap[:, 512:1024], start=True, stop=True)
            S.add("pe", sloc, reads=[Btm.k(), Xd.k()], writes=[PWB])
            S.add("dve", lambda e: e.tensor_tensor(out=Sst.ap.rearrange("p (h q) -> p h q", h=16), in0=Sst.ap.rearrange("p (h q) -> p h q", h=16),
                                                   in1=ex.ap[:, 16:32].unsqueeze(2).to_broadcast([128, 16, 64]), op=ALU.mult),
                  reads=[ex.k(), Sst.k()], writes=[Sst.k()])
            yield
            S.add("dve", lambda e: e.tensor_tensor(out=Sst.ap, in0=Sst.ap, in1=pwb[:, :], op=ALU.add), reads=[PWB, Sst.k()], writes=[Sst.k()])
            yield
            Btm.free(); Xd.free(); ex.free(); dd.free()

        def ssd_tile(t, xsT, BT, CT, dtb, ab, zs, ynT, full, dbg):
            cols = slice(t * 128, (t + 1) * 128)
            a_t = ab.ap[:, t, :]
            dt_t = dtb.ap[:, t, :]
            if not full:
                yield from ssd_tile_p1(t, xsT, BT, dtb, ab)
                return
            Xtm = B16(1024)
            xstm = B16(1024) if full else None
            Btm = B16(2, 128)

            def trx(e):
                r = None
                for c in range(8):
                    r = e.transpose(out=ptr[:, c * 128:(c + 1) * 128], in_=xsT.ap[:, c, cols], identity=idb.ap)
                return r
            S.add("pe", trx, reads=[xsT.k(), idb.k()], writes=["ptr"])
            yield
            S.add("dve", lambda e: e.tensor_tensor(out=Xtm.ap.rearrange("p (h q) -> p h q", h=16),
                                                   in0=ptr[:, :].rearrange("p (h q) -> p h q", h=16),
                                                   in1=dt_t.unsqueeze(2).to_broadcast([128, 16, 64]), op=ALU.mult),
                  reads=["ptr", dtb.k(t)], writes=[Xtm.k()])
            yield
            if full:
                S.add("dve", lambda e: e.tensor_tensor(out=xstm.ap.rearrange("p (h q) -> p h q", h=16),
                                                       in0=ptr[:, :].rearrange("p (h q) -> p h q", h=16),
                                                       in1=cvec(C_DSK, 16).unsqueeze(2).to_broadcast([128, 16, 64]), op=ALU.mult),
                      reads=["ptr", cst.k()], writes=[xstm.k()])
                yield

            def trb(e):
                e.transpose(out=ptr[:, 0:128], in_=BT.ap[:, 0, cols], identity=idb.ap)
                return e.transpose(out=ptr[:, 128:256], in_=BT.ap[:, 1, cols], identity=idb.ap)
            S.add("pe", trb, reads=[BT.k(), idb.k()], writes=["ptr"])
            yield
            S.add("act", lambda e: e.activation(out=Btm.ap, in_=ptr[:, 0:256].rearrange("p (g n) -> p g n", g=2), func=AF.Copy),
                  reads=["ptr"], writes=[Btm.k()])
            yield
            def smalls(e):
                e.matmul(out=psm[:, 0:16], lhsT=tri.ap, rhs=a_t, start=True, stop=True)
                e.matmul(out=psm[:, 16:32], lhsT=stri.ap, rhs=a_t, start=True, stop=True)
                e.matmul(out=psm[:, 32:48], lhsT=onesc.ap[:, 0, :], rhs=a_t, start=True, stop=True)
                return e.matmul(out=psm[:, 48:64], lhsT=onesc.ap[:, 1, :], rhs=a_t, start=True, stop=True)
            S.add("pe", smalls, reads=[ab.k(t), tri.k(), stri.k(), onesc.k()], writes=["psm"])
            yield
            ex = B32(64)
            S.add("act", lambda e: e.activation(out=ex.ap, in_=psm[:, 0:64], func=AF.Exp), reads=["psm"], writes=[ex.k()])
            yield
            eacs = ex.ap[:, 0:16]
            if dbg:
                dbg_dump("Xtm", Xtm, 1024, BF16)
                dbg_dump("ex", ex, 64)
                dbg_dump("dtb", dtb, 16)
            ds = B32(2, 16)

            def mkds(e):
                e.tensor_scalar(out=ds.ap[:, 0, :], in0=ex.ap[:, 16:32], scalar1=cm.ap[:, 0:1], scalar2=None, op0=ALU.mult)
                return e.tensor_scalar(out=ds.ap[:, 1, :], in0=ex.ap[:, 16:32], scalar1=cm.ap[:, 1:2], scalar2=None, op0=ALU.mult)
            S.add("dve", mkds, reads=[ex.k(), cm.k()], writes=[ds.k()])
            yield
            Xd = [B16(1024), B16(1024)]
            for c in range(2):
                S.add("dve", lambda e, c=c: e.tensor_tensor(out=Xd[c].ap.rearrange("p (h q) -> p h q", h=16),
                                                            in0=Xtm.ap.rearrange("p (h q) -> p h q", h=16),
                                                            in1=ds.ap[:, c, :].unsqueeze(2).to_broadcast([128, 16, 64]), op=ALU.mult),
                      reads=[Xtm.k(), ds.k()], writes=[Xd[c].k()])
                yield
            MT = None
            if full:
                MT = B16(16, 128)
                for g in range(2):
                    rhs_all = B32(8, 128)
                    S.add("dve", lambda e, g=g, rhs_all=rhs_all: e.tensor_tensor(
                        out=rhs_all.ap, in0=tri.ap.unsqueeze(1).to_broadcast([128, 8, 128]),
                        in1=a_t[:, g * 8:(g + 1) * 8].unsqueeze(2).to_broadcast([128, 8, 128]), op=ALU.mult),
                        reads=[tri.k(), ab.k(t)], writes=[rhs_all.k()])
                    yield

                    def dmm(e, rhs_all=rhs_all):
                        e.matmul(out=pwa[:, 0:512], lhsT=stri.ap, rhs=rhs_all.ap[:, 0:4, :], start=True, stop=True)
                        return e.matmul(out=pwa[:, 512:1024], lhsT=stri.ap, rhs=rhs_all.ap[:, 4:8, :], start=True, stop=True)
                    S.add("pe", dmm, reads=[rhs_all.k(), stri.k()], writes=[PWA])
                    yield
                    E = B16(8, 128)

                    def eexp(e, E=E):
                        e.activation(out=E.ap[:, 0:4, :], in_=pwa[:, 0:512].rearrange("p (h l) -> p h l", h=4), func=AF.Exp)
                        return e.activation(out=E.ap[:, 4:8, :], in_=pwa[:, 512:1024].rearrange("p (h l) -> p h l", h=4), func=AF.Exp)
                    S.add("act", eexp, reads=[PWA], writes=[E.k()])
                    yield
                    S.add("pe", lambda e, g=g: e.matmul(out=psm[:, 128 + g * 128:256 + g * 128], lhsT=BT.ap[:, g, cols], rhs=CT.ap[:, g, cols],
                                                        start=True, stop=True), reads=[BT.k(), CT.k()], writes=["psm"])
                    yield
                    cbm = B32(128)
                    S.add("dve", lambda e, g=g, cbm=cbm: e.tensor_tensor(out=cbm.ap, in0=psm[:, 128 + g * 128:256 + g * 128], in1=tri.ap, op=ALU.mult),
                          reads=["psm", tri.k()], writes=[cbm.k()])
                    yield
                    S.add("dve", lambda e, g=g, cbm=cbm, E=E: e.tensor_tensor(out=MT.ap[:, g * 8:(g + 1) * 8, :], in0=E.ap,
                                                                               in1=cbm.ap.unsqueeze(1).to_broadcast([128, 8, 128]), op=ALU.mult),
                          reads=[E.k(), cbm.k()], writes=[MT.k(g * 8, (g + 1) * 8)])
                    yield
                    rhs_all.free(); E.free(); cbm.free()
                for c in range(2):
                    S.add("act", lambda e, c=c: e.activation(out=ctpad[c].ap[:, :, c * 64:(c + 1) * 64],
                                                             in_=CT.ap[:, :, t * 128 + c * 64:t * 128 + (c + 1) * 64], func=AF.Copy),
                          reads=[CT.k()], writes=[ctpad[c].k()])
                    yield
            Sbf = [B16(1024), B16(1024)] if full else None
            for c in range(2):
                def sloc(e, c=c):
                    e.matmul(out=pwb[:, 0:512], lhsT=Btm.ap[:, 0, :], rhs=Xd[c].ap[:, 0:512], start=True, stop=True)
                    return e.matmul(out=pwb[:, 512:1024], lhsT=Btm.ap[:, 1, :], rhs=Xd[c].ap[:, 512:1024], start=True, stop=True)
                S.add("pe", sloc, reads=[Btm.k(), Xd[c].k()], writes=[PWB])
                yield
                if full:
                    S.add("act", lambda e, c=c: e.activation(out=Sbf[c].ap, in_=Sst.ap, func=AF.Copy), reads=[Sst.k()], writes=[Sbf[c].k()])
                    yield

                def supd(e, c=c):
                    e.tensor_tensor(out=dtot.ap[:, 0:16], in0=dtot.ap[:, 0:16], in1=ex.ap[:, 32 + 16 * c:48 + 16 * c], op=ALU.mult)
                    return e.tensor_tensor(out=Sst.ap.rearrange("p (h q) -> p h q", h=16), in0=Sst.ap.rearrange("p (h q) -> p h q", h=16),
                                           in1=ex.ap[:, 32 + 16 * c:48 + 16 * c].unsqueeze(2).to_broadcast([128, 16, 64]), op=ALU.mult)
                S.add("dve", supd, reads=[ex.k(), Sst.k(), dtot.kr(0, 16)], writes=[Sst.k(), dtot.kr(0, 16)])
                yield
                S.add("dve", lambda e: e.tensor_tensor(out=Sst.ap, in0=Sst.ap, in1=pwb[:, :], op=ALU.add), reads=[PWB, Sst.k()], writes=[Sst.k()])
                yield
            Xd[0].free(); Xd[1].free(); Btm.free(); ds.free()
            if not full:
                Xtm.free(); ex.free()
                return
            def ymm(e):
                r = None
                for b in range(2):
                    e.matmul(out=pwa[:, b * 512:(b + 1) * 512], lhsT=idb.ap, rhs=xstm.ap[:, b * 512:(b + 1) * 512], start=True, stop=False)
                    for hh in range(8):
                        h = b * 8 + hh
                        r = e.matmul(out=pwa[:, h * 64:(h + 1) * 64], lhsT=MT.ap[:, h, :], rhs=Xtm.ap[:, h * 64:(h + 1) * 64],
                                     start=False, stop=(hh == 7))
                return r
            S.add("pe", ymm, reads=[idb.k(), xstm.k(), MT.k(), Xtm.k()], writes=[PWA])
            yield

            def yoff(e):
                r = None
                for g in range(2):
                    for c in range(2):
                        r = e.matmul(out=pwb[:, g * 512:(g + 1) * 512], lhsT=ctpad[c].ap[:, g, :], rhs=Sbf[c].ap[:, g * 512:(g + 1) * 512],
                                     start=(c == 0), stop=(c == 1))
                return r
            S.add("pe", yoff, reads=[ctpad[0].k(), ctpad[1].k(), Sbf[0].k(), Sbf[1].k()], writes=[PWB])
            yield
            yt = B32(1024)

            S.add("dve", lambda e: e.tensor_tensor(out=yt.ap.rearrange("p (h q) -> p h q", h=16), in0=pwb[:, :].rearrange("p (h q) -> p h q", h=16),
                                                   in1=eacs.unsqueeze(2).to_broadcast([128, 16, 64]), op=ALU.mult),
                  reads=[PWB, ex.k()], writes=[yt.k()])
            yield
            S.add("dve", lambda e: e.tensor_tensor(out=yt.ap, in0=yt.ap, in1=pwa[:, :], op=ALU.add), reads=[PWA, yt.k()], writes=[yt.k()])
            yield
            S.add("dve", lambda e: e.tensor_tensor(out=yt.ap, in0=yt.ap, in1=zs.ap[:, t, :], op=ALU.mult), reads=[yt.k(), zs.k(t)], writes=[yt.k()])
            yield
            if dbg:
                dbg_dump("yt", yt, 1024)
                dbg_dump("MT", MT, 2048, BF16)
                dbg_dump("Sbf1", Sbf[1], 1024, BF16)
            Xtm.free(); xstm.free(); MT.free(); Sbf[0].free(); Sbf[1].free(); ex.free()
            junk = B16(1024)
            ss = B32(2)

            def ysq(e):
                e.activation(out=junk.ap[:, 0:512], in_=yt.ap[:, 0:512], func=AF.Square, accum_out=ss.ap[:, 0:1])
                return e.activation(out=junk.ap[:, 512:1024], in_=yt.ap[:, 512:1024], func=AF.Square, accum_out=ss.ap[:, 1:2])
            S.add("act", ysq, reads=[yt.k()], writes=[junk.k(), ss.k()])
            yield
            chain("act", [lambda e: e.activation(out=ss.ap, in_=ss.ap, func=AF.Ln, bias=EPS, scale=1.0 / 512.0),
                          lambda e: e.activation(out=ss.ap, in_=ss.ap, func=AF.Exp, scale=-0.5)], [], [ss.k()])
            yield
            yn = B16(1024)

            def ynorm(e):
                e.scalar_tensor_tensor(out=yn.ap[:, 0:512], in0=yt.ap[:, 0:512], scalar=ss.ap[:, 0:1], in1=cvec(C_SSDN, 512), op0=ALU.mult, op1=ALU.mult)
                return e.scalar_tensor_tensor(out=yn.ap[:, 512:1024], in0=yt.ap[:, 512:1024], scalar=ss.ap[:, 1:2], in1=cvec(C_SSDN + 512, 512),
                                              op0=ALU.mult, op1=ALU.mult)
            S.add("dve", ynorm, reads=[yt.k(), ss.k(), cst.k()], writes=[yn.k()])
            yield

            def try_(e):
                r = None
                for c in range(8):
                    r = e.transpose(out=ptr[:, c * 128:(c + 1) * 128], in_=yn.ap[:, c * 128:(c + 1) * 128], identity=idb.ap)
                return r
            S.add("pe", try_, reads=[yn.k(), idb.k()], writes=["ptr"])
            yield
            S.add("act", lambda e: e.activation(out=ynT.ap[:, :, cols], in_=ptr[:, :].rearrange("p (c n) -> p c n", c=8), func=AF.Copy),
                  reads=["ptr"], writes=[ynT.k()])
            yield
            yt.free(); junk.free(); ss.free(); yn.free()

        def gla_tile(t, qT, kT, ktm, vtm, rs, onT, full, dbg=False):
            cols = slice(t * 128, (t + 1) * 128)
            p0, pk0 = next_pa()
            S.add("pe", lambda e: e.matmul(out=p0[:, 0:512], lhsT=a1T.ap[:, cols], rhs=w2a.ap, start=True, stop=True),
                  reads=[a1T.k(), w2a.k()], writes=[pk0])
            yield
            la = B32(512)

            S.add("act", lambda e: e.activation(out=la.ap, in_=p0[:, 0:512], func=AF.Exp, scale=-1.0), reads=[pk0], writes=[la.k()])
            yield
            S.add("act", lambda e: e.activation(out=la.ap, in_=la.ap, func=AF.Ln, bias=1.0), reads=[la.k()], writes=[la.k()])
            yield
            if dbg:
                dbg_dump("la", la, 512)
            if not full:
                pd, pkd = next_pa()

                def dmm_(e):
                    r = None
                    for hd in range(4):
                        r = e.matmul(out=pd[:, hd * 2:hd * 2 + 2], lhsT=la.ap[:, hd * 128:(hd + 1) * 128], rhs=negc.ap, start=True, stop=True)
                    return r
                S.add("pe", dmm_, reads=[la.k(), negc.k()], writes=[pkd])
                dec8 = B32(8)
                S.add("act", lambda e: e.activation(out=dec8.ap, in_=pd[:, 0:8], func=AF.Exp), reads=[pkd], writes=[dec8.k()])
                yield
                pe_, pke = next_pa()
                S.add("pe", lambda e: e.matmul(out=pe_[:, 0:512], lhsT=strigf.ap, rhs=la.ap, start=True, stop=True), reads=[strigf.k(), la.k()], writes=[pke])
                khf = B16(512)
                Ekf = B32(512)
                S.add("act", lambda e: e.activation(out=Ekf.ap, in_=pe_[:, 0:512], func=AF.Exp), reads=[pke], writes=[Ekf.k()])
                yield
                S.add("dve", lambda e: e.tensor_tensor(out=khf.ap, in0=Ekf.ap, in1=ktm.ap[:, t, :], op=ALU.mult), reads=[Ekf.k(), ktm.k(t)], writes=[khf.k()])
                yield
                for hp in range(2):
                    pu, pku = next_pa()

                    def umm_(e, hp=hp, pu=pu):
                        r = None
                        for h2 in range(2):
                            hd = hp * 2 + h2
                            r = e.matmul(out=pu[:, h2 * 256:(h2 + 1) * 256], lhsT=khf.ap[:, hd * 128:(hd + 1) * 128],
                                         rhs=vtm.ap[:, t, hd * 256:(hd + 1) * 256], start=True, stop=True)
                        return r
                    S.add("pe", umm_, reads=[khf.k(), vtm.k(t)], writes=[pku])

                    def gupd_(e, hp=hp, pu=pu):
                        r = None
                        for h2 in range(2):
                            hd = hp * 2 + h2
                            r = e.scalar_tensor_tensor(out=Sg.ap[:, hd * 256:(hd + 1) * 256], in0=Sg.ap[:, hd * 256:(hd + 1) * 256],
                                                       scalar=dec8.ap[:, hd * 2:hd * 2 + 1], in1=pu[:, h2 * 256:(h2 + 1) * 256],
                                                       op0=ALU.mult, op1=ALU.add)
                        return r
                    S.add("dve", gupd_, reads=[pku, dec8.k(), Sg.kr(hp * 512, hp * 512 + 512)], writes=[Sg.kr(hp * 512, hp * 512 + 512)])
                    yield
                la.free(); dec8.free(); khf.free(); Ekf.free()
                return
            p1, pk1 = next_pa()

            def bc(e):
                r = None
                for hd in range(4):
                    r = e.matmul(out=p1[:, hd * 128:(hd + 1) * 128], lhsT=la.ap[:, hd * 128:(hd + 1) * 128], rhs=trig.ap, start=True, stop=True)
                return r
            S.add("pe", bc, reads=[la.k(), trig.k()], writes=[pk1])
            yield
            EqT = B32(4, 128)
            S.add("act", lambda e: e.activation(out=EqT.ap, in_=p1[:, 0:512].rearrange("p (h l) -> p h l", h=4), func=AF.Exp),
                  reads=[pk1], writes=[EqT.k()])
            yield
            ktT = None
            if full:
                EkT = B32(4, 128)
                S.add("act", lambda e: e.activation(out=EkT.ap, in_=p1[:, 0:512].rearrange("p (h l) -> p h l", h=4), func=AF.Exp, scale=-1.0),
                      reads=[pk1], writes=[EkT.k()])
                yield
                ktT = B16(4, 128)
                S.add("dve", lambda e: e.tensor_tensor(out=ktT.ap, in0=kT.ap[:, :, cols], in1=EkT.ap, op=ALU.mult),
                      reads=[kT.k(), EkT.k()], writes=[ktT.k()])
                yield
                for c in range(2):
                    S.add("dve", lambda e, c=c: e.tensor_tensor(out=qtpad[c].ap[:, :, c * 64:(c + 1) * 64],
                                                                in0=qT.ap[:, :, t * 128 + c * 64:t * 128 + (c + 1) * 64],
                                                                in1=EqT.ap[:, :, c * 64:(c + 1) * 64], op=ALU.mult),
                          reads=[qT.k(), EqT.k()], writes=[qtpad[c].k()])
                    yield
                EkT.free()
            p2, pk2 = next_pa()
            S.add("pe", lambda e: e.matmul(out=p2[:, 0:512], lhsT=strig.ap, rhs=la.ap, start=True, stop=True), reads=[strig.k(), la.k()], writes=[pk2])
            yield
            Ekh = B32(512)
            S.add("act", lambda e: e.activation(out=Ekh.ap, in_=p2[:, 0:512], func=AF.Exp), reads=[pk2], writes=[Ekh.k()])
            yield
            kh = [B16(512), B16(512)]
            for c in range(2):
                S.add("dve", lambda e, c=c: e.scalar_tensor_tensor(out=kh[c].ap, in0=Ekh.ap, scalar=cm.ap[:, c:c + 1], in1=ktm.ap[:, t, :],
                                                                   op0=ALU.mult, op1=ALU.mult),
                      reads=[Ekh.k(), cm.k(), ktm.k(t)], writes=[kh[c].k()])
                yield
            la.free(); Ekh.free()
            attT = None
            if full:
                p3, pk3 = next_pa()

                def att(e):
                    r = None
                    for hd in range(4):
                        e.matmul(out=p3[:, hd * 128:(hd + 1) * 128], lhsT=ktT.ap[:, hd, :], rhs=qtpad[0].ap[:, hd, :], start=True, stop=False)
                        r = e.matmul(out=p3[:, hd * 128:(hd + 1) * 128], lhsT=ktT.ap[:, hd, :], rhs=qtpad[1].ap[:, hd, :], start=False, stop=True)
                    return r
                S.add("pe", att, reads=[ktT.k(), qtpad[0].k(), qtpad[1].k()], writes=[pk3])
                yield
                attT = B16(4, 128)
                S.add("dve", lambda e: e.tensor_tensor(out=attT.ap, in0=p3[:, 0:512].rearrange("p (h l) -> p h l", h=4),
                                                       in1=tri.ap.unsqueeze(1).to_broadcast([128, 4, 128]), op=ALU.mult),
                      reads=[pk3, tri.k()], writes=[attT.k()])
                yield
                ktT.free()
            Sgb = [B16(1024), B16(1024)] if full else None
            for c in range(2):
                if full:
                    S.add("act", lambda e, c=c: e.activation(out=Sgb[c].ap, in_=Sg.ap, func=AF.Copy), reads=[Sg.k()], writes=[Sgb[c].k()])
                    yield
                for hp in range(2):
                    pu, pku = next_pa()

                    def umm(e, c=c, hp=hp, pu=pu):
                        r = None
                        for h2 in range(2):
                            hd = hp * 2 + h2
                            r = e.matmul(out=pu[:, h2 * 256:(h2 + 1) * 256], lhsT=kh[c].ap[:, hd * 128:(hd + 1) * 128],
                                         rhs=vtm.ap[:, t, hd * 256:(hd + 1) * 256], start=True, stop=True)
                        return r
                    S.add("pe", umm, reads=[kh[c].k(), vtm.k(t)], writes=[pku])
                    yield

                    def gupd(e, c=c, hp=hp, pu=pu):
                        r = None
                        for h2 in range(2):
                            hd = hp * 2 + h2
                            r = e.scalar_tensor_tensor(out=Sg.ap[:, hd * 256:(hd + 1) * 256], in0=Sg.ap[:, hd * 256:(hd + 1) * 256],
                                                       scalar=EqT.ap[:, hd, c * 64 + 63:c * 64 + 64], in1=pu[:, h2 * 256:(h2 + 1) * 256],
                                                       op0=ALU.mult, op1=ALU.add)
                        return r
                    S.add("dve", gupd, reads=[pku, EqT.k(), Sg.kr(hp * 512, hp * 512 + 512)], writes=[Sg.kr(hp * 512, hp * 512 + 512)])
                    yield
                    yield
            kh[0].free(); kh[1].free(); EqT.free()
            if not full:
                return

            junk = B16(1024)
            ss = B32(4)
            pos = []
            for hp in range(2):
                po, pko = next_pa()
                pos.append((po, pko))

                def omm(e, hp=hp, po=po):
                    r = None
                    for h2 in range(2):
                        hd = hp * 2 + h2
                        o = po[:, h2 * 256:(h2 + 1) * 256]
                        e.matmul(out=o, lhsT=attT.ap[:, hd, :], rhs=vtm.ap[:, t, hd * 256:(hd + 1) * 256], start=True, stop=False)
                        e.matmul(out=o, lhsT=qtpad[0].ap[:, hd, :], rhs=Sgb[0].ap[:, hd * 256:(hd + 1) * 256], start=False, stop=False)
                        r = e.matmul(out=o, lhsT=qtpad[1].ap[:, hd, :], rhs=Sgb[1].ap[:, hd * 256:(hd + 1) * 256], start=False, stop=True)
                    return r
                S.add("pe", omm, reads=[attT.k(), vtm.k(t), qtpad[0].k(), qtpad[1].k(), Sgb[0].k(), Sgb[1].k()], writes=[pko])
                yield

                def osq(e, hp=hp, po=po):
                    r = None
                    for h2 in range(2):
                        hd = hp * 2 + h2
                        r = e.activation(out=junk.ap[:, hd * 256:(hd + 1) * 256], in_=po[:, h2 * 256:(h2 + 1) * 256], func=AF.Square,
                                         accum_out=ss.ap[:, hd:hd + 1])
                    return r
                S.add("act", osq, reads=[pko], writes=[junk.kr(hp * 512, hp * 512 + 512), ss.k()])
                yield
                yield
            attT.free(); Sgb[0].free(); Sgb[1].free()
            chain("act", [lambda e: e.activation(out=ss.ap, in_=ss.ap, func=AF.Ln, bias=EPS, scale=1.0 / 256.0),
                          lambda e: e.activation(out=ss.ap, in_=ss.ap, func=AF.Exp, scale=-0.5)], [], [ss.k()])
            yield
            yield
            on = B32(1024)
            onb = B16(1024)
            for hp in range(2):
                po, pko = pos[hp]

                def onorm(e, hp=hp, po=po):
                    r = None
                    for h2 in range(2):
                        hd = hp * 2 + h2
                        r = e.scalar_tensor_tensor(out=on.ap[:, hd * 256:(hd + 1) * 256], in0=po[:, h2 * 256:(h2 + 1) * 256], scalar=ss.ap[:, hd:hd + 1],
                                                   in1=cvec(C_GLAN, 256), op0=ALU.mult, op1=ALU.mult)
                    return r
                S.add("dve", onorm, reads=[pko, ss.k(), cst.k()], writes=[on.kr(hp * 512, hp * 512 + 512)])
                yield
            S.add("dve", lambda e: e.tensor_tensor(out=onb.ap, in0=on.ap, in1=rs.ap[:, t, :], op=ALU.mult), reads=[on.k(), rs.k(t)], writes=[onb.k()])
            yield
            yield

            def tro(e):
                r = None
                for c in range(8):
                    r = e.transpose(out=ptr[:, c * 128:(c + 1) * 128], in_=onb.ap[:, c * 128:(c + 1) * 128], identity=idb.ap)
                return r
            S.add("pe", tro, reads=[onb.k(), idb.k()], writes=["ptr"])
            yield
            S.add("act", lambda e: e.activation(out=onT.ap[:, :, cols], in_=ptr[:, :].rearrange("p (c n) -> p c n", c=8), func=AF.Copy),
                  reads=["ptr"], writes=[onT.k()])
            yield
            junk.free(); ss.free(); on.free(); onb.free()

        def prologue_mem():
            mn = B16(2, 1024)
            for mc in range(2):
                ms = B32(1024)
                S.add("sp", lambda e, ms=ms, mc=mc: [e.dma_start(out=ms.ap, in_=mem_d[mc * 128:(mc + 1) * 128, :])], writes=[ms.k()], dma=True, dkey="ms%d" % ms.off)
                junk = B16(1024)
                ss = B32(1)
                S.add("act", lambda e, ms=ms, junk=junk, ss=ss: e.activation(out=junk.ap, in_=ms.ap, func=AF.Square, accum_out=ss.ap),
                      reads=[ms.k()], writes=[junk.k(), ss.k()])
                chain("act", [lambda e, ss=ss: e.activation(out=ss.ap, in_=ss.ap, func=AF.Ln, bias=EPS, scale=1.0 / 1024.0),
                              lambda e, ss=ss: e.activation(out=ss.ap, in_=ss.ap, func=AF.Exp, scale=-0.5)], [], [ss.k()])
                S.add("dve", lambda e, ms=ms, ss=ss, mc=mc: e.scalar_tensor_tensor(out=mn.ap[:, mc, :], in0=ms.ap, scalar=ss.ap[:, 0:1], in1=cvec(C_MEMN, 1024),
                                                                                   op0=ALU.mult, op1=ALU.mult),
                      reads=[ms.k(), ss.k(), cst.k()], writes=[mn.k(mc)])
                ms.free(); junk.free(); ss.free()
            mnT = B16(8, 256)
            for mc in range(2):
                def trm(e, mc=mc):
                    r = None
                    for c in range(8):
                        r = e.transpose(out=ptr[:, c * 128:(c + 1) * 128], in_=mn.ap[:, mc, c * 128:(c + 1) * 128], identity=idb.ap)
                    return r
                S.add("pe", trm, reads=[mn.k(mc), idb.k()], writes=["ptr"])
                S.add("act", lambda e, mc=mc: e.activation(out=mnT.ap[:, :, mc * 128:(mc + 1) * 128], in_=ptr[:, :].rearrange("p (c n) -> p c n", c=8), func=AF.Copy),
                      reads=["ptr"], writes=[mnT.k()])
            mn.free()

            def k_epi(c0, nch, pv, pk):
                S.add("act", lambda e: e.activation(out=KT.ap[:, c0:c0 + nch, :], in_=pv, func=AF.Copy), reads=[pk], writes=[KT.k(c0, c0 + nch)])
            proj_fm(w_xkv, 0, 1024, mnT, 256, k_epi, blk=256)

            def v_epi(t, col, cw, p, pk):
                S.add("act", lambda e: e.activation(out=Vx.ap[:, t, col:col + cw], in_=p, func=AF.Copy), reads=[pk],
                      writes=[Vx.kr(t * 1024 + col, t * 1024 + col + cw)])
            proj_tm(w_xkv, 1024, 1024, mnT, 2, v_epi)
            mnT.free()

        prologue_mem()
        for s_ in range(NPRE // NT):
            step(s_ * NT, NT, "p1", keepc=(s_ == NPRE // NT - 1))
            if (s_ + 1) % NSTEP == 0:
                zi = cst.ap[:, C_CMASK + (s_ + 1) // NSTEP - 1:C_CMASK + (s_ + 1) // NSTEP]

                def zs_(e, zi=zi):
                    e.tensor_scalar(out=Sst.ap, in0=Sst.ap, scalar1=zi, scalar2=None, op0=ALU.mult)
                    return e.tensor_scalar(out=Sg.ap, in0=Sg.ap, scalar1=zi, scalar2=None, op0=ALU.mult)
                S.add("dve", zs_, reads=[Sst.k(), Sg.k(), cst.k()], writes=[Sst.k(), Sg.k()])
        for s_ in range(NSTEP):
            step(NPRE + s_ * NT, NT, "p2", orow0=s_ * NT, first_dbg=(s_ == 0))
        print("arena peaks: A16 %d / %d, A32 %d / %d; ops %d" % (a16.peak, N16, a32.peak, N32, len(S.ops)))
        S.emit(nc, st)
    return nc, dbg_outs


_CACHE = {}


def host_inputs(x, mem, norm_mix, w_in, ssd_conv_w, ssd_conv_b, ssd_dt_bias, ssd_A_log, ssd_D, ssd_norm,
                gla_w_a2, gla_b_a, gla_norm, w_up_ssd, w_up_gla, w_o, norm_xattn, norm_mem, w_xq, w_xkv,
                w_xo, norm_ffn, w_ffn_in, w_ffn_out, norm_final):
    f = lambda a: np.ascontiguousarray(np.asarray(a, dtype=np.float32))
    x = f(x); mem = f(mem)

    def fm(g):
        return f(g).reshape(8, 128).T

    def rep(v):
        v = f(v).reshape(1, -1)
        return np.broadcast_to(v, (128, v.shape[1]))
    cst = np.zeros((128, NCST), np.float32)
    cst[:, C_GMIX:C_GMIX + 8] = fm(norm_mix[0])
    cst[:, C_GXA:C_GXA + 8] = fm(norm_xattn[0])
    cst[:, C_GFFN:C_GFFN + 8] = fm(norm_ffn[0])
    cst[:, C_GFIN:C_GFIN + 8] = fm(norm_final)
    cw = f(ssd_conv_w[0])[:, 0, :]
    cst[:, C_CONVW:C_CONVW + 48] = cw.reshape(4, 12, 128).transpose(2, 1, 0).reshape(128, 48)
    cst[:, C_CONVB:C_CONVB + 12] = f(ssd_conv_b[0]).reshape(12, 128).T
    cst[:, C_DTB:C_DTB + 16] = rep(ssd_dt_bias[0])
    cst[:, C_ALOG:C_ALOG + 16] = rep(ssd_A_log[0])
    cst[:, C_DSK:C_DSK + 16] = rep(ssd_D[0])
    cst[:, C_SSDN:C_SSDN + 1024] = rep(ssd_norm[0])
    cst[:, C_GLAN:C_GLAN + 256] = rep(gla_norm[0])
    cst[:, C_MEMN:C_MEMN + 1024] = rep(norm_mem[0])
    w2aug = np.concatenate([f(gla_w_a2[0]), f(gla_b_a[0]).reshape(1, 512)], 0)
    shared = {"w2aug": f(w2aug), "w_in": f(w_in[0]), "w_up_ssd": f(w_up_ssd[0]), "w_up_gla": f(w_up_gla[0]), "w_o": f(w_o[0]),
              "w_xq": f(w_xq[0]), "w_xkv": f(w_xkv[0]), "w_xo": f(w_xo[0]), "w_ffn_in": f(w_ffn_in[0]), "w_ffn_out": f(w_ffn_out[0])}
    in_maps = []
    for c in range(NCORES):
        b, j = divmod(c, 4)
        xe = np.zeros((4 * SEG, D), np.float32)
        xe[(3 - j) * SEG:3 * SEG] = x[b, 0:j * SEG]
        xe[3 * SEG:] = x[b, j * SEG:(j + 1) * SEG]
        cc = cst.copy()
        for i in range(3):
            cc[:, C_CMASK + i] = 1.0 if i >= 3 - j else 0.0
        m = {"x_ext": xe, "mem_b": mem[b], "cst": cc}
        m.update(shared)
        in_maps.append(m)
    return in_maps


def kernel(**inputs):
    if "nc" not in _CACHE:
        _CACHE["nc"] = build_program()
    nc, dbg = _CACHE["nc"]
    in_maps = host_inputs(**inputs)
    res = run_bass_kernel_spmd(nc, in_maps, core_ids=list(range(NCORES)))
    _CACHE["last"] = res
    out = np.zeros((2, 4 * SEG, D), np.float32)
    for c in range(NCORES):
        b, j = divmod(c, 4)
        out[b, j * SEG:(j + 1) * SEG] = res.results[c]["out"]
    return out
```
